# Optimizing a Trainium2 kernel written in Bass

```python
import math
import jax, jax.numpy as jnp
from jax import lax
import numpy as np

D_MODEL = 1024
BATCH = 8
SEQ = 2048
DEPTH = 1

MEM_TOKENS = 256
RET_HEADS = 4
RET_DK = 64
RET_DV = 128
RET_CHUNK = 128
DSA_HEADS = 8
DSA_DH = 64
IDX_HEADS = 8
IDX_DIM = 64
TOPK_MAX = 256
Q_BLOCK = 128
MEM_HEADS = 4
MEM_DH = 128
T5_BUCKETS = 32
T5_MAX_DIST = 128
D_FF = 2816
ROPE_BASE = 10000.0
LN_EPS = 1e-5
NEG_INF = -1e30

RET_QK_W = RET_HEADS * RET_DK
RET_V_W = RET_HEADS * RET_DV
DSA_W = DSA_HEADS * DSA_DH
IDX_Q_W = IDX_HEADS * IDX_DIM
MEM_W = MEM_HEADS * MEM_DH
N_BRANCH = 3
W_IN_COLS = 2 * RET_QK_W + 2 * RET_V_W + 3 * DSA_W + IDX_Q_W + IDX_DIM + IDX_HEADS + MEM_W + N_BRANCH * D_MODEL
DEEPNORM_ALPHA = (2.0 * DEPTH) ** 0.25
DEEPNORM_BETA = (8.0 * DEPTH) ** -0.25

kernel_name = "hybrid_retention_dsa_memory_macaron_deepnorm"


def _split_points():
    widths = [RET_QK_W, RET_QK_W, RET_V_W, RET_V_W,
              DSA_W, DSA_W, DSA_W,
              IDX_Q_W, IDX_DIM, IDX_HEADS,
              MEM_W,
              N_BRANCH * D_MODEL]
    pts, acc = [], 0
    for w in widths[:-1]:
        acc += w
        pts.append(acc)
    return pts


def layer_norm(x, g, b):
    xf = x.astype(jnp.float32)
    mu = jnp.mean(xf, axis=-1, keepdims=True)
    var = jnp.mean(jnp.square(xf - mu), axis=-1, keepdims=True)
    return ((xf - mu) * lax.rsqrt(var + LN_EPS) * g + b).astype(x.dtype)


def swiglu_ffn(x, w_in, w_out):
    a, u = jnp.split(x @ w_in, 2, axis=-1)
    return (jax.nn.silu(a) * u) @ w_out


def rope(x, pos):
    half = x.shape[-1] // 2
    freqs = ROPE_BASE ** (-jnp.arange(half, dtype=jnp.float32) / half)
    ang = pos.astype(jnp.float32)[:, None] * freqs[None, :]
    cos = jnp.cos(ang)[None, :, None, :].astype(x.dtype)
    sin = jnp.sin(ang)[None, :, None, :].astype(x.dtype)
    x1, x2 = x[..., :half], x[..., half:]
    return jnp.concatenate([x1 * cos - x2 * sin, x1 * sin + x2 * cos], axis=-1)


def t5_bucket(n):
    n = jnp.maximum(n, 0)
    max_exact = T5_BUCKETS // 2
    nf = jnp.maximum(n, 1).astype(jnp.float32)
    large = max_exact + (jnp.log(nf / max_exact) / math.log(T5_MAX_DIST / max_exact)
                         * (T5_BUCKETS - max_exact)).astype(jnp.int32)
    large = jnp.minimum(large, T5_BUCKETS - 1)
    return jnp.where(n < max_exact, n, large)


def retention(q, k, v, g, gn_g, gn_b):
    B, L = q.shape[0], q.shape[1]
    C = RET_CHUNK
    NC = L // C
    dt = q.dtype
    gamma = 1.0 - 2.0 ** (-5.0 - jnp.arange(RET_HEADS, dtype=jnp.float32))
    lg = jnp.log(gamma)
    i = jnp.arange(C)
    diff = i[:, None] - i[None, :]
    decay_in = jnp.where(diff[None] >= 0, jnp.exp(jnp.maximum(diff, 0)[None] * lg[:, None, None]), 0.0).astype(dt)
    k_dec = jnp.exp((C - 1 - i)[None, :] * lg[:, None]).astype(dt)
    q_dec = jnp.exp((i + 1)[None, :] * lg[:, None]).astype(dt)
    chunk_dec = jnp.exp(C * lg).astype(dt)

    qc = q.reshape(B, NC, C, RET_HEADS, RET_DK)
    kc = k.reshape(B, NC, C, RET_HEADS, RET_DK)
    vc = v.reshape(B, NC, C, RET_HEADS, RET_DV)

    scores = jnp.einsum('bnihd,bnjhd->bnhij', qc, kc) * decay_in[None, None]
    intra = jnp.einsum('bnhij,bnjhe->bnihe', scores, vc)

    kv = jnp.einsum('bnjhd,hj,bnjhe->bnhde', kc, k_dec, vc)

    def step(state, kv_n):
        return chunk_dec[None, :, None, None] * state + kv_n, state

    init = jnp.zeros((B, RET_HEADS, RET_DK, RET_DV), dt)
    _, r_prev = lax.scan(step, init, jnp.moveaxis(kv, 1, 0))
    r_prev = jnp.moveaxis(r_prev, 0, 1)
    cross = jnp.einsum('bnihd,hi,bnhde->bnihe', qc, q_dec, r_prev)

    o = (intra + cross).reshape(B, L, RET_HEADS, RET_DV).astype(jnp.float32)
    mu = jnp.mean(o, axis=-1, keepdims=True)
    var = jnp.mean(jnp.square(o - mu), axis=-1, keepdims=True)
    o = ((o - mu) * lax.rsqrt(var + LN_EPS)).reshape(B, L, RET_V_W) * gn_g + gn_b
    return jax.nn.silu(g) * o.astype(dt)


def dsa_attention(q, k, v, qi, ki, wi, t5_table):
    B, L = q.shape[0], q.shape[1]
    topk = min(TOPK_MAX, L // 4)
    NB = L // Q_BLOCK
    key_pos = jnp.arange(L)
    scale = DSA_DH ** -0.5
    gather = jax.vmap(lambda a, ix: a[ix])

    def to_blocks(a):
        return jnp.moveaxis(a.reshape(B, NB, Q_BLOCK, *a.shape[2:]), 1, 0)

    def block(args):
        n, qb, qib, wib = args
        t = n * Q_BLOCK + jnp.arange(Q_BLOCK)
        rel = jax.nn.relu(jnp.einsum('bqhd,bsd->bqhs', qib, ki).astype(jnp.float32))
        score = jnp.einsum('bqhs,bqh->bqs', rel, wib.astype(jnp.float32))
        causal = key_pos[None, :] <= t[:, None]
        score = jnp.where(causal[None], score, NEG_INF)
        _, idx = lax.top_k(score, topk)
        k_sel = gather(k, idx)
        v_sel = gather(v, idx)
        logits = jnp.einsum('bqhd,bqkhd->bhqk', qb, k_sel).astype(jnp.float32) * scale
        bias = jnp.transpose(t5_table[t5_bucket(t[None, :, None] - idx)], (0, 3, 1, 2))
        valid = (idx <= t[None, :, None])[:, None]
        logits = jnp.where(valid, logits + bias, NEG_INF)
        p = jax.nn.softmax(logits, axis=-1).astype(v.dtype)
        return jnp.einsum('bhqk,bqkhd->bqhd', p, v_sel)

    out = lax.map(block, (jnp.arange(NB), to_blocks(q), to_blocks(qi), to_blocks(wi)))
    return jnp.moveaxis(out, 0, 1).reshape(B, L, DSA_W)


def memory_attention(q, mem, w_kv):
    B, M = mem.shape[0], mem.shape[1]
    mk, mv = jnp.split(mem @ w_kv, 2, axis=-1)
    mk = mk.reshape(B, M, MEM_HEADS, MEM_DH)
    mv = mv.reshape(B, M, MEM_HEADS, MEM_DH)
    logits = jnp.einsum('bqhd,bmhd->bhqm', q, mk).astype(jnp.float32) * MEM_DH ** -0.5
    p = jax.nn.softmax(logits, axis=-1).astype(mv.dtype)
    out = jnp.einsum('bhqm,bmhd->bqhd', p, mv)
    return out.reshape(B, q.shape[1], MEM_W)


def hybrid_mixer(h, mem, w_in, t5_table, ret_gn_g, ret_gn_b, w_mem_kv,
                 w_br_ret, w_br_dsa, w_br_mem, w_out):
    B, L, _ = h.shape
    pos = jnp.arange(L)
    parts = jnp.split(h @ w_in, _split_points(), axis=-1)
    rq, rk, rv, rg, dq, dk, dv, iq, ik, iw, mq, gates = parts

    rq = rope(rq.reshape(B, L, RET_HEADS, RET_DK), pos)
    rk = rope(rk.reshape(B, L, RET_HEADS, RET_DK), pos) * (RET_DK ** -0.5)
    o_ret = retention(rq, rk, rv, rg, ret_gn_g, ret_gn_b)

    iw = iw * (IDX_HEADS ** -0.5 * IDX_DIM ** -0.5)
    o_dsa = dsa_attention(dq.reshape(B, L, DSA_HEADS, DSA_DH),
                          dk.reshape(B, L, DSA_HEADS, DSA_DH),
                          dv.reshape(B, L, DSA_HEADS, DSA_DH),
                          iq.reshape(B, L, IDX_HEADS, IDX_DIM), ik, iw, t5_table)

    o_mem = memory_attention(mq.reshape(B, L, MEM_HEADS, MEM_DH), mem, w_mem_kv)

    g_ret, g_dsa, g_mem = jnp.split(jax.nn.sigmoid(gates), N_BRANCH, axis=-1)
    merged = g_ret * (o_ret @ w_br_ret) + g_dsa * (o_dsa @ w_br_dsa) + g_mem * (o_mem @ w_br_mem)
    return merged @ w_out


def setup_inputs(seed: int = 0) -> dict:
    key = jax.random.key(seed)
    ks = jax.random.split(key, 26)
    f32 = jnp.float32

    def w(k, shape, fan_in, scale=1.0):
        return jax.random.normal(k, shape, f32) * (fan_in ** -0.5) * scale

    def gain(k, shape):
        return 1.0 + 0.02 * jax.random.normal(k, shape, f32)

    def bias(k, shape):
        return 0.02 * jax.random.normal(k, shape, f32)

    return {
        "x": jax.random.normal(ks[0], (BATCH, SEQ, D_MODEL), f32),
        "mem": jax.random.normal(ks[1], (BATCH, MEM_TOKENS, D_MODEL), f32),
        "ffn1_w_in": w(ks[2], (DEPTH, D_MODEL, 2 * D_FF), D_MODEL),
        "ffn1_w_out": w(ks[3], (DEPTH, D_FF, D_MODEL), D_FF, DEEPNORM_BETA),
        "ln1_g": gain(ks[4], (DEPTH, D_MODEL)),
        "ln1_b": bias(ks[5], (DEPTH, D_MODEL)),
        "w_in": w(ks[6], (DEPTH, D_MODEL, W_IN_COLS), D_MODEL),
        "t5_table": 0.5 * jax.random.normal(ks[7], (T5_BUCKETS, DSA_HEADS), f32),
        "ret_gn_g": gain(ks[8], (DEPTH, RET_V_W)),
        "ret_gn_b": bias(ks[9], (DEPTH, RET_V_W)),
        "w_mem_kv": w(ks[10], (DEPTH, D_MODEL, 2 * MEM_W), D_MODEL),
        "w_br_ret": w(ks[11], (DEPTH, RET_V_W, D_MODEL), RET_V_W, DEEPNORM_BETA),
        "w_br_dsa": w(ks[12], (DEPTH, DSA_W, D_MODEL), DSA_W, DEEPNORM_BETA),
        "w_br_mem": w(ks[13], (DEPTH, MEM_W, D_MODEL), MEM_W, DEEPNORM_BETA),
        "w_out": w(ks[14], (DEPTH, D_MODEL, D_MODEL), D_MODEL, DEEPNORM_BETA),
        "ln2_g": gain(ks[15], (DEPTH, D_MODEL)),
        "ln2_b": bias(ks[16], (DEPTH, D_MODEL)),
        "ffn2_w_in": w(ks[17], (DEPTH, D_MODEL, 2 * D_FF), D_MODEL),
        "ffn2_w_out": w(ks[18], (DEPTH, D_FF, D_MODEL), D_FF, DEEPNORM_BETA),
        "ln3_g": gain(ks[19], (DEPTH, D_MODEL)),
        "ln3_b": bias(ks[20], (DEPTH, D_MODEL)),
    }


def reference(x, mem, ffn1_w_in, ffn1_w_out, ln1_g, ln1_b, w_in, t5_table, ret_gn_g, ret_gn_b,
              w_mem_kv, w_br_ret, w_br_dsa, w_br_mem, w_out, ln2_g, ln2_b,
              ffn2_w_in, ffn2_w_out, ln3_g, ln3_b):
    for l in range(DEPTH):
        x = layer_norm(DEEPNORM_ALPHA * x + 0.5 * swiglu_ffn(x, ffn1_w_in[l], ffn1_w_out[l]),
                       ln1_g[l], ln1_b[l])
        mix = hybrid_mixer(x, mem, w_in[l], t5_table, ret_gn_g[l], ret_gn_b[l], w_mem_kv[l],
                           w_br_ret[l], w_br_dsa[l], w_br_mem[l], w_out[l])
        x = layer_norm(DEEPNORM_ALPHA * x + mix, ln2_g[l], ln2_b[l])
        x = layer_norm(DEEPNORM_ALPHA * x + 0.5 * swiglu_ffn(x, ffn2_w_in[l], ffn2_w_out[l]),
                       ln3_g[l], ln3_b[l])
    return x
```

```python
import contextlib
import numpy as np
import concourse.bass as bass
import concourse.mybir as mybir
from concourse.bass_utils import run_bass_kernel_spmd

F32 = mybir.dt.float32
BF16 = mybir.dt.bfloat16
AF = mybir.ActivationFunctionType
ALU = mybir.AluOpType

L = 2048
D = 1024
DFF = 2816
NCH = DFF // 128
ALPHA = 2.0 ** 0.25
LN_EPS = 1e-5
W_IN_COLS = 7240

ENGS = ("pe", "dve", "act", "pool", "sync")
SEM_ROT = {"c": 8000}
DMA_RING = 8


class Tok:
    __slots__ = ("w", "r", "name")

    def __init__(self, name=""):
        self.w = None
        self.r = {}
        self.name = name


class Ins:
    __slots__ = ("eng", "kind", "fn", "deps", "idx", "inc", "sem", "val")

    def __init__(self, eng, kind, fn):
        self.eng = eng
        self.kind = kind
        self.fn = fn
        self.deps = []
        self.inc = False
        self.sem = None
        self.val = 0


class Prog:
    def __init__(self, nc, semstack, dmaq):
        self.nc = nc
        self.semstack = semstack
        self.dmaq = dmaq
        self.streams = {e: [] for e in ENGS}
        self.n = 0

    def add(self, eng, fn, reads=(), writes=(), kind="c"):
        ins = Ins(eng, kind, fn)
        st = self.streams[eng]
        ins.idx = len(st)
        deps = {}

        def dep(d, typ):
            if d is None or d is ins:
                return
            if d.eng == eng and d.kind == "c" and kind == "c":
                if eng == "pe":
                    return
                if typ != "RAW":
                    return
                if eng in ("dve", "act") and ins.idx - d.idx >= 2:
                    return
            deps[id(d)] = d

        for t in reads:
            dep(t.w, "RAW")
        for t in writes:
            dep(t.w, "WAW")
            for d in t.r.values():
                dep(d, "WAR")
        if kind == "d":
            q = self.dmaq.setdefault(eng, {"n": 0, "last": {}, "sems": []})
            n = q["n"]
            q["n"] += 1
            r = n % DMA_RING
            if len(q["sems"]) <= r:
                q["sems"].append(self.semstack.enter_context(self.nc.semaphore(f"s_dma_{eng}_{r}")))
            ins.sem = q["sems"][r]
            ins.val = 16 * (n // DMA_RING + 1)
            ins.inc = True
            prev = q["last"].get(r)
            if prev is not None:
                deps[id(prev)] = prev
            q["last"][r] = ins
        ins.deps = list(deps.values())
        for d in ins.deps:
            d.inc = True
        for t in reads:
            t.r[eng + kind] = ins
        for t in writes:
            t.w = ins
            t.r = {}
        st.append(ins)
        self.n += 1
        return ins

    def wait_all(self, eng, toks):
        return self.add(eng, None, reads=toks, kind="w")

    def emit(self, name):
        nc = self.nc
        nsem = 0
        for e in ENGS:
            for kind in ("c",):
                cnt = 0
                cur = None
                for ins in self.streams[e]:
                    if ins.kind != kind or not ins.inc:
                        continue
                    if cur is None or cnt >= SEM_ROT[kind]:
                        cur = self.semstack.enter_context(nc.semaphore(f"s_{name}_{e}_{kind}_{nsem}"))
                        nsem += 1
                        cnt = 0
                    cnt += 1
                    ins.sem = cur
                    ins.val = cnt * (16 if kind == "d" else 1)
        with nc.Block() as block:
            engmap = {"pe": block.tensor, "dve": block.vector, "act": block.scalar,
                      "pool": block.gpsimd, "sync": block.sync}
            for e in ENGS:
                stream = self.streams[e]
                if not stream:
                    continue

                def body(eobj, stream=stream):
                    waited = {}
                    for ins in stream:
                        for d in ins.deps:
                            k = id(d.sem)
                            if waited.get(k, 0) >= d.val:
                                continue
                            eobj.wait_ge(d.sem, d.val)
                            waited[k] = d.val
                        if ins.fn is None:
                            continue
                        r = ins.fn(eobj)
                        if ins.inc:
                            r.then_inc(ins.sem, 16 if ins.kind == "d" else 1)

                engmap[e](body)


def bcast_rows(ap2d, nrows):
    return bass.AP(ap2d.tensor, ap2d.offset, [[0, nrows], [1, ap2d.shape[-1]]])


def ffn_stage(nc, semstack, dmaq, name, src_d, w_in_d, w_out_d, g_d, b_d, dst_d, ident_d):
    with contextlib.ExitStack() as st:
        P = Prog(nc, semstack, dmaq)
        sb = lambda n, shape, dt: st.enter_context(nc.sbuf_tensor(f"{name}_{n}", shape, dt))
        ps = lambda n, shape, dt: st.enter_context(nc.psum_tensor(f"{name}_{n}", shape, dt))
        HT = 1024
        xT = [sb(f"xT{i}", [128, 8, HT], BF16) for i in range(2)]
        gT = sb("gT", [128, NCH, HT], BF16)
        wout = sb("wout", [128, NCH, D], BF16)
        wst = [sb(f"wst{i}", [128, 2, 8, 128], F32) for i in range(2)]
        wbf = [sb(f"wbf{i}", [128, 2, 8, 128], BF16) for i in range(3)]
        wost = [sb(f"wost{i}", [128, D], F32) for i in range(2)]
        xs = [sb(f"xs{i}", [128, D], F32) for i in range(2)]
        xb = [sb(f"xb{i}", [128, D], BF16) for i in range(2)]
        sA = [sb(f"sA{i}", [128, 512], F32) for i in range(2)]
        rr = [sb(f"rr{i}", [128, D], F32) for i in range(2)]
        oo = [sb(f"oo{i}", [128, D], F32) for i in range(2)]
        gam = sb("gam", [128, D], F32)
        bet = sb("bet", [128, D], F32)
        idf = sb("idf", [128, 128], F32)
        idb = sb("idb", [128, 128], BF16)
        epst = sb("epst", [128, 1], F32)
        stats = [sb(f"stats{i}", [128, 2, 6], F32) for i in range(2)]
        mv = [sb(f"mv{i}", [128, 2], F32) for i in range(2)]
        sd = [sb(f"sd{i}", [128, 1], F32) for i in range(2)]
        rstd = [sb(f"rstd{i}", [128, 1], F32) for i in range(2)]
        pT = [ps(f"pT{i}", [128, 1024], BF16) for i in range(2)]
        pB = [ps(f"pB{i}", [128, 512], F32) for i in range(6)]

        def toks(n, k):
            return [Tok(f"{n}{i}") for i in range(k)]

        t_xT = [toks("xT", 8) for _ in range(2)]
        t_gT = [[Tok() for _ in range(2)] for _ in range(NCH)]
        t_wout = toks("wout", NCH)
        t_wst = toks("wst", 2); t_wbf = toks("wbf", 3); t_wost = toks("wost", 2)
        t_xs = toks("xs", 2); t_xb = toks("xb", 2); t_sA = toks("sA", 2)
        t_ys = toks("ys", 2); t_rr = toks("rr", 2); t_oo = toks("oo", 2)
        t_gam = Tok(); t_bet = Tok(); t_idf = Tok(); t_idb = Tok(); t_eps = Tok()
        t_stats = toks("st", 2); t_mv = toks("mv", 2); t_sd = toks("sd", 2); t_rstd = toks("rs", 2)
        t_pT = toks("pT", 2); t_pB = toks("pB", 6)
        t_dst = toks("dst", 16)

        P.add("sync", lambda e: e.dma_start(out=idf[:], in_=ident_d), writes=[t_idf], kind="d")
        P.add("sync", lambda e: e.dma_start(out=gam[:], in_=bcast_rows(g_d, 128)), writes=[t_gam], kind="d")
        P.add("sync", lambda e: e.dma_start(out=bet[:], in_=bcast_rows(b_d, 128)), writes=[t_bet], kind="d")
        P.add("dve", lambda e: e.tensor_copy(out=idb[:], in_=idf[:]), reads=[t_idf], writes=[t_idb])
        P.add("dve", lambda e: e.memset(epst[:], LN_EPS), writes=[t_eps])

        cnt = {"x": 0, "pb": 0, "sa": 0, "c": 0}

        def stage_a(h):
            for tl in range(8):
                t = h * 8 + tl
                s = cnt["x"] % 2
                cnt["x"] += 1
                P.add("sync", lambda e, s=s, t=t: e.dma_start(out=xs[s][:], in_=src_d[t * 128:(t + 1) * 128, :]),
                      writes=[t_xs[s]], kind="d")
                P.add("dve", lambda e, s=s: e.tensor_copy(out=xb[s][:], in_=xs[s][:]), reads=[t_xs[s]], writes=[t_xb[s]])
                for kc in range(8):
                    P.add("pe", lambda e, s=s, kc=kc: e.transpose(out=pT[s][:, kc * 128:(kc + 1) * 128],
                                                                  in_=xb[s][:, kc * 128:(kc + 1) * 128], identity=idb[:]),
                          reads=[t_xb[s], t_idb], writes=[t_pT[s]])
                P.add("act", lambda e, s=s, h=h, tl=tl: e.activation(
                    out=xT[h][:, :, tl * 128:(tl + 1) * 128],
                    in_=pT[s][:].rearrange("p (a b) -> p a b", a=8), func=AF.Copy),
                    reads=[t_pT[s]], writes=[t_xT[h][tl]])

        wcount = {"c": 0}

        def load_w(c, with_out):
            s = wcount["c"] % 2
            bs = wcount["c"] % 3
            wcount["c"] += 1
            for j in range(2):
                col0 = j * DFF + c * 128
                P.add("sync", lambda e, s=s, j=j, col0=col0: e.dma_start(
                    out=wst[s][:, j], in_=w_in_d[:, col0:col0 + 128].rearrange("(kc p) n -> p kc n", p=128)),
                    writes=[t_wst[s]], kind="d")
            P.add("pool", lambda e, s=s, bs=bs: e.tensor_copy(out=wbf[bs][:], in_=wst[s][:]),
                  reads=[t_wst[s]], writes=[t_wbf[bs]])
            if with_out:
                P.add("sync", lambda e, s=s, c=c: e.dma_start(out=wost[s][:], in_=w_out_d[c * 128:(c + 1) * 128, :]),
                      writes=[t_wost[s]], kind="d")
                P.add("pool", lambda e, s=s, c=c: e.tensor_copy(out=wout[:, c, :], in_=wost[s][:]),
                      reads=[t_wost[s]], writes=[t_wout[c]])
            return bs

        def stage_b(h):
            pending = load_w(0, h == 0)
            for c in range(NCH):
                bs = pending
                if c + 1 < NCH:
                    pending = load_w(c + 1, h == 0)
                for tb in range(2):
                    pa = (cnt["pb"] % 2) * 2
                    cnt["pb"] += 1
                    for j in range(2):
                        for kc in range(8):
                            P.add("pe", lambda e, pa=pa, j=j, kc=kc, bs=bs, tb=tb, h=h: e.matmul(
                                pB[pa + j][:], lhsT=wbf[bs][:, j, kc, :], rhs=xT[h][:, kc, tb * 512:(tb + 1) * 512],
                                start=(kc == 0), stop=(kc == 7)),
                                reads=[t_wbf[bs]] + t_xT[h][tb * 4:(tb + 1) * 4], writes=[t_pB[pa + j]])
                    s = cnt["sa"] % 2
                    cnt["sa"] += 1
                    P.add("act", lambda e, s=s, pa=pa: e.activation(out=sA[s][:], in_=pB[pa][:], func=AF.Silu),
                          reads=[t_pB[pa]], writes=[t_sA[s]])
                    P.add("dve", lambda e, s=s, pa=pa, c=c, tb=tb: e.scalar_tensor_tensor(
                        out=gT[:, c, tb * 512:(tb + 1) * 512], in0=sA[s][:], scalar=0.5, in1=pB[pa + 1][:],
                        op0=ALU.mult, op1=ALU.mult),
                        reads=[t_sA[s], t_pB[pa + 1]], writes=[t_gT[c][tb]])

        def stage_c(h):
            for tl in range(8):
                t = h * 8 + tl
                s = cnt["c"] % 2
                cnt["c"] += 1
                pa = (cnt["pb"] % 2) * 2
                cnt["pb"] += 1
                sx = cnt["x"] % 2
                cnt["x"] += 1
                P.add("sync", lambda e, sx=sx, t=t: e.dma_start(out=xs[sx][:], in_=src_d[t * 128:(t + 1) * 128, :]),
                      writes=[t_xs[sx]], kind="d")
                for nh in range(2):
                    for kc in range(NCH):
                        P.add("pe", lambda e, pa=pa, nh=nh, kc=kc, tl=tl: e.matmul(
                            pB[pa + nh][:], lhsT=gT[:, kc, tl * 128:(tl + 1) * 128], rhs=wout[:, kc, nh * 512:(nh + 1) * 512],
                            start=(kc == 0), stop=(kc == NCH - 1)),
                            reads=[t_gT[kc][tl // 4], t_wout[kc]], writes=[t_pB[pa + nh]])
                    P.add("dve", lambda e, pa=pa, nh=nh, s=s, sx=sx: e.scalar_tensor_tensor(
                        out=rr[s][:, nh * 512:(nh + 1) * 512], in0=xs[sx][:, nh * 512:(nh + 1) * 512], scalar=ALPHA,
                        in1=pB[pa + nh][:], op0=ALU.mult, op1=ALU.add),
                        reads=[t_xs[sx], t_pB[pa + nh]], writes=[t_rr[s]])
                for k in range(2):
                    P.add("dve", lambda e, s=s, k=k: e.bn_stats(out=stats[s][:, k, :], in_=rr[s][:, k * 512:(k + 1) * 512]),
                          reads=[t_rr[s]], writes=[t_stats[s]])
                P.add("dve", lambda e, s=s: e.bn_aggr(out=mv[s][:], in_=stats[s][:].rearrange("p a b -> p (a b)")),
                      reads=[t_stats[s]], writes=[t_mv[s]])
                P.add("act", lambda e, s=s: e.activation(out=sd[s][:], in_=mv[s][:, 1:2], func=AF.Sqrt, bias=epst[:, 0:1], scale=1.0),
                      reads=[t_mv[s], t_eps], writes=[t_sd[s]])
                P.add("dve", lambda e, s=s: e.reciprocal(out=rstd[s][:], in_=sd[s][:]), reads=[t_sd[s]], writes=[t_rstd[s]])
                P.add("dve", lambda e, s=s: e.tensor_scalar(
                    out=rr[s][:], in0=rr[s][:], scalar1=mv[s][:, 0:1], scalar2=rstd[s][:, 0:1], op0=ALU.subtract, op1=ALU.mult),
                    reads=[t_rr[s], t_mv[s], t_rstd[s]], writes=[t_rr[s]])
                P.add("pool", lambda e, s=s: e.tensor_tensor(out=oo[s][:], in0=rr[s][:], in1=gam[:], op=ALU.mult),
                      reads=[t_rr[s], t_gam], writes=[t_oo[s]])
                P.add("pool", lambda e, s=s: e.tensor_tensor(out=oo[s][:], in0=oo[s][:], in1=bet[:], op=ALU.add),
                      reads=[t_oo[s], t_bet], writes=[t_oo[s]])
                P.add("pool", lambda e, s=s, t=t: e.dma_start(out=dst_d[t * 128:(t + 1) * 128, :], in_=oo[s][:]),
                      reads=[t_oo[s]], writes=[t_dst[t]], kind="d")

        stage_a(0)
        stage_a(1)
        stage_b(0)
        stage_c(0)
        stage_b(1)
        stage_c(1)
        P.wait_all("pool", t_dst)
        P.emit(name)


C_RQ, C_RK, C_RV, C_RG = 0, 256, 512, 1024
C_DQ, C_DK, C_DV, C_IQ, C_IK, C_IW, C_MQ, C_G = 1536, 2048, 2560, 3072, 3584, 3648, 3656, 4168
IW_SCALE = float(8 ** -0.5 * 64 ** -0.5)
NIT = 18
GAMMAS = [1.0 - 2.0 ** (-5.0 - h) for h in range(4)]


def toks(k):
    return [Tok() for _ in range(k)]


class Env:
    def __init__(self, nc, semstack, dmaq, name, st):
        self.nc = nc
        self.P = Prog(nc, semstack, dmaq)
        self.name = name
        self.st = st
        self.k = 0

    def sb(self, n, shape, dt):
        return self.st.enter_context(self.nc.sbuf_tensor(f"{self.name}_{n}", shape, dt))

    def ps(self, n, shape, dt):
        return self.st.enter_context(self.nc.psum_tensor(f"{self.name}_{n}", shape, dt))


class WStream:
    def __init__(self, env, nst=2, nbf=3):
        self.env = env
        self.st = [env.sb(f"wsst{i}", [128, 8, 128], F32) for i in range(nst)]
        self.bf = [env.sb(f"wsbf{i}", [128, 8, 128], BF16) for i in range(nbf)]
        self.t_st = toks(nst)
        self.t_bf = toks(nbf)
        self.n = 0

    def load(self, pieces, KC=8, dst=None, t_dst=None):
        P = self.env.P
        s = self.n % len(self.st)
        b = self.n % len(self.bf)
        self.n += 1
        stt = self.st[s]
        for (c0, src, sc) in pieces:
            w = src.shape[-1]
            P.add("sync", lambda e, stt=stt, c0=c0, src=src, w=w, KC=KC: e.dma_start(
                out=stt[:, 0:KC, c0:c0 + w], in_=src.rearrange("(kc p) n -> p kc n", p=128)),
                writes=[self.t_st[s]], kind="d")
        if dst is None:
            tot = max(c0 + src.shape[-1] for (c0, src, sc) in pieces)
            out_t = self.bf[b]
            out_fn = lambda c0, w: out_t[:, 0:KC, c0:c0 + w]
            t_out = self.t_bf[b]
        else:
            out_fn = dst
            t_out = t_dst
        if all(sc == 1.0 for (_, _, sc) in pieces):
            lo = min(c0 for (c0, _, _) in pieces)
            hi = max(c0 + src.shape[-1] for (c0, src, _) in pieces)
            P.add("pool", lambda e, lo=lo, hi=hi, stt=stt, KC=KC: e.tensor_copy(out=out_fn(lo, hi - lo), in_=stt[:, 0:KC, lo:hi]),
                  reads=[self.t_st[s]], writes=[t_out])
        else:
            for (c0, src, sc) in pieces:
                w = src.shape[-1]
                P.add("dve", lambda e, c0=c0, w=w, sc=sc, stt=stt, KC=KC: e.tensor_scalar(
                    out=out_fn(c0, w), in0=stt[:, 0:KC, c0:c0 + w], scalar1=float(sc), scalar2=None, op0=ALU.mult),
                    reads=[self.t_st[s]], writes=[t_out])
        return (self.bf[b] if dst is None else None), t_out


def load_transposed(env, src_d, ntiles, dstT, t_dstT, idb, t_idb, pT, t_pT):
    P = env.P
    xs = [env.sb(f"ltxs{i}", [128, D], F32) for i in range(2)]
    xb = [env.sb(f"ltxb{i}", [128, D], BF16) for i in range(2)]
    t_xs = toks(2); t_xb = toks(2)
    for t in range(ntiles):
        s = t % 2
        P.add("sync", lambda e, s=s, t=t: e.dma_start(out=xs[s][:], in_=src_d[t * 128:(t + 1) * 128, :]),
              writes=[t_xs[s]], kind="d")
        P.add("dve", lambda e, s=s: e.tensor_copy(out=xb[s][:], in_=xs[s][:]), reads=[t_xs[s]], writes=[t_xb[s]])
        for kc in range(8):
            P.add("pe", lambda e, s=s, kc=kc: e.transpose(out=pT[s][:, kc * 128:(kc + 1) * 128],
                                                          in_=xb[s][:, kc * 128:(kc + 1) * 128], identity=idb[:]),
                  reads=[t_xb[s], t_idb], writes=[t_pT[s]])
        P.add("act", lambda e, s=s, t=t: e.activation(
            out=dstT[:, :, t * 128:(t + 1) * 128], in_=pT[s][:].rearrange("p (a b) -> p a b", a=8), func=AF.Copy),
            reads=[t_pT[s]], writes=[t_dstT[t]])


def load_ident(env, ident_d):
    P = env.P
    idf = env.sb("idf", [128, 128], F32)
    idb = env.sb("idb", [128, 128], BF16)
    t_idf = Tok(); t_idb = Tok()
    P.add("sync", lambda e: e.dma_start(out=idf[:], in_=ident_d), writes=[t_idf], kind="d")
    P.add("dve", lambda e: e.tensor_copy(out=idb[:], in_=idf[:]), reads=[t_idf], writes=[t_idb])
    return idf, t_idf, idb, t_idb


class Banks:
    def __init__(self, env, n):
        self.b = [env.ps(f"bk{i}", [128, 512], F32) for i in range(n)]
        self.t = toks(n)
        self.i = 0
        self.n = n

    def next(self):
        i = self.i % self.n
        self.i += 1
        return self.b[i], self.t[i]


def proj_fm(env, ws, banks, pieces, ncols, rhs_fn, rhs_toks_fn, nblk, blkw, evac, KC=8):
    P = env.P
    wt, t_w = ws.load(pieces, KC=KC)
    for tb in range(nblk):
        bank, t_bank = banks.next()
        for kc in range(KC):
            P.add("pe", lambda e, bank=bank, kc=kc, tb=tb, wt=wt: e.matmul(
                bank[0:ncols, 0:blkw], lhsT=wt[:, kc, 0:ncols], rhs=rhs_fn(kc, tb), start=(kc == 0), stop=(kc == KC - 1)),
                reads=[t_w] + rhs_toks_fn(tb), writes=[t_bank])
        evac(tb, bank, t_bank)


def mix_load_stage(nc, semstack, dmaq, x1_d, ident_d, x1T):
    with contextlib.ExitStack() as st:
        env = Env(nc, semstack, dmaq, "m0", st)
        idf, t_idf, idb, t_idb = load_ident(env, ident_d)
        pT = [env.ps(f"pT{i}", [128, 1024], BF16) for i in range(2)]
        t_pT = toks(2)
        t_x1T = toks(16)
        load_transposed(env, x1_d, 16, x1T, t_x1T, idb, t_idb, pT, t_pT)
        env.P.emit("m0")


def mem_stage(nc, semstack, dmaq, x1T, oT, mem_d, w_in, wmkv, ident_d):
    with contextlib.ExitStack() as st:
        env = Env(nc, semstack, dmaq, "m3", st)
        P = env.P
        idf, t_idf, idb, t_idb = load_ident(env, ident_d)
        pT = [env.ps(f"pT{i}", [128, 1024], BF16) for i in range(2)]
        t_pT = toks(2)
        banks = Banks(env, 4)
        pN = env.ps("pN", [128, 512], F32); t_pN = Tok()
        pD = env.ps("pD", [128, 512], F32); t_pD = Tok()
        memT = env.sb("memT", [128, 8, 256], BF16); t_memT = toks(2)
        mkT = env.sb("mkT", [128, 4, 256], BF16); t_mkT = toks(4)
        mvv = env.sb("mv", [128, 2, 512], BF16); t_mv = toks(2)
        wmv = env.sb("wmv", [128, 8, 512], BF16); t_wmv = toks(4)
        mqT = env.sb("mqT", [128, 4, L], BF16); t_mqT = [toks(4) for _ in range(4)]
        ones = env.sb("ones", [128, 128], BF16); t_ones = Tok()
        E = [env.sb(f"E{i}", [128, 512], BF16) for i in range(4)]; t_E = toks(4)
        rden = [env.sb(f"rden{i}", [128, 512], F32) for i in range(2)]; t_rden = toks(2)
        ws = WStream(env)
        P.add("pool", lambda e: e.memset(ones[:], 1.0), writes=[t_ones])
        load_transposed(env, mem_d, 2, memT, t_memT, idb, t_idb, pT, t_pT)
        t_x1T = []
        for h in range(4):
            def evac(tb, bank, t_bank, h=h):
                P.add("act", lambda e: e.activation(out=mkT[:, h, :], in_=bank[:, 0:256], func=AF.Copy),
                      reads=[t_bank], writes=[t_mkT[h]])
            proj_fm(env, ws, banks, [(0, wmkv[:, h * 128:(h + 1) * 128], 1.0)], 128,
                    lambda kc, tb: memT[:, kc, :], lambda tb: t_memT, 1, 256, evac)
        for c in range(4):
            ws.load([(0, wmkv[:, 512 + c * 128:512 + (c + 1) * 128], 1.0)],
                    dst=lambda c0, w, c=c: wmv[:, :, c * 128 + c0:c * 128 + c0 + w], t_dst=t_wmv[c])
        for mt in range(2):
            bank, t_bank = banks.next()
            for kc in range(8):
                P.add("pe", lambda e, bank=bank, kc=kc, mt=mt: e.matmul(
                    bank[:], lhsT=memT[:, kc, mt * 128:(mt + 1) * 128], rhs=wmv[:, kc, :], start=(kc == 0), stop=(kc == 7)),
                    reads=[t_memT[mt]] + t_wmv, writes=[t_bank])
            P.add("act", lambda e, bank=bank, mt=mt: e.activation(out=mvv[:, mt, :], in_=bank[:], func=AF.Copy),
                  reads=[t_bank], writes=[t_mv[mt]])
        for h in range(4):
            def evac(tb, bank, t_bank, h=h):
                P.add("act", lambda e: e.activation(out=mqT[:, h, tb * 512:(tb + 1) * 512], in_=bank[:], func=AF.Copy,
                                                    scale=float(128 ** -0.5)),
                      reads=[t_bank], writes=[t_mqT[h][tb]])
            proj_fm(env, ws, banks, [(0, w_in[:, C_MQ + h * 128:C_MQ + (h + 1) * 128], 1.0)], 128,
                    lambda kc, tb: x1T[:, kc, tb * 512:(tb + 1) * 512], lambda tb: [], 4, 512, evac)
        it = 0
        for h in range(4):
            for qb in range(4):
                for mt in range(2):
                    bank, t_bank = banks.next()
                    ei = (it * 2 + mt) % 4
                    P.add("pe", lambda e, bank=bank, h=h, qb=qb, mt=mt: e.matmul(
                        bank[:], lhsT=mkT[:, h, mt * 128:(mt + 1) * 128], rhs=mqT[:, h, qb * 512:(qb + 1) * 512],
                        start=True, stop=True), reads=[t_mkT[h], t_mqT[h][qb]], writes=[t_bank])
                    P.add("act", lambda e, bank=bank, ei=ei: e.activation(out=E[ei][:], in_=bank[:], func=AF.Exp),
                          reads=[t_bank], writes=[t_E[ei]])
                for mt in range(2):
                    ei = (it * 2 + mt) % 4
                    P.add("pe", lambda e, ei=ei, h=h, mt=mt: e.matmul(
                        pN[:], lhsT=mvv[:, mt, h * 128:(h + 1) * 128], rhs=E[ei][:], start=(mt == 0), stop=(mt == 1)),
                        reads=[t_mv[mt], t_E[ei]], writes=[t_pN])
                    P.add("pe", lambda e, ei=ei, mt=mt: e.matmul(
                        pD[:], lhsT=ones[:], rhs=E[ei][:], start=(mt == 0), stop=(mt == 1)),
                        reads=[t_ones, t_E[ei]], writes=[t_pD])
                r = it % 2
                P.add("dve", lambda e, r=r: e.reciprocal(out=rden[r][:], in_=pD[:]), reads=[t_pD], writes=[t_rden[r]])
                P.add("dve", lambda e, r=r, h=h, qb=qb: e.tensor_tensor(
                    out=oT[:, h, qb * 512:(qb + 1) * 512], in0=pN[:], in1=rden[r][:], op=ALU.mult),
                    reads=[t_pN, t_rden[r]], writes=[Tok()])
                it += 1
        env.P.emit("m3")


def dsa_stage(nc, semstack, dmaq, x1T, oT, w_in, t5_d, ident_d, J_d, oh_d, causal_d, pow2_d, gscr_d):
    with contextlib.ExitStack() as st:
        env = Env(nc, semstack, dmaq, "m1", st)
        P = env.P
        idf, t_idf, idb, t_idb = load_ident(env, ident_d)
        banks = Banks(env, 3)
        pN2 = [env.ps(f"pN{i}", [128, 512], F32) for i in range(2)]; t_pN2 = toks(2)
        pD2 = [env.ps(f"pD{i}", [128, 512], F32) for i in range(2)]; t_pD2 = toks(2)
        pM = env.ps("pM", [128, 1024], BF16); t_pM = Tok()
        qT = env.sb("qT", [128, 4, L], BF16); t_qT = [toks(4) for _ in range(4)]
        kT = env.sb("kT", [128, 4, L], BF16); t_kT = [toks(4) for _ in range(4)]
        qiT = env.sb("qiT", [128, 4, L], BF16); t_qiT = [toks(4) for _ in range(4)]
        kiT = env.sb("kiT", [128, L], BF16); t_kiT = toks(4)
        v = env.sb("v", [128, 16, 512], BF16); t_v = toks(16)
        iw = env.sb("iw", [128, 16, 8], F32); t_iw = toks(16)
        wv = env.sb("wv", [128, 8, 512], BF16); t_wv = toks(4)
        wiw = env.sb("wiw", [128, 8, 8], BF16); t_wiw = Tok()
        Sc2 = [env.sb(f"Sc{i}", [128, L], F32) for i in range(2)]; t_Sc2 = toks(2)
        junk2 = [env.sb(f"junk{i}", [128, L], BF16) for i in range(2)]; t_junk2 = toks(2)
        mask = [env.sb(f"mask{i}", [128, L], BF16) for i in range(2)]; t_mask = toks(2)
        maskT = [env.sb(f"maskT{i}", [128, L], BF16) for i in range(2)]; t_maskT = toks(2)
        tI = [env.sb(f"tI{i}", [128, 512], F32) for i in range(2)]; t_tI = toks(2)
        E = [env.sb(f"E{i}", [128, 512], BF16) for i in range(2)]; t_E = toks(2)
        PT = [env.sb(f"PT{i}", [128, 512], BF16) for i in range(3)]; t_PT = toks(3)
        BT = env.sb("BT", [128, 8, 256], BF16); t_BT = toks(8)
        H = Sc2[1][:].rearrange("p (h n) -> p h n", h=8); t_H = t_Sc2[1]
        rden2 = [env.sb(f"rden{i}", [128, 512], F32) for i in range(2)]; t_rden2 = toks(2)
        Jf = env.sb("Jf", [128, 128], F32); t_J = Tok()
        caus = env.sb("caus", [128, 128], F32); t_caus = Tok()
        pow2 = env.sb("pow2", [128, NIT], F32); t_pow2 = Tok()
        ones = env.sb("ones", [128, 64], BF16); t_ones = Tok()
        tabS = env.sb("tabS", [32, 8], F32); t_tab = Tok()
        ohS = env.sb("ohS", [32, 384], F32); t_oh = Tok()
        Gs = env.sb("Gs", [8, 384], F32); t_Gs = Tok()
        thrneg = env.sb("thrneg", [128, 1], F32); t_thrneg = Tok()
        mx8_2 = [env.sb(f"mx8{i}", [128, 8], F32) for i in range(2)]; t_mx8_2 = toks(2)
        mn_2 = [env.sb(f"mn{i}", [128, 1], F32) for i in range(2)]; t_mn_2 = toks(2)
        rng_2 = [env.sb(f"rng{i}", [128, 1], F32) for i in range(2)]; t_rng_2 = toks(2)
        thr_2 = [env.sb(f"thr{i}", [128, 1], F32) for i in range(2)]; t_thr_2 = toks(2)
        S2_2 = [env.sb(f"S2{i}", [128, NIT], F32) for i in range(2)]; t_S2_2 = toks(2)
        cnt_2 = [env.sb(f"cnt{i}", [128, 1], F32) for i in range(2)]; t_cnt_2 = toks(2)
        ee_2 = [env.sb(f"ee{i}", [128, 1], F32) for i in range(2)]; t_ee_2 = toks(2)
        ws = WStream(env)
        t_gscr = Tok()

        P.add("sync", lambda e: e.dma_start(out=Jf[:], in_=J_d), writes=[t_J], kind="d")
        P.add("sync", lambda e: e.dma_start(out=caus[:], in_=causal_d), writes=[t_caus], kind="d")
        P.add("sync", lambda e: e.dma_start(out=pow2[:], in_=pow2_d), writes=[t_pow2], kind="d")
        P.add("sync", lambda e: e.dma_start(out=tabS[:], in_=t5_d), writes=[t_tab], kind="d")
        P.add("sync", lambda e: e.dma_start(out=ohS[:], in_=oh_d), writes=[t_oh], kind="d")
        P.add("pool", lambda e: e.memset(ones[:], 1.0), writes=[t_ones])
        P.add("pool", lambda e: e.memset(thrneg[:], -1.0e29), writes=[t_thrneg])
        bank, t_bank = banks.next()
        P.add("pe", lambda e, bank=bank: e.matmul(bank[0:8, 0:384], lhsT=tabS[:], rhs=ohS[:], start=True, stop=True),
              reads=[t_tab, t_oh], writes=[t_bank])
        P.add("act", lambda e, bank=bank: e.activation(out=Gs[:], in_=bank[0:8, 0:384], func=AF.Copy), reads=[t_bank], writes=[t_Gs])
        P.add("sync", lambda e: e.dma_start(out=gscr_d, in_=Gs[:]), reads=[t_Gs], writes=[t_gscr], kind="d")
        hank = bass.AP(gscr_d.tensor, gscr_d.offset, [[1, 128], [384, 8], [1, 256]])
        P.add("sync", lambda e: e.dma_start(out=H, in_=hank), reads=[t_gscr], writes=[t_H], kind="d")
        for h in range(8):
            bank, t_bank = banks.next()
            P.add("pe", lambda e, bank=bank, h=h: e.matmul(bank[:, 0:256], lhsT=Jf[:], rhs=H[:, h, :], start=True, stop=True),
                  reads=[t_J, t_H], writes=[t_bank])
            P.add("act", lambda e, bank=bank, h=h: e.activation(out=BT[:, h, :], in_=bank[:, 0:256], func=AF.Copy),
                  reads=[t_bank], writes=[t_BT[h]])

        ev = {"i": 0}

        def evac_to(dstfn, t_dstfn, scale):
            def evac(tb, bank, t_bank):
                eng = "act" if ev["i"] % 2 == 0 else "dve"
                ev["i"] += 1
                if eng == "act":
                    P.add("act", lambda e: e.activation(out=dstfn(tb), in_=bank[:], func=AF.Copy, scale=float(scale)),
                          reads=[t_bank], writes=[t_dstfn(tb)])
                else:
                    P.add("dve", lambda e: e.tensor_scalar(out=dstfn(tb), in0=bank[:], scalar1=float(scale), scalar2=None, op0=ALU.mult),
                          reads=[t_bank], writes=[t_dstfn(tb)])
            return evac

        xrhs = lambda kc, tb: x1T[:, kc, tb * 512:(tb + 1) * 512]
        for c in range(4):
            proj_fm(env, ws, banks, [(0, w_in[:, C_DQ + c * 128:C_DQ + (c + 1) * 128], 1.0)], 128, xrhs, lambda tb: [], 4, 512,
                    evac_to(lambda tb, c=c: qT[:, c, tb * 512:(tb + 1) * 512], lambda tb, c=c: t_qT[c][tb], 0.125))
            proj_fm(env, ws, banks, [(0, w_in[:, C_DK + c * 128:C_DK + (c + 1) * 128], 1.0)], 128, xrhs, lambda tb: [], 4, 512,
                    evac_to(lambda tb, c=c: kT[:, c, tb * 512:(tb + 1) * 512], lambda tb, c=c: t_kT[c][tb], 1.0))
            proj_fm(env, ws, banks, [(0, w_in[:, C_IQ + c * 128:C_IQ + (c + 1) * 128], 1.0)], 128, xrhs, lambda tb: [], 4, 512,
                    evac_to(lambda tb, c=c: qiT[:, c, tb * 512:(tb + 1) * 512], lambda tb, c=c: t_qiT[c][tb], 1.0))
        proj_fm(env, ws, banks, [(0, w_in[:, C_IK:C_IK + 64], 1.0), (64, w_in[:, C_IK:C_IK + 64], 1.0)], 128, xrhs, lambda tb: [], 4, 512,
                evac_to(lambda tb: kiT[:, tb * 512:(tb + 1) * 512], lambda tb: t_kiT[tb], 1.0))
        for c in range(4):
            ws.load([(0, w_in[:, C_DV + c * 128:C_DV + (c + 1) * 128], 1.0)],
                    dst=lambda c0, w, c=c: wv[:, :, c * 128 + c0:c * 128 + c0 + w], t_dst=t_wv[c])
        ws.load([(0, w_in[:, C_IW:C_IW + 8], 1.0)], dst=lambda c0, w: wiw[:, :, c0:c0 + w], t_dst=t_wiw)
        for t in range(16):
            bank, t_bank = banks.next()
            for kc in range(8):
                P.add("pe", lambda e, bank=bank, kc=kc, t=t: e.matmul(
                    bank[:], lhsT=x1T[:, kc, t * 128:(t + 1) * 128], rhs=wv[:, kc, :], start=(kc == 0), stop=(kc == 7)),
                    reads=t_wv, writes=[t_bank])
            P.add("act", lambda e, bank=bank, t=t: e.activation(out=v[:, t, :], in_=bank[:], func=AF.Copy),
                  reads=[t_bank], writes=[t_v[t]])
            bank, t_bank = banks.next()
            for kc in range(8):
                P.add("pe", lambda e, bank=bank, kc=kc, t=t: e.matmul(
                    bank[:, 0:8], lhsT=x1T[:, kc, t * 128:(t + 1) * 128], rhs=wiw[:, kc, :], start=(kc == 0), stop=(kc == 7)),
                    reads=[t_wiw], writes=[t_bank])
            P.add("dve", lambda e, bank=bank, t=t: e.tensor_scalar(out=iw[:, t, :], in0=bank[:, 0:8], scalar1=IW_SCALE, scalar2=None, op0=ALU.mult),
                  reads=[t_bank], writes=[t_iw[t]])

        cI = {"i": 0}

        def idx_phase(n):
            Sc = Sc2[n % 2]; t_Sc = t_Sc2[n % 2]
            W = 128 * (n + 1)
            nb = (W + 511) // 512
            for h in range(8):
                hp = (h % 2) * 64
                for kb in range(nb):
                    w = min(512, W - kb * 512)
                    bank, t_bank = banks.next()
                    P.add("pe", lambda e, bank=bank, h=h, hp=hp, kb=kb, w=w, n=n: e.matmul(
                        bank[:, 0:w], lhsT=qiT[hp:hp + 64, h // 2, n * 128:(n + 1) * 128], rhs=kiT[hp:hp + 64, kb * 512:kb * 512 + w],
                        start=True, stop=True),
                        reads=[t_qiT[h // 2][n // 4]] + t_kiT[0:nb], writes=[t_bank])
                    s = cI["i"] % 2
                    cI["i"] += 1
                    P.add("act", lambda e, bank=bank, s=s, w=w: e.activation(out=tI[s][:, 0:w], in_=bank[:, 0:w], func=AF.Relu),
                          reads=[t_bank], writes=[t_tI[s]])
                    if h == 0:
                        P.add("dve", lambda e, s=s, kb=kb, w=w, n=n: e.tensor_scalar(
                            out=Sc[:, kb * 512:kb * 512 + w], in0=tI[s][:, 0:w], scalar1=iw[:, n, 0:1], scalar2=None, op0=ALU.mult),
                            reads=[t_tI[s], t_iw[n]], writes=[t_Sc])
                    else:
                        P.add("dve", lambda e, s=s, kb=kb, w=w, n=n, h=h: e.scalar_tensor_tensor(
                            out=Sc[:, kb * 512:kb * 512 + w], in0=tI[s][:, 0:w], scalar=iw[:, n, h:h + 1],
                            in1=Sc[:, kb * 512:kb * 512 + w], op0=ALU.mult, op1=ALU.add),
                            reads=[t_tI[s], t_iw[n], t_Sc], writes=[t_Sc])
            P.add("dve", lambda e, n=n: e.tensor_tensor(out=Sc[:, n * 128:(n + 1) * 128], in0=Sc[:, n * 128:(n + 1) * 128],
                                                        in1=caus[:], op=ALU.add),
                  reads=[t_Sc, t_caus], writes=[t_Sc])

        def bis_ops(n):
            ch = n % 2
            Sc = Sc2[ch]; t_Sc = t_Sc2[ch]; junk = junk2[ch]; t_junk = t_junk2[ch]
            mx8 = mx8_2[ch]; t_mx8 = t_mx8_2[ch]; mn = mn_2[ch]; t_mn = t_mn_2[ch]; rng = rng_2[ch]; t_rng = t_rng_2[ch]
            thr = thr_2[ch]; t_thr = t_thr_2[ch]; S2 = S2_2[ch]; t_S2 = t_S2_2[ch]; cnt = cnt_2[ch]; t_cnt = t_cnt_2[ch]
            ee = ee_2[ch]; t_ee = t_ee_2[ch]
            W = 128 * (n + 1)
            mk = mask[ch]
            t_mk = t_mask[ch]
            if n < 2:
                yield lambda: P.add("dve", lambda e: e.tensor_scalar(out=mk[:, 0:W], in0=Sc[:, 0:W], scalar1=thrneg[:, 0:1], scalar2=None, op0=ALU.is_ge),
                                    reads=[t_Sc, t_thrneg], writes=[t_mk])
                return
            yield lambda: P.add("dve", lambda e: e.max(out=mx8[:], in_=Sc[:, 0:W]), reads=[t_Sc], writes=[t_mx8])
            yield lambda: P.add("dve", lambda e: e.tensor_reduce(out=mn[:], in_=Sc[:, 0:n * 128], axis=mybir.AxisListType.X, op=ALU.min),
                                reads=[t_Sc], writes=[t_mn])
            yield lambda: P.add("dve", lambda e: e.tensor_tensor(out=rng[:], in0=mx8[:, 0:1], in1=mn[:], op=ALU.subtract),
                                reads=[t_mx8, t_mn], writes=[t_rng])
            yield lambda: P.add("dve", lambda e: e.scalar_tensor_tensor(out=thr[:], in0=rng[:], scalar=0.5, in1=mn[:], op0=ALU.mult, op1=ALU.add),
                                reads=[t_rng, t_mn], writes=[t_thr])
            yield lambda: P.add("dve", lambda e: e.tensor_scalar(out=S2[:], in0=pow2[:], scalar1=rng[:, 0:1], scalar2=None, op0=ALU.mult),
                                reads=[t_pow2, t_rng], writes=[t_S2])
            for k in range(NIT):
                yield lambda: P.add("dve", lambda e: e.tensor_scalar(out=junk[:, 0:W], in0=Sc[:, 0:W], scalar1=thr[:, 0:1], scalar2=None,
                                                                     op0=ALU.is_ge, op1=ALU.add, accum_out=cnt[:, 0:1]),
                                    reads=[t_Sc, t_thr], writes=[t_junk, t_cnt])
                yield lambda: P.add("dve", lambda e: e.tensor_scalar(out=ee[:], in0=cnt[:], scalar1=255.5, scalar2=0.5, op0=ALU.is_ge, op1=ALU.subtract),
                                    reads=[t_cnt], writes=[t_ee])
                yield lambda k=k: P.add("dve", lambda e: e.scalar_tensor_tensor(out=thr[:], in0=ee[:], scalar=S2[:, k:k + 1], in1=thr[:],
                                                                                op0=ALU.mult, op1=ALU.add),
                                        reads=[t_ee, t_S2, t_thr], writes=[t_thr])
            yield lambda: P.add("dve", lambda e: e.tensor_scalar(out=mk[:, 0:W], in0=Sc[:, 0:W], scalar1=thr[:, 0:1], scalar2=None, op0=ALU.is_ge),
                                reads=[t_Sc, t_thr], writes=[t_mk])

        def bis_pair(a, b):
            ga, gb = bis_ops(a), bis_ops(b)
            done_a = done_b = False
            while not (done_a and done_b):
                if not done_a:
                    f = next(ga, None)
                    if f is None:
                        done_a = True
                    else:
                        f()
                if not done_b:
                    f = next(gb, None)
                    if f is None:
                        done_b = True
                    else:
                        f()

        def maskT_phase(n):
            mk = mask[n % 2]; t_mk = t_mask[n % 2]
            mT = maskT[n % 2]; t_mT = t_maskT[n % 2]
            for m0 in range(0, n + 1, 8):
                k = min(8, n + 1 - m0)
                for i in range(k):
                    m = m0 + i
                    P.add("pe", lambda e, i=i, m=m: e.transpose(out=pM[:, i * 128:(i + 1) * 128], in_=mk[:, m * 128:(m + 1) * 128], identity=idb[:]),
                          reads=[t_mk, t_idb], writes=[t_pM])
                P.add("act", lambda e, m0=m0, k=k: e.activation(out=mT[:, m0 * 128:(m0 + k) * 128], in_=pM[:, 0:k * 128], func=AF.Copy),
                      reads=[t_pM], writes=[t_mT])

        cA = {"e": 0, "p": 0}

        def att_phase(n):
            mT = maskT[n % 2]; t_mT = t_maskT[n % 2]
            pN = pN2[n % 2]; t_pN = t_pN2[n % 2]; pD = pD2[n % 2]; t_pD = t_pD2[n % 2]
            rden = rden2[n % 2]; t_rden = t_rden2[n % 2]
            for h in range(8):
                hp = (h % 2) * 64
                c = h // 2
                for m0 in range(0, n + 1, 4):
                    k = min(4, n + 1 - m0)
                    bank, t_bank = banks.next()
                    for i in range(k):
                        m = m0 + i
                        near = (m >= n - 1)
                        P.add("pe", lambda e, bank=bank, i=i, m=m, hp=hp, c=c, near=near, n=n: e.matmul(
                            bank[:, i * 128:(i + 1) * 128], lhsT=kT[hp:hp + 64, c, m * 128:(m + 1) * 128],
                            rhs=qT[hp:hp + 64, c, n * 128:(n + 1) * 128], start=True, stop=(not near)),
                            reads=[t_kT[c][m // 4], t_qT[c][n // 4]], writes=[t_bank])
                        if near:
                            off = 0 if m == n else 128
                            P.add("pe", lambda e, bank=bank, i=i, h=h, off=off: e.matmul(
                                bank[:, i * 128:(i + 1) * 128], lhsT=idb[:], rhs=BT[:, h, off:off + 128], start=False, stop=True),
                                reads=[t_idb, t_BT[h]], writes=[t_bank])
                    se = cA["e"] % 2
                    cA["e"] += 1
                    P.add("act", lambda e, bank=bank, se=se, k=k: e.activation(out=E[se][:, 0:k * 128], in_=bank[:, 0:k * 128], func=AF.Exp),
                          reads=[t_bank], writes=[t_E[se]])
                    sp = cA["p"] % 3
                    cA["p"] += 1
                    P.add("pool", lambda e, se=se, sp=sp, k=k, m0=m0: e.tensor_tensor(
                        out=PT[sp][:, 0:k * 128], in0=E[se][:, 0:k * 128], in1=mT[:, m0 * 128:(m0 + k) * 128], op=ALU.mult),
                        reads=[t_E[se], t_mT], writes=[t_PT[sp]])
                    for i in range(k):
                        m = m0 + i
                        P.add("pe", lambda e, sp=sp, i=i, m=m, h=h, hp=hp, c=c, n=n: e.matmul(
                            pN[hp:hp + 64, c * 128:(c + 1) * 128], lhsT=v[:, m, h * 64:(h + 1) * 64], rhs=PT[sp][:, i * 128:(i + 1) * 128],
                            start=(m == 0), stop=(m == n)),
                            reads=[t_v[m], t_PT[sp]], writes=[t_pN])
                        P.add("pe", lambda e, sp=sp, i=i, m=m, hp=hp, c=c, n=n: e.matmul(
                            pD[hp:hp + 64, c * 128:(c + 1) * 128], lhsT=ones[:], rhs=PT[sp][:, i * 128:(i + 1) * 128],
                            start=(m == 0), stop=(m == n)),
                            reads=[t_ones, t_PT[sp]], writes=[t_pD])
            P.add("dve", lambda e: e.reciprocal(out=rden[:], in_=pD[:]), reads=[t_pD], writes=[t_rden])
            P.add("dve", lambda e, n=n: e.tensor_tensor(
                out=oT[:, :, n * 128:(n + 1) * 128], in0=pN[:].rearrange("p (c q) -> p c q", c=4),
                in1=rden[:].rearrange("p (c q) -> p c q", c=4), op=ALU.mult),
                reads=[t_pN, t_rden], writes=[Tok()])

        idx_phase(0); idx_phase(1); bis_pair(0, 1); maskT_phase(0); maskT_phase(1)
        for k in range(8):
            a, b = 2 * k, 2 * k + 1
            if k + 1 < 8:
                idx_phase(a + 2); idx_phase(b + 2)
                bis_pair(a + 2, b + 2)
            att_phase(a)
            att_phase(b)
            if k + 1 < 8:
                maskT_phase(a + 2); maskT_phase(b + 2)
        env.P.emit("m1")


def ret_stage(nc, semstack, dmaq, x1T, oT, w_in, gng_d, gnb_d, ident_d, cos_d, sin_d, decay_d, kdec_d, qdec_d):
    with contextlib.ExitStack() as st:
        env = Env(nc, semstack, dmaq, "m2", st)
        P = env.P
        idf, t_idf, idb, t_idb = load_ident(env, ident_d)
        banks = Banks(env, 7)
        pM = env.ps("pM", [128, 1024], BF16); t_pM = Tok()
        rqT = env.sb("rqT", [128, 2, L], BF16); t_rqT = [toks(4) for _ in range(2)]
        rkT = env.sb("rkT", [128, 2, L], BF16); t_rkT = [toks(4) for _ in range(2)]
        qdT = env.sb("qdT", [128, 2, L], BF16); t_qdT = toks(16)
        rv = env.sb("rv", [128, 16, 512], BF16); t_rv = toks(16)
        kd = env.sb("kd", [128, 16, 256], BF16); t_kd = toks(16)
        wrv = env.sb("wrv", [128, 8, 512], BF16); t_wrv = toks(4)
        wrg = env.sb("wrg", [128, 8, 512], BF16); t_wrg = toks(4)
        cosT = env.sb("cosT", [128, L], F32); t_cos = Tok()
        sinT = env.sb("sinT", [128, L], F32); t_sin = Tok()
        decT = env.sb("decT", [128, 512], F32); t_dec = Tok()
        kdec = env.sb("kdec", [128, 256], F32); t_kdec = Tok()
        qdec = env.sb("qdec", [128, 2, 128], F32); t_qdec = Tok()
        gng = env.sb("gng", [128, 512], F32); t_gng = Tok()
        gnb = env.sb("gnb", [128, 512], F32); t_gnb = Tok()
        tmp = [env.sb(f"tmp{i}", [128, 512], F32) for i in range(2)]; t_tmp = toks(2)
        tmp2 = [env.sb(f"tmpb{i}", [128, 512], F32) for i in range(2)]; t_tmp2 = toks(2)
        PdT = [env.sb(f"PdT{i}", [128, 512], BF16) for i in range(2)]; t_PdT = toks(2)
        state = env.sb("state", [128, 2, 128], F32); t_state = Tok()
        state_bf = env.sb("state_bf", [128, 2, 128], BF16); t_state_bf = Tok()
        on = [env.sb(f"on{i}", [128, 512], F32) for i in range(2)]; t_on = toks(2)
        sg = [env.sb(f"sg{i}", [128, 512], F32) for i in range(2)]; t_sg = toks(2)
        orb = [env.sb(f"orb{i}", [128, 512], BF16) for i in range(2)]; t_orb = toks(2)
        stats = [env.sb(f"stats{i}", [128, 4, 6], F32) for i in range(2)]; t_stats = toks(2)
        mvv = [env.sb(f"mvv{i}", [128, 4, 2], F32) for i in range(2)]; t_mvv = toks(2)
        sd = [env.sb(f"sd{i}", [128, 4], F32) for i in range(2)]; t_sd = toks(2)
        rstd = [env.sb(f"rstd{i}", [128, 4], F32) for i in range(2)]; t_rstd = toks(2)
        epst = env.sb("epst", [128, 1], F32); t_eps = Tok()
        ws = WStream(env)
        for (dst, src, tk) in ((cosT, cos_d, t_cos), (sinT, sin_d, t_sin), (decT, decay_d, t_dec), (kdec, kdec_d, t_kdec)):
            P.add("sync", lambda e, dst=dst, src=src: e.dma_start(out=dst[:], in_=src), writes=[tk], kind="d")
        P.add("sync", lambda e: e.dma_start(out=qdec[:].rearrange("p a b -> p (a b)"), in_=qdec_d), writes=[t_qdec], kind="d")
        P.add("sync", lambda e: e.dma_start(out=gng[:], in_=bcast_rows(gng_d, 128)), writes=[t_gng], kind="d")
        P.add("sync", lambda e: e.dma_start(out=gnb[:], in_=bcast_rows(gnb_d, 128)), writes=[t_gnb], kind="d")
        P.add("dve", lambda e: e.memset(epst[:], LN_EPS), writes=[t_eps])

        cR = {"i": 0}
        for (c0, dstT, t_dstT, sc) in ((C_RQ, rqT, t_rqT, 1.0), (C_RK, rkT, t_rkT, 0.125)):
            for c in range(2):
                base = c0 + c * 128
                wn, t_wn = ws.load([(0, w_in[:, base:base + 128], 1.0)])
                pieces = []
                for hh in range(2):
                    pieces.append((hh * 64, w_in[:, base + hh * 64 + 32:base + hh * 64 + 64], -1.0))
                    pieces.append((hh * 64 + 32, w_in[:, base + hh * 64:base + hh * 64 + 32], 1.0))
                wr, t_wr = ws.load(pieces)
                for tb in range(4):
                    bq, t_bq = banks.next()
                    br, t_br = banks.next()
                    for (bank, t_bank, wt, t_w) in ((bq, t_bq, wn, t_wn), (br, t_br, wr, t_wr)):
                        for kc in range(8):
                            P.add("pe", lambda e, bank=bank, wt=wt, kc=kc, tb=tb: e.matmul(
                                bank[:], lhsT=wt[:, kc, :], rhs=x1T[:, kc, tb * 512:(tb + 1) * 512], start=(kc == 0), stop=(kc == 7)),
                                reads=[t_w], writes=[t_bank])
                    s = cR["i"] % 2
                    cR["i"] += 1
                    tsl = slice(tb * 512, (tb + 1) * 512)
                    P.add("dve", lambda e, s=s, bq=bq, tsl=tsl: e.tensor_tensor(out=tmp[s][:], in0=bq[:], in1=cosT[:, tsl], op=ALU.mult),
                          reads=[t_bq, t_cos], writes=[t_tmp[s]])
                    P.add("dve", lambda e, s=s, br=br, tsl=tsl, sc=sc: e.scalar_tensor_tensor(
                        out=tmp2[s][:], in0=br[:], scalar=float(sc), in1=sinT[:, tsl], op0=ALU.mult, op1=ALU.mult),
                        reads=[t_br, t_sin], writes=[t_tmp2[s]])
                    P.add("dve", lambda e, s=s, tsl=tsl, sc=sc, dstT=dstT, c=c: e.scalar_tensor_tensor(
                        out=dstT[:, c, tsl], in0=tmp[s][:], scalar=float(sc), in1=tmp2[s][:], op0=ALU.mult, op1=ALU.add),
                        reads=[t_tmp[s], t_tmp2[s]], writes=[t_dstT[c][tb]])
        for c in range(4):
            ws.load([(0, w_in[:, C_RV + c * 128:C_RV + (c + 1) * 128], 1.0)],
                    dst=lambda c0, w, c=c: wrv[:, :, c * 128 + c0:c * 128 + c0 + w], t_dst=t_wrv[c])
        for c in range(4):
            ws.load([(0, w_in[:, C_RG + c * 128:C_RG + (c + 1) * 128], 1.0)],
                    dst=lambda c0, w, c=c: wrg[:, :, c * 128 + c0:c * 128 + c0 + w], t_dst=t_wrg[c])
        for t in range(16):
            tsl = slice(t * 128, (t + 1) * 128)
            bank, t_bank = banks.next()
            for kc in range(8):
                P.add("pe", lambda e, bank=bank, kc=kc, tsl=tsl: e.matmul(
                    bank[:], lhsT=x1T[:, kc, tsl], rhs=wrv[:, kc, :], start=(kc == 0), stop=(kc == 7)),
                    reads=t_wrv, writes=[t_bank])
            P.add("act", lambda e, bank=bank, t=t: e.activation(out=rv[:, t, :], in_=bank[:], func=AF.Copy),
                  reads=[t_bank], writes=[t_rv[t]])
            for c in range(2):
                P.add("pe", lambda e, c=c, tsl=tsl: e.transpose(out=pM[:, c * 128:(c + 1) * 128], in_=rkT[:, c, tsl], identity=idb[:]),
                      reads=[t_rkT[c][t // 4], t_idb], writes=[t_pM])
            P.add("dve", lambda e, t=t: e.tensor_tensor(out=kd[:, t, :], in0=pM[:, 0:256], in1=kdec[:], op=ALU.mult),
                  reads=[t_pM, t_kdec], writes=[t_kd[t]])
            for c in range(2):
                P.add("dve", lambda e, c=c, tsl=tsl: e.tensor_tensor(out=qdT[:, c, tsl], in0=rqT[:, c, tsl], in1=qdec[:, c, :], op=ALU.mult),
                      reads=[t_rqT[c][t // 4], t_qdec], writes=[t_qdT[t]])
        for n in range(16):
            tsl = slice(n * 128, (n + 1) * 128)
            s = n % 2
            bSe, t_bSe = banks.next()
            bSo, t_bSo = banks.next()
            bS2 = (bSe, bSo); t_bS2 = (t_bSe, t_bSo)
            for h in range(4):
                hp = (h % 2) * 64; c = h // 2
                P.add("pe", lambda e, bS=bS2[h % 2], hp=hp, c=c, tsl=tsl: e.matmul(
                    bS[:, c * 128:(c + 1) * 128], lhsT=rkT[hp:hp + 64, c, tsl], rhs=rqT[hp:hp + 64, c, tsl], start=True, stop=True),
                    reads=[t_rkT[c][n // 4], t_rqT[c][n // 4]], writes=[t_bS2[h % 2]])
            for par in range(2):
                P.add("dve", lambda e, bS=bS2[par], s=s, par=par: e.tensor_tensor(
                    out=PdT[s][:, par * 256:(par + 1) * 256], in0=bS[:, 0:256], in1=decT[:, par * 256:(par + 1) * 256], op=ALU.mult),
                    reads=[t_bS2[par], t_dec], writes=[t_PdT[s]])
            bOe, t_bOe = banks.next()
            bOo, t_bOo = banks.next()
            bO2 = (bOe, bOo); t_bO2 = (t_bOe, t_bOo)
            for h in range(4):
                hp = (h % 2) * 64; c = h // 2
                pos = (h % 2) * 2 + c
                P.add("pe", lambda e, bO=bO2[h % 2], h=h, c=c, pos=pos, s=s, n=n: e.matmul(
                    bO[:, c * 128:(c + 1) * 128], lhsT=PdT[s][:, pos * 128:(pos + 1) * 128], rhs=rv[:, n, h * 128:(h + 1) * 128],
                    start=True, stop=(n == 0)), reads=[t_PdT[s], t_rv[n]], writes=[t_bO2[h % 2]])
                if n > 0:
                    P.add("pe", lambda e, bO=bO2[h % 2], hp=hp, c=c, tsl=tsl: e.matmul(
                        bO[:, c * 128:(c + 1) * 128], lhsT=qdT[hp:hp + 64, c, tsl], rhs=state_bf[hp:hp + 64, c, :],
                        start=False, stop=True), reads=[t_qdT[n], t_state_bf], writes=[t_bO2[h % 2]])
            if n < 15:
                bK, t_bK = banks.next()
                for h in range(4):
                    hp = (h % 2) * 64; c = h // 2
                    P.add("pe", lambda e, bK=bK, h=h, hp=hp, c=c, n=n: e.matmul(
                        bK[hp:hp + 64, c * 128:(c + 1) * 128], lhsT=kd[:, n, h * 64:(h + 1) * 64], rhs=rv[:, n, h * 128:(h + 1) * 128],
                        start=True, stop=True), reads=[t_kd[n], t_rv[n]], writes=[t_bK])
                for h in range(4):
                    hp = (h % 2) * 64; c = h // 2
                    if n == 0:
                        P.add("dve", lambda e, bK=bK, hp=hp, c=c: e.tensor_copy(out=state[hp:hp + 64, c, :], in_=bK[hp:hp + 64, c * 128:(c + 1) * 128]),
                              reads=[t_bK], writes=[t_state])
                    else:
                        cd = float(np.float32(np.exp(np.float32(128.0) * np.log(np.float32(GAMMAS[h])))))
                        P.add("dve", lambda e, bK=bK, hp=hp, c=c, cd=cd: e.scalar_tensor_tensor(
                            out=state[hp:hp + 64, c, :], in0=state[hp:hp + 64, c, :], scalar=cd, in1=bK[hp:hp + 64, c * 128:(c + 1) * 128],
                            op0=ALU.mult, op1=ALU.add), reads=[t_bK, t_state], writes=[t_state])
                P.add("act", lambda e: e.activation(out=state_bf[:], in_=state[:], func=AF.Copy), reads=[t_state], writes=[t_state_bf])
            bG, t_bG = banks.next()
            for kc in range(8):
                P.add("pe", lambda e, bG=bG, kc=kc, tsl=tsl: e.matmul(
                    bG[:], lhsT=x1T[:, kc, tsl], rhs=wrg[:, kc, :], start=(kc == 0), stop=(kc == 7)), reads=t_wrg, writes=[t_bG])
            P.add("act", lambda e, bG=bG, s=s: e.activation(out=sg[s][:], in_=bG[:], func=AF.Silu), reads=[t_bG], writes=[t_sg[s]])
            for h in range(4):
                P.add("dve", lambda e, bO=bO2[h % 2], h=h, s=s: e.bn_stats(out=stats[s][:, h, :], in_=bO[:, (h // 2) * 128:(h // 2 + 1) * 128]),
                      reads=[t_bO2[h % 2]], writes=[t_stats[s]])
            for h in range(4):
                P.add("dve", lambda e, h=h, s=s: e.bn_aggr(out=mvv[s][:, h, :], in_=stats[s][:, h, :]), reads=[t_stats[s]], writes=[t_mvv[s]])
            P.add("act", lambda e, s=s: e.activation(out=sd[s][:], in_=mvv[s][:, :, 1], func=AF.Sqrt, bias=epst[:, 0:1], scale=1.0),
                  reads=[t_mvv[s], t_eps], writes=[t_sd[s]])
            P.add("dve", lambda e, s=s: e.reciprocal(out=rstd[s][:], in_=sd[s][:]), reads=[t_sd[s]], writes=[t_rstd[s]])
            for h in range(4):
                P.add("dve", lambda e, bO=bO2[h % 2], h=h, s=s: e.tensor_scalar(
                    out=on[s][:, h * 128:(h + 1) * 128], in0=bO[:, (h // 2) * 128:(h // 2 + 1) * 128], scalar1=mvv[s][:, h, 0:1],
                    scalar2=rstd[s][:, h:h + 1], op0=ALU.subtract, op1=ALU.mult),
                    reads=[t_bO2[h % 2], t_mvv[s], t_rstd[s]], writes=[t_on[s]])
            P.add("pool", lambda e, s=s: e.tensor_tensor(out=on[s][:], in0=on[s][:], in1=gng[:], op=ALU.mult),
                  reads=[t_on[s], t_gng], writes=[t_on[s]])
            P.add("pool", lambda e, s=s: e.tensor_tensor(out=on[s][:], in0=on[s][:], in1=gnb[:], op=ALU.add),
                  reads=[t_on[s], t_gnb], writes=[t_on[s]])
            P.add("dve", lambda e, s=s: e.tensor_tensor(out=orb[s][:], in0=on[s][:], in1=sg[s][:], op=ALU.mult),
                  reads=[t_on[s], t_sg[s]], writes=[t_orb[s]])
            for c4 in range(4):
                P.add("pe", lambda e, c4=c4, s=s: e.transpose(out=pM[:, c4 * 128:(c4 + 1) * 128], in_=orb[s][:, c4 * 128:(c4 + 1) * 128], identity=idb[:]),
                      reads=[t_orb[s], t_idb], writes=[t_pM])
            P.add("act", lambda e, tsl=tsl: e.activation(out=oT[:, :, tsl], in_=pM[:, 0:512].rearrange("p (a b) -> p a b", a=4), func=AF.Copy),
                  reads=[t_pM], writes=[Tok()])
        env.P.emit("m2")


def merge_stage(nc, semstack, dmaq, x1T, oTs, w_in, wbrs, wo_d, g_d, b_d, src_d, dst_d):
    with contextlib.ExitStack() as st:
        env = Env(nc, semstack, dmaq, "m4", st)
        P = env.P
        banks = Banks(env, 8)
        mT = env.sb("mT", [128, 8, L], BF16); t_mT = [toks(4) for _ in range(8)]
        wout = env.sb("wout", [128, 8, D], BF16); t_wout = toks(8)
        sgm = [env.sb(f"sgm{i}", [128, 512], F32) for i in range(2)]; t_sgm = toks(2)
        tmp = [env.sb(f"tmp{i}", [128, 512], F32) for i in range(2)]; t_tmp = toks(2)
        acc = env.sb("acc", [128, L], F32); t_acc = toks(4)
        xs = [env.sb(f"xs{i}", [128, D], F32) for i in range(2)]; t_xs = toks(2)
        rr = [env.sb(f"rr{i}", [128, D], F32) for i in range(2)]; t_rr = toks(2)
        oo = [env.sb(f"oo{i}", [128, D], F32) for i in range(2)]; t_oo = toks(2)
        gam = env.sb("gam", [128, D], F32); t_gam = Tok()
        bet = env.sb("bet", [128, D], F32); t_bet = Tok()
        epst = env.sb("epst", [128, 1], F32); t_eps = Tok()
        stats = [env.sb(f"stats{i}", [128, 2, 6], F32) for i in range(2)]; t_stats = toks(2)
        mv = [env.sb(f"mv{i}", [128, 2], F32) for i in range(2)]; t_mv = toks(2)
        sd = [env.sb(f"sd{i}", [128, 1], F32) for i in range(2)]; t_sd = toks(2)
        rstd = [env.sb(f"rstd{i}", [128, 1], F32) for i in range(2)]; t_rstd = toks(2)
        t_dst = toks(16)
        ws = WStream(env, nst=2, nbf=6)
        P.add("sync", lambda e: e.dma_start(out=gam[:], in_=bcast_rows(g_d, 128)), writes=[t_gam], kind="d")
        P.add("sync", lambda e: e.dma_start(out=bet[:], in_=bcast_rows(b_d, 128)), writes=[t_bet], kind="d")
        P.add("dve", lambda e: e.memset(epst[:], LN_EPS), writes=[t_eps])
        cM = {"s": 0}
        for j in range(8):
            ws.load([(0, wo_d[:, j * 128:(j + 1) * 128], 1.0)], dst=lambda c0, w, j=j: wout[:, :, j * 128 + c0:j * 128 + c0 + w], t_dst=t_wout[j])
            for b in range(3):
                wgb, t_wgb = ws.load([(0, w_in[:, C_G + b * D + j * 128:C_G + b * D + (j + 1) * 128], 1.0)])
                wbb, t_wbb = ws.load([(0, wbrs[b][:, j * 128:(j + 1) * 128], 1.0)], KC=4)
                for tb in range(4):
                    tsl = slice(tb * 512, (tb + 1) * 512)
                    bG, t_bG = banks.next()
                    for kc in range(8):
                        P.add("pe", lambda e, bG=bG, wgb=wgb, kc=kc, tsl=tsl: e.matmul(
                            bG[:], lhsT=wgb[:, kc, :], rhs=x1T[:, kc, tsl], start=(kc == 0), stop=(kc == 7)),
                            reads=[t_wgb], writes=[t_bG])
                    bB, t_bB = banks.next()
                    for kc in range(4):
                        P.add("pe", lambda e, bB=bB, wbb=wbb, kc=kc, tsl=tsl, b=b: e.matmul(
                            bB[:], lhsT=wbb[:, kc, :], rhs=oTs[b][:, kc, tsl], start=(kc == 0), stop=(kc == 3)),
                            reads=[t_wbb], writes=[t_bB])
                    s = cM["s"] % 2
                    cM["s"] += 1
                    P.add("act", lambda e, bG=bG, s=s: e.activation(out=sgm[s][:], in_=bG[:], func=AF.Sigmoid), reads=[t_bG], writes=[t_sgm[s]])
                    if b == 0:
                        P.add("dve", lambda e, bB=bB, s=s, tsl=tsl: e.tensor_tensor(out=acc[:, tsl], in0=sgm[s][:], in1=bB[:], op=ALU.mult),
                              reads=[t_sgm[s], t_bB], writes=[t_acc[tb]])
                    else:
                        P.add("dve", lambda e, bB=bB, s=s: e.tensor_tensor(out=tmp[s][:], in0=sgm[s][:], in1=bB[:], op=ALU.mult),
                              reads=[t_sgm[s], t_bB], writes=[t_tmp[s]])
                        if b == 1:
                            P.add("dve", lambda e, s=s, tsl=tsl: e.tensor_tensor(out=acc[:, tsl], in0=acc[:, tsl], in1=tmp[s][:], op=ALU.add),
                                  reads=[t_acc[tb], t_tmp[s]], writes=[t_acc[tb]])
                        else:
                            P.add("dve", lambda e, s=s, j=j, tsl=tsl: e.tensor_tensor(out=mT[:, j, tsl], in0=acc[:, tsl], in1=tmp[s][:], op=ALU.add),
                                  reads=[t_acc[tb], t_tmp[s]], writes=[t_mT[j][tb]])
        for t in range(16):
            s = t % 2
            tsl = slice(t * 128, (t + 1) * 128)
            P.add("sync", lambda e, s=s, tsl=tsl: e.dma_start(out=xs[s][:], in_=src_d[tsl, :]), writes=[t_xs[s]], kind="d")
            for nh in range(2):
                bank, t_bank = banks.next()
                for kc in range(8):
                    P.add("pe", lambda e, bank=bank, kc=kc, nh=nh, tsl=tsl: e.matmul(
                        bank[:], lhsT=mT[:, kc, tsl], rhs=wout[:, kc, nh * 512:(nh + 1) * 512], start=(kc == 0), stop=(kc == 7)),
                        reads=[t_mT[kc][t // 4]] + t_wout[nh * 4:(nh + 1) * 4], writes=[t_bank])
                P.add("dve", lambda e, bank=bank, nh=nh, s=s: e.scalar_tensor_tensor(
                    out=rr[s][:, nh * 512:(nh + 1) * 512], in0=xs[s][:, nh * 512:(nh + 1) * 512], scalar=ALPHA, in1=bank[:],
                    op0=ALU.mult, op1=ALU.add), reads=[t_xs[s], t_bank], writes=[t_rr[s]])
            for k in range(2):
                P.add("dve", lambda e, s=s, k=k: e.bn_stats(out=stats[s][:, k, :], in_=rr[s][:, k * 512:(k + 1) * 512]),
                      reads=[t_rr[s]], writes=[t_stats[s]])
            P.add("dve", lambda e, s=s: e.bn_aggr(out=mv[s][:], in_=stats[s][:].rearrange("p a b -> p (a b)")),
                  reads=[t_stats[s]], writes=[t_mv[s]])
            P.add("act", lambda e, s=s: e.activation(out=sd[s][:], in_=mv[s][:, 1:2], func=AF.Sqrt, bias=epst[:, 0:1], scale=1.0),
                  reads=[t_mv[s], t_eps], writes=[t_sd[s]])
            P.add("dve", lambda e, s=s: e.reciprocal(out=rstd[s][:], in_=sd[s][:]), reads=[t_sd[s]], writes=[t_rstd[s]])
            P.add("dve", lambda e, s=s: e.tensor_scalar(
                out=rr[s][:], in0=rr[s][:], scalar1=mv[s][:, 0:1], scalar2=rstd[s][:, 0:1], op0=ALU.subtract, op1=ALU.mult),
                reads=[t_rr[s], t_mv[s], t_rstd[s]], writes=[t_rr[s]])
            P.add("pool", lambda e, s=s: e.tensor_tensor(out=oo[s][:], in0=rr[s][:], in1=gam[:], op=ALU.mult),
                  reads=[t_rr[s], t_gam], writes=[t_oo[s]])
            P.add("pool", lambda e, s=s: e.tensor_tensor(out=oo[s][:], in0=oo[s][:], in1=bet[:], op=ALU.add),
                  reads=[t_oo[s], t_bet], writes=[t_oo[s]])
            P.add("pool", lambda e, s=s, tsl=tsl: e.dma_start(out=dst_d[tsl, :], in_=oo[s][:]),
                  reads=[t_oo[s]], writes=[t_dst[t]], kind="d")
        P.wait_all("pool", t_dst)
        env.P.emit("m4")

def build_nc(stages=("ffn1", "mix", "ffn2"), dbg_mix=False, parts=("dsa", "ret", "mem", "merge")):
    nc = bass.Bass("TRN2", target_bir_lowering=False)
    din = lambda n, shape: nc.dram_tensor(n, shape, F32, kind="ExternalInput").ap()
    x_d = din("x", [L, D])
    mem_d = din("mem", [256, D])
    f1wi = din("ffn1_w_in", [D, 2 * DFF]); f1wo = din("ffn1_w_out", [DFF, D])
    ln1g = din("ln1_g", [1, D]); ln1b = din("ln1_b", [1, D])
    w_in = din("w_in", [D, W_IN_COLS]); t5 = din("t5_table", [32, 8])
    gng = din("ret_gn_g", [1, 512]); gnb = din("ret_gn_b", [1, 512])
    wmkv = din("w_mem_kv", [D, 1024])
    wbr = din("w_br_ret", [512, D]); wbd = din("w_br_dsa", [512, D]); wbm = din("w_br_mem", [512, D])
    wo = din("w_out", [D, D]); ln2g = din("ln2_g", [1, D]); ln2b = din("ln2_b", [1, D])
    f2wi = din("ffn2_w_in", [D, 2 * DFF]); f2wo = din("ffn2_w_out", [DFF, D])
    ln3g = din("ln3_g", [1, D]); ln3b = din("ln3_b", [1, D])
    ident = din("c_ident", [128, 128])
    cJ = din("c_J", [128, 128]); coh = din("c_oh", [32, 384]); ccausal = din("c_causal", [128, 128])
    cpow2 = din("c_pow2", [128, NIT])
    ccos = din("c_cos", [128, L]); csin = din("c_sin", [128, L])
    cdecay = din("c_decay", [128, 512]); ckdec = din("c_kdec", [128, 256]); cqdec = din("c_qdec", [128, 256])
    gscr = nc.dram_tensor("g_scr", [8, 384], F32).ap()
    out_d = nc.dram_tensor("out", [L, D], F32, kind="ExternalOutput").ap()
    x1_d = nc.dram_tensor("x1_scr", [L, D], F32).ap()
    x2_d = nc.dram_tensor("x2_scr", [L, D], F32).ap()
    with contextlib.ExitStack() as semstack:
        dmaq = {}
        cur = x_d
        for i, s in enumerate(stages):
            last = i == len(stages) - 1
            if s == "ffn1":
                dst = out_d if last else x1_d
                ffn_stage(nc, semstack, dmaq, "f1", cur, f1wi, f1wo, ln1g, ln1b, dst, ident)
                cur = dst
            elif s == "mix":
                dst = out_d if last else x2_d
                with contextlib.ExitStack() as outer:
                    x1T = outer.enter_context(nc.sbuf_tensor("x1T", [128, 8, L], BF16))
                    mix_load_stage(nc, semstack, dmaq, cur, ident, x1T)
                    o_dsaT = outer.enter_context(nc.sbuf_tensor("o_dsaT", [128, 4, L], BF16))
                    if "dsa" in parts:
                        dsa_stage(nc, semstack, dmaq, x1T, o_dsaT, w_in, t5, ident, cJ, coh, ccausal, cpow2, gscr)
                    o_retT = outer.enter_context(nc.sbuf_tensor("o_retT", [128, 4, L], BF16))
                    if "ret" in parts:
                        ret_stage(nc, semstack, dmaq, x1T, o_retT, w_in, gng, gnb, ident, ccos, csin, cdecay, ckdec, cqdec)
                    o_memT = outer.enter_context(nc.sbuf_tensor("o_memT", [128, 4, L], BF16))
                    if "mem" in parts:
                        mem_stage(nc, semstack, dmaq, x1T, o_memT, mem_d, w_in, wmkv, ident)
                    if dbg_mix:
                        dbg = nc.dram_tensor("dbg", [128, 12, L], BF16, kind="ExternalOutput").ap()
                        with contextlib.ExitStack() as st:
                            env = Env(nc, semstack, dmaq, "dbg", st)
                            tt = toks(3)
                            for i, (o, pn) in enumerate(((o_retT, "ret"), (o_dsaT, "dsa"), (o_memT, "mem"))):
                                if pn not in parts:
                                    continue
                                env.P.add("sync", lambda e, i=i, o=o: e.dma_start(out=dbg[:, 4 * i:4 * i + 4, :], in_=o[:]), writes=[tt[i]], kind="d")
                            env.P.wait_all("sync", tt)
                            env.P.emit("dbg")
                    if "merge" in parts:
                        merge_stage(nc, semstack, dmaq, x1T, [o_retT, o_dsaT, o_memT], w_in, [wbr, wbd, wbm], wo, ln2g, ln2b, cur, dst)
                cur = dst
            elif s == "mixdbg":
                with contextlib.ExitStack() as outer:
                    x1T = outer.enter_context(nc.sbuf_tensor("x1T", [128, 8, L], BF16))
                    mix_load_stage(nc, semstack, dmaq, cur, ident, x1T)
                    o_dsaT = outer.enter_context(nc.sbuf_tensor("o_dsaT", [128, 4, L], BF16))
                    dsa_stage(nc, semstack, dmaq, x1T, o_dsaT, w_in, t5, ident, cJ, coh, ccausal, cpow2, gscr)
                    o_memT = outer.enter_context(nc.sbuf_tensor("o_memT", [128, 4, L], BF16))
                    mem_stage(nc, semstack, dmaq, x1T, o_memT, mem_d, w_in, wmkv, ident)
                    dbg = nc.dram_tensor("dbg", [128, 8, L], BF16, kind="ExternalOutput").ap()
                    with contextlib.ExitStack() as st:
                        env = Env(nc, semstack, dmaq, "dbg", st)
                        t1 = Tok(); t2 = Tok()
                        env.P.add("sync", lambda e: e.dma_start(out=dbg[:, 0:4, :], in_=o_dsaT[:]), writes=[t1], kind="d")
                        env.P.add("sync", lambda e: e.dma_start(out=dbg[:, 4:8, :], in_=o_memT[:]), writes=[t2], kind="d")
                        env.P.wait_all("sync", [t1, t2])
                        env.P.emit("dbg")
            elif s == "ffn2":
                dst = out_d if last else x2_d
                ffn_stage(nc, semstack, dmaq, "f2", cur, f2wi, f2wo, ln3g, ln3b, dst, ident)
                cur = dst
    return nc


_CACHE = {}


def _t5_bucket(n):
    n = np.maximum(n, 0)
    nf = np.maximum(n, 1).astype(np.float32)
    large = 16 + (np.log(nf / np.float32(16)) / np.float32(np.log(128 / 16)) * np.float32(16)).astype(np.int32)
    large = np.minimum(large, 31)
    return np.where(n < 16, n, large)


def make_consts():
    c = {}
    c["c_J"] = np.ascontiguousarray(np.eye(128, dtype=np.float32)[::-1])
    oh = np.zeros((32, 384), np.float32)
    for u in range(383):
        d = u - 127
        if d >= 0:
            oh[_t5_bucket(np.array(d)), u] += 1.0
            oh[31, u] -= 1.0
    c["c_oh"] = oh
    q = np.arange(128)[:, None]; sk = np.arange(128)[None, :]
    c["c_causal"] = np.where(sk <= q, 0.0, -1.0e30).astype(np.float32)
    half = 32
    freqs = (np.float32(10000.0) ** (-np.arange(half, dtype=np.float32) / np.float32(half))).astype(np.float32)
    ang = np.arange(L, dtype=np.float32)[None, :] * freqs[np.arange(128) % 32][:, None]
    c["c_cos"] = np.cos(ang).astype(np.float32)
    c["c_sin"] = np.sin(ang).astype(np.float32)
    lg = np.log(np.array(GAMMAS, dtype=np.float32))
    i = np.arange(128)
    dec = np.zeros((128, 4, 128), np.float32)
    for h in range(4):
        diff = i[None, :] - i[:, None]
        dec[:, h, :] = np.where(diff >= 0, np.exp(np.maximum(diff, 0).astype(np.float32) * lg[h]), 0.0)
    c["c_decay"] = np.ascontiguousarray(dec[:, [0, 2, 1, 3], :]).reshape(128, 512)
    kdec = np.zeros((128, 4, 64), np.float32)
    for h in range(4):
        kdec[:, h, :] = np.exp((127 - i).astype(np.float32) * lg[h])[:, None]
    c["c_kdec"] = kdec.reshape(128, 256)
    qdec = np.zeros((128, 2, 128), np.float32)
    for p in range(128):
        for cc in range(2):
            qdec[p, cc, :] = np.exp((i + 1).astype(np.float32) * lg[2 * cc + p // 64])
    c["c_qdec"] = qdec.reshape(128, 256)
    c["c_pow2"] = np.tile((2.0 ** -(np.arange(NIT) + 1.0)).astype(np.float32)[None, :], (128, 1))
    return c


def make_in_maps(inputs):
    f = lambda a: np.ascontiguousarray(np.asarray(a, dtype=np.float32))
    shared = {
        "ffn1_w_in": f(inputs["ffn1_w_in"][0]), "ffn1_w_out": f(inputs["ffn1_w_out"][0]),
        "ln1_g": f(inputs["ln1_g"]), "ln1_b": f(inputs["ln1_b"]),
        "w_in": f(inputs["w_in"][0]), "t5_table": f(inputs["t5_table"]),
        "ret_gn_g": f(inputs["ret_gn_g"]), "ret_gn_b": f(inputs["ret_gn_b"]),
        "w_mem_kv": f(inputs["w_mem_kv"][0]),
        "w_br_ret": f(inputs["w_br_ret"][0]), "w_br_dsa": f(inputs["w_br_dsa"][0]), "w_br_mem": f(inputs["w_br_mem"][0]),
        "w_out": f(inputs["w_out"][0]), "ln2_g": f(inputs["ln2_g"]), "ln2_b": f(inputs["ln2_b"]),
        "ffn2_w_in": f(inputs["ffn2_w_in"][0]), "ffn2_w_out": f(inputs["ffn2_w_out"][0]),
        "ln3_g": f(inputs["ln3_g"]), "ln3_b": f(inputs["ln3_b"]),
        "c_ident": np.eye(128, dtype=np.float32),
    }
    shared.update(make_consts())
    x = f(inputs["x"])
    mem = f(inputs["mem"])
    return [dict(shared, x=x[b], mem=mem[b]) for b in range(x.shape[0])]


def kernel(**inputs):
    if "nc" not in _CACHE:
        _CACHE["nc"] = build_nc()
    nc = _CACHE["nc"]
    in_maps = make_in_maps(inputs)
    res = run_bass_kernel_spmd(nc, in_maps, core_ids=list(range(len(in_maps))))
    return np.stack([np.asarray(r["out"], dtype=np.float32) for r in res.results], axis=0)
```

```python
import contextlib
import numpy as np
import concourse.bass as bass
import concourse.mybir as mybir
from concourse.bass_utils import run_bass_kernel_spmd

F32 = mybir.dt.float32
BF16 = mybir.dt.bfloat16
AF = mybir.ActivationFunctionType
ALU = mybir.AluOpType

L = 2048
D = 1024
DFF = 2816
NCH = DFF // 128
ALPHA = 2.0 ** 0.25
LN_EPS = 1e-5
W_IN_COLS = 7240

ENGS = ("pe", "dve", "act", "pool", "sync")
SEM_ROT = {"c": 8000}
DMA_RING = 8
RELAX_SAME_ENGINE = True


class Tok:
    __slots__ = ("w", "r", "name")

    def __init__(self, name=""):
        self.w = None
        self.r = {}
        self.name = name


class Ins:
    __slots__ = ("eng", "kind", "fn", "deps", "idx", "inc", "sem", "val")

    def __init__(self, eng, kind, fn):
        self.eng = eng
        self.kind = kind
        self.fn = fn
        self.deps = []
        self.inc = False
        self.sem = None
        self.val = 0


class Prog:
    def __init__(self, nc, semstack, dmaq):
        self.nc = nc
        self.semstack = semstack
        self.dmaq = dmaq
        self.streams = {e: [] for e in ENGS}
        self.n = 0

    def add(self, eng, fn, reads=(), writes=(), kind="c"):
        ins = Ins(eng, kind, fn)
        st = self.streams[eng]
        ins.idx = len(st)
        deps = {}

        def dep(d, typ):
            if d is None or d is ins:
                return
            if d.eng == eng and d.kind == "c" and kind == "c":
                if eng == "pe":
                    return
                if typ != "RAW":
                    return
                if RELAX_SAME_ENGINE and eng in ("dve", "act") and ins.idx - d.idx >= 2:
                    return
            deps[id(d)] = d

        for t in reads:
            dep(t.w, "RAW")
        for t in writes:
            dep(t.w, "WAW")
            for d in t.r.values():
                dep(d, "WAR")
        if kind == "d":
            q = self.dmaq.setdefault(eng, {"n": 0, "last": {}, "sems": []})
            n = q["n"]
            q["n"] += 1
            r = n % DMA_RING
            if len(q["sems"]) <= r:
                q["sems"].append(self.semstack.enter_context(self.nc.semaphore(f"s_dma_{eng}_{r}")))
            ins.sem = q["sems"][r]
            ins.val = 16 * (n // DMA_RING + 1)
            ins.inc = True
            prev = q["last"].get(r)
            if prev is not None:
                deps[id(prev)] = prev
            q["last"][r] = ins
        ins.deps = list(deps.values())
        for d in ins.deps:
            d.inc = True
        for t in reads:
            t.r[eng + kind] = ins
        for t in writes:
            t.w = ins
            t.r = {}
        st.append(ins)
        self.n += 1
        return ins

    def wait_all(self, eng, toks):
        return self.add(eng, None, reads=toks, kind="w")

    def emit(self, name):
        nc = self.nc
        nsem = 0
        for e in ENGS:
            for kind in ("c",):
                cnt = 0
                cur = None
                for ins in self.streams[e]:
                    if ins.kind != kind or not ins.inc:
                        continue
                    if cur is None or cnt >= SEM_ROT[kind]:
                        cur = self.semstack.enter_context(nc.semaphore(f"s_{name}_{e}_{kind}_{nsem}"))
                        nsem += 1
                        cnt = 0
                    cnt += 1
                    ins.sem = cur
                    ins.val = cnt * (16 if kind == "d" else 1)
        with nc.Block() as block:
            engmap = {"pe": block.tensor, "dve": block.vector, "act": block.scalar,
                      "pool": block.gpsimd, "sync": block.sync}
            for e in ENGS:
                stream = self.streams[e]
                if not stream:
                    continue

                def body(eobj, stream=stream):
                    waited = {}
                    for ins in stream:
                        for d in ins.deps:
                            k = id(d.sem)
                            if waited.get(k, 0) >= d.val:
                                continue
                            eobj.wait_ge(d.sem, d.val)
                            waited[k] = d.val
                        if ins.fn is None:
                            continue
                        r = ins.fn(eobj)
                        if ins.inc:
                            r.then_inc(ins.sem, 16 if ins.kind == "d" else 1)

                engmap[e](body)


def bcast_rows(ap2d, nrows):
    return bass.AP(ap2d.tensor, ap2d.offset, [[0, nrows], [1, ap2d.shape[-1]]])


def ffn_stage(nc, semstack, dmaq, name, src_d, w_in_d, w_out_d, g_d, b_d, dst_d, ident_d):
    with contextlib.ExitStack() as st:
        P = Prog(nc, semstack, dmaq)
        sb = lambda n, shape, dt: st.enter_context(nc.sbuf_tensor(f"{name}_{n}", shape, dt))
        ps = lambda n, shape, dt: st.enter_context(nc.psum_tensor(f"{name}_{n}", shape, dt))
        HT = 1024
        xT = [sb(f"xT{i}", [128, 8, HT], BF16) for i in range(2)]
        gT = sb("gT", [128, NCH, HT], BF16)
        wout = sb("wout", [128, NCH, D], BF16)
        wst = [sb(f"wst{i}", [128, 2, 8, 128], F32) for i in range(2)]
        wbf = [sb(f"wbf{i}", [128, 2, 8, 128], BF16) for i in range(3)]
        wost = [sb(f"wost{i}", [128, D], F32) for i in range(2)]
        xs = [sb(f"xs{i}", [128, D], F32) for i in range(2)]
        xb = [sb(f"xb{i}", [128, D], BF16) for i in range(2)]
        sA = [sb(f"sA{i}", [128, 512], F32) for i in range(2)]
        rr = [sb(f"rr{i}", [128, D], F32) for i in range(2)]
        oo = [sb(f"oo{i}", [128, D], F32) for i in range(2)]
        gam = sb("gam", [128, D], F32)
        bet = sb("bet", [128, D], F32)
        idf = sb("idf", [128, 128], F32)
        idb = sb("idb", [128, 128], BF16)
        epst = sb("epst", [128, 1], F32)
        stats = [sb(f"stats{i}", [128, 2, 6], F32) for i in range(2)]
        mv = [sb(f"mv{i}", [128, 2], F32) for i in range(2)]
        sd = [sb(f"sd{i}", [128, 1], F32) for i in range(2)]
        rstd = [sb(f"rstd{i}", [128, 1], F32) for i in range(2)]
        pT = [ps(f"pT{i}", [128, 1024], BF16) for i in range(2)]
        pB = [ps(f"pB{i}", [128, 512], F32) for i in range(6)]

        def toks(n, k):
            return [Tok(f"{n}{i}") for i in range(k)]

        t_xT = [toks("xT", 8) for _ in range(2)]
        t_gT = [[Tok() for _ in range(2)] for _ in range(NCH)]
        t_wout = toks("wout", NCH)
        t_wst = toks("wst", 2); t_wbf = toks("wbf", 3); t_wost = toks("wost", 2)
        t_xs = toks("xs", 2); t_xb = toks("xb", 2); t_sA = toks("sA", 2)
        t_ys = toks("ys", 2); t_rr = toks("rr", 2); t_oo = toks("oo", 2)
        t_gam = Tok(); t_bet = Tok(); t_idf = Tok(); t_idb = Tok(); t_eps = Tok()
        t_stats = toks("st", 2); t_mv = toks("mv", 2); t_sd = toks("sd", 2); t_rstd = toks("rs", 2)
        t_pT = toks("pT", 2); t_pB = toks("pB", 6)
        t_dst = toks("dst", 16)

        P.add("sync", lambda e: e.dma_start(out=idf[:], in_=ident_d), writes=[t_idf], kind="d")
        P.add("sync", lambda e: e.dma_start(out=gam[:], in_=bcast_rows(g_d, 128)), writes=[t_gam], kind="d")
        P.add("sync", lambda e: e.dma_start(out=bet[:], in_=bcast_rows(b_d, 128)), writes=[t_bet], kind="d")
        P.add("dve", lambda e: e.tensor_copy(out=idb[:], in_=idf[:]), reads=[t_idf], writes=[t_idb])
        P.add("dve", lambda e: e.memset(epst[:], LN_EPS), writes=[t_eps])

        cnt = {"x": 0, "pb": 0, "sa": 0, "c": 0}

        def stage_a(h):
            for tl in range(8):
                t = h * 8 + tl
                s = cnt["x"] % 2
                cnt["x"] += 1
                P.add("sync", lambda e, s=s, t=t: e.dma_start(out=xs[s][:], in_=src_d[t * 128:(t + 1) * 128, :]),
                      writes=[t_xs[s]], kind="d")
                P.add("dve", lambda e, s=s: e.tensor_copy(out=xb[s][:], in_=xs[s][:]), reads=[t_xs[s]], writes=[t_xb[s]])
                for kc in range(8):
                    P.add("pe", lambda e, s=s, kc=kc: e.transpose(out=pT[s][:, kc * 128:(kc + 1) * 128],
                                                                  in_=xb[s][:, kc * 128:(kc + 1) * 128], identity=idb[:]),
                          reads=[t_xb[s], t_idb], writes=[t_pT[s]])
                P.add("act", lambda e, s=s, h=h, tl=tl: e.activation(
                    out=xT[h][:, :, tl * 128:(tl + 1) * 128],
                    in_=pT[s][:].rearrange("p (a b) -> p a b", a=8), func=AF.Copy),
                    reads=[t_pT[s]], writes=[t_xT[h][tl]])

        wcount = {"c": 0}

        def load_w(c, with_out):
            s = wcount["c"] % 2
            bs = wcount["c"] % 3
            wcount["c"] += 1
            for j in range(2):
                col0 = j * DFF + c * 128
                P.add("sync", lambda e, s=s, j=j, col0=col0: e.dma_start(
                    out=wst[s][:, j], in_=w_in_d[:, col0:col0 + 128].rearrange("(kc p) n -> p kc n", p=128)),
                    writes=[t_wst[s]], kind="d")
            P.add("pool", lambda e, s=s, bs=bs: e.tensor_copy(out=wbf[bs][:], in_=wst[s][:]),
                  reads=[t_wst[s]], writes=[t_wbf[bs]])
            if with_out:
                P.add("sync", lambda e, s=s, c=c: e.dma_start(out=wost[s][:], in_=w_out_d[c * 128:(c + 1) * 128, :]),
                      writes=[t_wost[s]], kind="d")
                P.add("pool", lambda e, s=s, c=c: e.tensor_copy(out=wout[:, c, :], in_=wost[s][:]),
                      reads=[t_wost[s]], writes=[t_wout[c]])
            return bs

        def stage_b(h):
            pending = load_w(0, h == 0)
            for c in range(NCH):
                bs = pending
                if c + 1 < NCH:
                    pending = load_w(c + 1, h == 0)
                for tb in range(2):
                    pa = (cnt["pb"] % 2) * 2
                    cnt["pb"] += 1
                    for j in range(2):
                        for kc in range(8):
                            P.add("pe", lambda e, pa=pa, j=j, kc=kc, bs=bs, tb=tb, h=h: e.matmul(
                                pB[pa + j][:], lhsT=wbf[bs][:, j, kc, :], rhs=xT[h][:, kc, tb * 512:(tb + 1) * 512],
                                start=(kc == 0), stop=(kc == 7)),
                                reads=[t_wbf[bs]] + t_xT[h][tb * 4:(tb + 1) * 4], writes=[t_pB[pa + j]])
                    s = cnt["sa"] % 2
                    cnt["sa"] += 1
                    P.add("act", lambda e, s=s, pa=pa: e.activation(out=sA[s][:], in_=pB[pa][:], func=AF.Silu),
                          reads=[t_pB[pa]], writes=[t_sA[s]])
                    P.add("dve", lambda e, s=s, pa=pa, c=c, tb=tb: e.scalar_tensor_tensor(
                        out=gT[:, c, tb * 512:(tb + 1) * 512], in0=sA[s][:], scalar=0.5, in1=pB[pa + 1][:],
                        op0=ALU.mult, op1=ALU.mult),
                        reads=[t_sA[s], t_pB[pa + 1]], writes=[t_gT[c][tb]])

        def stage_c(h):
            for tl in range(8):
                t = h * 8 + tl
                s = cnt["c"] % 2
                cnt["c"] += 1
                pa = (cnt["pb"] % 2) * 2
                cnt["pb"] += 1
                sx = cnt["x"] % 2
                cnt["x"] += 1
                P.add("sync", lambda e, sx=sx, t=t: e.dma_start(out=xs[sx][:], in_=src_d[t * 128:(t + 1) * 128, :]),
                      writes=[t_xs[sx]], kind="d")
                for nh in range(2):
                    for kc in range(NCH):
                        P.add("pe", lambda e, pa=pa, nh=nh, kc=kc, tl=tl: e.matmul(
                            pB[pa + nh][:], lhsT=gT[:, kc, tl * 128:(tl + 1) * 128], rhs=wout[:, kc, nh * 512:(nh + 1) * 512],
                            start=(kc == 0), stop=(kc == NCH - 1)),
                            reads=[t_gT[kc][tl // 4], t_wout[kc]], writes=[t_pB[pa + nh]])
                    P.add("dve", lambda e, pa=pa, nh=nh, s=s, sx=sx: e.scalar_tensor_tensor(
                        out=rr[s][:, nh * 512:(nh + 1) * 512], in0=xs[sx][:, nh * 512:(nh + 1) * 512], scalar=ALPHA,
                        in1=pB[pa + nh][:], op0=ALU.mult, op1=ALU.add),
                        reads=[t_xs[sx], t_pB[pa + nh]], writes=[t_rr[s]])
                for k in range(2):
                    P.add("dve", lambda e, s=s, k=k: e.bn_stats(out=stats[s][:, k, :], in_=rr[s][:, k * 512:(k + 1) * 512]),
                          reads=[t_rr[s]], writes=[t_stats[s]])
                P.add("dve", lambda e, s=s: e.bn_aggr(out=mv[s][:], in_=stats[s][:].rearrange("p a b -> p (a b)")),
                      reads=[t_stats[s]], writes=[t_mv[s]])
                P.add("act", lambda e, s=s: e.activation(out=sd[s][:], in_=mv[s][:, 1:2], func=AF.Sqrt, bias=epst[:, 0:1], scale=1.0),
                      reads=[t_mv[s], t_eps], writes=[t_sd[s]])
                P.add("dve", lambda e, s=s: e.reciprocal(out=rstd[s][:], in_=sd[s][:]), reads=[t_sd[s]], writes=[t_rstd[s]])
                P.add("dve", lambda e, s=s: e.tensor_scalar(
                    out=rr[s][:], in0=rr[s][:], scalar1=mv[s][:, 0:1], scalar2=rstd[s][:, 0:1], op0=ALU.subtract, op1=ALU.mult),
                    reads=[t_rr[s], t_mv[s], t_rstd[s]], writes=[t_rr[s]])
                P.add("pool", lambda e, s=s: e.tensor_tensor(out=oo[s][:], in0=rr[s][:], in1=gam[:], op=ALU.mult),
                      reads=[t_rr[s], t_gam], writes=[t_oo[s]])
                P.add("pool", lambda e, s=s: e.tensor_tensor(out=oo[s][:], in0=oo[s][:], in1=bet[:], op=ALU.add),
                      reads=[t_oo[s], t_bet], writes=[t_oo[s]])
                P.add("pool", lambda e, s=s, t=t: e.dma_start(out=dst_d[t * 128:(t + 1) * 128, :], in_=oo[s][:]),
                      reads=[t_oo[s]], writes=[t_dst[t]], kind="d")

        stage_a(0)
        stage_a(1)
        stage_b(0)
        stage_c(0)
        stage_b(1)
        stage_c(1)
        P.wait_all("pool", t_dst)
        P.emit(name)


C_RQ, C_RK, C_RV, C_RG = 0, 256, 512, 1024
C_DQ, C_DK, C_DV, C_IQ, C_IK, C_IW, C_MQ, C_G = 1536, 2048, 2560, 3072, 3584, 3648, 3656, 4168
IW_SCALE = float(8 ** -0.5 * 64 ** -0.5)
NIT = 18
GAMMAS = [1.0 - 2.0 ** (-5.0 - h) for h in range(4)]


def toks(k):
    return [Tok() for _ in range(k)]


class Env:
    def __init__(self, nc, semstack, dmaq, name, st):
        self.nc = nc
        self.P = Prog(nc, semstack, dmaq)
        self.name = name
        self.st = st
        self.k = 0

    def sb(self, n, shape, dt):
        return self.st.enter_context(self.nc.sbuf_tensor(f"{self.name}_{n}", shape, dt))

    def ps(self, n, shape, dt):
        return self.st.enter_context(self.nc.psum_tensor(f"{self.name}_{n}", shape, dt))


class WStream:
    def __init__(self, env, nst=2, nbf=3):
        self.env = env
        self.st = [env.sb(f"wsst{i}", [128, 8, 128], F32) for i in range(nst)]
        self.bf = [env.sb(f"wsbf{i}", [128, 8, 128], BF16) for i in range(nbf)]
        self.t_st = toks(nst)
        self.t_bf = toks(nbf)
        self.n = 0

    def load(self, pieces, KC=8, dst=None, t_dst=None):
        P = self.env.P
        s = self.n % len(self.st)
        b = self.n % len(self.bf)
        self.n += 1
        stt = self.st[s]
        for (c0, src, sc) in pieces:
            w = src.shape[-1]
            P.add("sync", lambda e, stt=stt, c0=c0, src=src, w=w, KC=KC: e.dma_start(
                out=stt[:, 0:KC, c0:c0 + w], in_=src.rearrange("(kc p) n -> p kc n", p=128)),
                writes=[self.t_st[s]], kind="d")
        if dst is None:
            tot = max(c0 + src.shape[-1] for (c0, src, sc) in pieces)
            out_t = self.bf[b]
            out_fn = lambda c0, w: out_t[:, 0:KC, c0:c0 + w]
            t_out = self.t_bf[b]
        else:
            out_fn = dst
            t_out = t_dst
        if all(sc == 1.0 for (_, _, sc) in pieces):
            lo = min(c0 for (c0, _, _) in pieces)
            hi = max(c0 + src.shape[-1] for (c0, src, _) in pieces)
            P.add("pool", lambda e, lo=lo, hi=hi, stt=stt, KC=KC: e.tensor_copy(out=out_fn(lo, hi - lo), in_=stt[:, 0:KC, lo:hi]),
                  reads=[self.t_st[s]], writes=[t_out])
        else:
            for (c0, src, sc) in pieces:
                w = src.shape[-1]
                P.add("dve", lambda e, c0=c0, w=w, sc=sc, stt=stt, KC=KC: e.tensor_scalar(
                    out=out_fn(c0, w), in0=stt[:, 0:KC, c0:c0 + w], scalar1=float(sc), scalar2=None, op0=ALU.mult),
                    reads=[self.t_st[s]], writes=[t_out])
        return (self.bf[b] if dst is None else None), t_out


def load_transposed(env, src_d, ntiles, dstT, t_dstT, idb, t_idb, pT, t_pT):
    P = env.P
    xs = [env.sb(f"ltxs{i}", [128, D], F32) for i in range(2)]
    xb = [env.sb(f"ltxb{i}", [128, D], BF16) for i in range(2)]
    t_xs = toks(2); t_xb = toks(2)
    for t in range(ntiles):
        s = t % 2
        P.add("sync", lambda e, s=s, t=t: e.dma_start(out=xs[s][:], in_=src_d[t * 128:(t + 1) * 128, :]),
              writes=[t_xs[s]], kind="d")
        P.add("dve", lambda e, s=s: e.tensor_copy(out=xb[s][:], in_=xs[s][:]), reads=[t_xs[s]], writes=[t_xb[s]])
        for kc in range(8):
            P.add("pe", lambda e, s=s, kc=kc: e.transpose(out=pT[s][:, kc * 128:(kc + 1) * 128],
                                                          in_=xb[s][:, kc * 128:(kc + 1) * 128], identity=idb[:]),
                  reads=[t_xb[s], t_idb], writes=[t_pT[s]])
        P.add("act", lambda e, s=s, t=t: e.activation(
            out=dstT[:, :, t * 128:(t + 1) * 128], in_=pT[s][:].rearrange("p (a b) -> p a b", a=8), func=AF.Copy),
            reads=[t_pT[s]], writes=[t_dstT[t]])


def load_ident(env, ident_d):
    P = env.P
    idf = env.sb("idf", [128, 128], F32)
    idb = env.sb("idb", [128, 128], BF16)
    t_idf = Tok(); t_idb = Tok()
    P.add("sync", lambda e: e.dma_start(out=idf[:], in_=ident_d), writes=[t_idf], kind="d")
    P.add("dve", lambda e: e.tensor_copy(out=idb[:], in_=idf[:]), reads=[t_idf], writes=[t_idb])
    return idf, t_idf, idb, t_idb


class Banks:
    def __init__(self, env, n):
        self.b = [env.ps(f"bk{i}", [128, 512], F32) for i in range(n)]
        self.t = toks(n)
        self.i = 0
        self.n = n

    def next(self):
        i = self.i % self.n
        self.i += 1
        return self.b[i], self.t[i]


def proj_fm(env, ws, banks, pieces, ncols, rhs_fn, rhs_toks_fn, nblk, blkw, evac, KC=8):
    P = env.P
    wt, t_w = ws.load(pieces, KC=KC)
    for tb in range(nblk):
        bank, t_bank = banks.next()
        for kc in range(KC):
            P.add("pe", lambda e, bank=bank, kc=kc, tb=tb, wt=wt: e.matmul(
                bank[0:ncols, 0:blkw], lhsT=wt[:, kc, 0:ncols], rhs=rhs_fn(kc, tb), start=(kc == 0), stop=(kc == KC - 1)),
                reads=[t_w] + rhs_toks_fn(tb), writes=[t_bank])
        evac(tb, bank, t_bank)


def mix_load_stage(nc, semstack, dmaq, x1_d, ident_d, x1T):
    with contextlib.ExitStack() as st:
        env = Env(nc, semstack, dmaq, "m0", st)
        idf, t_idf, idb, t_idb = load_ident(env, ident_d)
        pT = [env.ps(f"pT{i}", [128, 1024], BF16) for i in range(2)]
        t_pT = toks(2)
        t_x1T = toks(16)
        load_transposed(env, x1_d, 16, x1T, t_x1T, idb, t_idb, pT, t_pT)
        env.P.emit("m0")


def mem_stage(nc, semstack, dmaq, x1T, oT, mem_d, w_in, wmkv, ident_d):
    with contextlib.ExitStack() as st:
        env = Env(nc, semstack, dmaq, "m3", st)
        P = env.P
        idf, t_idf, idb, t_idb = load_ident(env, ident_d)
        pT = [env.ps(f"pT{i}", [128, 1024], BF16) for i in range(2)]
        t_pT = toks(2)
        banks = Banks(env, 4)
        pN = env.ps("pN", [128, 512], F32); t_pN = Tok()
        pD = env.ps("pD", [128, 512], F32); t_pD = Tok()
        memT = env.sb("memT", [128, 8, 256], BF16); t_memT = toks(2)
        mkT = env.sb("mkT", [128, 4, 256], BF16); t_mkT = toks(4)
        mvv = env.sb("mv", [128, 2, 512], BF16); t_mv = toks(2)
        wmv = env.sb("wmv", [128, 8, 512], BF16); t_wmv = toks(4)
        mqT = env.sb("mqT", [128, 4, L], BF16); t_mqT = [toks(4) for _ in range(4)]
        ones = env.sb("ones", [128, 128], BF16); t_ones = Tok()
        E = [env.sb(f"E{i}", [128, 512], BF16) for i in range(4)]; t_E = toks(4)
        rden = [env.sb(f"rden{i}", [128, 512], F32) for i in range(2)]; t_rden = toks(2)
        ws = WStream(env)
        P.add("pool", lambda e: e.memset(ones[:], 1.0), writes=[t_ones])
        load_transposed(env, mem_d, 2, memT, t_memT, idb, t_idb, pT, t_pT)
        t_x1T = []
        for h in range(4):
            def evac(tb, bank, t_bank, h=h):
                P.add("act", lambda e: e.activation(out=mkT[:, h, :], in_=bank[:, 0:256], func=AF.Copy),
                      reads=[t_bank], writes=[t_mkT[h]])
            proj_fm(env, ws, banks, [(0, wmkv[:, h * 128:(h + 1) * 128], 1.0)], 128,
                    lambda kc, tb: memT[:, kc, :], lambda tb: t_memT, 1, 256, evac)
        for c in range(4):
            ws.load([(0, wmkv[:, 512 + c * 128:512 + (c + 1) * 128], 1.0)],
                    dst=lambda c0, w, c=c: wmv[:, :, c * 128 + c0:c * 128 + c0 + w], t_dst=t_wmv[c])
        for mt in range(2):
            bank, t_bank = banks.next()
            for kc in range(8):
                P.add("pe", lambda e, bank=bank, kc=kc, mt=mt: e.matmul(
                    bank[:], lhsT=memT[:, kc, mt * 128:(mt + 1) * 128], rhs=wmv[:, kc, :], start=(kc == 0), stop=(kc == 7)),
                    reads=[t_memT[mt]] + t_wmv, writes=[t_bank])
            P.add("act", lambda e, bank=bank, mt=mt: e.activation(out=mvv[:, mt, :], in_=bank[:], func=AF.Copy),
                  reads=[t_bank], writes=[t_mv[mt]])
        for h in range(4):
            def evac(tb, bank, t_bank, h=h):
                P.add("act", lambda e: e.activation(out=mqT[:, h, tb * 512:(tb + 1) * 512], in_=bank[:], func=AF.Copy,
                                                    scale=float(128 ** -0.5)),
                      reads=[t_bank], writes=[t_mqT[h][tb]])
            proj_fm(env, ws, banks, [(0, w_in[:, C_MQ + h * 128:C_MQ + (h + 1) * 128], 1.0)], 128,
                    lambda kc, tb: x1T[:, kc, tb * 512:(tb + 1) * 512], lambda tb: [], 4, 512, evac)
        it = 0
        for h in range(4):
            for qb in range(4):
                for mt in range(2):
                    bank, t_bank = banks.next()
                    ei = (it * 2 + mt) % 4
                    P.add("pe", lambda e, bank=bank, h=h, qb=qb, mt=mt: e.matmul(
                        bank[:], lhsT=mkT[:, h, mt * 128:(mt + 1) * 128], rhs=mqT[:, h, qb * 512:(qb + 1) * 512],
                        start=True, stop=True), reads=[t_mkT[h], t_mqT[h][qb]], writes=[t_bank])
                    P.add("act", lambda e, bank=bank, ei=ei: e.activation(out=E[ei][:], in_=bank[:], func=AF.Exp),
                          reads=[t_bank], writes=[t_E[ei]])
                for mt in range(2):
                    ei = (it * 2 + mt) % 4
                    P.add("pe", lambda e, ei=ei, h=h, mt=mt: e.matmul(
                        pN[:], lhsT=mvv[:, mt, h * 128:(h + 1) * 128], rhs=E[ei][:], start=(mt == 0), stop=(mt == 1)),
                        reads=[t_mv[mt], t_E[ei]], writes=[t_pN])
                    P.add("pe", lambda e, ei=ei, mt=mt: e.matmul(
                        pD[:], lhsT=ones[:], rhs=E[ei][:], start=(mt == 0), stop=(mt == 1)),
                        reads=[t_ones, t_E[ei]], writes=[t_pD])
                r = it % 2
                P.add("dve", lambda e, r=r: e.reciprocal(out=rden[r][:], in_=pD[:]), reads=[t_pD], writes=[t_rden[r]])
                P.add("dve", lambda e, r=r, h=h, qb=qb: e.tensor_tensor(
                    out=oT[:, h, qb * 512:(qb + 1) * 512], in0=pN[:], in1=rden[r][:], op=ALU.mult),
                    reads=[t_pN, t_rden[r]], writes=[Tok()])
                it += 1
        env.P.emit("m3")


def dsa_stage(nc, semstack, dmaq, x1T, oT, w_in, t5_d, ident_d, J_d, oh_d, causal_d, pow2_d, gscr_d):
    with contextlib.ExitStack() as st:
        env = Env(nc, semstack, dmaq, "m1", st)
        P = env.P
        idf, t_idf, idb, t_idb = load_ident(env, ident_d)
        banks = Banks(env, 3)
        pO = [[env.ps(f"pO{i}{j}", [128, 512], F32) for j in range(2)] for i in range(2)]
        t_pO = [toks(2) for _ in range(2)]
        pM = env.ps("pM", [128, 1024], BF16); t_pM = Tok()
        qT = env.sb("qT", [128, 4, L], BF16); t_qT = [toks(4) for _ in range(4)]
        kT = env.sb("kT", [128, 4, L], BF16); t_kT = [toks(4) for _ in range(4)]
        qiT = env.sb("qiT", [128, 4, L], BF16); t_qiT = [toks(4) for _ in range(4)]
        kiT = env.sb("kiT", [128, L], BF16); t_kiT = toks(4)
        vaug = env.sb("vaug", [128, 16, 8, 65], BF16); t_v = toks(16); t_vones = Tok()
        iw = env.sb("iw", [128, 16, 8], F32); t_iw = toks(16)
        wv = env.sb("wv", [128, 8, 512], BF16); t_wv = toks(4)
        wiw = env.sb("wiw", [128, 8, 8], BF16); t_wiw = Tok()
        Sc2 = [env.sb(f"Sc{i}", [128, L], F32) for i in range(2)]; t_Sc2 = toks(2)
        junk2 = [env.sb(f"junk{i}", [128, L], BF16) for i in range(2)]; t_junk2 = toks(2)
        mask = [env.sb(f"mask{i}", [128, L], BF16) for i in range(2)]; t_mask = toks(2)
        maskTp = [env.sb(f"maskTp{i}", [128, 16, 2, 128], BF16) for i in range(2)]
        t_maskTp = [toks(2) for _ in range(2)]
        tI = [env.sb(f"tI{i}", [128, 512], F32) for i in range(2)]; t_tI = toks(2)
        E = [env.sb(f"E{i}", [128, 512], BF16) for i in range(2)]; t_E = toks(2)
        PT = [env.sb(f"PT{i}", [128, 512], BF16) for i in range(3)]; t_PT = toks(3)
        BT = env.sb("BT", [128, 8, 256], BF16); t_BT = toks(8)
        H = Sc2[1][:].rearrange("p (h n) -> p h n", h=8); t_H = t_Sc2[1]
        rden8 = [env.sb(f"rden{i}", [128, 2, 4], F32) for i in range(2)]; t_rden8 = toks(2)
        o_tm = [env.sb(f"otm{i}", [128, 512], BF16) for i in range(2)]; t_otm = toks(2)
        Jf = env.sb("Jf", [128, 128], F32); t_J = Tok()
        caus = env.sb("caus", [128, 128], F32); t_caus = Tok()
        pow2 = env.sb("pow2", [128, NIT], F32); t_pow2 = Tok()
        tabS = env.sb("tabS", [32, 8], F32); t_tab = Tok()
        ohS = env.sb("ohS", [32, 384], F32); t_oh = Tok()
        Gs = env.sb("Gs", [8, 384], F32); t_Gs = Tok()
        thrneg = env.sb("thrneg", [128, 1], F32); t_thrneg = Tok()
        mx8_2 = [env.sb(f"mx8{i}", [128, 8], F32) for i in range(2)]; t_mx8_2 = toks(2)
        mn_2 = [env.sb(f"mn{i}", [128, 1], F32) for i in range(2)]; t_mn_2 = toks(2)
        rng_2 = [env.sb(f"rng{i}", [128, 1], F32) for i in range(2)]; t_rng_2 = toks(2)
        thr_2 = [env.sb(f"thr{i}", [128, 1], F32) for i in range(2)]; t_thr_2 = toks(2)
        S2_2 = [env.sb(f"S2{i}", [128, NIT], F32) for i in range(2)]; t_S2_2 = toks(2)
        cnt_2 = [env.sb(f"cnt{i}", [128, 1], F32) for i in range(2)]; t_cnt_2 = toks(2)
        ee_2 = [env.sb(f"ee{i}", [128, 1], F32) for i in range(2)]; t_ee_2 = toks(2)
        ws = WStream(env)
        t_gscr = Tok()
        print("[dsa] sbuf bytes remaining/partition:", nc.sbuf_bytes_remaining() if callable(nc.sbuf_bytes_remaining) else nc.sbuf_bytes_remaining)

        P.add("sync", lambda e: e.dma_start(out=Jf[:], in_=J_d), writes=[t_J], kind="d")
        P.add("sync", lambda e: e.dma_start(out=caus[:], in_=causal_d), writes=[t_caus], kind="d")
        P.add("sync", lambda e: e.dma_start(out=pow2[:], in_=pow2_d), writes=[t_pow2], kind="d")
        P.add("sync", lambda e: e.dma_start(out=tabS[:], in_=t5_d), writes=[t_tab], kind="d")
        P.add("sync", lambda e: e.dma_start(out=ohS[:], in_=oh_d), writes=[t_oh], kind="d")
        P.add("pool", lambda e: e.memset(vaug[:, :, :, 64:65], 1.0), writes=[t_vones])
        for i in range(2):
            P.add("pool", lambda e, i=i: e.memset(maskTp[i][:], 0.0), writes=t_maskTp[i])
        P.add("pool", lambda e: e.memset(thrneg[:], -1.0e29), writes=[t_thrneg])
        bank, t_bank = banks.next()
        P.add("pe", lambda e, bank=bank: e.matmul(bank[0:8, 0:384], lhsT=tabS[:], rhs=ohS[:], start=True, stop=True),
              reads=[t_tab, t_oh], writes=[t_bank])
        P.add("act", lambda e, bank=bank: e.activation(out=Gs[:], in_=bank[0:8, 0:384], func=AF.Copy), reads=[t_bank], writes=[t_Gs])
        P.add("sync", lambda e: e.dma_start(out=gscr_d, in_=Gs[:]), reads=[t_Gs], writes=[t_gscr], kind="d")
        hank = bass.AP(gscr_d.tensor, gscr_d.offset, [[1, 128], [384, 8], [1, 256]])
        P.add("sync", lambda e: e.dma_start(out=H, in_=hank), reads=[t_gscr], writes=[t_H], kind="d")
        for h in range(8):
            bank, t_bank = banks.next()
            P.add("pe", lambda e, bank=bank, h=h: e.matmul(bank[:, 0:256], lhsT=Jf[:], rhs=H[:, h, :], start=True, stop=True),
                  reads=[t_J, t_H], writes=[t_bank])
            P.add("act", lambda e, bank=bank, h=h: e.activation(out=BT[:, h, :], in_=bank[:, 0:256], func=AF.Copy),
                  reads=[t_bank], writes=[t_BT[h]])

        ev = {"i": 0}

        def evac_to(dstfn, t_dstfn, scale):
            def evac(tb, bank, t_bank):
                eng = "act" if ev["i"] % 2 == 0 else "dve"
                ev["i"] += 1
                if eng == "act":
                    P.add("act", lambda e: e.activation(out=dstfn(tb), in_=bank[:], func=AF.Copy, scale=float(scale)),
                          reads=[t_bank], writes=[t_dstfn(tb)])
                else:
                    P.add("dve", lambda e: e.tensor_scalar(out=dstfn(tb), in0=bank[:], scalar1=float(scale), scalar2=None, op0=ALU.mult),
                          reads=[t_bank], writes=[t_dstfn(tb)])
            return evac

        xrhs = lambda kc, tb: x1T[:, kc, tb * 512:(tb + 1) * 512]
        for c in range(4):
            proj_fm(env, ws, banks, [(0, w_in[:, C_DQ + c * 128:C_DQ + (c + 1) * 128], 1.0)], 128, xrhs, lambda tb: [], 4, 512,
                    evac_to(lambda tb, c=c: qT[:, c, tb * 512:(tb + 1) * 512], lambda tb, c=c: t_qT[c][tb], 0.125))
            proj_fm(env, ws, banks, [(0, w_in[:, C_DK + c * 128:C_DK + (c + 1) * 128], 1.0)], 128, xrhs, lambda tb: [], 4, 512,
                    evac_to(lambda tb, c=c: kT[:, c, tb * 512:(tb + 1) * 512], lambda tb, c=c: t_kT[c][tb], 1.0))
            proj_fm(env, ws, banks, [(0, w_in[:, C_IQ + c * 128:C_IQ + (c + 1) * 128], 1.0)], 128, xrhs, lambda tb: [], 4, 512,
                    evac_to(lambda tb, c=c: qiT[:, c, tb * 512:(tb + 1) * 512], lambda tb, c=c: t_qiT[c][tb], 1.0))
        proj_fm(env, ws, banks, [(0, w_in[:, C_IK:C_IK + 64], 1.0), (64, w_in[:, C_IK:C_IK + 64], 1.0)], 128, xrhs, lambda tb: [], 4, 512,
                evac_to(lambda tb: kiT[:, tb * 512:(tb + 1) * 512], lambda tb: t_kiT[tb], 1.0))
        for c in range(4):
            ws.load([(0, w_in[:, C_DV + c * 128:C_DV + (c + 1) * 128], 1.0)],
                    dst=lambda c0, w, c=c: wv[:, :, c * 128 + c0:c * 128 + c0 + w], t_dst=t_wv[c])
        ws.load([(0, w_in[:, C_IW:C_IW + 8], 1.0)], dst=lambda c0, w: wiw[:, :, c0:c0 + w], t_dst=t_wiw)
        for t in range(16):
            bank, t_bank = banks.next()
            for kc in range(8):
                P.add("pe", lambda e, bank=bank, kc=kc, t=t: e.matmul(
                    bank[:], lhsT=x1T[:, kc, t * 128:(t + 1) * 128], rhs=wv[:, kc, :], start=(kc == 0), stop=(kc == 7)),
                    reads=t_wv, writes=[t_bank])
            P.add("act", lambda e, bank=bank, t=t: e.activation(out=vaug[:, t, :, 0:64], in_=bank[:].rearrange("p (h d) -> p h d", h=8),
                                                                func=AF.Copy), reads=[t_bank], writes=[t_v[t]])
            bank, t_bank = banks.next()
            for kc in range(8):
                P.add("pe", lambda e, bank=bank, kc=kc, t=t: e.matmul(
                    bank[:, 0:8], lhsT=x1T[:, kc, t * 128:(t + 1) * 128], rhs=wiw[:, kc, :], start=(kc == 0), stop=(kc == 7)),
                    reads=[t_wiw], writes=[t_bank])
            P.add("dve", lambda e, bank=bank, t=t: e.tensor_scalar(out=iw[:, t, :], in0=bank[:, 0:8], scalar1=IW_SCALE, scalar2=None, op0=ALU.mult),
                  reads=[t_bank], writes=[t_iw[t]])

        cI = {"i": 0}

        def idx_phase(n):
            Sc = Sc2[n % 2]; t_Sc = t_Sc2[n % 2]
            W = 128 * (n + 1)
            nb = (W + 511) // 512
            for h in range(8):
                hp = (h % 2) * 64
                for kb in range(nb):
                    w = min(512, W - kb * 512)
                    bank, t_bank = banks.next()
                    P.add("pe", lambda e, bank=bank, h=h, hp=hp, kb=kb, w=w, n=n: e.matmul(
                        bank[:, 0:w], lhsT=qiT[hp:hp + 64, h // 2, n * 128:(n + 1) * 128], rhs=kiT[hp:hp + 64, kb * 512:kb * 512 + w],
                        start=True, stop=True),
                        reads=[t_qiT[h // 2][n // 4]] + t_kiT[0:nb], writes=[t_bank])
                    s = cI["i"] % 2
                    cI["i"] += 1
                    P.add("act", lambda e, bank=bank, s=s, w=w: e.activation(out=tI[s][:, 0:w], in_=bank[:, 0:w], func=AF.Relu),
                          reads=[t_bank], writes=[t_tI[s]])
                    if h == 0:
                        P.add("dve", lambda e, s=s, kb=kb, w=w, n=n: e.tensor_scalar(
                            out=Sc[:, kb * 512:kb * 512 + w], in0=tI[s][:, 0:w], scalar1=iw[:, n, 0:1], scalar2=None, op0=ALU.mult),
                            reads=[t_tI[s], t_iw[n]], writes=[t_Sc])
                    else:
                        P.add("dve", lambda e, s=s, kb=kb, w=w, n=n, h=h: e.scalar_tensor_tensor(
                            out=Sc[:, kb * 512:kb * 512 + w], in0=tI[s][:, 0:w], scalar=iw[:, n, h:h + 1],
                            in1=Sc[:, kb * 512:kb * 512 + w], op0=ALU.mult, op1=ALU.add),
                            reads=[t_tI[s], t_iw[n], t_Sc], writes=[t_Sc])
            P.add("dve", lambda e, n=n: e.tensor_tensor(out=Sc[:, n * 128:(n + 1) * 128], in0=Sc[:, n * 128:(n + 1) * 128],
                                                        in1=caus[:], op=ALU.add),
                  reads=[t_Sc, t_caus], writes=[t_Sc])

        def bis_ops(n):
            ch = n % 2
            Sc = Sc2[ch]; t_Sc = t_Sc2[ch]; junk = junk2[ch]; t_junk = t_junk2[ch]
            mx8 = mx8_2[ch]; t_mx8 = t_mx8_2[ch]; mn = mn_2[ch]; t_mn = t_mn_2[ch]; rng = rng_2[ch]; t_rng = t_rng_2[ch]
            thr = thr_2[ch]; t_thr = t_thr_2[ch]; S2 = S2_2[ch]; t_S2 = t_S2_2[ch]; cnt = cnt_2[ch]; t_cnt = t_cnt_2[ch]
            ee = ee_2[ch]; t_ee = t_ee_2[ch]
            W = 128 * (n + 1)
            mk = mask[ch]
            t_mk = t_mask[ch]
            if n < 2:
                yield lambda: P.add("dve", lambda e: e.tensor_scalar(out=mk[:, 0:W], in0=Sc[:, 0:W], scalar1=thrneg[:, 0:1], scalar2=None, op0=ALU.is_ge),
                                    reads=[t_Sc, t_thrneg], writes=[t_mk])
                return
            yield lambda: P.add("dve", lambda e: e.max(out=mx8[:], in_=Sc[:, 0:W]), reads=[t_Sc], writes=[t_mx8])
            yield lambda: P.add("dve", lambda e: e.tensor_reduce(out=mn[:], in_=Sc[:, 0:n * 128], axis=mybir.AxisListType.X, op=ALU.min),
                                reads=[t_Sc], writes=[t_mn])
            yield lambda: P.add("dve", lambda e: e.tensor_tensor(out=rng[:], in0=mx8[:, 0:1], in1=mn[:], op=ALU.subtract),
                                reads=[t_mx8, t_mn], writes=[t_rng])
            yield lambda: P.add("dve", lambda e: e.scalar_tensor_tensor(out=thr[:], in0=rng[:], scalar=0.5, in1=mn[:], op0=ALU.mult, op1=ALU.add),
                                reads=[t_rng, t_mn], writes=[t_thr])
            yield lambda: P.add("dve", lambda e: e.tensor_scalar(out=S2[:], in0=pow2[:], scalar1=rng[:, 0:1], scalar2=None, op0=ALU.mult),
                                reads=[t_pow2, t_rng], writes=[t_S2])
            for k in range(NIT):
                yield lambda: P.add("dve", lambda e: e.tensor_scalar(out=junk[:, 0:W], in0=Sc[:, 0:W], scalar1=thr[:, 0:1], scalar2=None,
                                                                     op0=ALU.is_ge, op1=ALU.add, accum_out=cnt[:, 0:1]),
                                    reads=[t_Sc, t_thr], writes=[t_junk, t_cnt])
                yield lambda: P.add("dve", lambda e: e.tensor_scalar(out=ee[:], in0=cnt[:], scalar1=255.5, scalar2=0.5, op0=ALU.is_ge, op1=ALU.subtract),
                                    reads=[t_cnt], writes=[t_ee])
                yield lambda k=k: P.add("dve", lambda e: e.scalar_tensor_tensor(out=thr[:], in0=ee[:], scalar=S2[:, k:k + 1], in1=thr[:],
                                                                                op0=ALU.mult, op1=ALU.add),
                                        reads=[t_ee, t_S2, t_thr], writes=[t_thr])
            yield lambda: P.add("dve", lambda e: e.tensor_scalar(out=mk[:, 0:W], in0=Sc[:, 0:W], scalar1=thr[:, 0:1], scalar2=None, op0=ALU.is_ge),
                                reads=[t_Sc, t_thr], writes=[t_mk])

        def bis_pair(a, b):
            ga, gb = bis_ops(a), bis_ops(b)
            done_a = done_b = False
            while not (done_a and done_b):
                if not done_a:
                    f = next(ga, None)
                    if f is None:
                        done_a = True
                    else:
                        f()
                if not done_b:
                    f = next(gb, None)
                    if f is None:
                        done_b = True
                    else:
                        f()

        def maskT_phase(n):
            mk = mask[n % 2]; t_mk = t_mask[n % 2]
            mp = maskTp[(n // 2) % 2]; t_mp = t_maskTp[(n // 2) % 2][n % 2]
            for m0 in range(0, n + 1, 8):
                k = min(8, n + 1 - m0)
                for i in range(k):
                    m = m0 + i
                    P.add("pe", lambda e, i=i, m=m: e.transpose(out=pM[:, i * 128:(i + 1) * 128], in_=mk[:, m * 128:(m + 1) * 128], identity=idb[:]),
                          reads=[t_mk, t_idb], writes=[t_pM])
                P.add("act", lambda e, m0=m0, k=k, n=n: e.activation(
                    out=mp[:, m0:m0 + k, n % 2, :], in_=pM[:, 0:k * 128].rearrange("p (a b) -> p a b", a=k), func=AF.Copy),
                    reads=[t_pM], writes=[t_mp])

        cA = {"e": 0, "p": 0}

        def att_pair(kp):
            a, b = 2 * kp, 2 * kp + 1
            mp = maskTp[kp % 2]; t_mp = t_maskTp[kp % 2]
            for c in range(4):
                for m0 in range(0, b + 1, 2):
                    bk = [banks.next(), banks.next()]
                    for i in range(2):
                        m = m0 + i
                        for hh in range(2):
                            h = 2 * c + hh; hp = hh * 64
                            bank, t_bank = bk[hh]
                            biases = []
                            if m == a - 1:
                                biases.append((0, 128))
                            if m == a:
                                biases.append((0, 0)); biases.append((128, 128))
                            if m == b:
                                biases.append((128, 0))
                            P.add("pe", lambda e, bank=bank, i=i, m=m, hp=hp, c=c, nb=len(biases): e.matmul(
                                bank[:, i * 256:(i + 1) * 256], lhsT=kT[hp:hp + 64, c, m * 128:(m + 1) * 128],
                                rhs=qT[hp:hp + 64, c, a * 128:(a + 2) * 128], start=True, stop=(nb == 0)),
                                reads=[t_kT[c][m // 4], t_qT[c][a // 4]], writes=[t_bank])
                            for bi, (qo, off) in enumerate(biases):
                                P.add("pe", lambda e, bank=bank, i=i, h=h, qo=qo, off=off, last=(bi == len(biases) - 1): e.matmul(
                                    bank[:, i * 256 + qo:i * 256 + qo + 128], lhsT=idb[:], rhs=BT[:, h, off:off + 128], start=False, stop=last),
                                    reads=[t_idb, t_BT[h]], writes=[t_bank])
                    pts = []
                    for hh in range(2):
                        bank, t_bank = bk[hh]
                        se = cA["e"] % 2
                        cA["e"] += 1
                        P.add("act", lambda e, bank=bank, se=se: e.activation(out=E[se][:], in_=bank[:], func=AF.Exp),
                              reads=[t_bank], writes=[t_E[se]])
                        sp = cA["p"] % 3
                        cA["p"] += 1
                        P.add("pool", lambda e, se=se, sp=sp, m0=m0: e.tensor_tensor(
                            out=PT[sp][:], in0=E[se][:], in1=mp[:, m0:m0 + 2, :, :].rearrange("p a b q -> p (a b q)"), op=ALU.mult),
                            reads=[t_E[se]] + t_mp, writes=[t_PT[sp]])
                        pts.append(sp)
                    for i in range(2):
                        m = m0 + i
                        for hh in range(2):
                            h = 2 * c + hh
                            sp = pts[hh]
                            for t in (a, b):
                                if m > t:
                                    continue
                                P.add("pe", lambda e, sp=sp, i=i, m=m, h=h, hh=hh, c=c, t=t, a=a: e.matmul(
                                    pO[t % 2][hh][:, c * 65:(c + 1) * 65], lhsT=PT[sp][:, i * 256 + (t - a) * 128:i * 256 + (t - a) * 128 + 128],
                                    rhs=vaug[:, m, h, :], start=(m == 0), stop=(m == t)),
                                    reads=[t_v[m], t_vones, t_PT[sp]], writes=[t_pO[t % 2][hh]])
            for t in (a, b):
                r = t % 2
                for hh in range(2):
                    P.add("dve", lambda e, r=r, hh=hh: e.reciprocal(
                        out=rden8[r][:, hh, :], in_=pO[r][hh][:, 0:260].rearrange("p (c d) -> p c d", c=4)[:, :, 64]),
                        reads=[t_pO[r][hh]], writes=[t_rden8[r]])
                for h in range(8):
                    P.add("act", lambda e, r=r, h=h: e.activation(
                        out=o_tm[r][:, h * 64:(h + 1) * 64], in_=pO[r][h % 2][:, (h // 2) * 65:(h // 2) * 65 + 64], func=AF.Copy,
                        scale=rden8[r][:, h % 2, h // 2:h // 2 + 1]),
                        reads=[t_pO[r][h % 2], t_rden8[r]], writes=[t_otm[r]])
                for c4 in range(4):
                    P.add("pe", lambda e, r=r, c4=c4: e.transpose(out=pM[:, c4 * 128:(c4 + 1) * 128], in_=o_tm[r][:, c4 * 128:(c4 + 1) * 128], identity=idb[:]),
                          reads=[t_otm[r], t_idb], writes=[t_pM])
                P.add("dve", lambda e, t=t: e.tensor_copy(out=oT[:, :, t * 128:(t + 1) * 128], in_=pM[:, 0:512].rearrange("p (a b) -> p a b", a=4)),
                      reads=[t_pM], writes=[Tok()])

        idx_phase(0); idx_phase(1); bis_pair(0, 1); maskT_phase(0); maskT_phase(1)
        for k in range(8):
            a, b = 2 * k, 2 * k + 1
            if k + 1 < 8:
                idx_phase(a + 2); idx_phase(b + 2)
                bis_pair(a + 2, b + 2)
            att_pair(k)
            if k + 1 < 8:
                maskT_phase(a + 2); maskT_phase(b + 2)
        env.P.emit("m1")


def ret_stage(nc, semstack, dmaq, x1T, oT, w_in, gng_d, gnb_d, ident_d, cos_d, sin_d, decay_d, kdec_d, qdec_d):
    with contextlib.ExitStack() as st:
        env = Env(nc, semstack, dmaq, "m2", st)
        P = env.P
        idf, t_idf, idb, t_idb = load_ident(env, ident_d)
        banks = Banks(env, 7)
        pM = env.ps("pM", [128, 1024], BF16); t_pM = Tok()
        rqT = env.sb("rqT", [128, 2, L], BF16); t_rqT = [toks(4) for _ in range(2)]
        rkT = env.sb("rkT", [128, 2, L], BF16); t_rkT = [toks(4) for _ in range(2)]
        qdT = env.sb("qdT", [128, 2, L], BF16); t_qdT = toks(16)
        rv = env.sb("rv", [128, 16, 512], BF16); t_rv = toks(16)
        kd = env.sb("kd", [128, 16, 256], BF16); t_kd = toks(16)
        wrv = env.sb("wrv", [128, 8, 512], BF16); t_wrv = toks(4)
        wrg = env.sb("wrg", [128, 8, 512], BF16); t_wrg = toks(4)
        cosT = env.sb("cosT", [128, L], F32); t_cos = Tok()
        sinT = env.sb("sinT", [128, L], F32); t_sin = Tok()
        decT = env.sb("decT", [128, 512], F32); t_dec = Tok()
        kdec = env.sb("kdec", [128, 256], F32); t_kdec = Tok()
        qdec = env.sb("qdec", [128, 2, 128], F32); t_qdec = Tok()
        gng = env.sb("gng", [128, 512], F32); t_gng = Tok()
        gnb = env.sb("gnb", [128, 512], F32); t_gnb = Tok()
        tmp = [env.sb(f"tmp{i}", [128, 512], F32) for i in range(2)]; t_tmp = toks(2)
        tmp2 = [env.sb(f"tmpb{i}", [128, 512], F32) for i in range(2)]; t_tmp2 = toks(2)
        PdT = [env.sb(f"PdT{i}", [128, 512], BF16) for i in range(2)]; t_PdT = toks(2)
        state = env.sb("state", [128, 2, 128], F32); t_state = Tok()
        state_bf = env.sb("state_bf", [128, 2, 128], BF16); t_state_bf = Tok()
        on = [env.sb(f"on{i}", [128, 512], F32) for i in range(2)]; t_on = toks(2)
        sg = [env.sb(f"sg{i}", [128, 512], F32) for i in range(2)]; t_sg = toks(2)
        orb = [env.sb(f"orb{i}", [128, 512], BF16) for i in range(2)]; t_orb = toks(2)
        stats = [env.sb(f"stats{i}", [128, 4, 6], F32) for i in range(2)]; t_stats = toks(2)
        mvv = [env.sb(f"mvv{i}", [128, 4, 2], F32) for i in range(2)]; t_mvv = toks(2)
        sd = [env.sb(f"sd{i}", [128, 4], F32) for i in range(2)]; t_sd = toks(2)
        rstd = [env.sb(f"rstd{i}", [128, 4], F32) for i in range(2)]; t_rstd = toks(2)
        epst = env.sb("epst", [128, 1], F32); t_eps = Tok()
        ws = WStream(env)
        for (dst, src, tk) in ((cosT, cos_d, t_cos), (sinT, sin_d, t_sin), (decT, decay_d, t_dec), (kdec, kdec_d, t_kdec)):
            P.add("sync", lambda e, dst=dst, src=src: e.dma_start(out=dst[:], in_=src), writes=[tk], kind="d")
        P.add("sync", lambda e: e.dma_start(out=qdec[:].rearrange("p a b -> p (a b)"), in_=qdec_d), writes=[t_qdec], kind="d")
        P.add("sync", lambda e: e.dma_start(out=gng[:], in_=bcast_rows(gng_d, 128)), writes=[t_gng], kind="d")
        P.add("sync", lambda e: e.dma_start(out=gnb[:], in_=bcast_rows(gnb_d, 128)), writes=[t_gnb], kind="d")
        P.add("dve", lambda e: e.memset(epst[:], LN_EPS), writes=[t_eps])

        cR = {"i": 0}
        for (c0, dstT, t_dstT, sc) in ((C_RQ, rqT, t_rqT, 1.0), (C_RK, rkT, t_rkT, 0.125)):
            for c in range(2):
                base = c0 + c * 128
                wn, t_wn = ws.load([(0, w_in[:, base:base + 128], 1.0)])
                pieces = []
                for hh in range(2):
                    pieces.append((hh * 64, w_in[:, base + hh * 64 + 32:base + hh * 64 + 64], -1.0))
                    pieces.append((hh * 64 + 32, w_in[:, base + hh * 64:base + hh * 64 + 32], 1.0))
                wr, t_wr = ws.load(pieces)
                for tb in range(4):
                    bq, t_bq = banks.next()
                    br, t_br = banks.next()
                    for (bank, t_bank, wt, t_w) in ((bq, t_bq, wn, t_wn), (br, t_br, wr, t_wr)):
                        for kc in range(8):
                            P.add("pe", lambda e, bank=bank, wt=wt, kc=kc, tb=tb: e.matmul(
                                bank[:], lhsT=wt[:, kc, :], rhs=x1T[:, kc, tb * 512:(tb + 1) * 512], start=(kc == 0), stop=(kc == 7)),
                                reads=[t_w], writes=[t_bank])
                    s = cR["i"] % 2
                    cR["i"] += 1
                    tsl = slice(tb * 512, (tb + 1) * 512)
                    P.add("dve", lambda e, s=s, bq=bq, tsl=tsl: e.tensor_tensor(out=tmp[s][:], in0=bq[:], in1=cosT[:, tsl], op=ALU.mult),
                          reads=[t_bq, t_cos], writes=[t_tmp[s]])
                    P.add("dve", lambda e, s=s, br=br, tsl=tsl, sc=sc: e.scalar_tensor_tensor(
                        out=tmp2[s][:], in0=br[:], scalar=float(sc), in1=sinT[:, tsl], op0=ALU.mult, op1=ALU.mult),
                        reads=[t_br, t_sin], writes=[t_tmp2[s]])
                    P.add("dve", lambda e, s=s, tsl=tsl, sc=sc, dstT=dstT, c=c: e.scalar_tensor_tensor(
                        out=dstT[:, c, tsl], in0=tmp[s][:], scalar=float(sc), in1=tmp2[s][:], op0=ALU.mult, op1=ALU.add),
                        reads=[t_tmp[s], t_tmp2[s]], writes=[t_dstT[c][tb]])
        for c in range(4):
            ws.load([(0, w_in[:, C_RV + c * 128:C_RV + (c + 1) * 128], 1.0)],
                    dst=lambda c0, w, c=c: wrv[:, :, c * 128 + c0:c * 128 + c0 + w], t_dst=t_wrv[c])
        for c in range(4):
            ws.load([(0, w_in[:, C_RG + c * 128:C_RG + (c + 1) * 128], 1.0)],
                    dst=lambda c0, w, c=c: wrg[:, :, c * 128 + c0:c * 128 + c0 + w], t_dst=t_wrg[c])
        for t in range(16):
            tsl = slice(t * 128, (t + 1) * 128)
            bank, t_bank = banks.next()
            for kc in range(8):
                P.add("pe", lambda e, bank=bank, kc=kc, tsl=tsl: e.matmul(
                    bank[:], lhsT=x1T[:, kc, tsl], rhs=wrv[:, kc, :], start=(kc == 0), stop=(kc == 7)),
                    reads=t_wrv, writes=[t_bank])
            P.add("act", lambda e, bank=bank, t=t: e.activation(out=rv[:, t, :], in_=bank[:], func=AF.Copy),
                  reads=[t_bank], writes=[t_rv[t]])
            for c in range(2):
                P.add("pe", lambda e, c=c, tsl=tsl: e.transpose(out=pM[:, c * 128:(c + 1) * 128], in_=rkT[:, c, tsl], identity=idb[:]),
                      reads=[t_rkT[c][t // 4], t_idb], writes=[t_pM])
            P.add("dve", lambda e, t=t: e.tensor_tensor(out=kd[:, t, :], in0=pM[:, 0:256], in1=kdec[:], op=ALU.mult),
                  reads=[t_pM, t_kdec], writes=[t_kd[t]])
            for c in range(2):
                P.add("dve", lambda e, c=c, tsl=tsl: e.tensor_tensor(out=qdT[:, c, tsl], in0=rqT[:, c, tsl], in1=qdec[:, c, :], op=ALU.mult),
                      reads=[t_rqT[c][t // 4], t_qdec], writes=[t_qdT[t]])
        for n in range(16):
            tsl = slice(n * 128, (n + 1) * 128)
            s = n % 2
            bSe, t_bSe = banks.next()
            bSo, t_bSo = banks.next()
            bS2 = (bSe, bSo); t_bS2 = (t_bSe, t_bSo)
            for h in range(4):
                hp = (h % 2) * 64; c = h // 2
                P.add("pe", lambda e, bS=bS2[h % 2], hp=hp, c=c, tsl=tsl: e.matmul(
                    bS[:, c * 128:(c + 1) * 128], lhsT=rkT[hp:hp + 64, c, tsl], rhs=rqT[hp:hp + 64, c, tsl], start=True, stop=True),
                    reads=[t_rkT[c][n // 4], t_rqT[c][n // 4]], writes=[t_bS2[h % 2]])
            for par in range(2):
                P.add("dve", lambda e, bS=bS2[par], s=s, par=par: e.tensor_tensor(
                    out=PdT[s][:, par * 256:(par + 1) * 256], in0=bS[:, 0:256], in1=decT[:, par * 256:(par + 1) * 256], op=ALU.mult),
                    reads=[t_bS2[par], t_dec], writes=[t_PdT[s]])
            bOe, t_bOe = banks.next()
            bOo, t_bOo = banks.next()
            bO2 = (bOe, bOo); t_bO2 = (t_bOe, t_bOo)
            for h in range(4):
                hp = (h % 2) * 64; c = h // 2
                pos = (h % 2) * 2 + c
                P.add("pe", lambda e, bO=bO2[h % 2], h=h, c=c, pos=pos, s=s, n=n: e.matmul(
                    bO[:, c * 128:(c + 1) * 128], lhsT=PdT[s][:, pos * 128:(pos + 1) * 128], rhs=rv[:, n, h * 128:(h + 1) * 128],
                    start=True, stop=(n == 0)), reads=[t_PdT[s], t_rv[n]], writes=[t_bO2[h % 2]])
                if n > 0:
                    P.add("pe", lambda e, bO=bO2[h % 2], hp=hp, c=c, tsl=tsl: e.matmul(
                        bO[:, c * 128:(c + 1) * 128], lhsT=qdT[hp:hp + 64, c, tsl], rhs=state_bf[hp:hp + 64, c, :],
                        start=False, stop=True), reads=[t_qdT[n], t_state_bf], writes=[t_bO2[h % 2]])
            if n < 15:
                bK, t_bK = banks.next()
                for h in range(4):
                    hp = (h % 2) * 64; c = h // 2
                    P.add("pe", lambda e, bK=bK, h=h, hp=hp, c=c, n=n: e.matmul(
                        bK[hp:hp + 64, c * 128:(c + 1) * 128], lhsT=kd[:, n, h * 64:(h + 1) * 64], rhs=rv[:, n, h * 128:(h + 1) * 128],
                        start=True, stop=True), reads=[t_kd[n], t_rv[n]], writes=[t_bK])
                for h in range(4):
                    hp = (h % 2) * 64; c = h // 2
                    if n == 0:
                        P.add("dve", lambda e, bK=bK, hp=hp, c=c: e.tensor_copy(out=state[hp:hp + 64, c, :], in_=bK[hp:hp + 64, c * 128:(c + 1) * 128]),
                              reads=[t_bK], writes=[t_state])
                    else:
                        cd = float(np.float32(np.exp(np.float32(128.0) * np.log(np.float32(GAMMAS[h])))))
                        P.add("dve", lambda e, bK=bK, hp=hp, c=c, cd=cd: e.scalar_tensor_tensor(
                            out=state[hp:hp + 64, c, :], in0=state[hp:hp + 64, c, :], scalar=cd, in1=bK[hp:hp + 64, c * 128:(c + 1) * 128],
                            op0=ALU.mult, op1=ALU.add), reads=[t_bK, t_state], writes=[t_state])
                P.add("act", lambda e: e.activation(out=state_bf[:], in_=state[:], func=AF.Copy), reads=[t_state], writes=[t_state_bf])
            bG, t_bG = banks.next()
            for kc in range(8):
                P.add("pe", lambda e, bG=bG, kc=kc, tsl=tsl: e.matmul(
                    bG[:], lhsT=x1T[:, kc, tsl], rhs=wrg[:, kc, :], start=(kc == 0), stop=(kc == 7)), reads=t_wrg, writes=[t_bG])
            P.add("act", lambda e, bG=bG, s=s: e.activation(out=sg[s][:], in_=bG[:], func=AF.Silu), reads=[t_bG], writes=[t_sg[s]])
            for h in range(4):
                P.add("dve", lambda e, bO=bO2[h % 2], h=h, s=s: e.bn_stats(out=stats[s][:, h, :], in_=bO[:, (h // 2) * 128:(h // 2 + 1) * 128]),
                      reads=[t_bO2[h % 2]], writes=[t_stats[s]])
            for h in range(4):
                P.add("dve", lambda e, h=h, s=s: e.bn_aggr(out=mvv[s][:, h, :], in_=stats[s][:, h, :]), reads=[t_stats[s]], writes=[t_mvv[s]])
            P.add("act", lambda e, s=s: e.activation(out=sd[s][:], in_=mvv[s][:, :, 1], func=AF.Sqrt, bias=epst[:, 0:1], scale=1.0),
                  reads=[t_mvv[s], t_eps], writes=[t_sd[s]])
            P.add("dve", lambda e, s=s: e.reciprocal(out=rstd[s][:], in_=sd[s][:]), reads=[t_sd[s]], writes=[t_rstd[s]])
            for h in range(4):
                P.add("dve", lambda e, bO=bO2[h % 2], h=h, s=s: e.tensor_scalar(
                    out=on[s][:, h * 128:(h + 1) * 128], in0=bO[:, (h // 2) * 128:(h // 2 + 1) * 128], scalar1=mvv[s][:, h, 0:1],
                    scalar2=rstd[s][:, h:h + 1], op0=ALU.subtract, op1=ALU.mult),
                    reads=[t_bO2[h % 2], t_mvv[s], t_rstd[s]], writes=[t_on[s]])
            P.add("pool", lambda e, s=s: e.tensor_tensor(out=on[s][:], in0=on[s][:], in1=gng[:], op=ALU.mult),
                  reads=[t_on[s], t_gng], writes=[t_on[s]])
            P.add("pool", lambda e, s=s: e.tensor_tensor(out=on[s][:], in0=on[s][:], in1=gnb[:], op=ALU.add),
                  reads=[t_on[s], t_gnb], writes=[t_on[s]])
            P.add("dve", lambda e, s=s: e.tensor_tensor(out=orb[s][:], in0=on[s][:], in1=sg[s][:], op=ALU.mult),
                  reads=[t_on[s], t_sg[s]], writes=[t_orb[s]])
            for c4 in range(4):
                P.add("pe", lambda e, c4=c4, s=s: e.transpose(out=pM[:, c4 * 128:(c4 + 1) * 128], in_=orb[s][:, c4 * 128:(c4 + 1) * 128], identity=idb[:]),
                      reads=[t_orb[s], t_idb], writes=[t_pM])
            P.add("act", lambda e, tsl=tsl: e.activation(out=oT[:, :, tsl], in_=pM[:, 0:512].rearrange("p (a b) -> p a b", a=4), func=AF.Copy),
                  reads=[t_pM], writes=[Tok()])
        env.P.emit("m2")


def merge_stage(nc, semstack, dmaq, x1T, oTs, w_in, wbrs, wo_d, g_d, b_d, src_d, dst_d):
    with contextlib.ExitStack() as st:
        env = Env(nc, semstack, dmaq, "m4", st)
        P = env.P
        banks = Banks(env, 8)
        mT = env.sb("mT", [128, 8, L], BF16); t_mT = [toks(4) for _ in range(8)]
        wout = env.sb("wout", [128, 8, D], BF16); t_wout = toks(8)
        sgm = [env.sb(f"sgm{i}", [128, 512], F32) for i in range(2)]; t_sgm = toks(2)
        tmp = [env.sb(f"tmp{i}", [128, 512], F32) for i in range(2)]; t_tmp = toks(2)
        acc = env.sb("acc", [128, L], F32); t_acc = toks(4)
        xs = [env.sb(f"xs{i}", [128, D], F32) for i in range(2)]; t_xs = toks(2)
        rr = [env.sb(f"rr{i}", [128, D], F32) for i in range(2)]; t_rr = toks(2)
        oo = [env.sb(f"oo{i}", [128, D], F32) for i in range(2)]; t_oo = toks(2)
        gam = env.sb("gam", [128, D], F32); t_gam = Tok()
        bet = env.sb("bet", [128, D], F32); t_bet = Tok()
        epst = env.sb("epst", [128, 1], F32); t_eps = Tok()
        stats = [env.sb(f"stats{i}", [128, 2, 6], F32) for i in range(2)]; t_stats = toks(2)
        mv = [env.sb(f"mv{i}", [128, 2], F32) for i in range(2)]; t_mv = toks(2)
        sd = [env.sb(f"sd{i}", [128, 1], F32) for i in range(2)]; t_sd = toks(2)
        rstd = [env.sb(f"rstd{i}", [128, 1], F32) for i in range(2)]; t_rstd = toks(2)
        t_dst = toks(16)
        ws = WStream(env, nst=2, nbf=6)
        P.add("sync", lambda e: e.dma_start(out=gam[:], in_=bcast_rows(g_d, 128)), writes=[t_gam], kind="d")
        P.add("sync", lambda e: e.dma_start(out=bet[:], in_=bcast_rows(b_d, 128)), writes=[t_bet], kind="d")
        P.add("dve", lambda e: e.memset(epst[:], LN_EPS), writes=[t_eps])
        cM = {"s": 0}
        for j in range(8):
            ws.load([(0, wo_d[:, j * 128:(j + 1) * 128], 1.0)], dst=lambda c0, w, j=j: wout[:, :, j * 128 + c0:j * 128 + c0 + w], t_dst=t_wout[j])
            for b in range(3):
                wgb, t_wgb = ws.load([(0, w_in[:, C_G + b * D + j * 128:C_G + b * D + (j + 1) * 128], 1.0)])
                wbb, t_wbb = ws.load([(0, wbrs[b][:, j * 128:(j + 1) * 128], 1.0)], KC=4)
                for tb in range(4):
                    tsl = slice(tb * 512, (tb + 1) * 512)
                    bG, t_bG = banks.next()
                    for kc in range(8):
                        P.add("pe", lambda e, bG=bG, wgb=wgb, kc=kc, tsl=tsl: e.matmul(
                            bG[:], lhsT=wgb[:, kc, :], rhs=x1T[:, kc, tsl], start=(kc == 0), stop=(kc == 7)),
                            reads=[t_wgb], writes=[t_bG])
                    bB, t_bB = banks.next()
                    for kc in range(4):
                        P.add("pe", lambda e, bB=bB, wbb=wbb, kc=kc, tsl=tsl, b=b: e.matmul(
                            bB[:], lhsT=wbb[:, kc, :], rhs=oTs[b][:, kc, tsl], start=(kc == 0), stop=(kc == 3)),
                            reads=[t_wbb], writes=[t_bB])
                    s = cM["s"] % 2
                    cM["s"] += 1
                    P.add("act", lambda e, bG=bG, s=s: e.activation(out=sgm[s][:], in_=bG[:], func=AF.Sigmoid), reads=[t_bG], writes=[t_sgm[s]])
                    if b == 0:
                        P.add("dve", lambda e, bB=bB, s=s, tsl=tsl: e.tensor_tensor(out=acc[:, tsl], in0=sgm[s][:], in1=bB[:], op=ALU.mult),
                              reads=[t_sgm[s], t_bB], writes=[t_acc[tb]])
                    else:
                        P.add("dve", lambda e, bB=bB, s=s: e.tensor_tensor(out=tmp[s][:], in0=sgm[s][:], in1=bB[:], op=ALU.mult),
                              reads=[t_sgm[s], t_bB], writes=[t_tmp[s]])
                        if b == 1:
                            P.add("dve", lambda e, s=s, tsl=tsl: e.tensor_tensor(out=acc[:, tsl], in0=acc[:, tsl], in1=tmp[s][:], op=ALU.add),
                                  reads=[t_acc[tb], t_tmp[s]], writes=[t_acc[tb]])
                        else:
                            P.add("dve", lambda e, s=s, j=j, tsl=tsl: e.tensor_tensor(out=mT[:, j, tsl], in0=acc[:, tsl], in1=tmp[s][:], op=ALU.add),
                                  reads=[t_acc[tb], t_tmp[s]], writes=[t_mT[j][tb]])
        for t in range(16):
            s = t % 2
            tsl = slice(t * 128, (t + 1) * 128)
            P.add("sync", lambda e, s=s, tsl=tsl: e.dma_start(out=xs[s][:], in_=src_d[tsl, :]), writes=[t_xs[s]], kind="d")
            for nh in range(2):
                bank, t_bank = banks.next()
                for kc in range(8):
                    P.add("pe", lambda e, bank=bank, kc=kc, nh=nh, tsl=tsl: e.matmul(
                        bank[:], lhsT=mT[:, kc, tsl], rhs=wout[:, kc, nh * 512:(nh + 1) * 512], start=(kc == 0), stop=(kc == 7)),
                        reads=[t_mT[kc][t // 4]] + t_wout[nh * 4:(nh + 1) * 4], writes=[t_bank])
                P.add("dve", lambda e, bank=bank, nh=nh, s=s: e.scalar_tensor_tensor(
                    out=rr[s][:, nh * 512:(nh + 1) * 512], in0=xs[s][:, nh * 512:(nh + 1) * 512], scalar=ALPHA, in1=bank[:],
                    op0=ALU.mult, op1=ALU.add), reads=[t_xs[s], t_bank], writes=[t_rr[s]])
            for k in range(2):
                P.add("dve", lambda e, s=s, k=k: e.bn_stats(out=stats[s][:, k, :], in_=rr[s][:, k * 512:(k + 1) * 512]),
                      reads=[t_rr[s]], writes=[t_stats[s]])
            P.add("dve", lambda e, s=s: e.bn_aggr(out=mv[s][:], in_=stats[s][:].rearrange("p a b -> p (a b)")),
                  reads=[t_stats[s]], writes=[t_mv[s]])
            P.add("act", lambda e, s=s: e.activation(out=sd[s][:], in_=mv[s][:, 1:2], func=AF.Sqrt, bias=epst[:, 0:1], scale=1.0),
                  reads=[t_mv[s], t_eps], writes=[t_sd[s]])
            P.add("dve", lambda e, s=s: e.reciprocal(out=rstd[s][:], in_=sd[s][:]), reads=[t_sd[s]], writes=[t_rstd[s]])
            P.add("dve", lambda e, s=s: e.tensor_scalar(
                out=rr[s][:], in0=rr[s][:], scalar1=mv[s][:, 0:1], scalar2=rstd[s][:, 0:1], op0=ALU.subtract, op1=ALU.mult),
                reads=[t_rr[s], t_mv[s], t_rstd[s]], writes=[t_rr[s]])
            P.add("pool", lambda e, s=s: e.tensor_tensor(out=oo[s][:], in0=rr[s][:], in1=gam[:], op=ALU.mult),
                  reads=[t_rr[s], t_gam], writes=[t_oo[s]])
            P.add("pool", lambda e, s=s: e.tensor_tensor(out=oo[s][:], in0=oo[s][:], in1=bet[:], op=ALU.add),
                  reads=[t_oo[s], t_bet], writes=[t_oo[s]])
            P.add("pool", lambda e, s=s, tsl=tsl: e.dma_start(out=dst_d[tsl, :], in_=oo[s][:]),
                  reads=[t_oo[s]], writes=[t_dst[t]], kind="d")
        P.wait_all("pool", t_dst)
        env.P.emit("m4")

def build_nc(stages=("ffn1", "mix", "ffn2"), dbg_mix=False, parts=("dsa", "ret", "mem", "merge")):
    nc = bass.Bass("TRN2", target_bir_lowering=False)
    din = lambda n, shape: nc.dram_tensor(n, shape, F32, kind="ExternalInput").ap()
    x_d = din("x", [L, D])
    mem_d = din("mem", [256, D])
    f1wi = din("ffn1_w_in", [D, 2 * DFF]); f1wo = din("ffn1_w_out", [DFF, D])
    ln1g = din("ln1_g", [1, D]); ln1b = din("ln1_b", [1, D])
    w_in = din("w_in", [D, W_IN_COLS]); t5 = din("t5_table", [32, 8])
    gng = din("ret_gn_g", [1, 512]); gnb = din("ret_gn_b", [1, 512])
    wmkv = din("w_mem_kv", [D, 1024])
    wbr = din("w_br_ret", [512, D]); wbd = din("w_br_dsa", [512, D]); wbm = din("w_br_mem", [512, D])
    wo = din("w_out", [D, D]); ln2g = din("ln2_g", [1, D]); ln2b = din("ln2_b", [1, D])
    f2wi = din("ffn2_w_in", [D, 2 * DFF]); f2wo = din("ffn2_w_out", [DFF, D])
    ln3g = din("ln3_g", [1, D]); ln3b = din("ln3_b", [1, D])
    ident = din("c_ident", [128, 128])
    cJ = din("c_J", [128, 128]); coh = din("c_oh", [32, 384]); ccausal = din("c_causal", [128, 128])
    cpow2 = din("c_pow2", [128, NIT])
    ccos = din("c_cos", [128, L]); csin = din("c_sin", [128, L])
    cdecay = din("c_decay", [128, 512]); ckdec = din("c_kdec", [128, 256]); cqdec = din("c_qdec", [128, 256])
    gscr = nc.dram_tensor("g_scr", [8, 384], F32).ap()
    out_d = nc.dram_tensor("out", [L, D], F32, kind="ExternalOutput").ap()
    x1_d = nc.dram_tensor("x1_scr", [L, D], F32).ap()
    x2_d = nc.dram_tensor("x2_scr", [L, D], F32).ap()
    with contextlib.ExitStack() as semstack:
        dmaq = {}
        cur = x_d
        for i, s in enumerate(stages):
            last = i == len(stages) - 1
            if s == "ffn1":
                dst = out_d if last else x1_d
                ffn_stage(nc, semstack, dmaq, "f1", cur, f1wi, f1wo, ln1g, ln1b, dst, ident)
                cur = dst
            elif s == "mix":
                dst = out_d if last else x2_d
                with contextlib.ExitStack() as outer:
                    x1T = outer.enter_context(nc.sbuf_tensor("x1T", [128, 8, L], BF16))
                    mix_load_stage(nc, semstack, dmaq, cur, ident, x1T)
                    o_dsaT = outer.enter_context(nc.sbuf_tensor("o_dsaT", [128, 4, L], BF16))
                    if "dsa" in parts:
                        dsa_stage(nc, semstack, dmaq, x1T, o_dsaT, w_in, t5, ident, cJ, coh, ccausal, cpow2, gscr)
                    o_retT = outer.enter_context(nc.sbuf_tensor("o_retT", [128, 4, L], BF16))
                    if "ret" in parts:
                        ret_stage(nc, semstack, dmaq, x1T, o_retT, w_in, gng, gnb, ident, ccos, csin, cdecay, ckdec, cqdec)
                    o_memT = outer.enter_context(nc.sbuf_tensor("o_memT", [128, 4, L], BF16))
                    if "mem" in parts:
                        mem_stage(nc, semstack, dmaq, x1T, o_memT, mem_d, w_in, wmkv, ident)
                    if dbg_mix:
                        dbg = nc.dram_tensor("dbg", [128, 12, L], BF16, kind="ExternalOutput").ap()
                        with contextlib.ExitStack() as st:
                            env = Env(nc, semstack, dmaq, "dbg", st)
                            tt = toks(3)
                            for i, (o, pn) in enumerate(((o_retT, "ret"), (o_dsaT, "dsa"), (o_memT, "mem"))):
                                if pn not in parts:
                                    continue
                                env.P.add("sync", lambda e, i=i, o=o: e.dma_start(out=dbg[:, 4 * i:4 * i + 4, :], in_=o[:]), writes=[tt[i]], kind="d")
                            env.P.wait_all("sync", tt)
                            env.P.emit("dbg")
                    if "merge" in parts:
                        merge_stage(nc, semstack, dmaq, x1T, [o_retT, o_dsaT, o_memT], w_in, [wbr, wbd, wbm], wo, ln2g, ln2b, cur, dst)
                cur = dst
            elif s == "mixdbg":
                with contextlib.ExitStack() as outer:
                    x1T = outer.enter_context(nc.sbuf_tensor("x1T", [128, 8, L], BF16))
                    mix_load_stage(nc, semstack, dmaq, cur, ident, x1T)
                    o_dsaT = outer.enter_context(nc.sbuf_tensor("o_dsaT", [128, 4, L], BF16))
                    dsa_stage(nc, semstack, dmaq, x1T, o_dsaT, w_in, t5, ident, cJ, coh, ccausal, cpow2, gscr)
                    o_memT = outer.enter_context(nc.sbuf_tensor("o_memT", [128, 4, L], BF16))
                    mem_stage(nc, semstack, dmaq, x1T, o_memT, mem_d, w_in, wmkv, ident)
                    dbg = nc.dram_tensor("dbg", [128, 8, L], BF16, kind="ExternalOutput").ap()
                    with contextlib.ExitStack() as st:
                        env = Env(nc, semstack, dmaq, "dbg", st)
                        t1 = Tok(); t2 = Tok()
                        env.P.add("sync", lambda e: e.dma_start(out=dbg[:, 0:4, :], in_=o_dsaT[:]), writes=[t1], kind="d")
                        env.P.add("sync", lambda e: e.dma_start(out=dbg[:, 4:8, :], in_=o_memT[:]), writes=[t2], kind="d")
                        env.P.wait_all("sync", [t1, t2])
                        env.P.emit("dbg")
            elif s == "ffn2":
                dst = out_d if last else x2_d
                ffn_stage(nc, semstack, dmaq, "f2", cur, f2wi, f2wo, ln3g, ln3b, dst, ident)
                cur = dst
    return nc


_CACHE = {}


def _t5_bucket(n):
    n = np.maximum(n, 0)
    nf = np.maximum(n, 1).astype(np.float32)
    large = 16 + (np.log(nf / np.float32(16)) / np.float32(np.log(128 / 16)) * np.float32(16)).astype(np.int32)
    large = np.minimum(large, 31)
    return np.where(n < 16, n, large)


def make_consts():
    c = {}
    c["c_J"] = np.ascontiguousarray(np.eye(128, dtype=np.float32)[::-1])
    oh = np.zeros((32, 384), np.float32)
    for u in range(383):
        d = u - 127
        if d >= 0:
            oh[_t5_bucket(np.array(d)), u] += 1.0
            oh[31, u] -= 1.0
    c["c_oh"] = oh
    q = np.arange(128)[:, None]; sk = np.arange(128)[None, :]
    c["c_causal"] = np.where(sk <= q, 0.0, -1.0e30).astype(np.float32)
    half = 32
    freqs = (np.float32(10000.0) ** (-np.arange(half, dtype=np.float32) / np.float32(half))).astype(np.float32)
    ang = np.arange(L, dtype=np.float32)[None, :] * freqs[np.arange(128) % 32][:, None]
    c["c_cos"] = np.cos(ang).astype(np.float32)
    c["c_sin"] = np.sin(ang).astype(np.float32)
    lg = np.log(np.array(GAMMAS, dtype=np.float32))
    i = np.arange(128)
    dec = np.zeros((128, 4, 128), np.float32)
    for h in range(4):
        diff = i[None, :] - i[:, None]
        dec[:, h, :] = np.where(diff >= 0, np.exp(np.maximum(diff, 0).astype(np.float32) * lg[h]), 0.0)
    c["c_decay"] = np.ascontiguousarray(dec[:, [0, 2, 1, 3], :]).reshape(128, 512)
    kdec = np.zeros((128, 4, 64), np.float32)
    for h in range(4):
        kdec[:, h, :] = np.exp((127 - i).astype(np.float32) * lg[h])[:, None]
    c["c_kdec"] = kdec.reshape(128, 256)
    qdec = np.zeros((128, 2, 128), np.float32)
    for p in range(128):
        for cc in range(2):
            qdec[p, cc, :] = np.exp((i + 1).astype(np.float32) * lg[2 * cc + p // 64])
    c["c_qdec"] = qdec.reshape(128, 256)
    c["c_pow2"] = np.tile((2.0 ** -(np.arange(NIT) + 1.0)).astype(np.float32)[None, :], (128, 1))
    return c


def make_in_maps(inputs):
    f = lambda a: np.ascontiguousarray(np.asarray(a, dtype=np.float32))
    shared = {
        "ffn1_w_in": f(inputs["ffn1_w_in"][0]), "ffn1_w_out": f(inputs["ffn1_w_out"][0]),
        "ln1_g": f(inputs["ln1_g"]), "ln1_b": f(inputs["ln1_b"]),
        "w_in": f(inputs["w_in"][0]), "t5_table": f(inputs["t5_table"]),
        "ret_gn_g": f(inputs["ret_gn_g"]), "ret_gn_b": f(inputs["ret_gn_b"]),
        "w_mem_kv": f(inputs["w_mem_kv"][0]),
        "w_br_ret": f(inputs["w_br_ret"][0]), "w_br_dsa": f(inputs["w_br_dsa"][0]), "w_br_mem": f(inputs["w_br_mem"][0]),
        "w_out": f(inputs["w_out"][0]), "ln2_g": f(inputs["ln2_g"]), "ln2_b": f(inputs["ln2_b"]),
        "ffn2_w_in": f(inputs["ffn2_w_in"][0]), "ffn2_w_out": f(inputs["ffn2_w_out"][0]),
        "ln3_g": f(inputs["ln3_g"]), "ln3_b": f(inputs["ln3_b"]),
        "c_ident": np.eye(128, dtype=np.float32),
    }
    shared.update(make_consts())
    x = f(inputs["x"])
    mem = f(inputs["mem"])
    return [dict(shared, x=x[b], mem=mem[b]) for b in range(x.shape[0])]


def kernel(**inputs):
    if "nc" not in _CACHE:
        _CACHE["nc"] = build_nc()
    nc = _CACHE["nc"]
    in_maps = make_in_maps(inputs)
    res = run_bass_kernel_spmd(nc, in_maps, core_ids=list(range(len(in_maps))))
    return np.stack([np.asarray(r["out"], dtype=np.float32) for r in res.results], axis=0)
```

```python
import contextlib
import numpy as np
import concourse.bass as bass
import concourse.mybir as mybir
from concourse.bass_utils import run_bass_kernel_spmd

F32 = mybir.dt.float32
BF16 = mybir.dt.bfloat16
AF = mybir.ActivationFunctionType
ALU = mybir.AluOpType

L = 2048
D = 1024
DFF = 2816
NCH = DFF // 128
ALPHA = 2.0 ** 0.25
LN_EPS = 1e-5
W_IN_COLS = 7240

ENGS = ("pe", "dve", "act", "pool", "sync")
SEM_ROT = {"c": 8000}
DMA_RING = 8
RELAX_SAME_ENGINE = True


class Tok:
    __slots__ = ("w", "r", "name")

    def __init__(self, name=""):
        self.w = None
        self.r = {}
        self.name = name


class Ins:
    __slots__ = ("eng", "kind", "fn", "deps", "idx", "inc", "sem", "val")

    def __init__(self, eng, kind, fn):
        self.eng = eng
        self.kind = kind
        self.fn = fn
        self.deps = []
        self.inc = False
        self.sem = None
        self.val = 0


class Prog:
    def __init__(self, nc, semstack, dmaq):
        self.nc = nc
        self.semstack = semstack
        self.dmaq = dmaq
        self.streams = {e: [] for e in ENGS}
        self.n = 0

    def add(self, eng, fn, reads=(), writes=(), kind="c"):
        ins = Ins(eng, kind, fn)
        st = self.streams[eng]
        ins.idx = len(st)
        deps = {}

        def dep(d, typ):
            if d is None or d is ins:
                return
            if d.eng == eng and d.kind == "c" and kind == "c":
                if eng == "pe":
                    return
                if typ != "RAW":
                    return
                if RELAX_SAME_ENGINE and eng in ("dve", "act") and ins.idx - d.idx >= 2:
                    return
            deps[id(d)] = d

        for t in reads:
            dep(t.w, "RAW")
        for t in writes:
            dep(t.w, "WAW")
            for d in t.r.values():
                dep(d, "WAR")
        if kind == "d":
            q = self.dmaq.setdefault(eng, {"n": 0, "last": {}, "sems": []})
            n = q["n"]
            q["n"] += 1
            r = n % DMA_RING
            if len(q["sems"]) <= r:
                q["sems"].append(self.semstack.enter_context(self.nc.semaphore(f"s_dma_{eng}_{r}")))
            ins.sem = q["sems"][r]
            ins.val = 16 * (n // DMA_RING + 1)
            ins.inc = True
            prev = q["last"].get(r)
            if prev is not None:
                deps[id(prev)] = prev
            q["last"][r] = ins
        ins.deps = list(deps.values())
        for d in ins.deps:
            d.inc = True
        for t in reads:
            t.r[eng + kind] = ins
        for t in writes:
            t.w = ins
            t.r = {}
        st.append(ins)
        self.n += 1
        return ins

    def wait_all(self, eng, toks):
        return self.add(eng, None, reads=toks, kind="w")

    def emit(self, name):
        nc = self.nc
        nsem = 0
        for e in ENGS:
            for kind in ("c",):
                cnt = 0
                cur = None
                for ins in self.streams[e]:
                    if ins.kind != kind or not ins.inc:
                        continue
                    if cur is None or cnt >= SEM_ROT[kind]:
                        cur = self.semstack.enter_context(nc.semaphore(f"s_{name}_{e}_{kind}_{nsem}"))
                        nsem += 1
                        cnt = 0
                    cnt += 1
                    ins.sem = cur
                    ins.val = cnt * (16 if kind == "d" else 1)
        with nc.Block() as block:
            engmap = {"pe": block.tensor, "dve": block.vector, "act": block.scalar,
                      "pool": block.gpsimd, "sync": block.sync}
            for e in ENGS:
                stream = self.streams[e]
                if not stream:
                    continue

                def body(eobj, stream=stream):
                    waited = {}
                    for ins in stream:
                        for d in ins.deps:
                            k = id(d.sem)
                            if waited.get(k, 0) >= d.val:
                                continue
                            eobj.wait_ge(d.sem, d.val)
                            waited[k] = d.val
                        if ins.fn is None:
                            continue
                        r = ins.fn(eobj)
                        if ins.inc:
                            r.then_inc(ins.sem, 16 if ins.kind == "d" else 1)

                engmap[e](body)


def bcast_rows(ap2d, nrows):
    return bass.AP(ap2d.tensor, ap2d.offset, [[0, nrows], [1, ap2d.shape[-1]]])


def ffn_stage(nc, semstack, dmaq, name, src_d, w_in_d, w_out_d, g_d, b_d, dst_d, ident_d):
    with contextlib.ExitStack() as st:
        P = Prog(nc, semstack, dmaq)
        sb = lambda n, shape, dt: st.enter_context(nc.sbuf_tensor(f"{name}_{n}", shape, dt))
        ps = lambda n, shape, dt: st.enter_context(nc.psum_tensor(f"{name}_{n}", shape, dt))
        HT = 1024
        xT = [sb(f"xT{i}", [128, 8, HT], BF16) for i in range(2)]
        gT = sb("gT", [128, NCH, HT], BF16)
        wout = sb("wout", [128, NCH, D], BF16)
        wst = [sb(f"wst{i}", [128, 2, 8, 128], F32) for i in range(2)]
        wbf = [sb(f"wbf{i}", [128, 2, 8, 128], BF16) for i in range(3)]
        wost = [sb(f"wost{i}", [128, D], F32) for i in range(2)]
        xs = [sb(f"xs{i}", [128, D], F32) for i in range(2)]
        xb = [sb(f"xb{i}", [128, D], BF16) for i in range(2)]
        sA = [sb(f"sA{i}", [128, 512], F32) for i in range(2)]
        rr = [sb(f"rr{i}", [128, D], F32) for i in range(2)]
        oo = [sb(f"oo{i}", [128, D], F32) for i in range(2)]
        gam = sb("gam", [128, D], F32)
        bet = sb("bet", [128, D], F32)
        idf = sb("idf", [128, 128], F32)
        idb = sb("idb", [128, 128], BF16)
        epst = sb("epst", [128, 1], F32)
        stats = [sb(f"stats{i}", [128, 2, 6], F32) for i in range(2)]
        mv = [sb(f"mv{i}", [128, 2], F32) for i in range(2)]
        sd = [sb(f"sd{i}", [128, 1], F32) for i in range(2)]
        rstd = [sb(f"rstd{i}", [128, 1], F32) for i in range(2)]
        pT = [ps(f"pT{i}", [128, 1024], BF16) for i in range(2)]
        pB = [ps(f"pB{i}", [128, 512], F32) for i in range(6)]

        def toks(n, k):
            return [Tok(f"{n}{i}") for i in range(k)]

        t_xT = [toks("xT", 8) for _ in range(2)]
        t_gT = [[Tok() for _ in range(2)] for _ in range(NCH)]
        t_wout = toks("wout", NCH)
        t_wst = toks("wst", 2); t_wbf = toks("wbf", 3); t_wost = toks("wost", 2)
        t_xs = toks("xs", 2); t_xb = toks("xb", 2); t_sA = toks("sA", 2)
        t_ys = toks("ys", 2); t_rr = toks("rr", 2); t_oo = toks("oo", 2)
        t_gam = Tok(); t_bet = Tok(); t_idf = Tok(); t_idb = Tok(); t_eps = Tok()
        t_stats = toks("st", 2); t_mv = toks("mv", 2); t_sd = toks("sd", 2); t_rstd = toks("rs", 2)
        t_pT = toks("pT", 2); t_pB = toks("pB", 6)
        t_dst = toks("dst", 16)

        P.add("sync", lambda e: e.dma_start(out=idf[:], in_=ident_d), writes=[t_idf], kind="d")
        P.add("sync", lambda e: e.dma_start(out=gam[:], in_=bcast_rows(g_d, 128)), writes=[t_gam], kind="d")
        P.add("sync", lambda e: e.dma_start(out=bet[:], in_=bcast_rows(b_d, 128)), writes=[t_bet], kind="d")
        P.add("dve", lambda e: e.tensor_copy(out=idb[:], in_=idf[:]), reads=[t_idf], writes=[t_idb])
        P.add("dve", lambda e: e.memset(epst[:], LN_EPS), writes=[t_eps])

        cnt = {"x": 0, "pb": 0, "sa": 0, "c": 0}

        def stage_a(h):
            for tl in range(8):
                t = h * 8 + tl
                s = cnt["x"] % 2
                cnt["x"] += 1
                P.add("sync", lambda e, s=s, t=t: e.dma_start(out=xs[s][:], in_=src_d[t * 128:(t + 1) * 128, :]),
                      writes=[t_xs[s]], kind="d")
                P.add("dve", lambda e, s=s: e.tensor_copy(out=xb[s][:], in_=xs[s][:]), reads=[t_xs[s]], writes=[t_xb[s]])
                for kc in range(8):
                    P.add("pe", lambda e, s=s, kc=kc: e.transpose(out=pT[s][:, kc * 128:(kc + 1) * 128],
                                                                  in_=xb[s][:, kc * 128:(kc + 1) * 128], identity=idb[:]),
                          reads=[t_xb[s], t_idb], writes=[t_pT[s]])
                P.add("act", lambda e, s=s, h=h, tl=tl: e.activation(
                    out=xT[h][:, :, tl * 128:(tl + 1) * 128],
                    in_=pT[s][:].rearrange("p (a b) -> p a b", a=8), func=AF.Copy),
                    reads=[t_pT[s]], writes=[t_xT[h][tl]])

        wcount = {"c": 0}

        def load_w(c, with_out):
            s = wcount["c"] % 2
            bs = wcount["c"] % 3
            wcount["c"] += 1
            for j in range(2):
                col0 = j * DFF + c * 128
                P.add("sync", lambda e, s=s, j=j, col0=col0: e.dma_start(
                    out=wst[s][:, j], in_=w_in_d[:, col0:col0 + 128].rearrange("(kc p) n -> p kc n", p=128)),
                    writes=[t_wst[s]], kind="d")
            P.add("act", lambda e, s=s, bs=bs: e.activation(out=wbf[bs][:].rearrange("p a b c -> p (a b c)"),
                                                            in_=wst[s][:].rearrange("p a b c -> p (a b c)"), func=AF.Copy),
                  reads=[t_wst[s]], writes=[t_wbf[bs]])
            if with_out:
                P.add("sync", lambda e, s=s, c=c: e.dma_start(out=wost[s][:], in_=w_out_d[c * 128:(c + 1) * 128, :]),
                      writes=[t_wost[s]], kind="d")
                P.add("pool", lambda e, s=s, c=c: e.tensor_copy(out=wout[:, c, :], in_=wost[s][:]),
                      reads=[t_wost[s]], writes=[t_wout[c]])
            return bs

        def stage_b(h):
            pending = load_w(0, h == 0)
            for c in range(NCH):
                bs = pending
                if c + 1 < NCH:
                    pending = load_w(c + 1, h == 0)
                for tb in range(2):
                    pa = (cnt["pb"] % 2) * 2
                    cnt["pb"] += 1
                    for j in range(2):
                        for kc in range(8):
                            P.add("pe", lambda e, pa=pa, j=j, kc=kc, bs=bs, tb=tb, h=h: e.matmul(
                                pB[pa + j][:], lhsT=wbf[bs][:, j, kc, :], rhs=xT[h][:, kc, tb * 512:(tb + 1) * 512],
                                start=(kc == 0), stop=(kc == 7)),
                                reads=[t_wbf[bs]] + t_xT[h][tb * 4:(tb + 1) * 4], writes=[t_pB[pa + j]])
                    s = cnt["sa"] % 2
                    cnt["sa"] += 1
                    P.add("act", lambda e, s=s, pa=pa: e.activation(out=sA[s][:], in_=pB[pa][:], func=AF.Silu),
                          reads=[t_pB[pa]], writes=[t_sA[s]])
                    P.add("dve", lambda e, s=s, pa=pa, c=c, tb=tb: e.scalar_tensor_tensor(
                        out=gT[:, c, tb * 512:(tb + 1) * 512], in0=sA[s][:], scalar=0.5, in1=pB[pa + 1][:],
                        op0=ALU.mult, op1=ALU.mult),
                        reads=[t_sA[s], t_pB[pa + 1]], writes=[t_gT[c][tb]])

        def stage_c(h):
            for tl in range(8):
                t = h * 8 + tl
                s = cnt["c"] % 2
                cnt["c"] += 1
                pa = (cnt["pb"] % 2) * 2
                cnt["pb"] += 1
                sx = cnt["x"] % 2
                cnt["x"] += 1
                P.add("sync", lambda e, sx=sx, t=t: e.dma_start(out=xs[sx][:], in_=src_d[t * 128:(t + 1) * 128, :]),
                      writes=[t_xs[sx]], kind="d")
                for nh in range(2):
                    for kc in range(NCH):
                        P.add("pe", lambda e, pa=pa, nh=nh, kc=kc, tl=tl: e.matmul(
                            pB[pa + nh][:], lhsT=gT[:, kc, tl * 128:(tl + 1) * 128], rhs=wout[:, kc, nh * 512:(nh + 1) * 512],
                            start=(kc == 0), stop=(kc == NCH - 1)),
                            reads=[t_gT[kc][tl // 4], t_wout[kc]], writes=[t_pB[pa + nh]])
                    P.add("dve", lambda e, pa=pa, nh=nh, s=s, sx=sx: e.scalar_tensor_tensor(
                        out=rr[s][:, nh * 512:(nh + 1) * 512], in0=xs[sx][:, nh * 512:(nh + 1) * 512], scalar=ALPHA,
                        in1=pB[pa + nh][:], op0=ALU.mult, op1=ALU.add),
                        reads=[t_xs[sx], t_pB[pa + nh]], writes=[t_rr[s]])
                for k in range(2):
                    P.add("dve", lambda e, s=s, k=k: e.bn_stats(out=stats[s][:, k, :], in_=rr[s][:, k * 512:(k + 1) * 512]),
                          reads=[t_rr[s]], writes=[t_stats[s]])
                P.add("dve", lambda e, s=s: e.bn_aggr(out=mv[s][:], in_=stats[s][:].rearrange("p a b -> p (a b)")),
                      reads=[t_stats[s]], writes=[t_mv[s]])
                P.add("act", lambda e, s=s: e.activation(out=sd[s][:], in_=mv[s][:, 1:2], func=AF.Sqrt, bias=epst[:, 0:1], scale=1.0),
                      reads=[t_mv[s], t_eps], writes=[t_sd[s]])
                P.add("dve", lambda e, s=s: e.reciprocal(out=rstd[s][:], in_=sd[s][:]), reads=[t_sd[s]], writes=[t_rstd[s]])
                P.add("dve", lambda e, s=s: e.tensor_scalar(
                    out=rr[s][:], in0=rr[s][:], scalar1=mv[s][:, 0:1], scalar2=rstd[s][:, 0:1], op0=ALU.subtract, op1=ALU.mult),
                    reads=[t_rr[s], t_mv[s], t_rstd[s]], writes=[t_rr[s]])
                P.add("pool", lambda e, s=s: e.tensor_tensor(out=oo[s][:], in0=rr[s][:], in1=gam[:], op=ALU.mult),
                      reads=[t_rr[s], t_gam], writes=[t_oo[s]])
                P.add("pool", lambda e, s=s: e.tensor_tensor(out=oo[s][:], in0=oo[s][:], in1=bet[:], op=ALU.add),
                      reads=[t_oo[s], t_bet], writes=[t_oo[s]])
                P.add("pool", lambda e, s=s, t=t: e.dma_start(out=dst_d[t * 128:(t + 1) * 128, :], in_=oo[s][:]),
                      reads=[t_oo[s]], writes=[t_dst[t]], kind="d")

        stage_a(0)
        stage_b(0)
        stage_a(1)
        stage_c(0)
        stage_b(1)
        stage_c(1)
        P.wait_all("pool", t_dst)
        P.emit(name)


C_RQ, C_RK, C_RV, C_RG = 0, 256, 512, 1024
C_DQ, C_DK, C_DV, C_IQ, C_IK, C_IW, C_MQ, C_G = 1536, 2048, 2560, 3072, 3584, 3648, 3656, 4168
IW_SCALE = float(8 ** -0.5 * 64 ** -0.5)
NIT = 18
GAMMAS = [1.0 - 2.0 ** (-5.0 - h) for h in range(4)]


def toks(k):
    return [Tok() for _ in range(k)]


class Env:
    def __init__(self, nc, semstack, dmaq, name, st):
        self.nc = nc
        self.P = Prog(nc, semstack, dmaq)
        self.name = name
        self.st = st
        self.k = 0

    def sb(self, n, shape, dt):
        return self.st.enter_context(self.nc.sbuf_tensor(f"{self.name}_{n}", shape, dt))

    def ps(self, n, shape, dt):
        return self.st.enter_context(self.nc.psum_tensor(f"{self.name}_{n}", shape, dt))


class WStream:
    def __init__(self, env, nst=2, nbf=3):
        self.env = env
        self.st = [env.sb(f"wsst{i}", [128, 8, 128], F32) for i in range(nst)]
        self.bf = [env.sb(f"wsbf{i}", [128, 8, 128], BF16) for i in range(nbf)]
        self.t_st = toks(nst)
        self.t_bf = toks(nbf)
        self.n = 0

    def load(self, pieces, KC=8, dst=None, t_dst=None):
        P = self.env.P
        s = self.n % len(self.st)
        b = self.n % len(self.bf)
        self.n += 1
        stt = self.st[s]
        for (c0, src, sc) in pieces:
            w = src.shape[-1]
            P.add("sync", lambda e, stt=stt, c0=c0, src=src, w=w, KC=KC: e.dma_start(
                out=stt[:, 0:KC, c0:c0 + w], in_=src.rearrange("(kc p) n -> p kc n", p=128)),
                writes=[self.t_st[s]], kind="d")
        if dst is None:
            tot = max(c0 + src.shape[-1] for (c0, src, sc) in pieces)
            out_t = self.bf[b]
            out_fn = lambda c0, w: out_t[:, 0:KC, c0:c0 + w]
            t_out = self.t_bf[b]
        else:
            out_fn = dst
            t_out = t_dst
        if all(sc == 1.0 for (_, _, sc) in pieces):
            lo = min(c0 for (c0, _, _) in pieces)
            hi = max(c0 + src.shape[-1] for (c0, src, _) in pieces)
            P.add("pool", lambda e, lo=lo, hi=hi, stt=stt, KC=KC: e.tensor_copy(out=out_fn(lo, hi - lo), in_=stt[:, 0:KC, lo:hi]),
                  reads=[self.t_st[s]], writes=[t_out])
        else:
            for (c0, src, sc) in pieces:
                w = src.shape[-1]
                P.add("dve", lambda e, c0=c0, w=w, sc=sc, stt=stt, KC=KC: e.tensor_scalar(
                    out=out_fn(c0, w), in0=stt[:, 0:KC, c0:c0 + w], scalar1=float(sc), scalar2=None, op0=ALU.mult),
                    reads=[self.t_st[s]], writes=[t_out])
        return (self.bf[b] if dst is None else None), t_out


def load_transposed(env, src_d, ntiles, dstT, t_dstT, idb, t_idb, pT, t_pT):
    P = env.P
    xs = [env.sb(f"ltxs{i}", [128, D], F32) for i in range(2)]
    xb = [env.sb(f"ltxb{i}", [128, D], BF16) for i in range(2)]
    t_xs = toks(2); t_xb = toks(2)
    for t in range(ntiles):
        s = t % 2
        P.add("sync", lambda e, s=s, t=t: e.dma_start(out=xs[s][:], in_=src_d[t * 128:(t + 1) * 128, :]),
              writes=[t_xs[s]], kind="d")
        P.add("dve", lambda e, s=s: e.tensor_copy(out=xb[s][:], in_=xs[s][:]), reads=[t_xs[s]], writes=[t_xb[s]])
        for kc in range(8):
            P.add("pe", lambda e, s=s, kc=kc: e.transpose(out=pT[s][:, kc * 128:(kc + 1) * 128],
                                                          in_=xb[s][:, kc * 128:(kc + 1) * 128], identity=idb[:]),
                  reads=[t_xb[s], t_idb], writes=[t_pT[s]])
        P.add("act", lambda e, s=s, t=t: e.activation(
            out=dstT[:, :, t * 128:(t + 1) * 128], in_=pT[s][:].rearrange("p (a b) -> p a b", a=8), func=AF.Copy),
            reads=[t_pT[s]], writes=[t_dstT[t]])


def load_ident(env, ident_d):
    P = env.P
    idf = env.sb("idf", [128, 128], F32)
    idb = env.sb("idb", [128, 128], BF16)
    t_idf = Tok(); t_idb = Tok()
    P.add("sync", lambda e: e.dma_start(out=idf[:], in_=ident_d), writes=[t_idf], kind="d")
    P.add("dve", lambda e: e.tensor_copy(out=idb[:], in_=idf[:]), reads=[t_idf], writes=[t_idb])
    return idf, t_idf, idb, t_idb


class Banks:
    def __init__(self, env, n):
        self.b = [env.ps(f"bk{i}", [128, 512], F32) for i in range(n)]
        self.t = toks(n)
        self.i = 0
        self.n = n

    def next(self):
        i = self.i % self.n
        self.i += 1
        return self.b[i], self.t[i]


def proj_fm(env, ws, banks, pieces, ncols, rhs_fn, rhs_toks_fn, nblk, blkw, evac, KC=8):
    P = env.P
    wt, t_w = ws.load(pieces, KC=KC)
    for tb in range(nblk):
        bank, t_bank = banks.next()
        for kc in range(KC):
            P.add("pe", lambda e, bank=bank, kc=kc, tb=tb, wt=wt: e.matmul(
                bank[0:ncols, 0:blkw], lhsT=wt[:, kc, 0:ncols], rhs=rhs_fn(kc, tb), start=(kc == 0), stop=(kc == KC - 1)),
                reads=[t_w] + rhs_toks_fn(tb), writes=[t_bank])
        evac(tb, bank, t_bank)


def mix_load_stage(nc, semstack, dmaq, x1_d, ident_d, x1T):
    with contextlib.ExitStack() as st:
        env = Env(nc, semstack, dmaq, "m0", st)
        idf, t_idf, idb, t_idb = load_ident(env, ident_d)
        pT = [env.ps(f"pT{i}", [128, 1024], BF16) for i in range(2)]
        t_pT = toks(2)
        t_x1T = toks(16)
        load_transposed(env, x1_d, 16, x1T, t_x1T, idb, t_idb, pT, t_pT)
        env.P.emit("m0")


def mem_stage(nc, semstack, dmaq, x1T, oT, mem_d, w_in, wmkv, ident_d):
    with contextlib.ExitStack() as st:
        env = Env(nc, semstack, dmaq, "m3", st)
        P = env.P
        idf, t_idf, idb, t_idb = load_ident(env, ident_d)
        pT = [env.ps(f"pT{i}", [128, 1024], BF16) for i in range(2)]
        t_pT = toks(2)
        banks = Banks(env, 4)
        pN = env.ps("pN", [128, 512], F32); t_pN = Tok()
        pD = env.ps("pD", [128, 512], F32); t_pD = Tok()
        memT = env.sb("memT", [128, 8, 256], BF16); t_memT = toks(2)
        mkT = env.sb("mkT", [128, 4, 256], BF16); t_mkT = toks(4)
        mvv = env.sb("mv", [128, 2, 512], BF16); t_mv = toks(2)
        wmv = env.sb("wmv", [128, 8, 512], BF16); t_wmv = toks(4)
        mqT = env.sb("mqT", [128, 4, L], BF16); t_mqT = [toks(4) for _ in range(4)]
        ones = env.sb("ones", [128, 128], BF16); t_ones = Tok()
        E = [env.sb(f"E{i}", [128, 512], BF16) for i in range(4)]; t_E = toks(4)
        rden = [env.sb(f"rden{i}", [128, 512], F32) for i in range(2)]; t_rden = toks(2)
        ws = WStream(env)
        P.add("pool", lambda e: e.memset(ones[:], 1.0), writes=[t_ones])
        load_transposed(env, mem_d, 2, memT, t_memT, idb, t_idb, pT, t_pT)
        t_x1T = []
        for h in range(4):
            def evac(tb, bank, t_bank, h=h):
                P.add("act", lambda e: e.activation(out=mkT[:, h, :], in_=bank[:, 0:256], func=AF.Copy),
                      reads=[t_bank], writes=[t_mkT[h]])
            proj_fm(env, ws, banks, [(0, wmkv[:, h * 128:(h + 1) * 128], 1.0)], 128,
                    lambda kc, tb: memT[:, kc, :], lambda tb: t_memT, 1, 256, evac)
        for c in range(4):
            ws.load([(0, wmkv[:, 512 + c * 128:512 + (c + 1) * 128], 1.0)],
                    dst=lambda c0, w, c=c: wmv[:, :, c * 128 + c0:c * 128 + c0 + w], t_dst=t_wmv[c])
        for mt in range(2):
            bank, t_bank = banks.next()
            for kc in range(8):
                P.add("pe", lambda e, bank=bank, kc=kc, mt=mt: e.matmul(
                    bank[:], lhsT=memT[:, kc, mt * 128:(mt + 1) * 128], rhs=wmv[:, kc, :], start=(kc == 0), stop=(kc == 7)),
                    reads=[t_memT[mt]] + t_wmv, writes=[t_bank])
            P.add("act", lambda e, bank=bank, mt=mt: e.activation(out=mvv[:, mt, :], in_=bank[:], func=AF.Copy),
                  reads=[t_bank], writes=[t_mv[mt]])
        for h in range(4):
            def evac(tb, bank, t_bank, h=h):
                P.add("act", lambda e: e.activation(out=mqT[:, h, tb * 512:(tb + 1) * 512], in_=bank[:], func=AF.Copy,
                                                    scale=float(128 ** -0.5)),
                      reads=[t_bank], writes=[t_mqT[h][tb]])
            proj_fm(env, ws, banks, [(0, w_in[:, C_MQ + h * 128:C_MQ + (h + 1) * 128], 1.0)], 128,
                    lambda kc, tb: x1T[:, kc, tb * 512:(tb + 1) * 512], lambda tb: [], 4, 512, evac)
        it = 0
        for h in range(4):
            for qb in range(4):
                for mt in range(2):
                    bank, t_bank = banks.next()
                    ei = (it * 2 + mt) % 4
                    P.add("pe", lambda e, bank=bank, h=h, qb=qb, mt=mt: e.matmul(
                        bank[:], lhsT=mkT[:, h, mt * 128:(mt + 1) * 128], rhs=mqT[:, h, qb * 512:(qb + 1) * 512],
                        start=True, stop=True), reads=[t_mkT[h], t_mqT[h][qb]], writes=[t_bank])
                    P.add("act", lambda e, bank=bank, ei=ei: e.activation(out=E[ei][:], in_=bank[:], func=AF.Exp),
                          reads=[t_bank], writes=[t_E[ei]])
                for mt in range(2):
                    ei = (it * 2 + mt) % 4
                    P.add("pe", lambda e, ei=ei, h=h, mt=mt: e.matmul(
                        pN[:], lhsT=mvv[:, mt, h * 128:(h + 1) * 128], rhs=E[ei][:], start=(mt == 0), stop=(mt == 1)),
                        reads=[t_mv[mt], t_E[ei]], writes=[t_pN])
                    P.add("pe", lambda e, ei=ei, mt=mt: e.matmul(
                        pD[:], lhsT=ones[:], rhs=E[ei][:], start=(mt == 0), stop=(mt == 1)),
                        reads=[t_ones, t_E[ei]], writes=[t_pD])
                r = it % 2
                P.add("dve", lambda e, r=r: e.reciprocal(out=rden[r][:], in_=pD[:]), reads=[t_pD], writes=[t_rden[r]])
                P.add("dve", lambda e, r=r, h=h, qb=qb: e.tensor_tensor(
                    out=oT[:, h, qb * 512:(qb + 1) * 512], in0=pN[:], in1=rden[r][:], op=ALU.mult),
                    reads=[t_pN, t_rden[r]], writes=[Tok()])
                it += 1
        env.P.emit("m3")


def dsa_stage(nc, semstack, dmaq, x1T, oT, w_in, t5_d, ident_d, J_d, oh_d, causal_d, pow2_d, gscr_d):
    with contextlib.ExitStack() as st:
        env = Env(nc, semstack, dmaq, "m1", st)
        P = env.P
        idf, t_idf, idb, t_idb = load_ident(env, ident_d)
        banks = Banks(env, 3)
        pO = [[env.ps(f"pO{i}{j}", [128, 512], F32) for j in range(2)] for i in range(2)]
        t_pO = [toks(2) for _ in range(2)]
        pM = env.ps("pM", [128, 1024], BF16); t_pM = Tok()
        qT = env.sb("qT", [128, 4, L], BF16); t_qT = [toks(4) for _ in range(4)]
        kT = env.sb("kT", [128, 4, L], BF16); t_kT = [toks(4) for _ in range(4)]
        qiT = env.sb("qiT", [128, 4, L], BF16); t_qiT = [toks(4) for _ in range(4)]
        kiT = env.sb("kiT", [128, L], BF16); t_kiT = toks(4)
        vaug = env.sb("vaug", [128, 16, 8, 65], BF16); t_v = toks(16); t_vones = Tok()
        iw = env.sb("iw", [128, 16, 8], F32); t_iw = toks(16)
        wv = env.sb("wv", [128, 8, 512], BF16); t_wv = toks(4)
        wiw = env.sb("wiw", [128, 8, 8], BF16); t_wiw = Tok()
        Sc2 = [env.sb(f"Sc{i}", [128, L], F32) for i in range(2)]; t_Sc2 = toks(2)
        junk2 = [env.sb(f"junk{i}", [128, L], BF16) for i in range(2)]; t_junk2 = toks(2)
        mask = [env.sb(f"mask{i}", [128, L], BF16) for i in range(2)]; t_mask = toks(2)
        maskTp = [env.sb(f"maskTp{i}", [128, 16, 2, 128], BF16) for i in range(2)]
        t_maskTp = [toks(2) for _ in range(2)]
        tI = [env.sb(f"tI{i}", [128, 512], F32) for i in range(2)]; t_tI = toks(2)
        E = [env.sb(f"E{i}", [128, 512], BF16) for i in range(2)]; t_E = toks(2)
        PT = [env.sb(f"PT{i}", [128, 512], BF16) for i in range(3)]; t_PT = toks(3)
        BT = env.sb("BT", [128, 8, 256], BF16); t_BT = toks(8)
        H = Sc2[1][:].rearrange("p (h n) -> p h n", h=8); t_H = t_Sc2[1]
        rden8 = [env.sb(f"rden{i}", [128, 2, 4], F32) for i in range(2)]; t_rden8 = toks(2)
        o_tm = [env.sb(f"otm{i}", [128, 512], BF16) for i in range(2)]; t_otm = toks(2)
        Jf = env.sb("Jf", [128, 128], F32); t_J = Tok()
        caus = env.sb("caus", [128, 128], F32); t_caus = Tok()
        pow2 = env.sb("pow2", [128, NIT], F32); t_pow2 = Tok()
        tabS = env.sb("tabS", [32, 8], F32); t_tab = Tok()
        ohS = env.sb("ohS", [32, 384], F32); t_oh = Tok()
        Gs = env.sb("Gs", [8, 384], F32); t_Gs = Tok()
        thrneg = env.sb("thrneg", [128, 1], F32); t_thrneg = Tok()
        mx8_2 = [env.sb(f"mx8{i}", [128, 8], F32) for i in range(2)]; t_mx8_2 = toks(2)
        mn_2 = [env.sb(f"mn{i}", [128, 1], F32) for i in range(2)]; t_mn_2 = toks(2)
        rng_2 = [env.sb(f"rng{i}", [128, 1], F32) for i in range(2)]; t_rng_2 = toks(2)
        thr_2 = [env.sb(f"thr{i}", [128, 1], F32) for i in range(2)]; t_thr_2 = toks(2)
        S2_2 = [env.sb(f"S2{i}", [128, NIT], F32) for i in range(2)]; t_S2_2 = toks(2)
        cnt_2 = [env.sb(f"cnt{i}", [128, 1], F32) for i in range(2)]; t_cnt_2 = toks(2)
        ee_2 = [env.sb(f"ee{i}", [128, 1], F32) for i in range(2)]; t_ee_2 = toks(2)
        ws = WStream(env)
        t_gscr = Tok()
        print("[dsa] sbuf bytes remaining/partition:", nc.sbuf_bytes_remaining() if callable(nc.sbuf_bytes_remaining) else nc.sbuf_bytes_remaining)

        P.add("sync", lambda e: e.dma_start(out=Jf[:], in_=J_d), writes=[t_J], kind="d")
        P.add("sync", lambda e: e.dma_start(out=caus[:], in_=causal_d), writes=[t_caus], kind="d")
        P.add("sync", lambda e: e.dma_start(out=pow2[:], in_=pow2_d), writes=[t_pow2], kind="d")
        P.add("sync", lambda e: e.dma_start(out=tabS[:], in_=t5_d), writes=[t_tab], kind="d")
        P.add("sync", lambda e: e.dma_start(out=ohS[:], in_=oh_d), writes=[t_oh], kind="d")
        P.add("pool", lambda e: e.memset(vaug[:, :, :, 64:65], 1.0), writes=[t_vones])
        for i in range(2):
            P.add("pool", lambda e, i=i: e.memset(maskTp[i][:], 0.0), writes=t_maskTp[i])
        P.add("pool", lambda e: e.memset(thrneg[:], -1.0e29), writes=[t_thrneg])
        bank, t_bank = banks.next()
        P.add("pe", lambda e, bank=bank: e.matmul(bank[0:8, 0:384], lhsT=tabS[:], rhs=ohS[:], start=True, stop=True),
              reads=[t_tab, t_oh], writes=[t_bank])
        P.add("act", lambda e, bank=bank: e.activation(out=Gs[:], in_=bank[0:8, 0:384], func=AF.Copy), reads=[t_bank], writes=[t_Gs])
        P.add("sync", lambda e: e.dma_start(out=gscr_d, in_=Gs[:]), reads=[t_Gs], writes=[t_gscr], kind="d")
        hank = bass.AP(gscr_d.tensor, gscr_d.offset, [[1, 128], [384, 8], [1, 256]])
        P.add("sync", lambda e: e.dma_start(out=H, in_=hank), reads=[t_gscr], writes=[t_H], kind="d")
        for h in range(8):
            bank, t_bank = banks.next()
            P.add("pe", lambda e, bank=bank, h=h: e.matmul(bank[:, 0:256], lhsT=Jf[:], rhs=H[:, h, :], start=True, stop=True),
                  reads=[t_J, t_H], writes=[t_bank])
            P.add("act", lambda e, bank=bank, h=h: e.activation(out=BT[:, h, :], in_=bank[:, 0:256], func=AF.Copy),
                  reads=[t_bank], writes=[t_BT[h]])

        ev = {"i": 0}

        def evac_to(dstfn, t_dstfn, scale):
            def evac(tb, bank, t_bank):
                eng = "act" if ev["i"] % 2 == 0 else "dve"
                ev["i"] += 1
                if eng == "act":
                    P.add("act", lambda e: e.activation(out=dstfn(tb), in_=bank[:], func=AF.Copy, scale=float(scale)),
                          reads=[t_bank], writes=[t_dstfn(tb)])
                else:
                    P.add("dve", lambda e: e.tensor_scalar(out=dstfn(tb), in0=bank[:], scalar1=float(scale), scalar2=None, op0=ALU.mult),
                          reads=[t_bank], writes=[t_dstfn(tb)])
            return evac

        xrhs = lambda kc, tb: x1T[:, kc, tb * 512:(tb + 1) * 512]
        for c in range(4):
            proj_fm(env, ws, banks, [(0, w_in[:, C_DQ + c * 128:C_DQ + (c + 1) * 128], 1.0)], 128, xrhs, lambda tb: [], 4, 512,
                    evac_to(lambda tb, c=c: qT[:, c, tb * 512:(tb + 1) * 512], lambda tb, c=c: t_qT[c][tb], 0.125))
            proj_fm(env, ws, banks, [(0, w_in[:, C_DK + c * 128:C_DK + (c + 1) * 128], 1.0)], 128, xrhs, lambda tb: [], 4, 512,
                    evac_to(lambda tb, c=c: kT[:, c, tb * 512:(tb + 1) * 512], lambda tb, c=c: t_kT[c][tb], 1.0))
            proj_fm(env, ws, banks, [(0, w_in[:, C_IQ + c * 128:C_IQ + (c + 1) * 128], 1.0)], 128, xrhs, lambda tb: [], 4, 512,
                    evac_to(lambda tb, c=c: qiT[:, c, tb * 512:(tb + 1) * 512], lambda tb, c=c: t_qiT[c][tb], 1.0))
        proj_fm(env, ws, banks, [(0, w_in[:, C_IK:C_IK + 64], 1.0), (64, w_in[:, C_IK:C_IK + 64], 1.0)], 128, xrhs, lambda tb: [], 4, 512,
                evac_to(lambda tb: kiT[:, tb * 512:(tb + 1) * 512], lambda tb: t_kiT[tb], 1.0))
        for c in range(4):
            ws.load([(0, w_in[:, C_DV + c * 128:C_DV + (c + 1) * 128], 1.0)],
                    dst=lambda c0, w, c=c: wv[:, :, c * 128 + c0:c * 128 + c0 + w], t_dst=t_wv[c])
        ws.load([(0, w_in[:, C_IW:C_IW + 8], 1.0)], dst=lambda c0, w: wiw[:, :, c0:c0 + w], t_dst=t_wiw)
        for t in range(16):
            bank, t_bank = banks.next()
            for kc in range(8):
                P.add("pe", lambda e, bank=bank, kc=kc, t=t: e.matmul(
                    bank[:], lhsT=x1T[:, kc, t * 128:(t + 1) * 128], rhs=wv[:, kc, :], start=(kc == 0), stop=(kc == 7)),
                    reads=t_wv, writes=[t_bank])
            P.add("act", lambda e, bank=bank, t=t: e.activation(out=vaug[:, t, :, 0:64], in_=bank[:].rearrange("p (h d) -> p h d", h=8),
                                                                func=AF.Copy), reads=[t_bank], writes=[t_v[t]])
            bank, t_bank = banks.next()
            for kc in range(8):
                P.add("pe", lambda e, bank=bank, kc=kc, t=t: e.matmul(
                    bank[:, 0:8], lhsT=x1T[:, kc, t * 128:(t + 1) * 128], rhs=wiw[:, kc, :], start=(kc == 0), stop=(kc == 7)),
                    reads=[t_wiw], writes=[t_bank])
            P.add("dve", lambda e, bank=bank, t=t: e.tensor_scalar(out=iw[:, t, :], in0=bank[:, 0:8], scalar1=IW_SCALE, scalar2=None, op0=ALU.mult),
                  reads=[t_bank], writes=[t_iw[t]])

        cI = {"i": 0}

        def idx_phase(n):
            Sc = Sc2[n % 2]; t_Sc = t_Sc2[n % 2]
            W = 128 * (n + 1)
            nb = (W + 511) // 512
            for h in range(8):
                hp = (h % 2) * 64
                for kb in range(nb):
                    w = min(512, W - kb * 512)
                    bank, t_bank = banks.next()
                    P.add("pe", lambda e, bank=bank, h=h, hp=hp, kb=kb, w=w, n=n: e.matmul(
                        bank[:, 0:w], lhsT=qiT[hp:hp + 64, h // 2, n * 128:(n + 1) * 128], rhs=kiT[hp:hp + 64, kb * 512:kb * 512 + w],
                        start=True, stop=True),
                        reads=[t_qiT[h // 2][n // 4]] + t_kiT[0:nb], writes=[t_bank])
                    s = cI["i"] % 2
                    cI["i"] += 1
                    P.add("act", lambda e, bank=bank, s=s, w=w: e.activation(out=tI[s][:, 0:w], in_=bank[:, 0:w], func=AF.Relu),
                          reads=[t_bank], writes=[t_tI[s]])
                    if h == 0:
                        P.add("dve", lambda e, s=s, kb=kb, w=w, n=n: e.tensor_scalar(
                            out=Sc[:, kb * 512:kb * 512 + w], in0=tI[s][:, 0:w], scalar1=iw[:, n, 0:1], scalar2=None, op0=ALU.mult),
                            reads=[t_tI[s], t_iw[n]], writes=[t_Sc])
                    else:
                        P.add("dve", lambda e, s=s, kb=kb, w=w, n=n, h=h: e.scalar_tensor_tensor(
                            out=Sc[:, kb * 512:kb * 512 + w], in0=tI[s][:, 0:w], scalar=iw[:, n, h:h + 1],
                            in1=Sc[:, kb * 512:kb * 512 + w], op0=ALU.mult, op1=ALU.add),
                            reads=[t_tI[s], t_iw[n], t_Sc], writes=[t_Sc])
            P.add("dve", lambda e, n=n: e.tensor_tensor(out=Sc[:, n * 128:(n + 1) * 128], in0=Sc[:, n * 128:(n + 1) * 128],
                                                        in1=caus[:], op=ALU.add),
                  reads=[t_Sc, t_caus], writes=[t_Sc])

        def bis_ops(n):
            ch = n % 2
            Sc = Sc2[ch]; t_Sc = t_Sc2[ch]; junk = junk2[ch]; t_junk = t_junk2[ch]
            mx8 = mx8_2[ch]; t_mx8 = t_mx8_2[ch]; mn = mn_2[ch]; t_mn = t_mn_2[ch]; rng = rng_2[ch]; t_rng = t_rng_2[ch]
            thr = thr_2[ch]; t_thr = t_thr_2[ch]; S2 = S2_2[ch]; t_S2 = t_S2_2[ch]; cnt = cnt_2[ch]; t_cnt = t_cnt_2[ch]
            ee = ee_2[ch]; t_ee = t_ee_2[ch]
            W = 128 * (n + 1)
            mk = mask[ch]
            t_mk = t_mask[ch]
            if n < 2:
                yield lambda: P.add("dve", lambda e: e.tensor_scalar(out=mk[:, 0:W], in0=Sc[:, 0:W], scalar1=thrneg[:, 0:1], scalar2=None, op0=ALU.is_ge),
                                    reads=[t_Sc, t_thrneg], writes=[t_mk])
                return
            yield lambda: P.add("dve", lambda e: e.max(out=mx8[:], in_=Sc[:, 0:W]), reads=[t_Sc], writes=[t_mx8])
            yield lambda: P.add("dve", lambda e: e.tensor_reduce(out=mn[:], in_=Sc[:, 0:n * 128], axis=mybir.AxisListType.X, op=ALU.min),
                                reads=[t_Sc], writes=[t_mn])
            yield lambda: P.add("dve", lambda e: e.tensor_tensor(out=rng[:], in0=mx8[:, 0:1], in1=mn[:], op=ALU.subtract),
                                reads=[t_mx8, t_mn], writes=[t_rng])
            yield lambda: P.add("dve", lambda e: e.scalar_tensor_tensor(out=thr[:], in0=rng[:], scalar=0.5, in1=mn[:], op0=ALU.mult, op1=ALU.add),
                                reads=[t_rng, t_mn], writes=[t_thr])
            yield lambda: P.add("dve", lambda e: e.tensor_scalar(out=S2[:], in0=pow2[:], scalar1=rng[:, 0:1], scalar2=None, op0=ALU.mult),
                                reads=[t_pow2, t_rng], writes=[t_S2])
            for k in range(NIT):
                yield lambda: P.add("dve", lambda e: e.tensor_scalar(out=junk[:, 0:W], in0=Sc[:, 0:W], scalar1=thr[:, 0:1], scalar2=None,
                                                                     op0=ALU.is_ge, op1=ALU.add, accum_out=cnt[:, 0:1]),
                                    reads=[t_Sc, t_thr], writes=[t_junk, t_cnt])
                yield lambda: P.add("dve", lambda e: e.tensor_scalar(out=ee[:], in0=cnt[:], scalar1=255.5, scalar2=0.5, op0=ALU.is_ge, op1=ALU.subtract),
                                    reads=[t_cnt], writes=[t_ee])
                yield lambda k=k: P.add("dve", lambda e: e.scalar_tensor_tensor(out=thr[:], in0=ee[:], scalar=S2[:, k:k + 1], in1=thr[:],
                                                                                op0=ALU.mult, op1=ALU.add),
                                        reads=[t_ee, t_S2, t_thr], writes=[t_thr])
            yield lambda: P.add("dve", lambda e: e.tensor_scalar(out=mk[:, 0:W], in0=Sc[:, 0:W], scalar1=thr[:, 0:1], scalar2=None, op0=ALU.is_ge),
                                reads=[t_Sc, t_thr], writes=[t_mk])

        def bis_pair(a, b):
            ga, gb = bis_ops(a), bis_ops(b)
            done_a = done_b = False
            while not (done_a and done_b):
                if not done_a:
                    f = next(ga, None)
                    if f is None:
                        done_a = True
                    else:
                        f()
                if not done_b:
                    f = next(gb, None)
                    if f is None:
                        done_b = True
                    else:
                        f()

        def maskT_phase(n):
            mk = mask[n % 2]; t_mk = t_mask[n % 2]
            mp = maskTp[(n // 2) % 2]; t_mp = t_maskTp[(n // 2) % 2][n % 2]
            for m0 in range(0, n + 1, 8):
                k = min(8, n + 1 - m0)
                for i in range(k):
                    m = m0 + i
                    P.add("pe", lambda e, i=i, m=m: e.transpose(out=pM[:, i * 128:(i + 1) * 128], in_=mk[:, m * 128:(m + 1) * 128], identity=idb[:]),
                          reads=[t_mk, t_idb], writes=[t_pM])
                P.add("act", lambda e, m0=m0, k=k, n=n: e.activation(
                    out=mp[:, m0:m0 + k, n % 2, :], in_=pM[:, 0:k * 128].rearrange("p (a b) -> p a b", a=k), func=AF.Copy),
                    reads=[t_pM], writes=[t_mp])

        cA = {"e": 0, "p": 0}

        def att_pair(kp):
            a, b = 2 * kp, 2 * kp + 1
            mp = maskTp[kp % 2]; t_mp = t_maskTp[kp % 2]
            steps = [(c, m0, hh) for c in range(4) for m0 in range(0, b + 1, 2) for hh in range(2)]

            def emit_scores(c, m0, hh):
                h = 2 * c + hh; hp = hh * 64
                bank, t_bank = banks.next()
                for i in range(2):
                    m = m0 + i
                    biases = []
                    if m == a - 1:
                        biases.append((0, 128))
                    if m == a:
                        biases.append((0, 0)); biases.append((128, 128))
                    if m == b:
                        biases.append((128, 0))
                    P.add("pe", lambda e, bank=bank, i=i, m=m, hp=hp, c=c, nb=len(biases): e.matmul(
                        bank[:, i * 256:(i + 1) * 256], lhsT=kT[hp:hp + 64, c, m * 128:(m + 1) * 128],
                        rhs=qT[hp:hp + 64, c, a * 128:(a + 2) * 128], start=True, stop=(nb == 0)),
                        reads=[t_kT[c][m // 4], t_qT[c][a // 4]], writes=[t_bank])
                    for bi, (qo, off) in enumerate(biases):
                        P.add("pe", lambda e, bank=bank, i=i, h=h, qo=qo, off=off, last=(bi == len(biases) - 1): e.matmul(
                            bank[:, i * 256 + qo:i * 256 + qo + 128], lhsT=idb[:], rhs=BT[:, h, off:off + 128], start=False, stop=last),
                            reads=[t_idb, t_BT[h]], writes=[t_bank])
                se = cA["e"] % 2
                cA["e"] += 1
                P.add("act", lambda e, bank=bank, se=se: e.activation(out=E[se][:], in_=bank[:], func=AF.Exp),
                      reads=[t_bank], writes=[t_E[se]])
                sp = cA["p"] % 3
                cA["p"] += 1
                P.add("pool", lambda e, se=se, sp=sp, m0=m0: e.tensor_tensor(
                    out=PT[sp][:], in0=E[se][:], in1=mp[:, m0:m0 + 2, :, :].rearrange("p a b q -> p (a b q)"), op=ALU.mult),
                    reads=[t_E[se]] + t_mp, writes=[t_PT[sp]])
                return sp

            def emit_pv(c, m0, hh, sp):
                h = 2 * c + hh
                for i in range(2):
                    m = m0 + i
                    for t in (a, b):
                        if m > t:
                            continue
                        P.add("pe", lambda e, sp=sp, i=i, m=m, h=h, hh=hh, c=c, t=t, a=a: e.matmul(
                            pO[t % 2][hh][:, c * 65:(c + 1) * 65], lhsT=PT[sp][:, i * 256 + (t - a) * 128:i * 256 + (t - a) * 128 + 128],
                            rhs=vaug[:, m, h, :], start=(m == 0), stop=(m == t)),
                            reads=[t_v[m], t_vones, t_PT[sp]], writes=[t_pO[t % 2][hh]])

            prev = None
            for st_ in steps:
                sp = emit_scores(*st_)
                if prev is not None:
                    emit_pv(*prev)
                prev = st_ + (sp,)
            emit_pv(*prev)
            for t in (a, b):
                r = t % 2
                for hh in range(2):
                    P.add("dve", lambda e, r=r, hh=hh: e.reciprocal(
                        out=rden8[r][:, hh, :], in_=pO[r][hh][:, 0:260].rearrange("p (c d) -> p c d", c=4)[:, :, 64]),
                        reads=[t_pO[r][hh]], writes=[t_rden8[r]])
                for h in range(8):
                    P.add("act", lambda e, r=r, h=h: e.activation(
                        out=o_tm[r][:, h * 64:(h + 1) * 64], in_=pO[r][h % 2][:, (h // 2) * 65:(h // 2) * 65 + 64], func=AF.Copy,
                        scale=rden8[r][:, h % 2, h // 2:h // 2 + 1]),
                        reads=[t_pO[r][h % 2], t_rden8[r]], writes=[t_otm[r]])
                for c4 in range(4):
                    P.add("pe", lambda e, r=r, c4=c4: e.transpose(out=pM[:, c4 * 128:(c4 + 1) * 128], in_=o_tm[r][:, c4 * 128:(c4 + 1) * 128], identity=idb[:]),
                          reads=[t_otm[r], t_idb], writes=[t_pM])
                P.add("dve", lambda e, t=t: e.tensor_copy(out=oT[:, :, t * 128:(t + 1) * 128], in_=pM[:, 0:512].rearrange("p (a b) -> p a b", a=4)),
                      reads=[t_pM], writes=[Tok()])

        idx_phase(0); idx_phase(1); bis_pair(0, 1); maskT_phase(0); maskT_phase(1)
        for k in range(8):
            a, b = 2 * k, 2 * k + 1
            if k + 1 < 8:
                idx_phase(a + 2); idx_phase(b + 2)
                bis_pair(a + 2, b + 2)
            att_pair(k)
            if k + 1 < 8:
                maskT_phase(a + 2); maskT_phase(b + 2)
        env.P.emit("m1")


def ret_stage(nc, semstack, dmaq, x1T, oT, w_in, gng_d, gnb_d, ident_d, cos_d, sin_d, decay_d, kdec_d, qdec_d):
    with contextlib.ExitStack() as st:
        env = Env(nc, semstack, dmaq, "m2", st)
        P = env.P
        idf, t_idf, idb, t_idb = load_ident(env, ident_d)
        banks = Banks(env, 7)
        pM = env.ps("pM", [128, 1024], BF16); t_pM = Tok()
        rqT = env.sb("rqT", [128, 2, L], BF16); t_rqT = [toks(4) for _ in range(2)]
        rkT = env.sb("rkT", [128, 2, L], BF16); t_rkT = [toks(4) for _ in range(2)]
        qdT = env.sb("qdT", [128, 2, L], BF16); t_qdT = toks(16)
        rv = env.sb("rv", [128, 16, 512], BF16); t_rv = toks(16)
        kd = env.sb("kd", [128, 16, 256], BF16); t_kd = toks(16)
        wrv = env.sb("wrv", [128, 8, 512], BF16); t_wrv = toks(4)
        wrg = env.sb("wrg", [128, 8, 512], BF16); t_wrg = toks(4)
        cosT = env.sb("cosT", [128, L], F32); t_cos = Tok()
        sinT = env.sb("sinT", [128, L], F32); t_sin = Tok()
        decT = env.sb("decT", [128, 512], F32); t_dec = Tok()
        kdec = env.sb("kdec", [128, 256], F32); t_kdec = Tok()
        qdec = env.sb("qdec", [128, 2, 128], F32); t_qdec = Tok()
        gng = env.sb("gng", [128, 512], F32); t_gng = Tok()
        gnb = env.sb("gnb", [128, 512], F32); t_gnb = Tok()
        tmp = [env.sb(f"tmp{i}", [128, 512], F32) for i in range(2)]; t_tmp = toks(2)
        tmp2 = [env.sb(f"tmpb{i}", [128, 512], F32) for i in range(2)]; t_tmp2 = toks(2)
        PdT = [env.sb(f"PdT{i}", [128, 512], BF16) for i in range(2)]; t_PdT = toks(2)
        state = env.sb("state", [128, 2, 128], F32); t_state = Tok()
        state_bf = env.sb("state_bf", [128, 2, 128], BF16); t_state_bf = Tok()
        on = [env.sb(f"on{i}", [128, 512], F32) for i in range(2)]; t_on = toks(2)
        sg = [env.sb(f"sg{i}", [128, 512], F32) for i in range(2)]; t_sg = toks(2)
        orb = [env.sb(f"orb{i}", [128, 512], BF16) for i in range(2)]; t_orb = toks(2)
        stats = [env.sb(f"stats{i}", [128, 4, 6], F32) for i in range(2)]; t_stats = toks(2)
        mvv = [env.sb(f"mvv{i}", [128, 4, 2], F32) for i in range(2)]; t_mvv = toks(2)
        sd = [env.sb(f"sd{i}", [128, 4], F32) for i in range(2)]; t_sd = toks(2)
        rstd = [env.sb(f"rstd{i}", [128, 4], F32) for i in range(2)]; t_rstd = toks(2)
        epst = env.sb("epst", [128, 1], F32); t_eps = Tok()
        ws = WStream(env)
        for (dst, src, tk) in ((cosT, cos_d, t_cos), (sinT, sin_d, t_sin), (decT, decay_d, t_dec), (kdec, kdec_d, t_kdec)):
            P.add("sync", lambda e, dst=dst, src=src: e.dma_start(out=dst[:], in_=src), writes=[tk], kind="d")
        P.add("sync", lambda e: e.dma_start(out=qdec[:].rearrange("p a b -> p (a b)"), in_=qdec_d), writes=[t_qdec], kind="d")
        P.add("sync", lambda e: e.dma_start(out=gng[:], in_=bcast_rows(gng_d, 128)), writes=[t_gng], kind="d")
        P.add("sync", lambda e: e.dma_start(out=gnb[:], in_=bcast_rows(gnb_d, 128)), writes=[t_gnb], kind="d")
        P.add("dve", lambda e: e.memset(epst[:], LN_EPS), writes=[t_eps])

        cR = {"i": 0}
        for (c0, dstT, t_dstT, sc) in ((C_RQ, rqT, t_rqT, 1.0), (C_RK, rkT, t_rkT, 0.125)):
            for c in range(2):
                base = c0 + c * 128
                wn, t_wn = ws.load([(0, w_in[:, base:base + 128], 1.0)])
                pieces = []
                for hh in range(2):
                    pieces.append((hh * 64, w_in[:, base + hh * 64 + 32:base + hh * 64 + 64], -1.0))
                    pieces.append((hh * 64 + 32, w_in[:, base + hh * 64:base + hh * 64 + 32], 1.0))
                wr, t_wr = ws.load(pieces)
                for tb in range(4):
                    bq, t_bq = banks.next()
                    br, t_br = banks.next()
                    for (bank, t_bank, wt, t_w) in ((bq, t_bq, wn, t_wn), (br, t_br, wr, t_wr)):
                        for kc in range(8):
                            P.add("pe", lambda e, bank=bank, wt=wt, kc=kc, tb=tb: e.matmul(
                                bank[:], lhsT=wt[:, kc, :], rhs=x1T[:, kc, tb * 512:(tb + 1) * 512], start=(kc == 0), stop=(kc == 7)),
                                reads=[t_w], writes=[t_bank])
                    s = cR["i"] % 2
                    cR["i"] += 1
                    tsl = slice(tb * 512, (tb + 1) * 512)
                    P.add("dve", lambda e, s=s, bq=bq, tsl=tsl: e.tensor_tensor(out=tmp[s][:], in0=bq[:], in1=cosT[:, tsl], op=ALU.mult),
                          reads=[t_bq, t_cos], writes=[t_tmp[s]])
                    P.add("dve", lambda e, s=s, br=br, tsl=tsl, sc=sc: e.scalar_tensor_tensor(
                        out=tmp2[s][:], in0=br[:], scalar=float(sc), in1=sinT[:, tsl], op0=ALU.mult, op1=ALU.mult),
                        reads=[t_br, t_sin], writes=[t_tmp2[s]])
                    P.add("dve", lambda e, s=s, tsl=tsl, sc=sc, dstT=dstT, c=c: e.scalar_tensor_tensor(
                        out=dstT[:, c, tsl], in0=tmp[s][:], scalar=float(sc), in1=tmp2[s][:], op0=ALU.mult, op1=ALU.add),
                        reads=[t_tmp[s], t_tmp2[s]], writes=[t_dstT[c][tb]])
        for c in range(4):
            ws.load([(0, w_in[:, C_RV + c * 128:C_RV + (c + 1) * 128], 1.0)],
                    dst=lambda c0, w, c=c: wrv[:, :, c * 128 + c0:c * 128 + c0 + w], t_dst=t_wrv[c])
        for c in range(4):
            ws.load([(0, w_in[:, C_RG + c * 128:C_RG + (c + 1) * 128], 1.0)],
                    dst=lambda c0, w, c=c: wrg[:, :, c * 128 + c0:c * 128 + c0 + w], t_dst=t_wrg[c])
        for t in range(16):
            tsl = slice(t * 128, (t + 1) * 128)
            bank, t_bank = banks.next()
            for kc in range(8):
                P.add("pe", lambda e, bank=bank, kc=kc, tsl=tsl: e.matmul(
                    bank[:], lhsT=x1T[:, kc, tsl], rhs=wrv[:, kc, :], start=(kc == 0), stop=(kc == 7)),
                    reads=t_wrv, writes=[t_bank])
            P.add("act", lambda e, bank=bank, t=t: e.activation(out=rv[:, t, :], in_=bank[:], func=AF.Copy),
                  reads=[t_bank], writes=[t_rv[t]])
            for c in range(2):
                P.add("pe", lambda e, c=c, tsl=tsl: e.transpose(out=pM[:, c * 128:(c + 1) * 128], in_=rkT[:, c, tsl], identity=idb[:]),
                      reads=[t_rkT[c][t // 4], t_idb], writes=[t_pM])
            P.add("dve", lambda e, t=t: e.tensor_tensor(out=kd[:, t, :], in0=pM[:, 0:256], in1=kdec[:], op=ALU.mult),
                  reads=[t_pM, t_kdec], writes=[t_kd[t]])
            for c in range(2):
                P.add("dve", lambda e, c=c, tsl=tsl: e.tensor_tensor(out=qdT[:, c, tsl], in0=rqT[:, c, tsl], in1=qdec[:, c, :], op=ALU.mult),
                      reads=[t_rqT[c][t // 4], t_qdec], writes=[t_qdT[t]])
        eo_banks = [[(banks.b[2 * i + j], banks.t[2 * i + j]) for j in range(2)] for i in range(2)]
        kg = Banks.__new__(Banks)
        kg.b = [banks.b[i] for i in (4, 5, 6)]; kg.t = [banks.t[i] for i in (4, 5, 6)]; kg.i = 0; kg.n = 3

        def phase_a(n):
            tsl = slice(n * 128, (n + 1) * 128)
            s = n % 2
            eo = eo_banks[n % 2]
            for h in range(4):
                hp = (h % 2) * 64; c = h // 2
                bk, t_bk = eo[h % 2]
                P.add("pe", lambda e, bk=bk, hp=hp, c=c, tsl=tsl: e.matmul(
                    bk[:, c * 128:(c + 1) * 128], lhsT=rkT[hp:hp + 64, c, tsl], rhs=rqT[hp:hp + 64, c, tsl], start=True, stop=True),
                    reads=[t_rkT[c][n // 4], t_rqT[c][n // 4]], writes=[t_bk])
            for par in range(2):
                bk, t_bk = eo[par]
                P.add("dve", lambda e, bk=bk, s=s, par=par: e.tensor_tensor(
                    out=PdT[s][:, par * 256:(par + 1) * 256], in0=bk[:, 0:256], in1=decT[:, par * 256:(par + 1) * 256], op=ALU.mult),
                    reads=[t_bk, t_dec], writes=[t_PdT[s]])
            for h in range(4):
                hp = (h % 2) * 64; c = h // 2
                pos = (h % 2) * 2 + c
                bk, t_bk = eo[h % 2]
                P.add("pe", lambda e, bk=bk, h=h, c=c, pos=pos, s=s, n=n: e.matmul(
                    bk[:, 256 + c * 128:256 + (c + 1) * 128], lhsT=PdT[s][:, pos * 128:(pos + 1) * 128], rhs=rv[:, n, h * 128:(h + 1) * 128],
                    start=True, stop=(n == 0)), reads=[t_PdT[s], t_rv[n]], writes=[t_bk])
                if n > 0:
                    P.add("pe", lambda e, bk=bk, hp=hp, c=c, tsl=tsl: e.matmul(
                        bk[:, 256 + c * 128:256 + (c + 1) * 128], lhsT=qdT[hp:hp + 64, c, tsl], rhs=state_bf[hp:hp + 64, c, :],
                        start=False, stop=True), reads=[t_qdT[n], t_state_bf], writes=[t_bk])
            if n < 15:
                bK, t_bK = kg.next()
                for h in range(4):
                    hp = (h % 2) * 64; c = h // 2
                    P.add("pe", lambda e, bK=bK, h=h, hp=hp, c=c, n=n: e.matmul(
                        bK[hp:hp + 64, c * 128:(c + 1) * 128], lhsT=kd[:, n, h * 64:(h + 1) * 64], rhs=rv[:, n, h * 128:(h + 1) * 128],
                        start=True, stop=True), reads=[t_kd[n], t_rv[n]], writes=[t_bK])
                for h in range(4):
                    hp = (h % 2) * 64; c = h // 2
                    if n == 0:
                        P.add("dve", lambda e, bK=bK, hp=hp, c=c: e.tensor_copy(out=state[hp:hp + 64, c, :], in_=bK[hp:hp + 64, c * 128:(c + 1) * 128]),
                              reads=[t_bK], writes=[t_state])
                    else:
                        cd = float(np.float32(np.exp(np.float32(128.0) * np.log(np.float32(GAMMAS[h])))))
                        P.add("dve", lambda e, bK=bK, hp=hp, c=c, cd=cd: e.scalar_tensor_tensor(
                            out=state[hp:hp + 64, c, :], in0=state[hp:hp + 64, c, :], scalar=cd, in1=bK[hp:hp + 64, c * 128:(c + 1) * 128],
                            op0=ALU.mult, op1=ALU.add), reads=[t_bK, t_state], writes=[t_state])
                P.add("act", lambda e: e.activation(out=state_bf[:], in_=state[:], func=AF.Copy), reads=[t_state], writes=[t_state_bf])
            bG, t_bG = kg.next()
            for kc in range(8):
                P.add("pe", lambda e, bG=bG, kc=kc, tsl=tsl: e.matmul(
                    bG[:], lhsT=x1T[:, kc, tsl], rhs=wrg[:, kc, :], start=(kc == 0), stop=(kc == 7)), reads=t_wrg, writes=[t_bG])
            P.add("act", lambda e, bG=bG, s=s: e.activation(out=sg[s][:], in_=bG[:], func=AF.Silu), reads=[t_bG], writes=[t_sg[s]])

        def phase_b(n):
            tsl = slice(n * 128, (n + 1) * 128)
            s = n % 2
            eo = eo_banks[n % 2]
            osl = lambda h: slice(256 + (h // 2) * 128, 256 + (h // 2 + 1) * 128)
            for h in range(4):
                bk, t_bk = eo[h % 2]
                P.add("dve", lambda e, bk=bk, h=h, s=s: e.bn_stats(out=stats[s][:, h, :], in_=bk[:, osl(h)]),
                      reads=[t_bk], writes=[t_stats[s]])
            for h in range(4):
                P.add("dve", lambda e, h=h, s=s: e.bn_aggr(out=mvv[s][:, h, :], in_=stats[s][:, h, :]), reads=[t_stats[s]], writes=[t_mvv[s]])
            P.add("act", lambda e, s=s: e.activation(out=sd[s][:], in_=mvv[s][:, :, 1], func=AF.Sqrt, bias=epst[:, 0:1], scale=1.0),
                  reads=[t_mvv[s], t_eps], writes=[t_sd[s]])
            P.add("dve", lambda e, s=s: e.reciprocal(out=rstd[s][:], in_=sd[s][:]), reads=[t_sd[s]], writes=[t_rstd[s]])
            for h in range(4):
                bk, t_bk = eo[h % 2]
                P.add("dve", lambda e, bk=bk, h=h, s=s: e.tensor_scalar(
                    out=on[s][:, h * 128:(h + 1) * 128], in0=bk[:, osl(h)], scalar1=mvv[s][:, h, 0:1],
                    scalar2=rstd[s][:, h:h + 1], op0=ALU.subtract, op1=ALU.mult),
                    reads=[t_bk, t_mvv[s], t_rstd[s]], writes=[t_on[s]])
            P.add("pool", lambda e, s=s: e.tensor_tensor(out=on[s][:], in0=on[s][:], in1=gng[:], op=ALU.mult),
                  reads=[t_on[s], t_gng], writes=[t_on[s]])
            P.add("pool", lambda e, s=s: e.tensor_tensor(out=on[s][:], in0=on[s][:], in1=gnb[:], op=ALU.add),
                  reads=[t_on[s], t_gnb], writes=[t_on[s]])
            P.add("dve", lambda e, s=s: e.tensor_tensor(out=orb[s][:], in0=on[s][:], in1=sg[s][:], op=ALU.mult),
                  reads=[t_on[s], t_sg[s]], writes=[t_orb[s]])
            for c4 in range(4):
                P.add("pe", lambda e, c4=c4, s=s: e.transpose(out=pM[:, c4 * 128:(c4 + 1) * 128], in_=orb[s][:, c4 * 128:(c4 + 1) * 128], identity=idb[:]),
                      reads=[t_orb[s], t_idb], writes=[t_pM])
            P.add("act", lambda e, tsl=tsl: e.activation(out=oT[:, :, tsl], in_=pM[:, 0:512].rearrange("p (a b) -> p a b", a=4), func=AF.Copy),
                  reads=[t_pM], writes=[Tok()])

        phase_a(0)
        for n in range(16):
            if n + 1 < 16:
                phase_a(n + 1)
            phase_b(n)
        env.P.emit("m2")


def merge_stage(nc, semstack, dmaq, x1T, oTs, w_in, wbrs, wo_d, g_d, b_d, src_d, dst_d):
    with contextlib.ExitStack() as st:
        env = Env(nc, semstack, dmaq, "m4", st)
        P = env.P
        banks = Banks(env, 8)
        mT = env.sb("mT", [128, 8, L], BF16); t_mT = [toks(4) for _ in range(8)]
        wout = env.sb("wout", [128, 8, D], BF16); t_wout = toks(8)
        sgm = [env.sb(f"sgm{i}", [128, 512], F32) for i in range(2)]; t_sgm = toks(2)
        tmp = [env.sb(f"tmp{i}", [128, 512], F32) for i in range(2)]; t_tmp = toks(2)
        acc = env.sb("acc", [128, L], F32); t_acc = toks(4)
        xs = [env.sb(f"xs{i}", [128, D], F32) for i in range(2)]; t_xs = toks(2)
        rr = [env.sb(f"rr{i}", [128, D], F32) for i in range(2)]; t_rr = toks(2)
        oo = [env.sb(f"oo{i}", [128, D], F32) for i in range(2)]; t_oo = toks(2)
        gam = env.sb("gam", [128, D], F32); t_gam = Tok()
        bet = env.sb("bet", [128, D], F32); t_bet = Tok()
        epst = env.sb("epst", [128, 1], F32); t_eps = Tok()
        stats = [env.sb(f"stats{i}", [128, 2, 6], F32) for i in range(2)]; t_stats = toks(2)
        mv = [env.sb(f"mv{i}", [128, 2], F32) for i in range(2)]; t_mv = toks(2)
        sd = [env.sb(f"sd{i}", [128, 1], F32) for i in range(2)]; t_sd = toks(2)
        rstd = [env.sb(f"rstd{i}", [128, 1], F32) for i in range(2)]; t_rstd = toks(2)
        t_dst = toks(16)
        ws = WStream(env, nst=2, nbf=6)
        P.add("sync", lambda e: e.dma_start(out=gam[:], in_=bcast_rows(g_d, 128)), writes=[t_gam], kind="d")
        P.add("sync", lambda e: e.dma_start(out=bet[:], in_=bcast_rows(b_d, 128)), writes=[t_bet], kind="d")
        P.add("dve", lambda e: e.memset(epst[:], LN_EPS), writes=[t_eps])
        cM = {"s": 0}
        for j in range(8):
            ws.load([(0, wo_d[:, j * 128:(j + 1) * 128], 1.0)], dst=lambda c0, w, j=j: wout[:, :, j * 128 + c0:j * 128 + c0 + w], t_dst=t_wout[j])
            for b in range(3):
                wgb, t_wgb = ws.load([(0, w_in[:, C_G + b * D + j * 128:C_G + b * D + (j + 1) * 128], 1.0)])
                wbb, t_wbb = ws.load([(0, wbrs[b][:, j * 128:(j + 1) * 128], 1.0)], KC=4)
                for tb in range(4):
                    tsl = slice(tb * 512, (tb + 1) * 512)
                    bG, t_bG = banks.next()
                    for kc in range(8):
                        P.add("pe", lambda e, bG=bG, wgb=wgb, kc=kc, tsl=tsl: e.matmul(
                            bG[:], lhsT=wgb[:, kc, :], rhs=x1T[:, kc, tsl], start=(kc == 0), stop=(kc == 7)),
                            reads=[t_wgb], writes=[t_bG])
                    bB, t_bB = banks.next()
                    for kc in range(4):
                        P.add("pe", lambda e, bB=bB, wbb=wbb, kc=kc, tsl=tsl, b=b: e.matmul(
                            bB[:], lhsT=wbb[:, kc, :], rhs=oTs[b][:, kc, tsl], start=(kc == 0), stop=(kc == 3)),
                            reads=[t_wbb], writes=[t_bB])
                    s = cM["s"] % 2
                    cM["s"] += 1
                    P.add("act", lambda e, bG=bG, s=s: e.activation(out=sgm[s][:], in_=bG[:], func=AF.Sigmoid), reads=[t_bG], writes=[t_sgm[s]])
                    if b == 0:
                        P.add("dve", lambda e, bB=bB, s=s, tsl=tsl: e.tensor_tensor(out=acc[:, tsl], in0=sgm[s][:], in1=bB[:], op=ALU.mult),
                              reads=[t_sgm[s], t_bB], writes=[t_acc[tb]])
                    else:
                        P.add("dve", lambda e, bB=bB, s=s: e.tensor_tensor(out=tmp[s][:], in0=sgm[s][:], in1=bB[:], op=ALU.mult),
                              reads=[t_sgm[s], t_bB], writes=[t_tmp[s]])
                        if b == 1:
                            P.add("dve", lambda e, s=s, tsl=tsl: e.tensor_tensor(out=acc[:, tsl], in0=acc[:, tsl], in1=tmp[s][:], op=ALU.add),
                                  reads=[t_acc[tb], t_tmp[s]], writes=[t_acc[tb]])
                        else:
                            P.add("dve", lambda e, s=s, j=j, tsl=tsl: e.tensor_tensor(out=mT[:, j, tsl], in0=acc[:, tsl], in1=tmp[s][:], op=ALU.add),
                                  reads=[t_acc[tb], t_tmp[s]], writes=[t_mT[j][tb]])
        for t in range(16):
            s = t % 2
            tsl = slice(t * 128, (t + 1) * 128)
            P.add("sync", lambda e, s=s, tsl=tsl: e.dma_start(out=xs[s][:], in_=src_d[tsl, :]), writes=[t_xs[s]], kind="d")
            for nh in range(2):
                bank, t_bank = banks.next()
                for kc in range(8):
                    P.add("pe", lambda e, bank=bank, kc=kc, nh=nh, tsl=tsl: e.matmul(
                        bank[:], lhsT=mT[:, kc, tsl], rhs=wout[:, kc, nh * 512:(nh + 1) * 512], start=(kc == 0), stop=(kc == 7)),
                        reads=[t_mT[kc][t // 4]] + t_wout[nh * 4:(nh + 1) * 4], writes=[t_bank])
                P.add("dve", lambda e, bank=bank, nh=nh, s=s: e.scalar_tensor_tensor(
                    out=rr[s][:, nh * 512:(nh + 1) * 512], in0=xs[s][:, nh * 512:(nh + 1) * 512], scalar=ALPHA, in1=bank[:],
                    op0=ALU.mult, op1=ALU.add), reads=[t_xs[s], t_bank], writes=[t_rr[s]])
            for k in range(2):
                P.add("dve", lambda e, s=s, k=k: e.bn_stats(out=stats[s][:, k, :], in_=rr[s][:, k * 512:(k + 1) * 512]),
                      reads=[t_rr[s]], writes=[t_stats[s]])
            P.add("dve", lambda e, s=s: e.bn_aggr(out=mv[s][:], in_=stats[s][:].rearrange("p a b -> p (a b)")),
                  reads=[t_stats[s]], writes=[t_mv[s]])
            P.add("act", lambda e, s=s: e.activation(out=sd[s][:], in_=mv[s][:, 1:2], func=AF.Sqrt, bias=epst[:, 0:1], scale=1.0),
                  reads=[t_mv[s], t_eps], writes=[t_sd[s]])
            P.add("dve", lambda e, s=s: e.reciprocal(out=rstd[s][:], in_=sd[s][:]), reads=[t_sd[s]], writes=[t_rstd[s]])
            P.add("dve", lambda e, s=s: e.tensor_scalar(
                out=rr[s][:], in0=rr[s][:], scalar1=mv[s][:, 0:1], scalar2=rstd[s][:, 0:1], op0=ALU.subtract, op1=ALU.mult),
                reads=[t_rr[s], t_mv[s], t_rstd[s]], writes=[t_rr[s]])
            P.add("pool", lambda e, s=s: e.tensor_tensor(out=oo[s][:], in0=rr[s][:], in1=gam[:], op=ALU.mult),
                  reads=[t_rr[s], t_gam], writes=[t_oo[s]])
            P.add("pool", lambda e, s=s: e.tensor_tensor(out=oo[s][:], in0=oo[s][:], in1=bet[:], op=ALU.add),
                  reads=[t_oo[s], t_bet], writes=[t_oo[s]])
            P.add("pool", lambda e, s=s, tsl=tsl: e.dma_start(out=dst_d[tsl, :], in_=oo[s][:]),
                  reads=[t_oo[s]], writes=[t_dst[t]], kind="d")
        P.wait_all("pool", t_dst)
        env.P.emit("m4")

def build_nc(stages=("ffn1", "mix", "ffn2"), dbg_mix=False, parts=("dsa", "ret", "mem", "merge")):
    nc = bass.Bass("TRN2", target_bir_lowering=False)
    din = lambda n, shape: nc.dram_tensor(n, shape, F32, kind="ExternalInput").ap()
    x_d = din("x", [L, D])
    mem_d = din("mem", [256, D])
    f1wi = din("ffn1_w_in", [D, 2 * DFF]); f1wo = din("ffn1_w_out", [DFF, D])
    ln1g = din("ln1_g", [1, D]); ln1b = din("ln1_b", [1, D])
    w_in = din("w_in", [D, W_IN_COLS]); t5 = din("t5_table", [32, 8])
    gng = din("ret_gn_g", [1, 512]); gnb = din("ret_gn_b", [1, 512])
    wmkv = din("w_mem_kv", [D, 1024])
    wbr = din("w_br_ret", [512, D]); wbd = din("w_br_dsa", [512, D]); wbm = din("w_br_mem", [512, D])
    wo = din("w_out", [D, D]); ln2g = din("ln2_g", [1, D]); ln2b = din("ln2_b", [1, D])
    f2wi = din("ffn2_w_in", [D, 2 * DFF]); f2wo = din("ffn2_w_out", [DFF, D])
    ln3g = din("ln3_g", [1, D]); ln3b = din("ln3_b", [1, D])
    ident = din("c_ident", [128, 128])
    cJ = din("c_J", [128, 128]); coh = din("c_oh", [32, 384]); ccausal = din("c_causal", [128, 128])
    cpow2 = din("c_pow2", [128, NIT])
    ccos = din("c_cos", [128, L]); csin = din("c_sin", [128, L])
    cdecay = din("c_decay", [128, 512]); ckdec = din("c_kdec", [128, 256]); cqdec = din("c_qdec", [128, 256])
    gscr = nc.dram_tensor("g_scr", [8, 384], F32).ap()
    out_d = nc.dram_tensor("out", [L, D], F32, kind="ExternalOutput").ap()
    x1_d = nc.dram_tensor("x1_scr", [L, D], F32).ap()
    x2_d = nc.dram_tensor("x2_scr", [L, D], F32).ap()
    with contextlib.ExitStack() as semstack:
        dmaq = {}
        cur = x_d
        for i, s in enumerate(stages):
            last = i == len(stages) - 1
            if s == "ffn1":
                dst = out_d if last else x1_d
                ffn_stage(nc, semstack, dmaq, "f1", cur, f1wi, f1wo, ln1g, ln1b, dst, ident)
                cur = dst
            elif s == "mix":
                dst = out_d if last else x2_d
                with contextlib.ExitStack() as outer:
                    x1T = outer.enter_context(nc.sbuf_tensor("x1T", [128, 8, L], BF16))
                    mix_load_stage(nc, semstack, dmaq, cur, ident, x1T)
                    o_dsaT = outer.enter_context(nc.sbuf_tensor("o_dsaT", [128, 4, L], BF16))
                    if "dsa" in parts:
                        dsa_stage(nc, semstack, dmaq, x1T, o_dsaT, w_in, t5, ident, cJ, coh, ccausal, cpow2, gscr)
                    o_retT = outer.enter_context(nc.sbuf_tensor("o_retT", [128, 4, L], BF16))
                    if "ret" in parts:
                        ret_stage(nc, semstack, dmaq, x1T, o_retT, w_in, gng, gnb, ident, ccos, csin, cdecay, ckdec, cqdec)
                    o_memT = outer.enter_context(nc.sbuf_tensor("o_memT", [128, 4, L], BF16))
                    if "mem" in parts:
                        mem_stage(nc, semstack, dmaq, x1T, o_memT, mem_d, w_in, wmkv, ident)
                    if dbg_mix:
                        dbg = nc.dram_tensor("dbg", [128, 12, L], BF16, kind="ExternalOutput").ap()
                        with contextlib.ExitStack() as st:
                            env = Env(nc, semstack, dmaq, "dbg", st)
                            tt = toks(3)
                            for i, (o, pn) in enumerate(((o_retT, "ret"), (o_dsaT, "dsa"), (o_memT, "mem"))):
                                if pn not in parts:
                                    continue
                                env.P.add("sync", lambda e, i=i, o=o: e.dma_start(out=dbg[:, 4 * i:4 * i + 4, :], in_=o[:]), writes=[tt[i]], kind="d")
                            env.P.wait_all("sync", tt)
                            env.P.emit("dbg")
                    if "merge" in parts:
                        merge_stage(nc, semstack, dmaq, x1T, [o_retT, o_dsaT, o_memT], w_in, [wbr, wbd, wbm], wo, ln2g, ln2b, cur, dst)
                cur = dst
            elif s == "mixdbg":
                with contextlib.ExitStack() as outer:
                    x1T = outer.enter_context(nc.sbuf_tensor("x1T", [128, 8, L], BF16))
                    mix_load_stage(nc, semstack, dmaq, cur, ident, x1T)
                    o_dsaT = outer.enter_context(nc.sbuf_tensor("o_dsaT", [128, 4, L], BF16))
                    dsa_stage(nc, semstack, dmaq, x1T, o_dsaT, w_in, t5, ident, cJ, coh, ccausal, cpow2, gscr)
                    o_memT = outer.enter_context(nc.sbuf_tensor("o_memT", [128, 4, L], BF16))
                    mem_stage(nc, semstack, dmaq, x1T, o_memT, mem_d, w_in, wmkv, ident)
                    dbg = nc.dram_tensor("dbg", [128, 8, L], BF16, kind="ExternalOutput").ap()
                    with contextlib.ExitStack() as st:
                        env = Env(nc, semstack, dmaq, "dbg", st)
                        t1 = Tok(); t2 = Tok()
                        env.P.add("sync", lambda e: e.dma_start(out=dbg[:, 0:4, :], in_=o_dsaT[:]), writes=[t1], kind="d")
                        env.P.add("sync", lambda e: e.dma_start(out=dbg[:, 4:8, :], in_=o_memT[:]), writes=[t2], kind="d")
                        env.P.wait_all("sync", [t1, t2])
                        env.P.emit("dbg")
            elif s == "ffn2":
                dst = out_d if last else x2_d
                ffn_stage(nc, semstack, dmaq, "f2", cur, f2wi, f2wo, ln3g, ln3b, dst, ident)
                cur = dst
    return nc


_CACHE = {}


def _t5_bucket(n):
    n = np.maximum(n, 0)
    nf = np.maximum(n, 1).astype(np.float32)
    large = 16 + (np.log(nf / np.float32(16)) / np.float32(np.log(128 / 16)) * np.float32(16)).astype(np.int32)
    large = np.minimum(large, 31)
    return np.where(n < 16, n, large)


def make_consts():
    c = {}
    c["c_J"] = np.ascontiguousarray(np.eye(128, dtype=np.float32)[::-1])
    oh = np.zeros((32, 384), np.float32)
    for u in range(383):
        d = u - 127
        if d >= 0:
            oh[_t5_bucket(np.array(d)), u] += 1.0
            oh[31, u] -= 1.0
    c["c_oh"] = oh
    q = np.arange(128)[:, None]; sk = np.arange(128)[None, :]
    c["c_causal"] = np.where(sk <= q, 0.0, -1.0e30).astype(np.float32)
    half = 32
    freqs = (np.float32(10000.0) ** (-np.arange(half, dtype=np.float32) / np.float32(half))).astype(np.float32)
    ang = np.arange(L, dtype=np.float32)[None, :] * freqs[np.arange(128) % 32][:, None]
    c["c_cos"] = np.cos(ang).astype(np.float32)
    c["c_sin"] = np.sin(ang).astype(np.float32)
    lg = np.log(np.array(GAMMAS, dtype=np.float32))
    i = np.arange(128)
    dec = np.zeros((128, 4, 128), np.float32)
    for h in range(4):
        diff = i[None, :] - i[:, None]
        dec[:, h, :] = np.where(diff >= 0, np.exp(np.maximum(diff, 0).astype(np.float32) * lg[h]), 0.0)
    c["c_decay"] = np.ascontiguousarray(dec[:, [0, 2, 1, 3], :]).reshape(128, 512)
    kdec = np.zeros((128, 4, 64), np.float32)
    for h in range(4):
        kdec[:, h, :] = np.exp((127 - i).astype(np.float32) * lg[h])[:, None]
    c["c_kdec"] = kdec.reshape(128, 256)
    qdec = np.zeros((128, 2, 128), np.float32)
    for p in range(128):
        for cc in range(2):
            qdec[p, cc, :] = np.exp((i + 1).astype(np.float32) * lg[2 * cc + p // 64])
    c["c_qdec"] = qdec.reshape(128, 256)
    c["c_pow2"] = np.tile((2.0 ** -(np.arange(NIT) + 1.0)).astype(np.float32)[None, :], (128, 1))
    return c


def make_in_maps(inputs):
    f = lambda a: np.ascontiguousarray(np.asarray(a, dtype=np.float32))
    shared = {
        "ffn1_w_in": f(inputs["ffn1_w_in"][0]), "ffn1_w_out": f(inputs["ffn1_w_out"][0]),
        "ln1_g": f(inputs["ln1_g"]), "ln1_b": f(inputs["ln1_b"]),
        "w_in": f(inputs["w_in"][0]), "t5_table": f(inputs["t5_table"]),
        "ret_gn_g": f(inputs["ret_gn_g"]), "ret_gn_b": f(inputs["ret_gn_b"]),
        "w_mem_kv": f(inputs["w_mem_kv"][0]),
        "w_br_ret": f(inputs["w_br_ret"][0]), "w_br_dsa": f(inputs["w_br_dsa"][0]), "w_br_mem": f(inputs["w_br_mem"][0]),
        "w_out": f(inputs["w_out"][0]), "ln2_g": f(inputs["ln2_g"]), "ln2_b": f(inputs["ln2_b"]),
        "ffn2_w_in": f(inputs["ffn2_w_in"][0]), "ffn2_w_out": f(inputs["ffn2_w_out"][0]),
        "ln3_g": f(inputs["ln3_g"]), "ln3_b": f(inputs["ln3_b"]),
        "c_ident": np.eye(128, dtype=np.float32),
    }
    shared.update(make_consts())
    x = f(inputs["x"])
    mem = f(inputs["mem"])
    return [dict(shared, x=x[b], mem=mem[b]) for b in range(x.shape[0])]


def kernel(**inputs):
    if "nc" not in _CACHE:
        _CACHE["nc"] = build_nc()
    nc = _CACHE["nc"]
    in_maps = make_in_maps(inputs)
    res = run_bass_kernel_spmd(nc, in_maps, core_ids=list(range(len(in_maps))))
    return np.stack([np.asarray(r["out"], dtype=np.float32) for r in res.results], axis=0)
```

```python
import contextlib
import numpy as np
import concourse.bass as bass
import concourse.mybir as mybir
from concourse.bass_utils import run_bass_kernel_spmd

F32 = mybir.dt.float32
BF16 = mybir.dt.bfloat16
AF = mybir.ActivationFunctionType
ALU = mybir.AluOpType

L = 2048
D = 1024
DFF = 2816
NCH = DFF // 128
ALPHA = 2.0 ** 0.25
LN_EPS = 1e-5
W_IN_COLS = 7240

ENGS = ("pe", "dve", "act", "pool", "sync")
SEM_ROT = {"c": 8000}
DMA_RING = 8
RELAX_SAME_ENGINE = True


class Tok:
    __slots__ = ("w", "r", "name")

    def __init__(self, name=""):
        self.w = None
        self.r = {}
        self.name = name


class Ins:
    __slots__ = ("eng", "kind", "fn", "deps", "idx", "inc", "sem", "val")

    def __init__(self, eng, kind, fn):
        self.eng = eng
        self.kind = kind
        self.fn = fn
        self.deps = []
        self.inc = False
        self.sem = None
        self.val = 0


class Prog:
    def __init__(self, nc, semstack, dmaq):
        self.nc = nc
        self.semstack = semstack
        self.dmaq = dmaq
        self.streams = {e: [] for e in ENGS}
        self.n = 0

    def add(self, eng, fn, reads=(), writes=(), kind="c"):
        ins = Ins(eng, kind, fn)
        st = self.streams[eng]
        ins.idx = len(st)
        deps = {}

        def dep(d, typ):
            if d is None or d is ins:
                return
            if d.eng == eng and d.kind == "c" and kind == "c":
                if eng == "pe":
                    return
                if typ != "RAW":
                    return
                if RELAX_SAME_ENGINE and eng in ("dve", "act") and ins.idx - d.idx >= 2:
                    return
            deps[id(d)] = d

        for t in reads:
            dep(t.w, "RAW")
        for t in writes:
            dep(t.w, "WAW")
            for d in t.r.values():
                dep(d, "WAR")
        if kind == "d":
            q = self.dmaq.setdefault(eng, {"n": 0, "last": {}, "sems": []})
            n = q["n"]
            q["n"] += 1
            r = n % DMA_RING
            if len(q["sems"]) <= r:
                q["sems"].append(self.semstack.enter_context(self.nc.semaphore(f"s_dma_{eng}_{r}")))
            ins.sem = q["sems"][r]
            ins.val = 16 * (n // DMA_RING + 1)
            ins.inc = True
            prev = q["last"].get(r)
            if prev is not None:
                deps[id(prev)] = prev
            q["last"][r] = ins
        ins.deps = list(deps.values())
        for d in ins.deps:
            d.inc = True
        for t in reads:
            t.r[eng + kind] = ins
        for t in writes:
            t.w = ins
            t.r = {}
        st.append(ins)
        self.n += 1
        return ins

    def wait_all(self, eng, toks):
        return self.add(eng, None, reads=toks, kind="w")

    def emit(self, name):
        nc = self.nc
        nsem = 0
        for e in ENGS:
            for kind in ("c",):
                cnt = 0
                cur = None
                for ins in self.streams[e]:
                    if ins.kind != kind or not ins.inc:
                        continue
                    if cur is None or cnt >= SEM_ROT[kind]:
                        cur = self.semstack.enter_context(nc.semaphore(f"s_{name}_{e}_{kind}_{nsem}"))
                        nsem += 1
                        cnt = 0
                    cnt += 1
                    ins.sem = cur
                    ins.val = cnt * (16 if kind == "d" else 1)
        with nc.Block() as block:
            engmap = {"pe": block.tensor, "dve": block.vector, "act": block.scalar,
                      "pool": block.gpsimd, "sync": block.sync}
            for e in ENGS:
                stream = self.streams[e]
                if not stream:
                    continue

                def body(eobj, stream=stream):
                    waited = {}
                    for ins in stream:
                        for d in ins.deps:
                            k = id(d.sem)
                            if waited.get(k, 0) >= d.val:
                                continue
                            eobj.wait_ge(d.sem, d.val)
                            waited[k] = d.val
                        if ins.fn is None:
                            continue
                        r = ins.fn(eobj)
                        if ins.inc:
                            r.then_inc(ins.sem, 16 if ins.kind == "d" else 1)

                engmap[e](body)


def bcast_rows(ap2d, nrows):
    return bass.AP(ap2d.tensor, ap2d.offset, [[0, nrows], [1, ap2d.shape[-1]]])


def ffn_stage(nc, semstack, dmaq, name, src_d, w_in_d, w_out_d, g_d, b_d, dst_d, ident_d):
    with contextlib.ExitStack() as st:
        P = Prog(nc, semstack, dmaq)
        sb = lambda n, shape, dt: st.enter_context(nc.sbuf_tensor(f"{name}_{n}", shape, dt))
        ps = lambda n, shape, dt: st.enter_context(nc.psum_tensor(f"{name}_{n}", shape, dt))
        HT = 1024
        xT = [sb(f"xT{i}", [128, 8, HT], BF16) for i in range(2)]
        gT = sb("gT", [128, NCH, HT], BF16)
        wout = sb("wout", [128, NCH, D], BF16)
        wst = [sb(f"wst{i}", [128, 2, 8, 128], F32) for i in range(2)]
        wbf = [sb(f"wbf{i}", [128, 2, 8, 128], BF16) for i in range(3)]
        wost = [sb(f"wost{i}", [128, D], F32) for i in range(2)]
        xs = [sb(f"xs{i}", [128, D], F32) for i in range(2)]
        xb = [sb(f"xb{i}", [128, D], BF16) for i in range(2)]
        sA = [sb(f"sA{i}", [128, 512], F32) for i in range(2)]
        rr = [sb(f"rr{i}", [128, D], F32) for i in range(2)]
        oo = [sb(f"oo{i}", [128, D], F32) for i in range(2)]
        gam = sb("gam", [128, D], F32)
        bet = sb("bet", [128, D], F32)
        idf = sb("idf", [128, 128], F32)
        idb = sb("idb", [128, 128], BF16)
        epst = sb("epst", [128, 1], F32)
        stats = [sb(f"stats{i}", [128, 2, 6], F32) for i in range(2)]
        mv = [sb(f"mv{i}", [128, 2], F32) for i in range(2)]
        sd = [sb(f"sd{i}", [128, 1], F32) for i in range(2)]
        rstd = [sb(f"rstd{i}", [128, 1], F32) for i in range(2)]
        pT = [ps(f"pT{i}", [128, 1024], BF16) for i in range(2)]
        pB = [ps(f"pB{i}", [128, 512], F32) for i in range(6)]

        def toks(n, k):
            return [Tok(f"{n}{i}") for i in range(k)]

        t_xT = [toks("xT", 8) for _ in range(2)]
        t_gT = [[Tok() for _ in range(2)] for _ in range(NCH)]
        t_wout = toks("wout", NCH)
        t_wst = toks("wst", 2); t_wbf = toks("wbf", 3); t_wost = toks("wost", 2)
        t_xs = toks("xs", 2); t_xb = toks("xb", 2); t_sA = toks("sA", 2)
        t_ys = toks("ys", 2); t_rr = toks("rr", 2); t_oo = toks("oo", 2)
        t_gam = Tok(); t_bet = Tok(); t_idf = Tok(); t_idb = Tok(); t_eps = Tok()
        t_stats = toks("st", 2); t_mv = toks("mv", 2); t_sd = toks("sd", 2); t_rstd = toks("rs", 2)
        t_pT = toks("pT", 2); t_pB = toks("pB", 6)
        t_dst = toks("dst", 16)

        P.add("sync", lambda e: e.dma_start(out=idf[:], in_=ident_d), writes=[t_idf], kind="d")
        P.add("sync", lambda e: e.dma_start(out=gam[:], in_=bcast_rows(g_d, 128)), writes=[t_gam], kind="d")
        P.add("sync", lambda e: e.dma_start(out=bet[:], in_=bcast_rows(b_d, 128)), writes=[t_bet], kind="d")
        P.add("dve", lambda e: e.tensor_copy(out=idb[:], in_=idf[:]), reads=[t_idf], writes=[t_idb])
        P.add("dve", lambda e: e.memset(epst[:], LN_EPS), writes=[t_eps])

        cnt = {"x": 0, "pb": 0, "sa": 0, "c": 0}

        def stage_a(h):
            for tl in range(8):
                t = h * 8 + tl
                s = cnt["x"] % 2
                cnt["x"] += 1
                P.add("sync", lambda e, s=s, t=t: e.dma_start(out=xs[s][:], in_=src_d[t * 128:(t + 1) * 128, :]),
                      writes=[t_xs[s]], kind="d")
                P.add("dve", lambda e, s=s: e.tensor_copy(out=xb[s][:], in_=xs[s][:]), reads=[t_xs[s]], writes=[t_xb[s]])
                for kc in range(8):
                    P.add("pe", lambda e, s=s, kc=kc: e.transpose(out=pT[s][:, kc * 128:(kc + 1) * 128],
                                                                  in_=xb[s][:, kc * 128:(kc + 1) * 128], identity=idb[:]),
                          reads=[t_xb[s], t_idb], writes=[t_pT[s]])
                P.add("act", lambda e, s=s, h=h, tl=tl: e.activation(
                    out=xT[h][:, :, tl * 128:(tl + 1) * 128],
                    in_=pT[s][:].rearrange("p (a b) -> p a b", a=8), func=AF.Copy),
                    reads=[t_pT[s]], writes=[t_xT[h][tl]])

        wcount = {"c": 0}

        def load_w(c, with_out):
            s = wcount["c"] % 2
            bs = wcount["c"] % 3
            wcount["c"] += 1
            for j in range(2):
                col0 = j * DFF + c * 128
                P.add("sync", lambda e, s=s, j=j, col0=col0: e.dma_start(
                    out=wst[s][:, j], in_=w_in_d[:, col0:col0 + 128].rearrange("(kc p) n -> p kc n", p=128)),
                    writes=[t_wst[s]], kind="d")
            P.add("act", lambda e, s=s, bs=bs: e.activation(out=wbf[bs][:].rearrange("p a b c -> p (a b c)"),
                                                            in_=wst[s][:].rearrange("p a b c -> p (a b c)"), func=AF.Copy),
                  reads=[t_wst[s]], writes=[t_wbf[bs]])
            if with_out:
                P.add("sync", lambda e, s=s, c=c: e.dma_start(out=wost[s][:], in_=w_out_d[c * 128:(c + 1) * 128, :]),
                      writes=[t_wost[s]], kind="d")
                P.add("pool", lambda e, s=s, c=c: e.tensor_copy(out=wout[:, c, :], in_=wost[s][:]),
                      reads=[t_wost[s]], writes=[t_wout[c]])
            return bs

        def stage_b(h):
            pending = load_w(0, h == 0)
            for c in range(NCH):
                bs = pending
                if c + 1 < NCH:
                    pending = load_w(c + 1, h == 0)
                for tb in range(2):
                    pa = (cnt["pb"] % 2) * 2
                    cnt["pb"] += 1
                    for j in range(2):
                        for kc in range(8):
                            P.add("pe", lambda e, pa=pa, j=j, kc=kc, bs=bs, tb=tb, h=h: e.matmul(
                                pB[pa + j][:], lhsT=wbf[bs][:, j, kc, :], rhs=xT[h][:, kc, tb * 512:(tb + 1) * 512],
                                start=(kc == 0), stop=(kc == 7)),
                                reads=[t_wbf[bs]] + t_xT[h][tb * 4:(tb + 1) * 4], writes=[t_pB[pa + j]])
                    s = cnt["sa"] % 2
                    cnt["sa"] += 1
                    P.add("act", lambda e, s=s, pa=pa: e.activation(out=sA[s][:], in_=pB[pa][:], func=AF.Silu),
                          reads=[t_pB[pa]], writes=[t_sA[s]])
                    P.add("dve", lambda e, s=s, pa=pa, c=c, tb=tb: e.scalar_tensor_tensor(
                        out=gT[:, c, tb * 512:(tb + 1) * 512], in0=sA[s][:], scalar=0.5, in1=pB[pa + 1][:],
                        op0=ALU.mult, op1=ALU.mult),
                        reads=[t_sA[s], t_pB[pa + 1]], writes=[t_gT[c][tb]])

        def stage_c(h):
            for tl in range(8):
                t = h * 8 + tl
                s = cnt["c"] % 2
                cnt["c"] += 1
                pa = (cnt["pb"] % 2) * 2
                cnt["pb"] += 1
                sx = cnt["x"] % 2
                cnt["x"] += 1
                P.add("sync", lambda e, sx=sx, t=t: e.dma_start(out=xs[sx][:], in_=src_d[t * 128:(t + 1) * 128, :]),
                      writes=[t_xs[sx]], kind="d")
                for nh in range(2):
                    for kc in range(NCH):
                        P.add("pe", lambda e, pa=pa, nh=nh, kc=kc, tl=tl: e.matmul(
                            pB[pa + nh][:], lhsT=gT[:, kc, tl * 128:(tl + 1) * 128], rhs=wout[:, kc, nh * 512:(nh + 1) * 512],
                            start=(kc == 0), stop=(kc == NCH - 1)),
                            reads=[t_gT[kc][tl // 4], t_wout[kc]], writes=[t_pB[pa + nh]])
                    P.add("dve", lambda e, pa=pa, nh=nh, s=s, sx=sx: e.scalar_tensor_tensor(
                        out=rr[s][:, nh * 512:(nh + 1) * 512], in0=xs[sx][:, nh * 512:(nh + 1) * 512], scalar=ALPHA,
                        in1=pB[pa + nh][:], op0=ALU.mult, op1=ALU.add),
                        reads=[t_xs[sx], t_pB[pa + nh]], writes=[t_rr[s]])
                for k in range(2):
                    P.add("dve", lambda e, s=s, k=k: e.bn_stats(out=stats[s][:, k, :], in_=rr[s][:, k * 512:(k + 1) * 512]),
                          reads=[t_rr[s]], writes=[t_stats[s]])
                P.add("dve", lambda e, s=s: e.bn_aggr(out=mv[s][:], in_=stats[s][:].rearrange("p a b -> p (a b)")),
                      reads=[t_stats[s]], writes=[t_mv[s]])
                P.add("act", lambda e, s=s: e.activation(out=sd[s][:], in_=mv[s][:, 1:2], func=AF.Sqrt, bias=epst[:, 0:1], scale=1.0),
                      reads=[t_mv[s], t_eps], writes=[t_sd[s]])
                P.add("dve", lambda e, s=s: e.reciprocal(out=rstd[s][:], in_=sd[s][:]), reads=[t_sd[s]], writes=[t_rstd[s]])
                P.add("dve", lambda e, s=s: e.tensor_scalar(
                    out=rr[s][:], in0=rr[s][:], scalar1=mv[s][:, 0:1], scalar2=rstd[s][:, 0:1], op0=ALU.subtract, op1=ALU.mult),
                    reads=[t_rr[s], t_mv[s], t_rstd[s]], writes=[t_rr[s]])
                P.add("pool", lambda e, s=s: e.tensor_tensor(out=oo[s][:], in0=rr[s][:], in1=gam[:], op=ALU.mult),
                      reads=[t_rr[s], t_gam], writes=[t_oo[s]])
                P.add("pool", lambda e, s=s: e.tensor_tensor(out=oo[s][:], in0=oo[s][:], in1=bet[:], op=ALU.add),
                      reads=[t_oo[s], t_bet], writes=[t_oo[s]])
                P.add("pool", lambda e, s=s, t=t: e.dma_start(out=dst_d[t * 128:(t + 1) * 128, :], in_=oo[s][:]),
                      reads=[t_oo[s]], writes=[t_dst[t]], kind="d")

        stage_a(0)
        stage_b(0)
        stage_a(1)
        stage_c(0)
        stage_b(1)
        stage_c(1)
        P.wait_all("pool", t_dst)
        P.emit(name)


C_RQ, C_RK, C_RV, C_RG = 0, 256, 512, 1024
C_DQ, C_DK, C_DV, C_IQ, C_IK, C_IW, C_MQ, C_G = 1536, 2048, 2560, 3072, 3584, 3648, 3656, 4168
IW_SCALE = float(8 ** -0.5 * 64 ** -0.5)
NIT = 16
GAMMAS = [1.0 - 2.0 ** (-5.0 - h) for h in range(4)]


def toks(k):
    return [Tok() for _ in range(k)]


class Env:
    def __init__(self, nc, semstack, dmaq, name, st):
        self.nc = nc
        self.P = Prog(nc, semstack, dmaq)
        self.name = name
        self.st = st
        self.k = 0

    def sb(self, n, shape, dt):
        return self.st.enter_context(self.nc.sbuf_tensor(f"{self.name}_{n}", shape, dt))

    def ps(self, n, shape, dt):
        return self.st.enter_context(self.nc.psum_tensor(f"{self.name}_{n}", shape, dt))


class WStream:
    def __init__(self, env, nst=2, nbf=3):
        self.env = env
        self.st = [env.sb(f"wsst{i}", [128, 8, 128], F32) for i in range(nst)]
        self.bf = [env.sb(f"wsbf{i}", [128, 8, 128], BF16) for i in range(nbf)]
        self.t_st = toks(nst)
        self.t_bf = toks(nbf)
        self.n = 0

    def load(self, pieces, KC=8, dst=None, t_dst=None):
        P = self.env.P
        s = self.n % len(self.st)
        b = self.n % len(self.bf)
        self.n += 1
        stt = self.st[s]
        for (c0, src, sc) in pieces:
            w = src.shape[-1]
            P.add("sync", lambda e, stt=stt, c0=c0, src=src, w=w, KC=KC: e.dma_start(
                out=stt[:, 0:KC, c0:c0 + w], in_=src.rearrange("(kc p) n -> p kc n", p=128)),
                writes=[self.t_st[s]], kind="d")
        if dst is None:
            tot = max(c0 + src.shape[-1] for (c0, src, sc) in pieces)
            out_t = self.bf[b]
            out_fn = lambda c0, w: out_t[:, 0:KC, c0:c0 + w]
            t_out = self.t_bf[b]
        else:
            out_fn = dst
            t_out = t_dst
        if all(sc == 1.0 for (_, _, sc) in pieces):
            lo = min(c0 for (c0, _, _) in pieces)
            hi = max(c0 + src.shape[-1] for (c0, src, _) in pieces)
            P.add("pool", lambda e, lo=lo, hi=hi, stt=stt, KC=KC: e.tensor_copy(out=out_fn(lo, hi - lo), in_=stt[:, 0:KC, lo:hi]),
                  reads=[self.t_st[s]], writes=[t_out])
        else:
            for (c0, src, sc) in pieces:
                w = src.shape[-1]
                P.add("dve", lambda e, c0=c0, w=w, sc=sc, stt=stt, KC=KC: e.tensor_scalar(
                    out=out_fn(c0, w), in0=stt[:, 0:KC, c0:c0 + w], scalar1=float(sc), scalar2=None, op0=ALU.mult),
                    reads=[self.t_st[s]], writes=[t_out])
        return (self.bf[b] if dst is None else None), t_out


def load_transposed(env, src_d, ntiles, dstT, t_dstT, idb, t_idb, pT, t_pT):
    P = env.P
    xs = [env.sb(f"ltxs{i}", [128, D], F32) for i in range(2)]
    xb = [env.sb(f"ltxb{i}", [128, D], BF16) for i in range(2)]
    t_xs = toks(2); t_xb = toks(2)
    for t in range(ntiles):
        s = t % 2
        P.add("sync", lambda e, s=s, t=t: e.dma_start(out=xs[s][:], in_=src_d[t * 128:(t + 1) * 128, :]),
              writes=[t_xs[s]], kind="d")
        P.add("dve", lambda e, s=s: e.tensor_copy(out=xb[s][:], in_=xs[s][:]), reads=[t_xs[s]], writes=[t_xb[s]])
        for kc in range(8):
            P.add("pe", lambda e, s=s, kc=kc: e.transpose(out=pT[s][:, kc * 128:(kc + 1) * 128],
                                                          in_=xb[s][:, kc * 128:(kc + 1) * 128], identity=idb[:]),
                  reads=[t_xb[s], t_idb], writes=[t_pT[s]])
        P.add("act", lambda e, s=s, t=t: e.activation(
            out=dstT[:, :, t * 128:(t + 1) * 128], in_=pT[s][:].rearrange("p (a b) -> p a b", a=8), func=AF.Copy),
            reads=[t_pT[s]], writes=[t_dstT[t]])


def load_ident(env, ident_d):
    P = env.P
    idf = env.sb("idf", [128, 128], F32)
    idb = env.sb("idb", [128, 128], BF16)
    t_idf = Tok(); t_idb = Tok()
    P.add("sync", lambda e: e.dma_start(out=idf[:], in_=ident_d), writes=[t_idf], kind="d")
    P.add("dve", lambda e: e.tensor_copy(out=idb[:], in_=idf[:]), reads=[t_idf], writes=[t_idb])
    return idf, t_idf, idb, t_idb


class Banks:
    def __init__(self, env, n):
        self.b = [env.ps(f"bk{i}", [128, 512], F32) for i in range(n)]
        self.t = toks(n)
        self.i = 0
        self.n = n

    def next(self):
        i = self.i % self.n
        self.i += 1
        return self.b[i], self.t[i]


def proj_fm(env, ws, banks, pieces, ncols, rhs_fn, rhs_toks_fn, nblk, blkw, evac, KC=8):
    P = env.P
    wt, t_w = ws.load(pieces, KC=KC)
    for tb in range(nblk):
        bank, t_bank = banks.next()
        for kc in range(KC):
            P.add("pe", lambda e, bank=bank, kc=kc, tb=tb, wt=wt: e.matmul(
                bank[0:ncols, 0:blkw], lhsT=wt[:, kc, 0:ncols], rhs=rhs_fn(kc, tb), start=(kc == 0), stop=(kc == KC - 1)),
                reads=[t_w] + rhs_toks_fn(tb), writes=[t_bank])
        evac(tb, bank, t_bank)


def mix_load_stage(nc, semstack, dmaq, x1_d, ident_d, x1T):
    with contextlib.ExitStack() as st:
        env = Env(nc, semstack, dmaq, "m0", st)
        idf, t_idf, idb, t_idb = load_ident(env, ident_d)
        pT = [env.ps(f"pT{i}", [128, 1024], BF16) for i in range(2)]
        t_pT = toks(2)
        t_x1T = toks(16)
        load_transposed(env, x1_d, 16, x1T, t_x1T, idb, t_idb, pT, t_pT)
        env.P.emit("m0")


def mem_stage(nc, semstack, dmaq, x1T, oT, mem_d, w_in, wmkv, ident_d):
    with contextlib.ExitStack() as st:
        env = Env(nc, semstack, dmaq, "m3", st)
        P = env.P
        idf, t_idf, idb, t_idb = load_ident(env, ident_d)
        pT = [env.ps(f"pT{i}", [128, 1024], BF16) for i in range(2)]
        t_pT = toks(2)
        banks = Banks(env, 4)
        pN = env.ps("pN", [128, 512], F32); t_pN = Tok()
        pD = env.ps("pD", [128, 512], F32); t_pD = Tok()
        memT = env.sb("memT", [128, 8, 256], BF16); t_memT = toks(2)
        mkT = env.sb("mkT", [128, 4, 256], BF16); t_mkT = toks(4)
        mvv = env.sb("mv", [128, 2, 512], BF16); t_mv = toks(2)
        wmv = env.sb("wmv", [128, 8, 512], BF16); t_wmv = toks(4)
        mqT = env.sb("mqT", [128, 4, L], BF16); t_mqT = [toks(4) for _ in range(4)]
        ones = env.sb("ones", [128, 128], BF16); t_ones = Tok()
        E = [env.sb(f"E{i}", [128, 512], BF16) for i in range(4)]; t_E = toks(4)
        rden = [env.sb(f"rden{i}", [128, 512], F32) for i in range(2)]; t_rden = toks(2)
        ws = WStream(env)
        P.add("pool", lambda e: e.memset(ones[:], 1.0), writes=[t_ones])
        load_transposed(env, mem_d, 2, memT, t_memT, idb, t_idb, pT, t_pT)
        t_x1T = []
        for h in range(4):
            def evac(tb, bank, t_bank, h=h):
                P.add("act", lambda e: e.activation(out=mkT[:, h, :], in_=bank[:, 0:256], func=AF.Copy),
                      reads=[t_bank], writes=[t_mkT[h]])
            proj_fm(env, ws, banks, [(0, wmkv[:, h * 128:(h + 1) * 128], 1.0)], 128,
                    lambda kc, tb: memT[:, kc, :], lambda tb: t_memT, 1, 256, evac)
        for c in range(4):
            ws.load([(0, wmkv[:, 512 + c * 128:512 + (c + 1) * 128], 1.0)],
                    dst=lambda c0, w, c=c: wmv[:, :, c * 128 + c0:c * 128 + c0 + w], t_dst=t_wmv[c])
        for mt in range(2):
            bank, t_bank = banks.next()
            for kc in range(8):
                P.add("pe", lambda e, bank=bank, kc=kc, mt=mt: e.matmul(
                    bank[:], lhsT=memT[:, kc, mt * 128:(mt + 1) * 128], rhs=wmv[:, kc, :], start=(kc == 0), stop=(kc == 7)),
                    reads=[t_memT[mt]] + t_wmv, writes=[t_bank])
            P.add("act", lambda e, bank=bank, mt=mt: e.activation(out=mvv[:, mt, :], in_=bank[:], func=AF.Copy),
                  reads=[t_bank], writes=[t_mv[mt]])
        for h in range(4):
            def evac(tb, bank, t_bank, h=h):
                P.add("act", lambda e: e.activation(out=mqT[:, h, tb * 512:(tb + 1) * 512], in_=bank[:], func=AF.Copy,
                                                    scale=float(128 ** -0.5)),
                      reads=[t_bank], writes=[t_mqT[h][tb]])
            proj_fm(env, ws, banks, [(0, w_in[:, C_MQ + h * 128:C_MQ + (h + 1) * 128], 1.0)], 128,
                    lambda kc, tb: x1T[:, kc, tb * 512:(tb + 1) * 512], lambda tb: [], 4, 512, evac)
        it = 0
        for h in range(4):
            for qb in range(4):
                for mt in range(2):
                    bank, t_bank = banks.next()
                    ei = (it * 2 + mt) % 4
                    P.add("pe", lambda e, bank=bank, h=h, qb=qb, mt=mt: e.matmul(
                        bank[:], lhsT=mkT[:, h, mt * 128:(mt + 1) * 128], rhs=mqT[:, h, qb * 512:(qb + 1) * 512],
                        start=True, stop=True), reads=[t_mkT[h], t_mqT[h][qb]], writes=[t_bank])
                    P.add("act", lambda e, bank=bank, ei=ei: e.activation(out=E[ei][:], in_=bank[:], func=AF.Exp),
                          reads=[t_bank], writes=[t_E[ei]])
                for mt in range(2):
                    ei = (it * 2 + mt) % 4
                    P.add("pe", lambda e, ei=ei, h=h, mt=mt: e.matmul(
                        pN[:], lhsT=mvv[:, mt, h * 128:(h + 1) * 128], rhs=E[ei][:], start=(mt == 0), stop=(mt == 1)),
                        reads=[t_mv[mt], t_E[ei]], writes=[t_pN])
                    P.add("pe", lambda e, ei=ei, mt=mt: e.matmul(
                        pD[:], lhsT=ones[:], rhs=E[ei][:], start=(mt == 0), stop=(mt == 1)),
                        reads=[t_ones, t_E[ei]], writes=[t_pD])
                r = it % 2
                P.add("dve", lambda e, r=r: e.reciprocal(out=rden[r][:], in_=pD[:]), reads=[t_pD], writes=[t_rden[r]])
                P.add("dve", lambda e, r=r, h=h, qb=qb: e.tensor_tensor(
                    out=oT[:, h, qb * 512:(qb + 1) * 512], in0=pN[:], in1=rden[r][:], op=ALU.mult),
                    reads=[t_pN, t_rden[r]], writes=[Tok()])
                it += 1
        env.P.emit("m3")


def dsa_stage(nc, semstack, dmaq, x1T, oT, w_in, t5_d, ident_d, J_d, oh_d, causal_d, pow2_d, gscr_d):
    with contextlib.ExitStack() as st:
        env = Env(nc, semstack, dmaq, "m1", st)
        P = env.P
        idf, t_idf, idb, t_idb = load_ident(env, ident_d)
        banks = Banks(env, 3)
        pO = [[env.ps(f"pO{i}{j}", [128, 512], F32) for j in range(2)] for i in range(2)]
        t_pO = [toks(2) for _ in range(2)]
        pM = env.ps("pM", [128, 1024], BF16); t_pM = Tok()
        qT = env.sb("qT", [128, 4, L], BF16); t_qT = [toks(4) for _ in range(4)]
        kT = env.sb("kT", [128, 4, L], BF16); t_kT = [toks(4) for _ in range(4)]
        qiT = env.sb("qiT", [128, 4, L], BF16); t_qiT = [toks(4) for _ in range(4)]
        kiT = env.sb("kiT", [128, L], BF16); t_kiT = toks(4)
        vaug = env.sb("vaug", [128, 16, 8, 65], BF16); t_v = toks(16); t_vones = Tok()
        iw = env.sb("iw", [128, 16, 8], F32); t_iw = toks(16)
        wv = env.sb("wv", [128, 8, 512], BF16); t_wv = toks(4)
        wiw = env.sb("wiw", [128, 8, 8], BF16); t_wiw = Tok()
        Sc2 = [env.sb(f"Sc{i}", [128, L], F32) for i in range(2)]; t_Sc2 = toks(2)
        junk2 = [env.sb(f"junk{i}", [128, L], BF16) for i in range(2)]; t_junk2 = toks(2)
        mask = [env.sb(f"mask{i}", [128, L], BF16) for i in range(2)]; t_mask = toks(2)
        maskTp = [env.sb(f"maskTp{i}", [128, 16, 2, 128], BF16) for i in range(2)]
        t_maskTp = [toks(2) for _ in range(2)]
        tI = [env.sb(f"tI{i}", [128, 512], F32) for i in range(2)]; t_tI = toks(2)
        E = [env.sb(f"E{i}", [128, 512], BF16) for i in range(2)]; t_E = toks(2)
        PT = [env.sb(f"PT{i}", [128, 512], BF16) for i in range(3)]; t_PT = toks(3)
        BT = env.sb("BT", [128, 8, 256], BF16); t_BT = toks(8)
        H = Sc2[1][:].rearrange("p (h n) -> p h n", h=8); t_H = t_Sc2[1]
        rden8 = [env.sb(f"rden{i}", [128, 2, 4], F32) for i in range(2)]; t_rden8 = toks(2)
        o_tm = [env.sb(f"otm{i}", [128, 512], BF16) for i in range(2)]; t_otm = toks(2)
        Jf = env.sb("Jf", [128, 128], F32); t_J = Tok()
        caus = env.sb("caus", [128, 128], F32); t_caus = Tok()
        pow2 = env.sb("pow2", [128, NIT], F32); t_pow2 = Tok()
        tabS = env.sb("tabS", [32, 8], F32); t_tab = Tok()
        ohS = env.sb("ohS", [32, 384], F32); t_oh = Tok()
        Gs = env.sb("Gs", [8, 384], F32); t_Gs = Tok()
        thrneg = env.sb("thrneg", [128, 1], F32); t_thrneg = Tok()
        mx8_2 = [env.sb(f"mx8{i}", [128, 8], F32) for i in range(2)]; t_mx8_2 = toks(2)
        mn_2 = [env.sb(f"mn{i}", [128, 1], F32) for i in range(2)]; t_mn_2 = toks(2)
        rng_2 = [env.sb(f"rng{i}", [128, 1], F32) for i in range(2)]; t_rng_2 = toks(2)
        thr_2 = [env.sb(f"thr{i}", [128, 1], F32) for i in range(2)]; t_thr_2 = toks(2)
        S2_2 = [env.sb(f"S2{i}", [128, NIT], F32) for i in range(2)]; t_S2_2 = toks(2)
        cnt_2 = [env.sb(f"cnt{i}", [128, 1], F32) for i in range(2)]; t_cnt_2 = toks(2)
        ee_2 = [env.sb(f"ee{i}", [128, 1], F32) for i in range(2)]; t_ee_2 = toks(2)
        ws = WStream(env)
        t_gscr = Tok()
        print("[dsa] sbuf bytes remaining/partition:", nc.sbuf_bytes_remaining() if callable(nc.sbuf_bytes_remaining) else nc.sbuf_bytes_remaining)

        P.add("sync", lambda e: e.dma_start(out=Jf[:], in_=J_d), writes=[t_J], kind="d")
        P.add("sync", lambda e: e.dma_start(out=caus[:], in_=causal_d), writes=[t_caus], kind="d")
        P.add("sync", lambda e: e.dma_start(out=pow2[:], in_=pow2_d), writes=[t_pow2], kind="d")
        P.add("sync", lambda e: e.dma_start(out=tabS[:], in_=t5_d), writes=[t_tab], kind="d")
        P.add("sync", lambda e: e.dma_start(out=ohS[:], in_=oh_d), writes=[t_oh], kind="d")
        P.add("pool", lambda e: e.memset(vaug[:, :, :, 64:65], 1.0), writes=[t_vones])
        for i in range(2):
            P.add("pool", lambda e, i=i: e.memset(maskTp[i][:], 0.0), writes=t_maskTp[i])
        P.add("pool", lambda e: e.memset(thrneg[:], -1.0e29), writes=[t_thrneg])
        bank, t_bank = banks.next()
        P.add("pe", lambda e, bank=bank: e.matmul(bank[0:8, 0:384], lhsT=tabS[:], rhs=ohS[:], start=True, stop=True),
              reads=[t_tab, t_oh], writes=[t_bank])
        P.add("act", lambda e, bank=bank: e.activation(out=Gs[:], in_=bank[0:8, 0:384], func=AF.Copy), reads=[t_bank], writes=[t_Gs])
        P.add("sync", lambda e: e.dma_start(out=gscr_d, in_=Gs[:]), reads=[t_Gs], writes=[t_gscr], kind="d")
        hank = bass.AP(gscr_d.tensor, gscr_d.offset, [[1, 128], [384, 8], [1, 256]])
        P.add("sync", lambda e: e.dma_start(out=H, in_=hank), reads=[t_gscr], writes=[t_H], kind="d")
        for h in range(8):
            bank, t_bank = banks.next()
            P.add("pe", lambda e, bank=bank, h=h: e.matmul(bank[:, 0:256], lhsT=Jf[:], rhs=H[:, h, :], start=True, stop=True),
                  reads=[t_J, t_H], writes=[t_bank])
            P.add("act", lambda e, bank=bank, h=h: e.activation(out=BT[:, h, :], in_=bank[:, 0:256], func=AF.Copy),
                  reads=[t_bank], writes=[t_BT[h]])

        ev = {"i": 0}

        def evac_to(dstfn, t_dstfn, scale, only_act=False):
            def evac(tb, bank, t_bank):
                eng = "act" if (only_act or ev["i"] % 2 == 0) else "dve"
                ev["i"] += 1
                if eng == "act":
                    P.add("act", lambda e: e.activation(out=dstfn(tb), in_=bank[:], func=AF.Copy, scale=float(scale)),
                          reads=[t_bank], writes=[t_dstfn(tb)])
                else:
                    P.add("dve", lambda e: e.tensor_scalar(out=dstfn(tb), in0=bank[:], scalar1=float(scale), scalar2=None, op0=ALU.mult),
                          reads=[t_bank], writes=[t_dstfn(tb)])
            return evac

        xrhs = lambda kc, tb: x1T[:, kc, tb * 512:(tb + 1) * 512]

        def proj_indexer():
            for c in range(4):
                proj_fm(env, ws, banks, [(0, w_in[:, C_IQ + c * 128:C_IQ + (c + 1) * 128], 1.0)], 128, xrhs, lambda tb: [], 4, 512,
                        evac_to(lambda tb, c=c: qiT[:, c, tb * 512:(tb + 1) * 512], lambda tb, c=c: t_qiT[c][tb], 1.0))
            proj_fm(env, ws, banks, [(0, w_in[:, C_IK:C_IK + 64], 1.0), (64, w_in[:, C_IK:C_IK + 64], 1.0)], 128, xrhs, lambda tb: [], 4, 512,
                    evac_to(lambda tb: kiT[:, tb * 512:(tb + 1) * 512], lambda tb: t_kiT[tb], 1.0))
            ws.load([(0, w_in[:, C_IW:C_IW + 8], 1.0)], dst=lambda c0, w: wiw[:, :, c0:c0 + w], t_dst=t_wiw)
            for t in range(16):
                bank, t_bank = banks.next()
                for kc in range(8):
                    P.add("pe", lambda e, bank=bank, kc=kc, t=t: e.matmul(
                        bank[:, 0:8], lhsT=x1T[:, kc, t * 128:(t + 1) * 128], rhs=wiw[:, kc, :], start=(kc == 0), stop=(kc == 7)),
                        reads=[t_wiw], writes=[t_bank])
                P.add("dve", lambda e, bank=bank, t=t: e.tensor_scalar(out=iw[:, t, :], in0=bank[:, 0:8], scalar1=IW_SCALE, scalar2=None, op0=ALU.mult),
                      reads=[t_bank], writes=[t_iw[t]])

        def proj_qkv():
            for c in range(4):
                proj_fm(env, ws, banks, [(0, w_in[:, C_DQ + c * 128:C_DQ + (c + 1) * 128], 1.0)], 128, xrhs, lambda tb: [], 4, 512,
                        evac_to(lambda tb, c=c: qT[:, c, tb * 512:(tb + 1) * 512], lambda tb, c=c: t_qT[c][tb], 0.125, only_act=True))
                proj_fm(env, ws, banks, [(0, w_in[:, C_DK + c * 128:C_DK + (c + 1) * 128], 1.0)], 128, xrhs, lambda tb: [], 4, 512,
                        evac_to(lambda tb, c=c: kT[:, c, tb * 512:(tb + 1) * 512], lambda tb, c=c: t_kT[c][tb], 1.0, only_act=True))
            for c in range(4):
                ws.load([(0, w_in[:, C_DV + c * 128:C_DV + (c + 1) * 128], 1.0)],
                        dst=lambda c0, w, c=c: wv[:, :, c * 128 + c0:c * 128 + c0 + w], t_dst=t_wv[c])
            for t in range(16):
                bank, t_bank = banks.next()
                for kc in range(8):
                    P.add("pe", lambda e, bank=bank, kc=kc, t=t: e.matmul(
                        bank[:], lhsT=x1T[:, kc, t * 128:(t + 1) * 128], rhs=wv[:, kc, :], start=(kc == 0), stop=(kc == 7)),
                        reads=t_wv, writes=[t_bank])
                P.add("act", lambda e, bank=bank, t=t: e.activation(out=vaug[:, t, :, 0:64], in_=bank[:].rearrange("p (h d) -> p h d", h=8),
                                                                    func=AF.Copy), reads=[t_bank], writes=[t_v[t]])

        cI = {"i": 0}

        def idx_phase(n):
            Sc = Sc2[n % 2]; t_Sc = t_Sc2[n % 2]
            W = 128 * (n + 1)
            nb = (W + 511) // 512
            for h in range(8):
                hp = (h % 2) * 64
                for kb in range(nb):
                    w = min(512, W - kb * 512)
                    bank, t_bank = banks.next()
                    P.add("pe", lambda e, bank=bank, h=h, hp=hp, kb=kb, w=w, n=n: e.matmul(
                        bank[:, 0:w], lhsT=qiT[hp:hp + 64, h // 2, n * 128:(n + 1) * 128], rhs=kiT[hp:hp + 64, kb * 512:kb * 512 + w],
                        start=True, stop=True),
                        reads=[t_qiT[h // 2][n // 4]] + t_kiT[0:nb], writes=[t_bank])
                    s = cI["i"] % 2
                    cI["i"] += 1
                    P.add("act", lambda e, bank=bank, s=s, w=w: e.activation(out=tI[s][:, 0:w], in_=bank[:, 0:w], func=AF.Relu),
                          reads=[t_bank], writes=[t_tI[s]])
                    if h == 0:
                        P.add("dve", lambda e, s=s, kb=kb, w=w, n=n: e.tensor_scalar(
                            out=Sc[:, kb * 512:kb * 512 + w], in0=tI[s][:, 0:w], scalar1=iw[:, n, 0:1], scalar2=None, op0=ALU.mult),
                            reads=[t_tI[s], t_iw[n]], writes=[t_Sc])
                    else:
                        P.add("dve", lambda e, s=s, kb=kb, w=w, n=n, h=h: e.scalar_tensor_tensor(
                            out=Sc[:, kb * 512:kb * 512 + w], in0=tI[s][:, 0:w], scalar=iw[:, n, h:h + 1],
                            in1=Sc[:, kb * 512:kb * 512 + w], op0=ALU.mult, op1=ALU.add),
                            reads=[t_tI[s], t_iw[n], t_Sc], writes=[t_Sc])
            P.add("dve", lambda e, n=n: e.tensor_tensor(out=Sc[:, n * 128:(n + 1) * 128], in0=Sc[:, n * 128:(n + 1) * 128],
                                                        in1=caus[:], op=ALU.add),
                  reads=[t_Sc, t_caus], writes=[t_Sc])

        def bis_ops(n):
            ch = n % 2
            Sc = Sc2[ch]; t_Sc = t_Sc2[ch]; junk = junk2[ch]; t_junk = t_junk2[ch]
            mx8 = mx8_2[ch]; t_mx8 = t_mx8_2[ch]; mn = mn_2[ch]; t_mn = t_mn_2[ch]; rng = rng_2[ch]; t_rng = t_rng_2[ch]
            thr = thr_2[ch]; t_thr = t_thr_2[ch]; S2 = S2_2[ch]; t_S2 = t_S2_2[ch]; cnt = cnt_2[ch]; t_cnt = t_cnt_2[ch]
            ee = ee_2[ch]; t_ee = t_ee_2[ch]
            W = 128 * (n + 1)
            mk = mask[ch]
            t_mk = t_mask[ch]
            if n < 2:
                yield lambda: P.add("dve", lambda e: e.tensor_scalar(out=mk[:, 0:W], in0=Sc[:, 0:W], scalar1=thrneg[:, 0:1], scalar2=None, op0=ALU.is_ge),
                                    reads=[t_Sc, t_thrneg], writes=[t_mk])
                return
            yield lambda: P.add("dve", lambda e: e.max(out=mx8[:], in_=Sc[:, 0:W]), reads=[t_Sc], writes=[t_mx8])
            yield lambda: P.add("dve", lambda e: e.tensor_reduce(out=mn[:], in_=Sc[:, 0:n * 128], axis=mybir.AxisListType.X, op=ALU.min),
                                reads=[t_Sc], writes=[t_mn])
            yield lambda: P.add("dve", lambda e: e.tensor_tensor(out=rng[:], in0=mx8[:, 0:1], in1=mn[:], op=ALU.subtract),
                                reads=[t_mx8, t_mn], writes=[t_rng])
            yield lambda: P.add("dve", lambda e: e.scalar_tensor_tensor(out=thr[:], in0=rng[:], scalar=0.5, in1=mn[:], op0=ALU.mult, op1=ALU.add),
                                reads=[t_rng, t_mn], writes=[t_thr])
            yield lambda: P.add("dve", lambda e: e.tensor_scalar(out=S2[:], in0=pow2[:], scalar1=rng[:, 0:1], scalar2=None, op0=ALU.mult),
                                reads=[t_pow2, t_rng], writes=[t_S2])
            for k in range(NIT):
                yield lambda: P.add("dve", lambda e: e.tensor_scalar(out=junk[:, 0:W], in0=Sc[:, 0:W], scalar1=thr[:, 0:1], scalar2=None,
                                                                     op0=ALU.is_ge, op1=ALU.add, accum_out=cnt[:, 0:1]),
                                    reads=[t_Sc, t_thr], writes=[t_junk, t_cnt])
                yield lambda: P.add("dve", lambda e: e.tensor_scalar(out=ee[:], in0=cnt[:], scalar1=255.5, scalar2=0.5, op0=ALU.is_ge, op1=ALU.subtract),
                                    reads=[t_cnt], writes=[t_ee])
                yield lambda k=k: P.add("dve", lambda e: e.scalar_tensor_tensor(out=thr[:], in0=ee[:], scalar=S2[:, k:k + 1], in1=thr[:],
                                                                                op0=ALU.mult, op1=ALU.add),
                                        reads=[t_ee, t_S2, t_thr], writes=[t_thr])
            yield lambda: P.add("dve", lambda e: e.tensor_scalar(out=mk[:, 0:W], in0=Sc[:, 0:W], scalar1=thr[:, 0:1], scalar2=None, op0=ALU.is_ge),
                                reads=[t_Sc, t_thr], writes=[t_mk])

        def bis_pair(a, b):
            ga, gb = bis_ops(a), bis_ops(b)
            done_a = done_b = False
            while not (done_a and done_b):
                if not done_a:
                    f = next(ga, None)
                    if f is None:
                        done_a = True
                    else:
                        f()
                if not done_b:
                    f = next(gb, None)
                    if f is None:
                        done_b = True
                    else:
                        f()

        def maskT_phase(n):
            mk = mask[n % 2]; t_mk = t_mask[n % 2]
            mp = maskTp[(n // 2) % 2]; t_mp = t_maskTp[(n // 2) % 2][n % 2]
            for m0 in range(0, n + 1, 8):
                k = min(8, n + 1 - m0)
                for i in range(k):
                    m = m0 + i
                    P.add("pe", lambda e, i=i, m=m: e.transpose(out=pM[:, i * 128:(i + 1) * 128], in_=mk[:, m * 128:(m + 1) * 128], identity=idb[:]),
                          reads=[t_mk, t_idb], writes=[t_pM])
                P.add("act", lambda e, m0=m0, k=k, n=n: e.activation(
                    out=mp[:, m0:m0 + k, n % 2, :], in_=pM[:, 0:k * 128].rearrange("p (a b) -> p a b", a=k), func=AF.Copy),
                    reads=[t_pM], writes=[t_mp])

        cA = {"e": 0, "p": 0}

        def att_pair(kp):
            a, b = 2 * kp, 2 * kp + 1
            mp = maskTp[kp % 2]; t_mp = t_maskTp[kp % 2]
            steps = [(c, m0, hh) for c in range(4) for m0 in range(0, b + 1, 2) for hh in range(2)]

            def emit_scores(c, m0, hh):
                h = 2 * c + hh; hp = hh * 64
                bank, t_bank = banks.next()
                for i in range(2):
                    m = m0 + i
                    biases = []
                    if m == a - 1:
                        biases.append((0, 128))
                    if m == a:
                        biases.append((0, 0)); biases.append((128, 128))
                    if m == b:
                        biases.append((128, 0))
                    P.add("pe", lambda e, bank=bank, i=i, m=m, hp=hp, c=c, nb=len(biases): e.matmul(
                        bank[:, i * 256:(i + 1) * 256], lhsT=kT[hp:hp + 64, c, m * 128:(m + 1) * 128],
                        rhs=qT[hp:hp + 64, c, a * 128:(a + 2) * 128], start=True, stop=(nb == 0)),
                        reads=[t_kT[c][m // 4], t_qT[c][a // 4]], writes=[t_bank])
                    for bi, (qo, off) in enumerate(biases):
                        P.add("pe", lambda e, bank=bank, i=i, h=h, qo=qo, off=off, last=(bi == len(biases) - 1): e.matmul(
                            bank[:, i * 256 + qo:i * 256 + qo + 128], lhsT=idb[:], rhs=BT[:, h, off:off + 128], start=False, stop=last),
                            reads=[t_idb, t_BT[h]], writes=[t_bank])
                se = cA["e"] % 2
                cA["e"] += 1
                P.add("act", lambda e, bank=bank, se=se: e.activation(out=E[se][:], in_=bank[:], func=AF.Exp),
                      reads=[t_bank], writes=[t_E[se]])
                sp = cA["p"] % 3
                cA["p"] += 1
                P.add("pool", lambda e, se=se, sp=sp, m0=m0: e.tensor_tensor(
                    out=PT[sp][:], in0=E[se][:], in1=mp[:, m0:m0 + 2, :, :].rearrange("p a b q -> p (a b q)"), op=ALU.mult),
                    reads=[t_E[se]] + t_mp, writes=[t_PT[sp]])
                return sp

            def emit_pv(c, m0, hh, sp):
                h = 2 * c + hh
                for i in range(2):
                    m = m0 + i
                    for t in (a, b):
                        if m > t:
                            continue
                        P.add("pe", lambda e, sp=sp, i=i, m=m, h=h, hh=hh, c=c, t=t, a=a: e.matmul(
                            pO[t % 2][hh][:, c * 65:(c + 1) * 65], lhsT=PT[sp][:, i * 256 + (t - a) * 128:i * 256 + (t - a) * 128 + 128],
                            rhs=vaug[:, m, h, :], start=(m == 0), stop=(m == t)),
                            reads=[t_v[m], t_vones, t_PT[sp]], writes=[t_pO[t % 2][hh]])

            prev = None
            for st_ in steps:
                sp = emit_scores(*st_)
                if prev is not None:
                    emit_pv(*prev)
                prev = st_ + (sp,)
            emit_pv(*prev)
            for t in (a, b):
                r = t % 2
                for hh in range(2):
                    P.add("dve", lambda e, r=r, hh=hh: e.reciprocal(
                        out=rden8[r][:, hh, :], in_=pO[r][hh][:, 0:260].rearrange("p (c d) -> p c d", c=4)[:, :, 64]),
                        reads=[t_pO[r][hh]], writes=[t_rden8[r]])
                for h in range(8):
                    P.add("act", lambda e, r=r, h=h: e.activation(
                        out=o_tm[r][:, h * 64:(h + 1) * 64], in_=pO[r][h % 2][:, (h // 2) * 65:(h // 2) * 65 + 64], func=AF.Copy,
                        scale=rden8[r][:, h % 2, h // 2:h // 2 + 1]),
                        reads=[t_pO[r][h % 2], t_rden8[r]], writes=[t_otm[r]])
                for c4 in range(4):
                    P.add("pe", lambda e, r=r, c4=c4: e.transpose(out=pM[:, c4 * 128:(c4 + 1) * 128], in_=o_tm[r][:, c4 * 128:(c4 + 1) * 128], identity=idb[:]),
                          reads=[t_otm[r], t_idb], writes=[t_pM])
                P.add("dve", lambda e, t=t: e.tensor_copy(out=oT[:, :, t * 128:(t + 1) * 128], in_=pM[:, 0:512].rearrange("p (a b) -> p a b", a=4)),
                      reads=[t_pM], writes=[Tok()])

        proj_indexer()
        idx_phase(14); idx_phase(15); bis_pair(14, 15)
        proj_qkv()
        maskT_phase(14); maskT_phase(15)
        for k in range(7, -1, -1):
            a, b = 2 * k, 2 * k + 1
            if k > 0:
                idx_phase(a - 2); idx_phase(b - 2)
                bis_pair(a - 2, b - 2)
            att_pair(k)
            if k > 0:
                maskT_phase(a - 2); maskT_phase(b - 2)
        env.P.emit("m1")


def ret_stage(nc, semstack, dmaq, x1T, oT, w_in, gng_d, gnb_d, ident_d, cos_d, sin_d, decay_d, kdec_d, qdec_d):
    with contextlib.ExitStack() as st:
        env = Env(nc, semstack, dmaq, "m2", st)
        P = env.P
        idf, t_idf, idb, t_idb = load_ident(env, ident_d)
        banks = Banks(env, 7)
        pM = env.ps("pM", [128, 1024], BF16); t_pM = Tok()
        rqT = env.sb("rqT", [128, 2, L], BF16); t_rqT = [toks(4) for _ in range(2)]
        rkT = env.sb("rkT", [128, 2, L], BF16); t_rkT = [toks(4) for _ in range(2)]
        qdT = env.sb("qdT", [128, 2, L], BF16); t_qdT = toks(16)
        rv = env.sb("rv", [128, 16, 512], BF16); t_rv = toks(16)
        kd = env.sb("kd", [128, 16, 256], BF16); t_kd = toks(16)
        wrv = env.sb("wrv", [128, 8, 512], BF16); t_wrv = toks(4)
        wrg = env.sb("wrg", [128, 8, 512], BF16); t_wrg = toks(4)
        cosT = env.sb("cosT", [128, L], F32); t_cos = Tok()
        sinT = env.sb("sinT", [128, L], F32); t_sin = Tok()
        decT = env.sb("decT", [128, 512], F32); t_dec = Tok()
        kdec = env.sb("kdec", [128, 256], F32); t_kdec = Tok()
        qdec = env.sb("qdec", [128, 2, 128], F32); t_qdec = Tok()
        gng = env.sb("gng", [128, 512], F32); t_gng = Tok()
        gnb = env.sb("gnb", [128, 512], F32); t_gnb = Tok()
        tmp = [env.sb(f"tmp{i}", [128, 512], F32) for i in range(2)]; t_tmp = toks(2)
        tmp2 = [env.sb(f"tmpb{i}", [128, 512], F32) for i in range(2)]; t_tmp2 = toks(2)
        PdT = [env.sb(f"PdT{i}", [128, 512], BF16) for i in range(2)]; t_PdT = toks(2)
        state = env.sb("state", [128, 2, 128], F32); t_state = Tok()
        state_bf = env.sb("state_bf", [128, 2, 128], BF16); t_state_bf = Tok()
        on = [env.sb(f"on{i}", [128, 512], F32) for i in range(2)]; t_on = toks(2)
        sg = [env.sb(f"sg{i}", [128, 512], F32) for i in range(2)]; t_sg = toks(2)
        orb = [env.sb(f"orb{i}", [128, 512], BF16) for i in range(2)]; t_orb = toks(2)
        stats = [env.sb(f"stats{i}", [128, 4, 6], F32) for i in range(2)]; t_stats = toks(2)
        mvv = [env.sb(f"mvv{i}", [128, 4, 2], F32) for i in range(2)]; t_mvv = toks(2)
        sd = [env.sb(f"sd{i}", [128, 4], F32) for i in range(2)]; t_sd = toks(2)
        rstd = [env.sb(f"rstd{i}", [128, 4], F32) for i in range(2)]; t_rstd = toks(2)
        epst = env.sb("epst", [128, 1], F32); t_eps = Tok()
        ws = WStream(env)
        for (dst, src, tk) in ((cosT, cos_d, t_cos), (sinT, sin_d, t_sin), (decT, decay_d, t_dec), (kdec, kdec_d, t_kdec)):
            P.add("sync", lambda e, dst=dst, src=src: e.dma_start(out=dst[:], in_=src), writes=[tk], kind="d")
        P.add("sync", lambda e: e.dma_start(out=qdec[:].rearrange("p a b -> p (a b)"), in_=qdec_d), writes=[t_qdec], kind="d")
        P.add("sync", lambda e: e.dma_start(out=gng[:], in_=bcast_rows(gng_d, 128)), writes=[t_gng], kind="d")
        P.add("sync", lambda e: e.dma_start(out=gnb[:], in_=bcast_rows(gnb_d, 128)), writes=[t_gnb], kind="d")
        P.add("dve", lambda e: e.memset(epst[:], LN_EPS), writes=[t_eps])

        cR = {"i": 0}
        for (c0, dstT, t_dstT, sc) in ((C_RQ, rqT, t_rqT, 1.0), (C_RK, rkT, t_rkT, 0.125)):
            for c in range(2):
                base = c0 + c * 128
                wn, t_wn = ws.load([(0, w_in[:, base:base + 128], 1.0)])
                pieces = []
                for hh in range(2):
                    pieces.append((hh * 64, w_in[:, base + hh * 64 + 32:base + hh * 64 + 64], -1.0))
                    pieces.append((hh * 64 + 32, w_in[:, base + hh * 64:base + hh * 64 + 32], 1.0))
                wr, t_wr = ws.load(pieces)
                for tb in range(4):
                    bq, t_bq = banks.next()
                    br, t_br = banks.next()
                    for (bank, t_bank, wt, t_w) in ((bq, t_bq, wn, t_wn), (br, t_br, wr, t_wr)):
                        for kc in range(8):
                            P.add("pe", lambda e, bank=bank, wt=wt, kc=kc, tb=tb: e.matmul(
                                bank[:], lhsT=wt[:, kc, :], rhs=x1T[:, kc, tb * 512:(tb + 1) * 512], start=(kc == 0), stop=(kc == 7)),
                                reads=[t_w], writes=[t_bank])
                    s = cR["i"] % 2
                    cR["i"] += 1
                    tsl = slice(tb * 512, (tb + 1) * 512)
                    P.add("dve", lambda e, s=s, bq=bq, tsl=tsl: e.tensor_tensor(out=tmp[s][:], in0=bq[:], in1=cosT[:, tsl], op=ALU.mult),
                          reads=[t_bq, t_cos], writes=[t_tmp[s]])
                    P.add("dve", lambda e, s=s, br=br, tsl=tsl, sc=sc: e.scalar_tensor_tensor(
                        out=tmp2[s][:], in0=br[:], scalar=float(sc), in1=sinT[:, tsl], op0=ALU.mult, op1=ALU.mult),
                        reads=[t_br, t_sin], writes=[t_tmp2[s]])
                    P.add("dve", lambda e, s=s, tsl=tsl, sc=sc, dstT=dstT, c=c: e.scalar_tensor_tensor(
                        out=dstT[:, c, tsl], in0=tmp[s][:], scalar=float(sc), in1=tmp2[s][:], op0=ALU.mult, op1=ALU.add),
                        reads=[t_tmp[s], t_tmp2[s]], writes=[t_dstT[c][tb]])
        for c in range(4):
            ws.load([(0, w_in[:, C_RV + c * 128:C_RV + (c + 1) * 128], 1.0)],
                    dst=lambda c0, w, c=c: wrv[:, :, c * 128 + c0:c * 128 + c0 + w], t_dst=t_wrv[c])
        for c in range(4):
            ws.load([(0, w_in[:, C_RG + c * 128:C_RG + (c + 1) * 128], 1.0)],
                    dst=lambda c0, w, c=c: wrg[:, :, c * 128 + c0:c * 128 + c0 + w], t_dst=t_wrg[c])
        for t in range(16):
            tsl = slice(t * 128, (t + 1) * 128)
            bank, t_bank = banks.next()
            for kc in range(8):
                P.add("pe", lambda e, bank=bank, kc=kc, tsl=tsl: e.matmul(
                    bank[:], lhsT=x1T[:, kc, tsl], rhs=wrv[:, kc, :], start=(kc == 0), stop=(kc == 7)),
                    reads=t_wrv, writes=[t_bank])
            P.add("act", lambda e, bank=bank, t=t: e.activation(out=rv[:, t, :], in_=bank[:], func=AF.Copy),
                  reads=[t_bank], writes=[t_rv[t]])
            for c in range(2):
                P.add("pe", lambda e, c=c, tsl=tsl: e.transpose(out=pM[:, c * 128:(c + 1) * 128], in_=rkT[:, c, tsl], identity=idb[:]),
                      reads=[t_rkT[c][t // 4], t_idb], writes=[t_pM])
            P.add("dve", lambda e, t=t: e.tensor_tensor(out=kd[:, t, :], in0=pM[:, 0:256], in1=kdec[:], op=ALU.mult),
                  reads=[t_pM, t_kdec], writes=[t_kd[t]])
            for c in range(2):
                P.add("dve", lambda e, c=c, tsl=tsl: e.tensor_tensor(out=qdT[:, c, tsl], in0=rqT[:, c, tsl], in1=qdec[:, c, :], op=ALU.mult),
                      reads=[t_rqT[c][t // 4], t_qdec], writes=[t_qdT[t]])
        eo_banks = [[(banks.b[2 * i + j], banks.t[2 * i + j]) for j in range(2)] for i in range(2)]
        kg = Banks.__new__(Banks)
        kg.b = [banks.b[i] for i in (4, 5, 6)]; kg.t = [banks.t[i] for i in (4, 5, 6)]; kg.i = 0; kg.n = 3

        def phase_a(n):
            tsl = slice(n * 128, (n + 1) * 128)
            s = n % 2
            eo = eo_banks[n % 2]
            for h in range(4):
                hp = (h % 2) * 64; c = h // 2
                bk, t_bk = eo[h % 2]
                P.add("pe", lambda e, bk=bk, hp=hp, c=c, tsl=tsl: e.matmul(
                    bk[:, c * 128:(c + 1) * 128], lhsT=rkT[hp:hp + 64, c, tsl], rhs=rqT[hp:hp + 64, c, tsl], start=True, stop=True),
                    reads=[t_rkT[c][n // 4], t_rqT[c][n // 4]], writes=[t_bk])
            for par in range(2):
                bk, t_bk = eo[par]
                P.add("dve", lambda e, bk=bk, s=s, par=par: e.tensor_tensor(
                    out=PdT[s][:, par * 256:(par + 1) * 256], in0=bk[:, 0:256], in1=decT[:, par * 256:(par + 1) * 256], op=ALU.mult),
                    reads=[t_bk, t_dec], writes=[t_PdT[s]])
            for h in range(4):
                hp = (h % 2) * 64; c = h // 2
                pos = (h % 2) * 2 + c
                bk, t_bk = eo[h % 2]
                P.add("pe", lambda e, bk=bk, h=h, c=c, pos=pos, s=s, n=n: e.matmul(
                    bk[:, 256 + c * 128:256 + (c + 1) * 128], lhsT=PdT[s][:, pos * 128:(pos + 1) * 128], rhs=rv[:, n, h * 128:(h + 1) * 128],
                    start=True, stop=(n == 0)), reads=[t_PdT[s], t_rv[n]], writes=[t_bk])
                if n > 0:
                    P.add("pe", lambda e, bk=bk, hp=hp, c=c, tsl=tsl: e.matmul(
                        bk[:, 256 + c * 128:256 + (c + 1) * 128], lhsT=qdT[hp:hp + 64, c, tsl], rhs=state_bf[hp:hp + 64, c, :],
                        start=False, stop=True), reads=[t_qdT[n], t_state_bf], writes=[t_bk])
            if n < 15:
                bK, t_bK = kg.next()
                for h in range(4):
                    hp = (h % 2) * 64; c = h // 2
                    P.add("pe", lambda e, bK=bK, h=h, hp=hp, c=c, n=n: e.matmul(
                        bK[hp:hp + 64, c * 128:(c + 1) * 128], lhsT=kd[:, n, h * 64:(h + 1) * 64], rhs=rv[:, n, h * 128:(h + 1) * 128],
                        start=True, stop=True), reads=[t_kd[n], t_rv[n]], writes=[t_bK])
                for h in range(4):
                    hp = (h % 2) * 64; c = h // 2
                    if n == 0:
                        P.add("dve", lambda e, bK=bK, hp=hp, c=c: e.tensor_copy(out=state[hp:hp + 64, c, :], in_=bK[hp:hp + 64, c * 128:(c + 1) * 128]),
                              reads=[t_bK], writes=[t_state])
                    else:
                        cd = float(np.float32(np.exp(np.float32(128.0) * np.log(np.float32(GAMMAS[h])))))
                        P.add("dve", lambda e, bK=bK, hp=hp, c=c, cd=cd: e.scalar_tensor_tensor(
                            out=state[hp:hp + 64, c, :], in0=state[hp:hp + 64, c, :], scalar=cd, in1=bK[hp:hp + 64, c * 128:(c + 1) * 128],
                            op0=ALU.mult, op1=ALU.add), reads=[t_bK, t_state], writes=[t_state])
                P.add("act", lambda e: e.activation(out=state_bf[:], in_=state[:], func=AF.Copy), reads=[t_state], writes=[t_state_bf])
            bG, t_bG = kg.next()
            for kc in range(8):
                P.add("pe", lambda e, bG=bG, kc=kc, tsl=tsl: e.matmul(
                    bG[:], lhsT=x1T[:, kc, tsl], rhs=wrg[:, kc, :], start=(kc == 0), stop=(kc == 7)), reads=t_wrg, writes=[t_bG])
            P.add("act", lambda e, bG=bG, s=s: e.activation(out=sg[s][:], in_=bG[:], func=AF.Silu), reads=[t_bG], writes=[t_sg[s]])

        def phase_b(n):
            tsl = slice(n * 128, (n + 1) * 128)
            s = n % 2
            eo = eo_banks[n % 2]
            osl = lambda h: slice(256 + (h // 2) * 128, 256 + (h // 2 + 1) * 128)
            for h in range(4):
                bk, t_bk = eo[h % 2]
                P.add("dve", lambda e, bk=bk, h=h, s=s: e.bn_stats(out=stats[s][:, h, :], in_=bk[:, osl(h)]),
                      reads=[t_bk], writes=[t_stats[s]])
            for h in range(4):
                P.add("dve", lambda e, h=h, s=s: e.bn_aggr(out=mvv[s][:, h, :], in_=stats[s][:, h, :]), reads=[t_stats[s]], writes=[t_mvv[s]])
            P.add("act", lambda e, s=s: e.activation(out=sd[s][:], in_=mvv[s][:, :, 1], func=AF.Sqrt, bias=epst[:, 0:1], scale=1.0),
                  reads=[t_mvv[s], t_eps], writes=[t_sd[s]])
            P.add("dve", lambda e, s=s: e.reciprocal(out=rstd[s][:], in_=sd[s][:]), reads=[t_sd[s]], writes=[t_rstd[s]])
            for h in range(4):
                bk, t_bk = eo[h % 2]
                P.add("dve", lambda e, bk=bk, h=h, s=s: e.tensor_scalar(
                    out=on[s][:, h * 128:(h + 1) * 128], in0=bk[:, osl(h)], scalar1=mvv[s][:, h, 0:1],
                    scalar2=rstd[s][:, h:h + 1], op0=ALU.subtract, op1=ALU.mult),
                    reads=[t_bk, t_mvv[s], t_rstd[s]], writes=[t_on[s]])
            P.add("pool", lambda e, s=s: e.tensor_tensor(out=on[s][:], in0=on[s][:], in1=gng[:], op=ALU.mult),
                  reads=[t_on[s], t_gng], writes=[t_on[s]])
            P.add("pool", lambda e, s=s: e.tensor_tensor(out=on[s][:], in0=on[s][:], in1=gnb[:], op=ALU.add),
                  reads=[t_on[s], t_gnb], writes=[t_on[s]])
            P.add("dve", lambda e, s=s: e.tensor_tensor(out=orb[s][:], in0=on[s][:], in1=sg[s][:], op=ALU.mult),
                  reads=[t_on[s], t_sg[s]], writes=[t_orb[s]])
            for c4 in range(4):
                P.add("pe", lambda e, c4=c4, s=s: e.transpose(out=pM[:, c4 * 128:(c4 + 1) * 128], in_=orb[s][:, c4 * 128:(c4 + 1) * 128], identity=idb[:]),
                      reads=[t_orb[s], t_idb], writes=[t_pM])
            P.add("act", lambda e, tsl=tsl: e.activation(out=oT[:, :, tsl], in_=pM[:, 0:512].rearrange("p (a b) -> p a b", a=4), func=AF.Copy),
                  reads=[t_pM], writes=[Tok()])

        phase_a(0)
        for n in range(16):
            if n + 1 < 16:
                phase_a(n + 1)
            phase_b(n)
        env.P.emit("m2")


def merge_stage(nc, semstack, dmaq, x1T, oTs, w_in, wbrs, wo_d, g_d, b_d, src_d, dst_d):
    with contextlib.ExitStack() as st:
        env = Env(nc, semstack, dmaq, "m4", st)
        P = env.P
        banks = Banks(env, 8)
        mT = env.sb("mT", [128, 8, L], BF16); t_mT = [toks(4) for _ in range(8)]
        wout = env.sb("wout", [128, 8, D], BF16); t_wout = toks(8)
        sgm = [env.sb(f"sgm{i}", [128, 512], F32) for i in range(2)]; t_sgm = toks(2)
        tmp = [env.sb(f"tmp{i}", [128, 512], F32) for i in range(2)]; t_tmp = toks(2)
        acc = env.sb("acc", [128, L], F32); t_acc = toks(4)
        xs = [env.sb(f"xs{i}", [128, D], F32) for i in range(2)]; t_xs = toks(2)
        rr = [env.sb(f"rr{i}", [128, D], F32) for i in range(2)]; t_rr = toks(2)
        oo = [env.sb(f"oo{i}", [128, D], F32) for i in range(2)]; t_oo = toks(2)
        gam = env.sb("gam", [128, D], F32); t_gam = Tok()
        bet = env.sb("bet", [128, D], F32); t_bet = Tok()
        epst = env.sb("epst", [128, 1], F32); t_eps = Tok()
        stats = [env.sb(f"stats{i}", [128, 2, 6], F32) for i in range(2)]; t_stats = toks(2)
        mv = [env.sb(f"mv{i}", [128, 2], F32) for i in range(2)]; t_mv = toks(2)
        sd = [env.sb(f"sd{i}", [128, 1], F32) for i in range(2)]; t_sd = toks(2)
        rstd = [env.sb(f"rstd{i}", [128, 1], F32) for i in range(2)]; t_rstd = toks(2)
        t_dst = toks(16)
        ws = WStream(env, nst=2, nbf=6)
        P.add("sync", lambda e: e.dma_start(out=gam[:], in_=bcast_rows(g_d, 128)), writes=[t_gam], kind="d")
        P.add("sync", lambda e: e.dma_start(out=bet[:], in_=bcast_rows(b_d, 128)), writes=[t_bet], kind="d")
        P.add("dve", lambda e: e.memset(epst[:], LN_EPS), writes=[t_eps])
        cM = {"s": 0}
        for j in range(8):
            ws.load([(0, wo_d[:, j * 128:(j + 1) * 128], 1.0)], dst=lambda c0, w, j=j: wout[:, :, j * 128 + c0:j * 128 + c0 + w], t_dst=t_wout[j])
            for b in range(3):
                wgb, t_wgb = ws.load([(0, w_in[:, C_G + b * D + j * 128:C_G + b * D + (j + 1) * 128], 1.0)])
                wbb, t_wbb = ws.load([(0, wbrs[b][:, j * 128:(j + 1) * 128], 1.0)], KC=4)
                for tb in range(4):
                    tsl = slice(tb * 512, (tb + 1) * 512)
                    bG, t_bG = banks.next()
                    for kc in range(8):
                        P.add("pe", lambda e, bG=bG, wgb=wgb, kc=kc, tsl=tsl: e.matmul(
                            bG[:], lhsT=wgb[:, kc, :], rhs=x1T[:, kc, tsl], start=(kc == 0), stop=(kc == 7)),
                            reads=[t_wgb], writes=[t_bG])
                    bB, t_bB = banks.next()
                    for kc in range(4):
                        P.add("pe", lambda e, bB=bB, wbb=wbb, kc=kc, tsl=tsl, b=b: e.matmul(
                            bB[:], lhsT=wbb[:, kc, :], rhs=oTs[b][:, kc, tsl], start=(kc == 0), stop=(kc == 3)),
                            reads=[t_wbb], writes=[t_bB])
                    s = cM["s"] % 2
                    cM["s"] += 1
                    P.add("act", lambda e, bG=bG, s=s: e.activation(out=sgm[s][:], in_=bG[:], func=AF.Sigmoid), reads=[t_bG], writes=[t_sgm[s]])
                    if b == 0:
                        P.add("dve", lambda e, bB=bB, s=s, tsl=tsl: e.tensor_tensor(out=acc[:, tsl], in0=sgm[s][:], in1=bB[:], op=ALU.mult),
                              reads=[t_sgm[s], t_bB], writes=[t_acc[tb]])
                    else:
                        P.add("dve", lambda e, bB=bB, s=s: e.tensor_tensor(out=tmp[s][:], in0=sgm[s][:], in1=bB[:], op=ALU.mult),
                              reads=[t_sgm[s], t_bB], writes=[t_tmp[s]])
                        if b == 1:
                            P.add("dve", lambda e, s=s, tsl=tsl: e.tensor_tensor(out=acc[:, tsl], in0=acc[:, tsl], in1=tmp[s][:], op=ALU.add),
                                  reads=[t_acc[tb], t_tmp[s]], writes=[t_acc[tb]])
                        else:
                            P.add("dve", lambda e, s=s, j=j, tsl=tsl: e.tensor_tensor(out=mT[:, j, tsl], in0=acc[:, tsl], in1=tmp[s][:], op=ALU.add),
                                  reads=[t_acc[tb], t_tmp[s]], writes=[t_mT[j][tb]])
        for t in range(16):
            s = t % 2
            tsl = slice(t * 128, (t + 1) * 128)
            P.add("sync", lambda e, s=s, tsl=tsl: e.dma_start(out=xs[s][:], in_=src_d[tsl, :]), writes=[t_xs[s]], kind="d")
            for nh in range(2):
                bank, t_bank = banks.next()
                for kc in range(8):
                    P.add("pe", lambda e, bank=bank, kc=kc, nh=nh, tsl=tsl: e.matmul(
                        bank[:], lhsT=mT[:, kc, tsl], rhs=wout[:, kc, nh * 512:(nh + 1) * 512], start=(kc == 0), stop=(kc == 7)),
                        reads=[t_mT[kc][t // 4]] + t_wout[nh * 4:(nh + 1) * 4], writes=[t_bank])
                P.add("dve", lambda e, bank=bank, nh=nh, s=s: e.scalar_tensor_tensor(
                    out=rr[s][:, nh * 512:(nh + 1) * 512], in0=xs[s][:, nh * 512:(nh + 1) * 512], scalar=ALPHA, in1=bank[:],
                    op0=ALU.mult, op1=ALU.add), reads=[t_xs[s], t_bank], writes=[t_rr[s]])
            for k in range(2):
                P.add("dve", lambda e, s=s, k=k: e.bn_stats(out=stats[s][:, k, :], in_=rr[s][:, k * 512:(k + 1) * 512]),
                      reads=[t_rr[s]], writes=[t_stats[s]])
            P.add("dve", lambda e, s=s: e.bn_aggr(out=mv[s][:], in_=stats[s][:].rearrange("p a b -> p (a b)")),
                  reads=[t_stats[s]], writes=[t_mv[s]])
            P.add("act", lambda e, s=s: e.activation(out=sd[s][:], in_=mv[s][:, 1:2], func=AF.Sqrt, bias=epst[:, 0:1], scale=1.0),
                  reads=[t_mv[s], t_eps], writes=[t_sd[s]])
            P.add("dve", lambda e, s=s: e.reciprocal(out=rstd[s][:], in_=sd[s][:]), reads=[t_sd[s]], writes=[t_rstd[s]])
            P.add("dve", lambda e, s=s: e.tensor_scalar(
                out=rr[s][:], in0=rr[s][:], scalar1=mv[s][:, 0:1], scalar2=rstd[s][:, 0:1], op0=ALU.subtract, op1=ALU.mult),
                reads=[t_rr[s], t_mv[s], t_rstd[s]], writes=[t_rr[s]])
            P.add("pool", lambda e, s=s: e.tensor_tensor(out=oo[s][:], in0=rr[s][:], in1=gam[:], op=ALU.mult),
                  reads=[t_rr[s], t_gam], writes=[t_oo[s]])
            P.add("pool", lambda e, s=s: e.tensor_tensor(out=oo[s][:], in0=oo[s][:], in1=bet[:], op=ALU.add),
                  reads=[t_oo[s], t_bet], writes=[t_oo[s]])
            P.add("pool", lambda e, s=s, tsl=tsl: e.dma_start(out=dst_d[tsl, :], in_=oo[s][:]),
                  reads=[t_oo[s]], writes=[t_dst[t]], kind="d")
        P.wait_all("pool", t_dst)
        env.P.emit("m4")

def build_nc(stages=("ffn1", "mix", "ffn2"), dbg_mix=False, parts=("dsa", "ret", "mem", "merge")):
    nc = bass.Bass("TRN2", target_bir_lowering=False)
    din = lambda n, shape: nc.dram_tensor(n, shape, F32, kind="ExternalInput").ap()
    x_d = din("x", [L, D])
    mem_d = din("mem", [256, D])
    f1wi = din("ffn1_w_in", [D, 2 * DFF]); f1wo = din("ffn1_w_out", [DFF, D])
    ln1g = din("ln1_g", [1, D]); ln1b = din("ln1_b", [1, D])
    w_in = din("w_in", [D, W_IN_COLS]); t5 = din("t5_table", [32, 8])
    gng = din("ret_gn_g", [1, 512]); gnb = din("ret_gn_b", [1, 512])
    wmkv = din("w_mem_kv", [D, 1024])
    wbr = din("w_br_ret", [512, D]); wbd = din("w_br_dsa", [512, D]); wbm = din("w_br_mem", [512, D])
    wo = din("w_out", [D, D]); ln2g = din("ln2_g", [1, D]); ln2b = din("ln2_b", [1, D])
    f2wi = din("ffn2_w_in", [D, 2 * DFF]); f2wo = din("ffn2_w_out", [DFF, D])
    ln3g = din("ln3_g", [1, D]); ln3b = din("ln3_b", [1, D])
    ident = din("c_ident", [128, 128])
    cJ = din("c_J", [128, 128]); coh = din("c_oh", [32, 384]); ccausal = din("c_causal", [128, 128])
    cpow2 = din("c_pow2", [128, NIT])
    ccos = din("c_cos", [128, L]); csin = din("c_sin", [128, L])
    cdecay = din("c_decay", [128, 512]); ckdec = din("c_kdec", [128, 256]); cqdec = din("c_qdec", [128, 256])
    gscr = nc.dram_tensor("g_scr", [8, 384], F32).ap()
    out_d = nc.dram_tensor("out", [L, D], F32, kind="ExternalOutput").ap()
    x1_d = nc.dram_tensor("x1_scr", [L, D], F32).ap()
    x2_d = nc.dram_tensor("x2_scr", [L, D], F32).ap()
    with contextlib.ExitStack() as semstack:
        dmaq = {}
        cur = x_d
        for i, s in enumerate(stages):
            last = i == len(stages) - 1
            if s == "ffn1":
                dst = out_d if last else x1_d
                ffn_stage(nc, semstack, dmaq, "f1", cur, f1wi, f1wo, ln1g, ln1b, dst, ident)
                cur = dst
            elif s == "mix":
                dst = out_d if last else x2_d
                with contextlib.ExitStack() as outer:
                    x1T = outer.enter_context(nc.sbuf_tensor("x1T", [128, 8, L], BF16))
                    mix_load_stage(nc, semstack, dmaq, cur, ident, x1T)
                    o_dsaT = outer.enter_context(nc.sbuf_tensor("o_dsaT", [128, 4, L], BF16))
                    if "dsa" in parts:
                        dsa_stage(nc, semstack, dmaq, x1T, o_dsaT, w_in, t5, ident, cJ, coh, ccausal, cpow2, gscr)
                    o_retT = outer.enter_context(nc.sbuf_tensor("o_retT", [128, 4, L], BF16))
                    if "ret" in parts:
                        ret_stage(nc, semstack, dmaq, x1T, o_retT, w_in, gng, gnb, ident, ccos, csin, cdecay, ckdec, cqdec)
                    o_memT = outer.enter_context(nc.sbuf_tensor("o_memT", [128, 4, L], BF16))
                    if "mem" in parts:
                        mem_stage(nc, semstack, dmaq, x1T, o_memT, mem_d, w_in, wmkv, ident)
                    if dbg_mix:
                        dbg = nc.dram_tensor("dbg", [128, 12, L], BF16, kind="ExternalOutput").ap()
                        with contextlib.ExitStack() as st:
                            env = Env(nc, semstack, dmaq, "dbg", st)
                            tt = toks(3)
                            for i, (o, pn) in enumerate(((o_retT, "ret"), (o_dsaT, "dsa"), (o_memT, "mem"))):
                                if pn not in parts:
                                    continue
                                env.P.add("sync", lambda e, i=i, o=o: e.dma_start(out=dbg[:, 4 * i:4 * i + 4, :], in_=o[:]), writes=[tt[i]], kind="d")
                            env.P.wait_all("sync", tt)
                            env.P.emit("dbg")
                    if "merge" in parts:
                        merge_stage(nc, semstack, dmaq, x1T, [o_retT, o_dsaT, o_memT], w_in, [wbr, wbd, wbm], wo, ln2g, ln2b, cur, dst)
                cur = dst
            elif s == "mixdbg":
                with contextlib.ExitStack() as outer:
                    x1T = outer.enter_context(nc.sbuf_tensor("x1T", [128, 8, L], BF16))
                    mix_load_stage(nc, semstack, dmaq, cur, ident, x1T)
                    o_dsaT = outer.enter_context(nc.sbuf_tensor("o_dsaT", [128, 4, L], BF16))
                    dsa_stage(nc, semstack, dmaq, x1T, o_dsaT, w_in, t5, ident, cJ, coh, ccausal, cpow2, gscr)
                    o_memT = outer.enter_context(nc.sbuf_tensor("o_memT", [128, 4, L], BF16))
                    mem_stage(nc, semstack, dmaq, x1T, o_memT, mem_d, w_in, wmkv, ident)
                    dbg = nc.dram_tensor("dbg", [128, 8, L], BF16, kind="ExternalOutput").ap()
                    with contextlib.ExitStack() as st:
                        env = Env(nc, semstack, dmaq, "dbg", st)
                        t1 = Tok(); t2 = Tok()
                        env.P.add("sync", lambda e: e.dma_start(out=dbg[:, 0:4, :], in_=o_dsaT[:]), writes=[t1], kind="d")
                        env.P.add("sync", lambda e: e.dma_start(out=dbg[:, 4:8, :], in_=o_memT[:]), writes=[t2], kind="d")
                        env.P.wait_all("sync", [t1, t2])
                        env.P.emit("dbg")
            elif s == "ffn2":
                dst = out_d if last else x2_d
                ffn_stage(nc, semstack, dmaq, "f2", cur, f2wi, f2wo, ln3g, ln3b, dst, ident)
                cur = dst
    return nc


_CACHE = {}


def _t5_bucket(n):
    n = np.maximum(n, 0)
    nf = np.maximum(n, 1).astype(np.float32)
    large = 16 + (np.log(nf / np.float32(16)) / np.float32(np.log(128 / 16)) * np.float32(16)).astype(np.int32)
    large = np.minimum(large, 31)
    return np.where(n < 16, n, large)


def make_consts():
    c = {}
    c["c_J"] = np.ascontiguousarray(np.eye(128, dtype=np.float32)[::-1])
    oh = np.zeros((32, 384), np.float32)
    for u in range(383):
        d = u - 127
        if d >= 0:
            oh[_t5_bucket(np.array(d)), u] += 1.0
            oh[31, u] -= 1.0
    c["c_oh"] = oh
    q = np.arange(128)[:, None]; sk = np.arange(128)[None, :]
    c["c_causal"] = np.where(sk <= q, 0.0, -1.0e30).astype(np.float32)
    half = 32
    freqs = (np.float32(10000.0) ** (-np.arange(half, dtype=np.float32) / np.float32(half))).astype(np.float32)
    ang = np.arange(L, dtype=np.float32)[None, :] * freqs[np.arange(128) % 32][:, None]
    c["c_cos"] = np.cos(ang).astype(np.float32)
    c["c_sin"] = np.sin(ang).astype(np.float32)
    lg = np.log(np.array(GAMMAS, dtype=np.float32))
    i = np.arange(128)
    dec = np.zeros((128, 4, 128), np.float32)
    for h in range(4):
        diff = i[None, :] - i[:, None]
        dec[:, h, :] = np.where(diff >= 0, np.exp(np.maximum(diff, 0).astype(np.float32) * lg[h]), 0.0)
    c["c_decay"] = np.ascontiguousarray(dec[:, [0, 2, 1, 3], :]).reshape(128, 512)
    kdec = np.zeros((128, 4, 64), np.float32)
    for h in range(4):
        kdec[:, h, :] = np.exp((127 - i).astype(np.float32) * lg[h])[:, None]
    c["c_kdec"] = kdec.reshape(128, 256)
    qdec = np.zeros((128, 2, 128), np.float32)
    for p in range(128):
        for cc in range(2):
            qdec[p, cc, :] = np.exp((i + 1).astype(np.float32) * lg[2 * cc + p // 64])
    c["c_qdec"] = qdec.reshape(128, 256)
    c["c_pow2"] = np.tile((2.0 ** -(np.arange(NIT) + 1.0)).astype(np.float32)[None, :], (128, 1))
    return c


def make_in_maps(inputs):
    f = lambda a: np.ascontiguousarray(np.asarray(a, dtype=np.float32))
    shared = {
        "ffn1_w_in": f(inputs["ffn1_w_in"][0]), "ffn1_w_out": f(inputs["ffn1_w_out"][0]),
        "ln1_g": f(inputs["ln1_g"]), "ln1_b": f(inputs["ln1_b"]),
        "w_in": f(inputs["w_in"][0]), "t5_table": f(inputs["t5_table"]),
        "ret_gn_g": f(inputs["ret_gn_g"]), "ret_gn_b": f(inputs["ret_gn_b"]),
        "w_mem_kv": f(inputs["w_mem_kv"][0]),
        "w_br_ret": f(inputs["w_br_ret"][0]), "w_br_dsa": f(inputs["w_br_dsa"][0]), "w_br_mem": f(inputs["w_br_mem"][0]),
        "w_out": f(inputs["w_out"][0]), "ln2_g": f(inputs["ln2_g"]), "ln2_b": f(inputs["ln2_b"]),
        "ffn2_w_in": f(inputs["ffn2_w_in"][0]), "ffn2_w_out": f(inputs["ffn2_w_out"][0]),
        "ln3_g": f(inputs["ln3_g"]), "ln3_b": f(inputs["ln3_b"]),
        "c_ident": np.eye(128, dtype=np.float32),
    }
    shared.update(make_consts())
    x = f(inputs["x"])
    mem = f(inputs["mem"])
    return [dict(shared, x=x[b], mem=mem[b]) for b in range(x.shape[0])]


def kernel(**inputs):
    if "nc" not in _CACHE:
        _CACHE["nc"] = build_nc()
    nc = _CACHE["nc"]
    in_maps = make_in_maps(inputs)
    res = run_bass_kernel_spmd(nc, in_maps, core_ids=list(range(len(in_maps))))
    return np.stack([np.asarray(r["out"], dtype=np.float32) for r in res.results], axis=0)
```

```python
import contextlib
import numpy as np
import concourse.bass as bass
import concourse.mybir as mybir
from concourse.bass_utils import run_bass_kernel_spmd

F32 = mybir.dt.float32
BF16 = mybir.dt.bfloat16
AF = mybir.ActivationFunctionType
ALU = mybir.AluOpType

L = 2048
D = 1024
DFF = 2816
NCH = DFF // 128
ALPHA = 2.0 ** 0.25
LN_EPS = 1e-5
W_IN_COLS = 7240

ENGS = ("pe", "dve", "act", "pool", "sync")
SEM_ROT = {"c": 8000}
DMA_RING = 8
RELAX_SAME_ENGINE = True


class Tok:
    __slots__ = ("w", "r", "name")

    def __init__(self, name=""):
        self.w = None
        self.r = {}
        self.name = name


class Ins:
    __slots__ = ("eng", "kind", "fn", "deps", "idx", "inc", "sem", "val")

    def __init__(self, eng, kind, fn):
        self.eng = eng
        self.kind = kind
        self.fn = fn
        self.deps = []
        self.inc = False
        self.sem = None
        self.val = 0


class Prog:
    def __init__(self, nc, semstack, dmaq):
        self.nc = nc
        self.semstack = semstack
        self.dmaq = dmaq
        self.streams = {e: [] for e in ENGS}
        self.n = 0

    def add(self, eng, fn, reads=(), writes=(), kind="c"):
        ins = Ins(eng, kind, fn)
        st = self.streams[eng]
        ins.idx = len(st)
        deps = {}

        def dep(d, typ):
            if d is None or d is ins:
                return
            if d.eng == eng and d.kind == "c" and kind == "c":
                if eng == "pe":
                    return
                if typ != "RAW":
                    return
                if RELAX_SAME_ENGINE and eng in ("dve", "act") and ins.idx - d.idx >= 2:
                    return
            deps[id(d)] = d

        for t in reads:
            dep(t.w, "RAW")
        for t in writes:
            dep(t.w, "WAW")
            for d in t.r.values():
                dep(d, "WAR")
        if kind == "d":
            q = self.dmaq.setdefault(eng, {"n": 0, "last": {}, "sems": []})
            n = q["n"]
            q["n"] += 1
            r = n % DMA_RING
            if len(q["sems"]) <= r:
                q["sems"].append(self.semstack.enter_context(self.nc.semaphore(f"s_dma_{eng}_{r}")))
            ins.sem = q["sems"][r]
            ins.val = 16 * (n // DMA_RING + 1)
            ins.inc = True
            prev = q["last"].get(r)
            if prev is not None:
                deps[id(prev)] = prev
            q["last"][r] = ins
        ins.deps = list(deps.values())
        for d in ins.deps:
            d.inc = True
        for t in reads:
            t.r[eng + kind] = ins
        for t in writes:
            t.w = ins
            t.r = {}
        st.append(ins)
        self.n += 1
        return ins

    def wait_all(self, eng, toks):
        return self.add(eng, None, reads=toks, kind="w")

    def emit(self, name):
        nc = self.nc
        nsem = 0
        for e in ENGS:
            for kind in ("c",):
                cnt = 0
                cur = None
                for ins in self.streams[e]:
                    if ins.kind != kind or not ins.inc:
                        continue
                    if cur is None or cnt >= SEM_ROT[kind]:
                        cur = self.semstack.enter_context(nc.semaphore(f"s_{name}_{e}_{kind}_{nsem}"))
                        nsem += 1
                        cnt = 0
                    cnt += 1
                    ins.sem = cur
                    ins.val = cnt * (16 if kind == "d" else 1)
        with nc.Block() as block:
            engmap = {"pe": block.tensor, "dve": block.vector, "act": block.scalar,
                      "pool": block.gpsimd, "sync": block.sync}
            for e in ENGS:
                stream = self.streams[e]
                if not stream:
                    continue

                def body(eobj, stream=stream):
                    waited = {}
                    for ins in stream:
                        for d in ins.deps:
                            k = id(d.sem)
                            if waited.get(k, 0) >= d.val:
                                continue
                            eobj.wait_ge(d.sem, d.val)
                            waited[k] = d.val
                        if ins.fn is None:
                            continue
                        r = ins.fn(eobj)
                        if ins.inc:
                            r.then_inc(ins.sem, 16 if ins.kind == "d" else 1)

                engmap[e](body)


def bcast_rows(ap2d, nrows):
    return bass.AP(ap2d.tensor, ap2d.offset, [[0, nrows], [1, ap2d.shape[-1]]])


def ffn_stage(nc, semstack, dmaq, name, src_d, w_in_d, w_out_d, g_d, b_d, dst_d, ident_d):
    with contextlib.ExitStack() as st:
        P = Prog(nc, semstack, dmaq)
        sb = lambda n, shape, dt: st.enter_context(nc.sbuf_tensor(f"{name}_{n}", shape, dt))
        ps = lambda n, shape, dt: st.enter_context(nc.psum_tensor(f"{name}_{n}", shape, dt))
        HT = 1024
        xT = [sb(f"xT{i}", [128, 8, HT], BF16) for i in range(2)]
        gT = sb("gT", [128, NCH, HT], BF16)
        wout = sb("wout", [128, NCH, D], BF16)
        wst = [sb(f"wst{i}", [128, 2, 8, 128], F32) for i in range(2)]
        wbf = [sb(f"wbf{i}", [128, 2, 8, 128], BF16) for i in range(3)]
        wost = [sb(f"wost{i}", [128, D], F32) for i in range(2)]
        xs = [sb(f"xs{i}", [128, D], F32) for i in range(2)]
        xb = [sb(f"xb{i}", [128, D], BF16) for i in range(2)]
        sA = [sb(f"sA{i}", [128, 512], F32) for i in range(2)]
        rr = [sb(f"rr{i}", [128, D], F32) for i in range(2)]
        oo = [sb(f"oo{i}", [128, D], F32) for i in range(2)]
        gam = sb("gam", [128, D], F32)
        bet = sb("bet", [128, D], F32)
        idf = sb("idf", [128, 128], F32)
        idb = sb("idb", [128, 128], BF16)
        epst = sb("epst", [128, 1], F32)
        stats = [sb(f"stats{i}", [128, 2, 6], F32) for i in range(2)]
        mv = [sb(f"mv{i}", [128, 2], F32) for i in range(2)]
        sd = [sb(f"sd{i}", [128, 1], F32) for i in range(2)]
        rstd = [sb(f"rstd{i}", [128, 1], F32) for i in range(2)]
        pT = [ps(f"pT{i}", [128, 1024], BF16) for i in range(2)]
        pB = [ps(f"pB{i}", [128, 512], F32) for i in range(6)]

        def toks(n, k):
            return [Tok(f"{n}{i}") for i in range(k)]

        t_xT = [toks("xT", 8) for _ in range(2)]
        t_gT = [[Tok() for _ in range(2)] for _ in range(NCH)]
        t_wout = toks("wout", NCH)
        t_wst = toks("wst", 2); t_wbf = toks("wbf", 3); t_wost = toks("wost", 2)
        t_xs = toks("xs", 2); t_xb = toks("xb", 2); t_sA = toks("sA", 2)
        t_ys = toks("ys", 2); t_rr = toks("rr", 2); t_oo = toks("oo", 2)
        t_gam = Tok(); t_bet = Tok(); t_idf = Tok(); t_idb = Tok(); t_eps = Tok()
        t_stats = toks("st", 2); t_mv = toks("mv", 2); t_sd = toks("sd", 2); t_rstd = toks("rs", 2)
        t_pT = toks("pT", 2); t_pB = toks("pB", 6)
        t_dst = toks("dst", 16)

        P.add("sync", lambda e: e.dma_start(out=idf[:], in_=ident_d), writes=[t_idf], kind="d")
        P.add("sync", lambda e: e.dma_start(out=gam[:], in_=bcast_rows(g_d, 128)), writes=[t_gam], kind="d")
        P.add("sync", lambda e: e.dma_start(out=bet[:], in_=bcast_rows(b_d, 128)), writes=[t_bet], kind="d")
        P.add("dve", lambda e: e.tensor_copy(out=idb[:], in_=idf[:]), reads=[t_idf], writes=[t_idb])
        P.add("dve", lambda e: e.memset(epst[:], LN_EPS), writes=[t_eps])

        cnt = {"x": 0, "pb": 0, "sa": 0, "c": 0}

        def stage_a(h):
            for tl in range(8):
                t = h * 8 + tl
                s = cnt["x"] % 2
                cnt["x"] += 1
                P.add("sync", lambda e, s=s, t=t: e.dma_start(out=xs[s][:], in_=src_d[t * 128:(t + 1) * 128, :]),
                      writes=[t_xs[s]], kind="d")
                P.add("dve", lambda e, s=s: e.tensor_copy(out=xb[s][:], in_=xs[s][:]), reads=[t_xs[s]], writes=[t_xb[s]])
                for kc in range(8):
                    P.add("pe", lambda e, s=s, kc=kc: e.transpose(out=pT[s][:, kc * 128:(kc + 1) * 128],
                                                                  in_=xb[s][:, kc * 128:(kc + 1) * 128], identity=idb[:]),
                          reads=[t_xb[s], t_idb], writes=[t_pT[s]])
                P.add("act", lambda e, s=s, h=h, tl=tl: e.activation(
                    out=xT[h][:, :, tl * 128:(tl + 1) * 128],
                    in_=pT[s][:].rearrange("p (a b) -> p a b", a=8), func=AF.Copy),
                    reads=[t_pT[s]], writes=[t_xT[h][tl]])

        wcount = {"c": 0}

        def load_w(c, with_out):
            s = wcount["c"] % 2
            bs = wcount["c"] % 3
            wcount["c"] += 1
            for j in range(2):
                col0 = j * DFF + c * 128
                P.add("sync", lambda e, s=s, j=j, col0=col0: e.dma_start(
                    out=wst[s][:, j], in_=w_in_d[:, col0:col0 + 128].rearrange("(kc p) n -> p kc n", p=128)),
                    writes=[t_wst[s]], kind="d")
            P.add("act", lambda e, s=s, bs=bs: e.activation(out=wbf[bs][:].rearrange("p a b c -> p (a b c)"),
                                                            in_=wst[s][:].rearrange("p a b c -> p (a b c)"), func=AF.Copy),
                  reads=[t_wst[s]], writes=[t_wbf[bs]])
            if with_out:
                P.add("sync", lambda e, s=s, c=c: e.dma_start(out=wost[s][:], in_=w_out_d[c * 128:(c + 1) * 128, :]),
                      writes=[t_wost[s]], kind="d")
                P.add("pool", lambda e, s=s, c=c: e.tensor_copy(out=wout[:, c, :], in_=wost[s][:]),
                      reads=[t_wost[s]], writes=[t_wout[c]])
            return bs

        wq = {"issued": 0, "bs": {}}

        def ensure_w(k):
            while wq["issued"] <= min(k, 2 * NCH - 1):
                i = wq["issued"]
                wq["bs"][i] = load_w(i % NCH, i < NCH)
                wq["issued"] += 1

        def stage_b(h):
            for c in range(NCH):
                ensure_w(h * NCH + c + 2)
                bs = wq["bs"][h * NCH + c]
                for tb in range(2):
                    pa = (cnt["pb"] % 2) * 2
                    cnt["pb"] += 1
                    for j in range(2):
                        for kc in range(8):
                            P.add("pe", lambda e, pa=pa, j=j, kc=kc, bs=bs, tb=tb, h=h: e.matmul(
                                pB[pa + j][:], lhsT=wbf[bs][:, j, kc, :], rhs=xT[h][:, kc, tb * 512:(tb + 1) * 512],
                                start=(kc == 0), stop=(kc == 7)),
                                reads=[t_wbf[bs]] + t_xT[h][tb * 4:(tb + 1) * 4], writes=[t_pB[pa + j]])
                    s = cnt["sa"] % 2
                    cnt["sa"] += 1
                    P.add("act", lambda e, s=s, pa=pa: e.activation(out=sA[s][:], in_=pB[pa][:], func=AF.Silu),
                          reads=[t_pB[pa]], writes=[t_sA[s]])
                    P.add("dve", lambda e, s=s, pa=pa, c=c, tb=tb: e.scalar_tensor_tensor(
                        out=gT[:, c, tb * 512:(tb + 1) * 512], in0=sA[s][:], scalar=0.5, in1=pB[pa + 1][:],
                        op0=ALU.mult, op1=ALU.mult),
                        reads=[t_sA[s], t_pB[pa + 1]], writes=[t_gT[c][tb]])

        def stage_c(h):
            for tl in range(8):
                t = h * 8 + tl
                s = cnt["c"] % 2
                cnt["c"] += 1
                pa = (cnt["pb"] % 2) * 2
                cnt["pb"] += 1
                sx = cnt["x"] % 2
                cnt["x"] += 1
                P.add("sync", lambda e, sx=sx, t=t: e.dma_start(out=xs[sx][:], in_=src_d[t * 128:(t + 1) * 128, :]),
                      writes=[t_xs[sx]], kind="d")
                for nh in range(2):
                    for kc in range(NCH):
                        P.add("pe", lambda e, pa=pa, nh=nh, kc=kc, tl=tl: e.matmul(
                            pB[pa + nh][:], lhsT=gT[:, kc, tl * 128:(tl + 1) * 128], rhs=wout[:, kc, nh * 512:(nh + 1) * 512],
                            start=(kc == 0), stop=(kc == NCH - 1)),
                            reads=[t_gT[kc][tl // 4], t_wout[kc]], writes=[t_pB[pa + nh]])
                    P.add("dve", lambda e, pa=pa, nh=nh, s=s, sx=sx: e.scalar_tensor_tensor(
                        out=rr[s][:, nh * 512:(nh + 1) * 512], in0=xs[sx][:, nh * 512:(nh + 1) * 512], scalar=ALPHA,
                        in1=pB[pa + nh][:], op0=ALU.mult, op1=ALU.add),
                        reads=[t_xs[sx], t_pB[pa + nh]], writes=[t_rr[s]])
                for k in range(2):
                    P.add("dve", lambda e, s=s, k=k: e.bn_stats(out=stats[s][:, k, :], in_=rr[s][:, k * 512:(k + 1) * 512]),
                          reads=[t_rr[s]], writes=[t_stats[s]])
                P.add("dve", lambda e, s=s: e.bn_aggr(out=mv[s][:], in_=stats[s][:].rearrange("p a b -> p (a b)")),
                      reads=[t_stats[s]], writes=[t_mv[s]])
                P.add("act", lambda e, s=s: e.activation(out=sd[s][:], in_=mv[s][:, 1:2], func=AF.Sqrt, bias=epst[:, 0:1], scale=1.0),
                      reads=[t_mv[s], t_eps], writes=[t_sd[s]])
                P.add("dve", lambda e, s=s: e.reciprocal(out=rstd[s][:], in_=sd[s][:]), reads=[t_sd[s]], writes=[t_rstd[s]])
                P.add("dve", lambda e, s=s: e.tensor_scalar(
                    out=rr[s][:], in0=rr[s][:], scalar1=mv[s][:, 0:1], scalar2=rstd[s][:, 0:1], op0=ALU.subtract, op1=ALU.mult),
                    reads=[t_rr[s], t_mv[s], t_rstd[s]], writes=[t_rr[s]])
                P.add("pool", lambda e, s=s: e.tensor_tensor(out=oo[s][:], in0=rr[s][:], in1=gam[:], op=ALU.mult),
                      reads=[t_rr[s], t_gam], writes=[t_oo[s]])
                P.add("pool", lambda e, s=s: e.tensor_tensor(out=oo[s][:], in0=oo[s][:], in1=bet[:], op=ALU.add),
                      reads=[t_oo[s], t_bet], writes=[t_oo[s]])
                P.add("pool", lambda e, s=s, t=t: e.dma_start(out=dst_d[t * 128:(t + 1) * 128, :], in_=oo[s][:]),
                      reads=[t_oo[s]], writes=[t_dst[t]], kind="d")

        ensure_w(1)
        stage_a(0)
        stage_b(0)
        stage_a(1)
        stage_c(0)
        stage_b(1)
        stage_c(1)
        P.wait_all("pool", t_dst)
        P.emit(name)


C_RQ, C_RK, C_RV, C_RG = 0, 256, 512, 1024
C_DQ, C_DK, C_DV, C_IQ, C_IK, C_IW, C_MQ, C_G = 1536, 2048, 2560, 3072, 3584, 3648, 3656, 4168
IW_SCALE = float(8 ** -0.5 * 64 ** -0.5)
NIT = 16
GAMMAS = [1.0 - 2.0 ** (-5.0 - h) for h in range(4)]


def toks(k):
    return [Tok() for _ in range(k)]


class Env:
    def __init__(self, nc, semstack, dmaq, name, st):
        self.nc = nc
        self.P = Prog(nc, semstack, dmaq)
        self.name = name
        self.st = st
        self.k = 0

    def sb(self, n, shape, dt):
        return self.st.enter_context(self.nc.sbuf_tensor(f"{self.name}_{n}", shape, dt))

    def ps(self, n, shape, dt):
        return self.st.enter_context(self.nc.psum_tensor(f"{self.name}_{n}", shape, dt))


class WStream:
    def __init__(self, env, nst=2, nbf=3):
        self.env = env
        self.st = [env.sb(f"wsst{i}", [128, 8, 128], F32) for i in range(nst)]
        self.bf = [env.sb(f"wsbf{i}", [128, 8, 128], BF16) for i in range(nbf)]
        self.t_st = toks(nst)
        self.t_bf = toks(nbf)
        self.n = 0

    def load(self, pieces, KC=8, dst=None, t_dst=None):
        P = self.env.P
        s = self.n % len(self.st)
        b = self.n % len(self.bf)
        self.n += 1
        stt = self.st[s]
        for (c0, src, sc) in pieces:
            w = src.shape[-1]
            P.add("sync", lambda e, stt=stt, c0=c0, src=src, w=w, KC=KC: e.dma_start(
                out=stt[:, 0:KC, c0:c0 + w], in_=src.rearrange("(kc p) n -> p kc n", p=128)),
                writes=[self.t_st[s]], kind="d")
        if dst is None:
            tot = max(c0 + src.shape[-1] for (c0, src, sc) in pieces)
            out_t = self.bf[b]
            out_fn = lambda c0, w: out_t[:, 0:KC, c0:c0 + w]
            t_out = self.t_bf[b]
        else:
            out_fn = dst
            t_out = t_dst
        if all(sc == 1.0 for (_, _, sc) in pieces):
            lo = min(c0 for (c0, _, _) in pieces)
            hi = max(c0 + src.shape[-1] for (c0, src, _) in pieces)
            P.add("pool", lambda e, lo=lo, hi=hi, stt=stt, KC=KC: e.tensor_copy(out=out_fn(lo, hi - lo), in_=stt[:, 0:KC, lo:hi]),
                  reads=[self.t_st[s]], writes=[t_out])
        else:
            for (c0, src, sc) in pieces:
                w = src.shape[-1]
                P.add("dve", lambda e, c0=c0, w=w, sc=sc, stt=stt, KC=KC: e.tensor_scalar(
                    out=out_fn(c0, w), in0=stt[:, 0:KC, c0:c0 + w], scalar1=float(sc), scalar2=None, op0=ALU.mult),
                    reads=[self.t_st[s]], writes=[t_out])
        return (self.bf[b] if dst is None else None), t_out


def load_transposed(env, src_d, ntiles, dstT, t_dstT, idb, t_idb, pT, t_pT):
    P = env.P
    xs = [env.sb(f"ltxs{i}", [128, D], F32) for i in range(2)]
    xb = [env.sb(f"ltxb{i}", [128, D], BF16) for i in range(2)]
    t_xs = toks(2); t_xb = toks(2)
    for t in range(ntiles):
        s = t % 2
        P.add("sync", lambda e, s=s, t=t: e.dma_start(out=xs[s][:], in_=src_d[t * 128:(t + 1) * 128, :]),
              writes=[t_xs[s]], kind="d")
        P.add("dve", lambda e, s=s: e.tensor_copy(out=xb[s][:], in_=xs[s][:]), reads=[t_xs[s]], writes=[t_xb[s]])
        for kc in range(8):
            P.add("pe", lambda e, s=s, kc=kc: e.transpose(out=pT[s][:, kc * 128:(kc + 1) * 128],
                                                          in_=xb[s][:, kc * 128:(kc + 1) * 128], identity=idb[:]),
                  reads=[t_xb[s], t_idb], writes=[t_pT[s]])
        P.add("act", lambda e, s=s, t=t: e.activation(
            out=dstT[:, :, t * 128:(t + 1) * 128], in_=pT[s][:].rearrange("p (a b) -> p a b", a=8), func=AF.Copy),
            reads=[t_pT[s]], writes=[t_dstT[t]])


def load_ident(env, ident_d):
    P = env.P
    idf = env.sb("idf", [128, 128], F32)
    idb = env.sb("idb", [128, 128], BF16)
    t_idf = Tok(); t_idb = Tok()
    P.add("sync", lambda e: e.dma_start(out=idf[:], in_=ident_d), writes=[t_idf], kind="d")
    P.add("dve", lambda e: e.tensor_copy(out=idb[:], in_=idf[:]), reads=[t_idf], writes=[t_idb])
    return idf, t_idf, idb, t_idb


class Banks:
    def __init__(self, env, n):
        self.b = [env.ps(f"bk{i}", [128, 512], F32) for i in range(n)]
        self.t = toks(n)
        self.i = 0
        self.n = n

    def next(self):
        i = self.i % self.n
        self.i += 1
        return self.b[i], self.t[i]


def proj_fm(env, ws, banks, pieces, ncols, rhs_fn, rhs_toks_fn, nblk, blkw, evac, KC=8):
    P = env.P
    wt, t_w = ws.load(pieces, KC=KC)
    for tb in range(nblk):
        bank, t_bank = banks.next()
        for kc in range(KC):
            P.add("pe", lambda e, bank=bank, kc=kc, tb=tb, wt=wt: e.matmul(
                bank[0:ncols, 0:blkw], lhsT=wt[:, kc, 0:ncols], rhs=rhs_fn(kc, tb), start=(kc == 0), stop=(kc == KC - 1)),
                reads=[t_w] + rhs_toks_fn(tb), writes=[t_bank])
        evac(tb, bank, t_bank)


def mix_load_stage(nc, semstack, dmaq, x1_d, ident_d, x1T):
    with contextlib.ExitStack() as st:
        env = Env(nc, semstack, dmaq, "m0", st)
        idf, t_idf, idb, t_idb = load_ident(env, ident_d)
        pT = [env.ps(f"pT{i}", [128, 1024], BF16) for i in range(2)]
        t_pT = toks(2)
        t_x1T = toks(16)
        load_transposed(env, x1_d, 16, x1T, t_x1T, idb, t_idb, pT, t_pT)
        env.P.emit("m0")


def mem_stage(nc, semstack, dmaq, x1T, oT, mem_d, w_in, wmkv, ident_d):
    with contextlib.ExitStack() as st:
        env = Env(nc, semstack, dmaq, "m3", st)
        P = env.P
        idf, t_idf, idb, t_idb = load_ident(env, ident_d)
        pT = [env.ps(f"pT{i}", [128, 1024], BF16) for i in range(2)]
        t_pT = toks(2)
        banks = Banks(env, 2)
        pN2 = [env.ps(f"pN{i}", [128, 512], F32) for i in range(2)]; t_pN2 = toks(2)
        pD2 = [env.ps(f"pD{i}", [128, 512], F32) for i in range(2)]; t_pD2 = toks(2)
        memT = env.sb("memT", [128, 8, 256], BF16); t_memT = toks(2)
        mkT = env.sb("mkT", [128, 4, 256], BF16); t_mkT = toks(4)
        mvv = env.sb("mv", [128, 2, 512], BF16); t_mv = toks(2)
        wmv = env.sb("wmv", [128, 8, 512], BF16); t_wmv = toks(4)
        mqT = env.sb("mqT", [128, 4, L], BF16); t_mqT = [toks(4) for _ in range(4)]
        ones = env.sb("ones", [128, 128], BF16); t_ones = Tok()
        E = [env.sb(f"E{i}", [128, 512], BF16) for i in range(4)]; t_E = toks(4)
        rden = [env.sb(f"rden{i}", [128, 512], F32) for i in range(2)]; t_rden = toks(2)
        ws = WStream(env)
        P.add("pool", lambda e: e.memset(ones[:], 1.0), writes=[t_ones])
        load_transposed(env, mem_d, 2, memT, t_memT, idb, t_idb, pT, t_pT)
        t_x1T = []
        for h in range(4):
            def evac(tb, bank, t_bank, h=h):
                P.add("act", lambda e: e.activation(out=mkT[:, h, :], in_=bank[:, 0:256], func=AF.Copy),
                      reads=[t_bank], writes=[t_mkT[h]])
            proj_fm(env, ws, banks, [(0, wmkv[:, h * 128:(h + 1) * 128], 1.0)], 128,
                    lambda kc, tb: memT[:, kc, :], lambda tb: t_memT, 1, 256, evac)
        for c in range(4):
            ws.load([(0, wmkv[:, 512 + c * 128:512 + (c + 1) * 128], 1.0)],
                    dst=lambda c0, w, c=c: wmv[:, :, c * 128 + c0:c * 128 + c0 + w], t_dst=t_wmv[c])
        for mt in range(2):
            bank, t_bank = banks.next()
            for kc in range(8):
                P.add("pe", lambda e, bank=bank, kc=kc, mt=mt: e.matmul(
                    bank[:], lhsT=memT[:, kc, mt * 128:(mt + 1) * 128], rhs=wmv[:, kc, :], start=(kc == 0), stop=(kc == 7)),
                    reads=[t_memT[mt]] + t_wmv, writes=[t_bank])
            P.add("act", lambda e, bank=bank, mt=mt: e.activation(out=mvv[:, mt, :], in_=bank[:], func=AF.Copy),
                  reads=[t_bank], writes=[t_mv[mt]])
        for h in range(4):
            def evac(tb, bank, t_bank, h=h):
                P.add("act", lambda e: e.activation(out=mqT[:, h, tb * 512:(tb + 1) * 512], in_=bank[:], func=AF.Copy,
                                                    scale=float(128 ** -0.5)),
                      reads=[t_bank], writes=[t_mqT[h][tb]])
            proj_fm(env, ws, banks, [(0, w_in[:, C_MQ + h * 128:C_MQ + (h + 1) * 128], 1.0)], 128,
                    lambda kc, tb: x1T[:, kc, tb * 512:(tb + 1) * 512], lambda tb: [], 4, 512, evac)
        it = 0
        for h in range(4):
            for qb in range(4):
                for mt in range(2):
                    bank, t_bank = banks.next()
                    ei = (it * 2 + mt) % 4
                    P.add("pe", lambda e, bank=bank, h=h, qb=qb, mt=mt: e.matmul(
                        bank[:], lhsT=mkT[:, h, mt * 128:(mt + 1) * 128], rhs=mqT[:, h, qb * 512:(qb + 1) * 512],
                        start=True, stop=True), reads=[t_mkT[h], t_mqT[h][qb]], writes=[t_bank])
                    P.add("act", lambda e, bank=bank, ei=ei: e.activation(out=E[ei][:], in_=bank[:], func=AF.Exp),
                          reads=[t_bank], writes=[t_E[ei]])
                pN = pN2[it % 2]; t_pN = t_pN2[it % 2]; pD = pD2[it % 2]; t_pD = t_pD2[it % 2]
                for mt in range(2):
                    ei = (it * 2 + mt) % 4
                    P.add("pe", lambda e, ei=ei, h=h, mt=mt, pN=pN: e.matmul(
                        pN[:], lhsT=mvv[:, mt, h * 128:(h + 1) * 128], rhs=E[ei][:], start=(mt == 0), stop=(mt == 1)),
                        reads=[t_mv[mt], t_E[ei]], writes=[t_pN])
                    P.add("pe", lambda e, ei=ei, mt=mt, pD=pD: e.matmul(
                        pD[:], lhsT=ones[:], rhs=E[ei][:], start=(mt == 0), stop=(mt == 1)),
                        reads=[t_ones, t_E[ei]], writes=[t_pD])
                r = it % 2
                P.add("dve", lambda e, r=r, pD=pD: e.reciprocal(out=rden[r][:], in_=pD[:]), reads=[t_pD], writes=[t_rden[r]])
                P.add("dve", lambda e, r=r, h=h, qb=qb, pN=pN: e.tensor_tensor(
                    out=oT[:, h, qb * 512:(qb + 1) * 512], in0=pN[:], in1=rden[r][:], op=ALU.mult),
                    reads=[t_pN, t_rden[r]], writes=[Tok()])
                it += 1
        env.P.emit("m3")


def dsa_stage(nc, semstack, dmaq, x1T, oT, w_in, t5_d, ident_d, J_d, oh_d, causal_d, pow2_d, gscr_d):
    with contextlib.ExitStack() as st:
        env = Env(nc, semstack, dmaq, "m1", st)
        P = env.P
        idf, t_idf, idb, t_idb = load_ident(env, ident_d)
        banks = Banks(env, 3)
        pO = [[env.ps(f"pO{i}{j}", [128, 512], F32) for j in range(2)] for i in range(2)]
        t_pO = [toks(2) for _ in range(2)]
        pM = env.ps("pM", [128, 1024], BF16); t_pM = Tok()
        qT = env.sb("qT", [128, 4, L], BF16); t_qT = [toks(4) for _ in range(4)]
        kT = env.sb("kT", [128, 4, L], BF16); t_kT = [toks(4) for _ in range(4)]
        qiT = env.sb("qiT", [128, 4, L], BF16); t_qiT = [toks(4) for _ in range(4)]
        kiT = env.sb("kiT", [128, L], BF16); t_kiT = toks(4)
        vaug = env.sb("vaug", [128, 16, 8, 65], BF16); t_v = toks(16); t_vones = Tok()
        iw = env.sb("iw", [128, 16, 8], F32); t_iw = toks(16)
        wv = env.sb("wv", [128, 8, 512], BF16); t_wv = toks(4)
        wiw = env.sb("wiw", [128, 8, 8], BF16); t_wiw = Tok()
        Sc2 = [env.sb(f"Sc{i}", [128, L], F32) for i in range(2)]; t_Sc2 = toks(2)
        junk2 = [env.sb(f"junk{i}", [128, L], BF16) for i in range(2)]; t_junk2 = toks(2)
        mask = [env.sb(f"mask{i}", [128, L], BF16) for i in range(2)]; t_mask = toks(2)
        maskTp = [env.sb(f"maskTp{i}", [128, 16, 2, 128], BF16) for i in range(2)]
        t_maskTp = [toks(2) for _ in range(2)]
        tI = [env.sb(f"tI{i}", [128, 512], F32) for i in range(2)]; t_tI = toks(2)
        E = [env.sb(f"E{i}", [128, 512], BF16) for i in range(2)]; t_E = toks(2)
        PT = [env.sb(f"PT{i}", [128, 512], BF16) for i in range(3)]; t_PT = toks(3)
        BT = env.sb("BT", [128, 8, 256], BF16); t_BT = toks(8)
        H = Sc2[1][:].rearrange("p (h n) -> p h n", h=8); t_H = t_Sc2[1]
        rden8 = [env.sb(f"rden{i}", [128, 2, 4], F32) for i in range(2)]; t_rden8 = toks(2)
        o_tm = [env.sb(f"otm{i}", [128, 512], BF16) for i in range(2)]; t_otm = toks(2)
        Jf = env.sb("Jf", [128, 128], F32); t_J = Tok()
        caus = env.sb("caus", [128, 128], F32); t_caus = Tok()
        pow2 = env.sb("pow2", [128, NIT], F32); t_pow2 = Tok()
        tabS = env.sb("tabS", [32, 8], F32); t_tab = Tok()
        ohS = env.sb("ohS", [32, 384], F32); t_oh = Tok()
        Gs = env.sb("Gs", [8, 384], F32); t_Gs = Tok()
        thrneg = env.sb("thrneg", [128, 1], F32); t_thrneg = Tok()
        mx8_2 = [env.sb(f"mx8{i}", [128, 8], F32) for i in range(2)]; t_mx8_2 = toks(2)
        mn_2 = [env.sb(f"mn{i}", [128, 1], F32) for i in range(2)]; t_mn_2 = toks(2)
        rng_2 = [env.sb(f"rng{i}", [128, 1], F32) for i in range(2)]; t_rng_2 = toks(2)
        thr_2 = [env.sb(f"thr{i}", [128, 1], F32) for i in range(2)]; t_thr_2 = toks(2)
        S2_2 = [env.sb(f"S2{i}", [128, NIT], F32) for i in range(2)]; t_S2_2 = toks(2)
        cnt_2 = [env.sb(f"cnt{i}", [128, 1], F32) for i in range(2)]; t_cnt_2 = toks(2)
        ee_2 = [env.sb(f"ee{i}", [128, 1], F32) for i in range(2)]; t_ee_2 = toks(2)
        ws = WStream(env)
        t_gscr = Tok()
        print("[dsa] sbuf bytes remaining/partition:", nc.sbuf_bytes_remaining() if callable(nc.sbuf_bytes_remaining) else nc.sbuf_bytes_remaining)

        P.add("sync", lambda e: e.dma_start(out=Jf[:], in_=J_d), writes=[t_J], kind="d")
        P.add("sync", lambda e: e.dma_start(out=caus[:], in_=causal_d), writes=[t_caus], kind="d")
        P.add("sync", lambda e: e.dma_start(out=pow2[:], in_=pow2_d), writes=[t_pow2], kind="d")
        P.add("sync", lambda e: e.dma_start(out=tabS[:], in_=t5_d), writes=[t_tab], kind="d")
        P.add("sync", lambda e: e.dma_start(out=ohS[:], in_=oh_d), writes=[t_oh], kind="d")
        P.add("pool", lambda e: e.memset(vaug[:, :, :, 64:65], 1.0), writes=[t_vones])
        for i in range(2):
            P.add("pool", lambda e, i=i: e.memset(maskTp[i][:], 0.0), writes=t_maskTp[i])
        P.add("pool", lambda e: e.memset(thrneg[:], -1.0e29), writes=[t_thrneg])
        bank, t_bank = banks.next()
        P.add("pe", lambda e, bank=bank: e.matmul(bank[0:8, 0:384], lhsT=tabS[:], rhs=ohS[:], start=True, stop=True),
              reads=[t_tab, t_oh], writes=[t_bank])
        P.add("act", lambda e, bank=bank: e.activation(out=Gs[:], in_=bank[0:8, 0:384], func=AF.Copy), reads=[t_bank], writes=[t_Gs])
        P.add("sync", lambda e: e.dma_start(out=gscr_d, in_=Gs[:]), reads=[t_Gs], writes=[t_gscr], kind="d")
        hank = bass.AP(gscr_d.tensor, gscr_d.offset, [[1, 128], [384, 8], [1, 256]])
        P.add("sync", lambda e: e.dma_start(out=H, in_=hank), reads=[t_gscr], writes=[t_H], kind="d")
        for h in range(8):
            bank, t_bank = banks.next()
            P.add("pe", lambda e, bank=bank, h=h: e.matmul(bank[:, 0:256], lhsT=Jf[:], rhs=H[:, h, :], start=True, stop=True),
                  reads=[t_J, t_H], writes=[t_bank])
            P.add("act", lambda e, bank=bank, h=h: e.activation(out=BT[:, h, :], in_=bank[:, 0:256], func=AF.Copy),
                  reads=[t_bank], writes=[t_BT[h]])

        ev = {"i": 0}

        def evac_to(dstfn, t_dstfn, scale, only_act=False):
            def evac(tb, bank, t_bank):
                eng = "act" if (only_act or ev["i"] % 2 == 0) else "dve"
                ev["i"] += 1
                if eng == "act":
                    P.add("act", lambda e: e.activation(out=dstfn(tb), in_=bank[:], func=AF.Copy, scale=float(scale)),
                          reads=[t_bank], writes=[t_dstfn(tb)])
                else:
                    P.add("dve", lambda e: e.tensor_scalar(out=dstfn(tb), in0=bank[:], scalar1=float(scale), scalar2=None, op0=ALU.mult),
                          reads=[t_bank], writes=[t_dstfn(tb)])
            return evac

        xrhs = lambda kc, tb: x1T[:, kc, tb * 512:(tb + 1) * 512]

        def proj_indexer():
            for c in range(4):
                proj_fm(env, ws, banks, [(0, w_in[:, C_IQ + c * 128:C_IQ + (c + 1) * 128], 1.0)], 128, xrhs, lambda tb: [], 4, 512,
                        evac_to(lambda tb, c=c: qiT[:, c, tb * 512:(tb + 1) * 512], lambda tb, c=c: t_qiT[c][tb], 1.0))
            proj_fm(env, ws, banks, [(0, w_in[:, C_IK:C_IK + 64], 1.0), (64, w_in[:, C_IK:C_IK + 64], 1.0)], 128, xrhs, lambda tb: [], 4, 512,
                    evac_to(lambda tb: kiT[:, tb * 512:(tb + 1) * 512], lambda tb: t_kiT[tb], 1.0))
            ws.load([(0, w_in[:, C_IW:C_IW + 8], 1.0)], dst=lambda c0, w: wiw[:, :, c0:c0 + w], t_dst=t_wiw)
            for t in range(16):
                bank, t_bank = banks.next()
                for kc in range(8):
                    P.add("pe", lambda e, bank=bank, kc=kc, t=t: e.matmul(
                        bank[:, 0:8], lhsT=x1T[:, kc, t * 128:(t + 1) * 128], rhs=wiw[:, kc, :], start=(kc == 0), stop=(kc == 7)),
                        reads=[t_wiw], writes=[t_bank])
                P.add("dve", lambda e, bank=bank, t=t: e.tensor_scalar(out=iw[:, t, :], in0=bank[:, 0:8], scalar1=IW_SCALE, scalar2=None, op0=ALU.mult),
                      reads=[t_bank], writes=[t_iw[t]])

        def proj_qkv():
            for c in range(4):
                proj_fm(env, ws, banks, [(0, w_in[:, C_DQ + c * 128:C_DQ + (c + 1) * 128], 1.0)], 128, xrhs, lambda tb: [], 4, 512,
                        evac_to(lambda tb, c=c: qT[:, c, tb * 512:(tb + 1) * 512], lambda tb, c=c: t_qT[c][tb], 0.125, only_act=True))
                proj_fm(env, ws, banks, [(0, w_in[:, C_DK + c * 128:C_DK + (c + 1) * 128], 1.0)], 128, xrhs, lambda tb: [], 4, 512,
                        evac_to(lambda tb, c=c: kT[:, c, tb * 512:(tb + 1) * 512], lambda tb, c=c: t_kT[c][tb], 1.0, only_act=True))
            for c in range(4):
                ws.load([(0, w_in[:, C_DV + c * 128:C_DV + (c + 1) * 128], 1.0)],
                        dst=lambda c0, w, c=c: wv[:, :, c * 128 + c0:c * 128 + c0 + w], t_dst=t_wv[c])
            for t in range(16):
                bank, t_bank = banks.next()
                for kc in range(8):
                    P.add("pe", lambda e, bank=bank, kc=kc, t=t: e.matmul(
                        bank[:], lhsT=x1T[:, kc, t * 128:(t + 1) * 128], rhs=wv[:, kc, :], start=(kc == 0), stop=(kc == 7)),
                        reads=t_wv, writes=[t_bank])
                P.add("act", lambda e, bank=bank, t=t: e.activation(out=vaug[:, t, :, 0:64], in_=bank[:].rearrange("p (h d) -> p h d", h=8),
                                                                    func=AF.Copy), reads=[t_bank], writes=[t_v[t]])

        cI = {"i": 0}

        def idx_phase(n):
            Sc = Sc2[n % 2]; t_Sc = t_Sc2[n % 2]
            W = 128 * (n + 1)
            nb = (W + 511) // 512
            for h in range(8):
                hp = (h % 2) * 64
                for kb in range(nb):
                    w = min(512, W - kb * 512)
                    bank, t_bank = banks.next()
                    P.add("pe", lambda e, bank=bank, h=h, hp=hp, kb=kb, w=w, n=n: e.matmul(
                        bank[:, 0:w], lhsT=qiT[hp:hp + 64, h // 2, n * 128:(n + 1) * 128], rhs=kiT[hp:hp + 64, kb * 512:kb * 512 + w],
                        start=True, stop=True),
                        reads=[t_qiT[h // 2][n // 4]] + t_kiT[0:nb], writes=[t_bank])
                    s = cI["i"] % 2
                    cI["i"] += 1
                    P.add("act", lambda e, bank=bank, s=s, w=w: e.activation(out=tI[s][:, 0:w], in_=bank[:, 0:w], func=AF.Relu),
                          reads=[t_bank], writes=[t_tI[s]])
                    if h == 0:
                        P.add("dve", lambda e, s=s, kb=kb, w=w, n=n: e.tensor_scalar(
                            out=Sc[:, kb * 512:kb * 512 + w], in0=tI[s][:, 0:w], scalar1=iw[:, n, 0:1], scalar2=None, op0=ALU.mult),
                            reads=[t_tI[s], t_iw[n]], writes=[t_Sc])
                    else:
                        P.add("dve", lambda e, s=s, kb=kb, w=w, n=n, h=h: e.scalar_tensor_tensor(
                            out=Sc[:, kb * 512:kb * 512 + w], in0=tI[s][:, 0:w], scalar=iw[:, n, h:h + 1],
                            in1=Sc[:, kb * 512:kb * 512 + w], op0=ALU.mult, op1=ALU.add),
                            reads=[t_tI[s], t_iw[n], t_Sc], writes=[t_Sc])
            P.add("dve", lambda e, n=n: e.tensor_tensor(out=Sc[:, n * 128:(n + 1) * 128], in0=Sc[:, n * 128:(n + 1) * 128],
                                                        in1=caus[:], op=ALU.add),
                  reads=[t_Sc, t_caus], writes=[t_Sc])

        def bis_ops(n):
            ch = n % 2
            Sc = Sc2[ch]; t_Sc = t_Sc2[ch]; junk = junk2[ch]; t_junk = t_junk2[ch]
            mx8 = mx8_2[ch]; t_mx8 = t_mx8_2[ch]; mn = mn_2[ch]; t_mn = t_mn_2[ch]; rng = rng_2[ch]; t_rng = t_rng_2[ch]
            thr = thr_2[ch]; t_thr = t_thr_2[ch]; S2 = S2_2[ch]; t_S2 = t_S2_2[ch]; cnt = cnt_2[ch]; t_cnt = t_cnt_2[ch]
            ee = ee_2[ch]; t_ee = t_ee_2[ch]
            W = 128 * (n + 1)
            mk = mask[ch]
            t_mk = t_mask[ch]
            if n < 2:
                yield lambda: P.add("dve", lambda e: e.tensor_scalar(out=mk[:, 0:W], in0=Sc[:, 0:W], scalar1=thrneg[:, 0:1], scalar2=None, op0=ALU.is_ge),
                                    reads=[t_Sc, t_thrneg], writes=[t_mk])
                return
            yield lambda: P.add("dve", lambda e: e.max(out=mx8[:], in_=Sc[:, 0:W]), reads=[t_Sc], writes=[t_mx8])
            yield lambda: P.add("dve", lambda e: e.tensor_reduce(out=mn[:], in_=Sc[:, 0:n * 128], axis=mybir.AxisListType.X, op=ALU.min),
                                reads=[t_Sc], writes=[t_mn])
            yield lambda: P.add("dve", lambda e: e.tensor_tensor(out=rng[:], in0=mx8[:, 0:1], in1=mn[:], op=ALU.subtract),
                                reads=[t_mx8, t_mn], writes=[t_rng])
            yield lambda: P.add("dve", lambda e: e.scalar_tensor_tensor(out=thr[:], in0=rng[:], scalar=0.5, in1=mn[:], op0=ALU.mult, op1=ALU.add),
                                reads=[t_rng, t_mn], writes=[t_thr])
            yield lambda: P.add("dve", lambda e: e.tensor_scalar(out=S2[:], in0=pow2[:], scalar1=rng[:, 0:1], scalar2=None, op0=ALU.mult),
                                reads=[t_pow2, t_rng], writes=[t_S2])
            for k in range(NIT):
                yield lambda: P.add("dve", lambda e: e.tensor_scalar(out=junk[:, 0:W], in0=Sc[:, 0:W], scalar1=thr[:, 0:1], scalar2=None,
                                                                     op0=ALU.is_ge, op1=ALU.add, accum_out=cnt[:, 0:1]),
                                    reads=[t_Sc, t_thr], writes=[t_junk, t_cnt])
                yield lambda: P.add("dve", lambda e: e.tensor_scalar(out=ee[:], in0=cnt[:], scalar1=255.5, scalar2=0.5, op0=ALU.is_ge, op1=ALU.subtract),
                                    reads=[t_cnt], writes=[t_ee])
                yield lambda k=k: P.add("dve", lambda e: e.scalar_tensor_tensor(out=thr[:], in0=ee[:], scalar=S2[:, k:k + 1], in1=thr[:],
                                                                                op0=ALU.mult, op1=ALU.add),
                                        reads=[t_ee, t_S2, t_thr], writes=[t_thr])
            yield lambda: P.add("dve", lambda e: e.tensor_scalar(out=mk[:, 0:W], in0=Sc[:, 0:W], scalar1=thr[:, 0:1], scalar2=None, op0=ALU.is_ge),
                                reads=[t_Sc, t_thr], writes=[t_mk])

        def bis_pair(a, b):
            ga, gb = bis_ops(a), bis_ops(b)
            done_a = done_b = False
            while not (done_a and done_b):
                if not done_a:
                    f = next(ga, None)
                    if f is None:
                        done_a = True
                    else:
                        f()
                if not done_b:
                    f = next(gb, None)
                    if f is None:
                        done_b = True
                    else:
                        f()

        def maskT_phase(n):
            mk = mask[n % 2]; t_mk = t_mask[n % 2]
            mp = maskTp[(n // 2) % 2]; t_mp = t_maskTp[(n // 2) % 2][n % 2]
            for m0 in range(0, n + 1, 8):
                k = min(8, n + 1 - m0)
                for i in range(k):
                    m = m0 + i
                    P.add("pe", lambda e, i=i, m=m: e.transpose(out=pM[:, i * 128:(i + 1) * 128], in_=mk[:, m * 128:(m + 1) * 128], identity=idb[:]),
                          reads=[t_mk, t_idb], writes=[t_pM])
                P.add("act", lambda e, m0=m0, k=k, n=n: e.activation(
                    out=mp[:, m0:m0 + k, n % 2, :], in_=pM[:, 0:k * 128].rearrange("p (a b) -> p a b", a=k), func=AF.Copy),
                    reads=[t_pM], writes=[t_mp])

        cA = {"e": 0, "p": 0}

        def att_pair(kp):
            a, b = 2 * kp, 2 * kp + 1
            mp = maskTp[kp % 2]; t_mp = t_maskTp[kp % 2]
            steps = [(c, m0, hh) for c in range(4) for m0 in range(0, b + 1, 2) for hh in range(2)]

            def emit_scores(c, m0, hh):
                h = 2 * c + hh; hp = hh * 64
                bank, t_bank = banks.next()
                for i in range(2):
                    m = m0 + i
                    biases = []
                    if m == a - 1:
                        biases.append((0, 128))
                    if m == a:
                        biases.append((0, 0)); biases.append((128, 128))
                    if m == b:
                        biases.append((128, 0))
                    P.add("pe", lambda e, bank=bank, i=i, m=m, hp=hp, c=c, nb=len(biases): e.matmul(
                        bank[:, i * 256:(i + 1) * 256], lhsT=kT[hp:hp + 64, c, m * 128:(m + 1) * 128],
                        rhs=qT[hp:hp + 64, c, a * 128:(a + 2) * 128], start=True, stop=(nb == 0)),
                        reads=[t_kT[c][m // 4], t_qT[c][a // 4]], writes=[t_bank])
                    for bi, (qo, off) in enumerate(biases):
                        P.add("pe", lambda e, bank=bank, i=i, h=h, qo=qo, off=off, last=(bi == len(biases) - 1): e.matmul(
                            bank[:, i * 256 + qo:i * 256 + qo + 128], lhsT=idb[:], rhs=BT[:, h, off:off + 128], start=False, stop=last),
                            reads=[t_idb, t_BT[h]], writes=[t_bank])
                se = cA["e"] % 2
                cA["e"] += 1
                P.add("act", lambda e, bank=bank, se=se: e.activation(out=E[se][:], in_=bank[:], func=AF.Exp),
                      reads=[t_bank], writes=[t_E[se]])
                sp = cA["p"] % 3
                cA["p"] += 1
                P.add("pool", lambda e, se=se, sp=sp, m0=m0: e.tensor_tensor(
                    out=PT[sp][:], in0=E[se][:], in1=mp[:, m0:m0 + 2, :, :].rearrange("p a b q -> p (a b q)"), op=ALU.mult),
                    reads=[t_E[se]] + t_mp, writes=[t_PT[sp]])
                return sp

            def emit_pv(c, m0, hh, sp):
                h = 2 * c + hh
                for i in range(2):
                    m = m0 + i
                    for t in (a, b):
                        if m > t:
                            continue
                        P.add("pe", lambda e, sp=sp, i=i, m=m, h=h, hh=hh, c=c, t=t, a=a: e.matmul(
                            pO[t % 2][hh][:, c * 65:(c + 1) * 65], lhsT=PT[sp][:, i * 256 + (t - a) * 128:i * 256 + (t - a) * 128 + 128],
                            rhs=vaug[:, m, h, :], start=(m == 0), stop=(m == t)),
                            reads=[t_v[m], t_vones, t_PT[sp]], writes=[t_pO[t % 2][hh]])

            prev = None
            for st_ in steps:
                sp = emit_scores(*st_)
                if prev is not None:
                    emit_pv(*prev)
                prev = st_ + (sp,)
            emit_pv(*prev)
            for t in (a, b):
                r = t % 2
                for hh in range(2):
                    P.add("dve", lambda e, r=r, hh=hh: e.reciprocal(
                        out=rden8[r][:, hh, :], in_=pO[r][hh][:, 0:260].rearrange("p (c d) -> p c d", c=4)[:, :, 64]),
                        reads=[t_pO[r][hh]], writes=[t_rden8[r]])
                for h in range(8):
                    P.add("act", lambda e, r=r, h=h: e.activation(
                        out=o_tm[r][:, h * 64:(h + 1) * 64], in_=pO[r][h % 2][:, (h // 2) * 65:(h // 2) * 65 + 64], func=AF.Copy,
                        scale=rden8[r][:, h % 2, h // 2:h // 2 + 1]),
                        reads=[t_pO[r][h % 2], t_rden8[r]], writes=[t_otm[r]])
                for c4 in range(4):
                    P.add("pe", lambda e, r=r, c4=c4: e.transpose(out=pM[:, c4 * 128:(c4 + 1) * 128], in_=o_tm[r][:, c4 * 128:(c4 + 1) * 128], identity=idb[:]),
                          reads=[t_otm[r], t_idb], writes=[t_pM])
                P.add("dve", lambda e, t=t: e.tensor_copy(out=oT[:, :, t * 128:(t + 1) * 128], in_=pM[:, 0:512].rearrange("p (a b) -> p a b", a=4)),
                      reads=[t_pM], writes=[Tok()])

        proj_indexer()
        idx_phase(14); idx_phase(15); bis_pair(14, 15)
        proj_qkv()
        maskT_phase(14); maskT_phase(15)
        for k in range(7, -1, -1):
            a, b = 2 * k, 2 * k + 1
            if k > 0:
                idx_phase(a - 2); idx_phase(b - 2)
                bis_pair(a - 2, b - 2)
            att_pair(k)
            if k > 0:
                maskT_phase(a - 2); maskT_phase(b - 2)
        env.P.emit("m1")


def ret_stage(nc, semstack, dmaq, x1T, oT, w_in, gng_d, gnb_d, ident_d, cos_d, sin_d, decay_d, kdec_d, qdec_d):
    with contextlib.ExitStack() as st:
        env = Env(nc, semstack, dmaq, "m2", st)
        P = env.P
        idf, t_idf, idb, t_idb = load_ident(env, ident_d)
        banks = Banks(env, 7)
        pM = env.ps("pM", [128, 1024], BF16); t_pM = Tok()
        rqT = env.sb("rqT", [128, 2, L], BF16); t_rqT = [toks(4) for _ in range(2)]
        rkT = env.sb("rkT", [128, 2, L], BF16); t_rkT = [toks(4) for _ in range(2)]
        qdT = env.sb("qdT", [128, 2, L], BF16); t_qdT = toks(16)
        rv = env.sb("rv", [128, 16, 512], BF16); t_rv = toks(16)
        kd = env.sb("kd", [128, 16, 256], BF16); t_kd = toks(16)
        wrv = env.sb("wrv", [128, 8, 512], BF16); t_wrv = toks(4)
        wrg = env.sb("wrg", [128, 8, 512], BF16); t_wrg = toks(4)
        cosT = env.sb("cosT", [128, L], F32); t_cos = Tok()
        sinT = env.sb("sinT", [128, L], F32); t_sin = Tok()
        decT = env.sb("decT", [128, 512], F32); t_dec = Tok()
        kdec = env.sb("kdec", [128, 256], F32); t_kdec = Tok()
        qdec = env.sb("qdec", [128, 2, 128], F32); t_qdec = Tok()
        gng = env.sb("gng", [128, 512], F32); t_gng = Tok()
        gnb = env.sb("gnb", [128, 512], F32); t_gnb = Tok()
        tmp = [env.sb(f"tmp{i}", [128, 512], F32) for i in range(2)]; t_tmp = toks(2)
        tmp2 = [env.sb(f"tmpb{i}", [128, 512], F32) for i in range(2)]; t_tmp2 = toks(2)
        PdT = [env.sb(f"PdT{i}", [128, 512], BF16) for i in range(2)]; t_PdT = toks(2)
        state = env.sb("state", [128, 2, 128], F32); t_state = Tok()
        state_bf = env.sb("state_bf", [128, 2, 128], BF16); t_state_bf = Tok()
        on = [env.sb(f"on{i}", [128, 512], F32) for i in range(2)]; t_on = toks(2)
        sg = [env.sb(f"sg{i}", [128, 512], F32) for i in range(2)]; t_sg = toks(2)
        orb = [env.sb(f"orb{i}", [128, 512], BF16) for i in range(2)]; t_orb = toks(2)
        stats = [env.sb(f"stats{i}", [128, 4, 6], F32) for i in range(2)]; t_stats = toks(2)
        mvv = [env.sb(f"mvv{i}", [128, 4, 2], F32) for i in range(2)]; t_mvv = toks(2)
        sd = [env.sb(f"sd{i}", [128, 4], F32) for i in range(2)]; t_sd = toks(2)
        rstd = [env.sb(f"rstd{i}", [128, 4], F32) for i in range(2)]; t_rstd = toks(2)
        epst = env.sb("epst", [128, 1], F32); t_eps = Tok()
        ws = WStream(env)
        for (dst, src, tk) in ((cosT, cos_d, t_cos), (sinT, sin_d, t_sin), (decT, decay_d, t_dec), (kdec, kdec_d, t_kdec)):
            P.add("sync", lambda e, dst=dst, src=src: e.dma_start(out=dst[:], in_=src), writes=[tk], kind="d")
        P.add("sync", lambda e: e.dma_start(out=qdec[:].rearrange("p a b -> p (a b)"), in_=qdec_d), writes=[t_qdec], kind="d")
        P.add("sync", lambda e: e.dma_start(out=gng[:], in_=bcast_rows(gng_d, 128)), writes=[t_gng], kind="d")
        P.add("sync", lambda e: e.dma_start(out=gnb[:], in_=bcast_rows(gnb_d, 128)), writes=[t_gnb], kind="d")
        P.add("dve", lambda e: e.memset(epst[:], LN_EPS), writes=[t_eps])
        neghalf = env.sb("neghalf", [128, 4], F32); t_neghalf = Tok()
        P.add("pool", lambda e: e.memset(neghalf[:], -0.5), writes=[t_neghalf])

        cR = {"i": 0}
        for (c0, dstT, t_dstT, sc) in ((C_RQ, rqT, t_rqT, 1.0), (C_RK, rkT, t_rkT, 0.125)):
            for c in range(2):
                base = c0 + c * 128
                wn, t_wn = ws.load([(0, w_in[:, base:base + 128], 1.0)])
                pieces = []
                for hh in range(2):
                    pieces.append((hh * 64, w_in[:, base + hh * 64 + 32:base + hh * 64 + 64], -1.0))
                    pieces.append((hh * 64 + 32, w_in[:, base + hh * 64:base + hh * 64 + 32], 1.0))
                wr, t_wr = ws.load(pieces)
                for tb in range(4):
                    bq, t_bq = banks.next()
                    br, t_br = banks.next()
                    for (bank, t_bank, wt, t_w) in ((bq, t_bq, wn, t_wn), (br, t_br, wr, t_wr)):
                        for kc in range(8):
                            P.add("pe", lambda e, bank=bank, wt=wt, kc=kc, tb=tb: e.matmul(
                                bank[:], lhsT=wt[:, kc, :], rhs=x1T[:, kc, tb * 512:(tb + 1) * 512], start=(kc == 0), stop=(kc == 7)),
                                reads=[t_w], writes=[t_bank])
                    s = cR["i"] % 2
                    cR["i"] += 1
                    tsl = slice(tb * 512, (tb + 1) * 512)
                    P.add("dve", lambda e, s=s, bq=bq, tsl=tsl: e.tensor_tensor(out=tmp[s][:], in0=bq[:], in1=cosT[:, tsl], op=ALU.mult),
                          reads=[t_bq, t_cos], writes=[t_tmp[s]])
                    P.add("dve", lambda e, s=s, br=br, tsl=tsl, sc=sc: e.scalar_tensor_tensor(
                        out=tmp2[s][:], in0=br[:], scalar=float(sc), in1=sinT[:, tsl], op0=ALU.mult, op1=ALU.mult),
                        reads=[t_br, t_sin], writes=[t_tmp2[s]])
                    P.add("dve", lambda e, s=s, tsl=tsl, sc=sc, dstT=dstT, c=c: e.scalar_tensor_tensor(
                        out=dstT[:, c, tsl], in0=tmp[s][:], scalar=float(sc), in1=tmp2[s][:], op0=ALU.mult, op1=ALU.add),
                        reads=[t_tmp[s], t_tmp2[s]], writes=[t_dstT[c][tb]])
        for c in range(4):
            ws.load([(0, w_in[:, C_RV + c * 128:C_RV + (c + 1) * 128], 1.0)],
                    dst=lambda c0, w, c=c: wrv[:, :, c * 128 + c0:c * 128 + c0 + w], t_dst=t_wrv[c])
        for c in range(4):
            ws.load([(0, w_in[:, C_RG + c * 128:C_RG + (c + 1) * 128], 1.0)],
                    dst=lambda c0, w, c=c: wrg[:, :, c * 128 + c0:c * 128 + c0 + w], t_dst=t_wrg[c])
        for t in range(16):
            tsl = slice(t * 128, (t + 1) * 128)
            bank, t_bank = banks.next()
            for kc in range(8):
                P.add("pe", lambda e, bank=bank, kc=kc, tsl=tsl: e.matmul(
                    bank[:], lhsT=x1T[:, kc, tsl], rhs=wrv[:, kc, :], start=(kc == 0), stop=(kc == 7)),
                    reads=t_wrv, writes=[t_bank])
            P.add("act", lambda e, bank=bank, t=t: e.activation(out=rv[:, t, :], in_=bank[:], func=AF.Copy),
                  reads=[t_bank], writes=[t_rv[t]])
            for c in range(2):
                P.add("pe", lambda e, c=c, tsl=tsl: e.transpose(out=pM[:, c * 128:(c + 1) * 128], in_=rkT[:, c, tsl], identity=idb[:]),
                      reads=[t_rkT[c][t // 4], t_idb], writes=[t_pM])
            P.add("dve", lambda e, t=t: e.tensor_tensor(out=kd[:, t, :], in0=pM[:, 0:256], in1=kdec[:], op=ALU.mult),
                  reads=[t_pM, t_kdec], writes=[t_kd[t]])
            for c in range(2):
                P.add("dve", lambda e, c=c, tsl=tsl: e.tensor_tensor(out=qdT[:, c, tsl], in0=rqT[:, c, tsl], in1=qdec[:, c, :], op=ALU.mult),
                      reads=[t_rqT[c][t // 4], t_qdec], writes=[t_qdT[t]])
        eo_banks = [[(banks.b[2 * i + j], banks.t[2 * i + j]) for j in range(2)] for i in range(2)]
        kg = Banks.__new__(Banks)
        kg.b = [banks.b[i] for i in (4, 5, 6)]; kg.t = [banks.t[i] for i in (4, 5, 6)]; kg.i = 0; kg.n = 3

        def phase_a(n):
            tsl = slice(n * 128, (n + 1) * 128)
            s = n % 2
            eo = eo_banks[n % 2]
            for h in range(4):
                hp = (h % 2) * 64; c = h // 2
                bk, t_bk = eo[h % 2]
                P.add("pe", lambda e, bk=bk, hp=hp, c=c, tsl=tsl: e.matmul(
                    bk[:, c * 128:(c + 1) * 128], lhsT=rkT[hp:hp + 64, c, tsl], rhs=rqT[hp:hp + 64, c, tsl], start=True, stop=True),
                    reads=[t_rkT[c][n // 4], t_rqT[c][n // 4]], writes=[t_bk])
            for par in range(2):
                bk, t_bk = eo[par]
                P.add("dve", lambda e, bk=bk, s=s, par=par: e.tensor_tensor(
                    out=PdT[s][:, par * 256:(par + 1) * 256], in0=bk[:, 0:256], in1=decT[:, par * 256:(par + 1) * 256], op=ALU.mult),
                    reads=[t_bk, t_dec], writes=[t_PdT[s]])
            for h in range(4):
                hp = (h % 2) * 64; c = h // 2
                pos = (h % 2) * 2 + c
                bk, t_bk = eo[h % 2]
                P.add("pe", lambda e, bk=bk, h=h, c=c, pos=pos, s=s, n=n: e.matmul(
                    bk[:, 256 + c * 128:256 + (c + 1) * 128], lhsT=PdT[s][:, pos * 128:(pos + 1) * 128], rhs=rv[:, n, h * 128:(h + 1) * 128],
                    start=True, stop=(n == 0)), reads=[t_PdT[s], t_rv[n]], writes=[t_bk])
                if n > 0:
                    P.add("pe", lambda e, bk=bk, hp=hp, c=c, tsl=tsl: e.matmul(
                        bk[:, 256 + c * 128:256 + (c + 1) * 128], lhsT=qdT[hp:hp + 64, c, tsl], rhs=state_bf[hp:hp + 64, c, :],
                        start=False, stop=True), reads=[t_qdT[n], t_state_bf], writes=[t_bk])
            if n < 15:
                bK, t_bK = kg.next()
                for h in range(4):
                    hp = (h % 2) * 64; c = h // 2
                    P.add("pe", lambda e, bK=bK, h=h, hp=hp, c=c, n=n: e.matmul(
                        bK[hp:hp + 64, c * 128:(c + 1) * 128], lhsT=kd[:, n, h * 64:(h + 1) * 64], rhs=rv[:, n, h * 128:(h + 1) * 128],
                        start=True, stop=True), reads=[t_kd[n], t_rv[n]], writes=[t_bK])
                for h in range(4):
                    hp = (h % 2) * 64; c = h // 2
                    if n == 0:
                        P.add("dve", lambda e, bK=bK, hp=hp, c=c: e.tensor_copy(out=state[hp:hp + 64, c, :], in_=bK[hp:hp + 64, c * 128:(c + 1) * 128]),
                              reads=[t_bK], writes=[t_state])
                    else:
                        cd = float(np.float32(np.exp(np.float32(128.0) * np.log(np.float32(GAMMAS[h])))))
                        P.add("dve", lambda e, bK=bK, hp=hp, c=c, cd=cd: e.scalar_tensor_tensor(
                            out=state[hp:hp + 64, c, :], in0=state[hp:hp + 64, c, :], scalar=cd, in1=bK[hp:hp + 64, c * 128:(c + 1) * 128],
                            op0=ALU.mult, op1=ALU.add), reads=[t_bK, t_state], writes=[t_state])
                P.add("act", lambda e: e.activation(out=state_bf[:], in_=state[:], func=AF.Copy), reads=[t_state], writes=[t_state_bf])
            bG, t_bG = kg.next()
            for kc in range(8):
                P.add("pe", lambda e, bG=bG, kc=kc, tsl=tsl: e.matmul(
                    bG[:], lhsT=x1T[:, kc, tsl], rhs=wrg[:, kc, :], start=(kc == 0), stop=(kc == 7)), reads=t_wrg, writes=[t_bG])
            P.add("act", lambda e, bG=bG, s=s: e.activation(out=sg[s][:], in_=bG[:], func=AF.Silu), reads=[t_bG], writes=[t_sg[s]])

        def phase_b(n):
            tsl = slice(n * 128, (n + 1) * 128)
            s = n % 2
            eo = eo_banks[n % 2]
            osl = lambda h: slice(256 + (h // 2) * 128, 256 + (h // 2 + 1) * 128)
            for h in range(4):
                bk, t_bk = eo[h % 2]
                P.add("dve", lambda e, bk=bk, h=h, s=s: e.bn_stats(out=stats[s][:, h, :], in_=bk[:, osl(h)]),
                      reads=[t_bk], writes=[t_stats[s]])
            for h in range(4):
                P.add("dve", lambda e, h=h, s=s: e.bn_aggr(out=mvv[s][:, h, :], in_=stats[s][:, h, :]), reads=[t_stats[s]], writes=[t_mvv[s]])
            P.add("pool", lambda e, s=s: e.tensor_scalar(out=sd[s][:], in0=mvv[s][:, :, 1], scalar1=LN_EPS, scalar2=None, op0=ALU.add),
                  reads=[t_mvv[s]], writes=[t_sd[s]])
            P.add("pool", lambda e, s=s: e.tensor_tensor(out=rstd[s][:], in0=sd[s][:], in1=neghalf[:], op=ALU.pow),
                  reads=[t_sd[s], t_neghalf], writes=[t_rstd[s]])
            for h in range(4):
                bk, t_bk = eo[h % 2]
                P.add("dve", lambda e, bk=bk, h=h, s=s: e.tensor_scalar(
                    out=on[s][:, h * 128:(h + 1) * 128], in0=bk[:, osl(h)], scalar1=mvv[s][:, h, 0:1],
                    scalar2=rstd[s][:, h:h + 1], op0=ALU.subtract, op1=ALU.mult),
                    reads=[t_bk, t_mvv[s], t_rstd[s]], writes=[t_on[s]])
            P.add("pool", lambda e, s=s: e.tensor_tensor(out=on[s][:], in0=on[s][:], in1=gng[:], op=ALU.mult),
                  reads=[t_on[s], t_gng], writes=[t_on[s]])
            P.add("pool", lambda e, s=s: e.tensor_tensor(out=on[s][:], in0=on[s][:], in1=gnb[:], op=ALU.add),
                  reads=[t_on[s], t_gnb], writes=[t_on[s]])
            P.add("dve", lambda e, s=s: e.tensor_tensor(out=orb[s][:], in0=on[s][:], in1=sg[s][:], op=ALU.mult),
                  reads=[t_on[s], t_sg[s]], writes=[t_orb[s]])
            for c4 in range(4):
                P.add("pe", lambda e, c4=c4, s=s: e.transpose(out=pM[:, c4 * 128:(c4 + 1) * 128], in_=orb[s][:, c4 * 128:(c4 + 1) * 128], identity=idb[:]),
                      reads=[t_orb[s], t_idb], writes=[t_pM])
            P.add("act", lambda e, tsl=tsl: e.activation(out=oT[:, :, tsl], in_=pM[:, 0:512].rearrange("p (a b) -> p a b", a=4), func=AF.Copy),
                  reads=[t_pM], writes=[Tok()])

        phase_a(0)
        for n in range(16):
            if n + 1 < 16:
                phase_a(n + 1)
            phase_b(n)
        env.P.emit("m2")


def merge_stage(nc, semstack, dmaq, x1T, oTs, w_in, wbrs, wo_d, g_d, b_d, src_d, dst_d):
    with contextlib.ExitStack() as st:
        env = Env(nc, semstack, dmaq, "m4", st)
        P = env.P
        banks = Banks(env, 8)
        mT = env.sb("mT", [128, 8, L], BF16); t_mT = [toks(4) for _ in range(8)]
        wout = env.sb("wout", [128, 8, D], BF16); t_wout = toks(8)
        sgm = [env.sb(f"sgm{i}", [128, 512], F32) for i in range(2)]; t_sgm = toks(2)
        tmp = [env.sb(f"tmp{i}", [128, 512], F32) for i in range(2)]; t_tmp = toks(2)
        acc = env.sb("acc", [128, L], F32); t_acc = toks(4)
        xs = [env.sb(f"xs{i}", [128, D], F32) for i in range(2)]; t_xs = toks(2)
        rr = [env.sb(f"rr{i}", [128, D], F32) for i in range(2)]; t_rr = toks(2)
        oo = [env.sb(f"oo{i}", [128, D], F32) for i in range(2)]; t_oo = toks(2)
        gam = env.sb("gam", [128, D], F32); t_gam = Tok()
        bet = env.sb("bet", [128, D], F32); t_bet = Tok()
        epst = env.sb("epst", [128, 1], F32); t_eps = Tok()
        stats = [env.sb(f"stats{i}", [128, 2, 6], F32) for i in range(2)]; t_stats = toks(2)
        mv = [env.sb(f"mv{i}", [128, 2], F32) for i in range(2)]; t_mv = toks(2)
        sd = [env.sb(f"sd{i}", [128, 1], F32) for i in range(2)]; t_sd = toks(2)
        rstd = [env.sb(f"rstd{i}", [128, 1], F32) for i in range(2)]; t_rstd = toks(2)
        t_dst = toks(16)
        ws = WStream(env, nst=2, nbf=6)
        P.add("sync", lambda e: e.dma_start(out=gam[:], in_=bcast_rows(g_d, 128)), writes=[t_gam], kind="d")
        P.add("sync", lambda e: e.dma_start(out=bet[:], in_=bcast_rows(b_d, 128)), writes=[t_bet], kind="d")
        P.add("dve", lambda e: e.memset(epst[:], LN_EPS), writes=[t_eps])
        cM = {"s": 0}
        for j in range(8):
            ws.load([(0, wo_d[:, j * 128:(j + 1) * 128], 1.0)], dst=lambda c0, w, j=j: wout[:, :, j * 128 + c0:j * 128 + c0 + w], t_dst=t_wout[j])
            for b in range(3):
                wgb, t_wgb = ws.load([(0, w_in[:, C_G + b * D + j * 128:C_G + b * D + (j + 1) * 128], 1.0)])
                wbb, t_wbb = ws.load([(0, wbrs[b][:, j * 128:(j + 1) * 128], 1.0)], KC=4)
                for tb in range(4):
                    tsl = slice(tb * 512, (tb + 1) * 512)
                    bG, t_bG = banks.next()
                    for kc in range(8):
                        P.add("pe", lambda e, bG=bG, wgb=wgb, kc=kc, tsl=tsl: e.matmul(
                            bG[:], lhsT=wgb[:, kc, :], rhs=x1T[:, kc, tsl], start=(kc == 0), stop=(kc == 7)),
                            reads=[t_wgb], writes=[t_bG])
                    bB, t_bB = banks.next()
                    for kc in range(4):
                        P.add("pe", lambda e, bB=bB, wbb=wbb, kc=kc, tsl=tsl, b=b: e.matmul(
                            bB[:], lhsT=wbb[:, kc, :], rhs=oTs[b][:, kc, tsl], start=(kc == 0), stop=(kc == 3)),
                            reads=[t_wbb], writes=[t_bB])
                    s = cM["s"] % 2
                    cM["s"] += 1
                    P.add("act", lambda e, bG=bG, s=s: e.activation(out=sgm[s][:], in_=bG[:], func=AF.Sigmoid), reads=[t_bG], writes=[t_sgm[s]])
                    if b == 0:
                        P.add("dve", lambda e, bB=bB, s=s, tsl=tsl: e.tensor_tensor(out=acc[:, tsl], in0=sgm[s][:], in1=bB[:], op=ALU.mult),
                              reads=[t_sgm[s], t_bB], writes=[t_acc[tb]])
                    else:
                        P.add("dve", lambda e, bB=bB, s=s: e.tensor_tensor(out=tmp[s][:], in0=sgm[s][:], in1=bB[:], op=ALU.mult),
                              reads=[t_sgm[s], t_bB], writes=[t_tmp[s]])
                        if b == 1:
                            P.add("dve", lambda e, s=s, tsl=tsl: e.tensor_tensor(out=acc[:, tsl], in0=acc[:, tsl], in1=tmp[s][:], op=ALU.add),
                                  reads=[t_acc[tb], t_tmp[s]], writes=[t_acc[tb]])
                        else:
                            P.add("dve", lambda e, s=s, j=j, tsl=tsl: e.tensor_tensor(out=mT[:, j, tsl], in0=acc[:, tsl], in1=tmp[s][:], op=ALU.add),
                                  reads=[t_acc[tb], t_tmp[s]], writes=[t_mT[j][tb]])
        for t in range(16):
            s = t % 2
            tsl = slice(t * 128, (t + 1) * 128)
            P.add("sync", lambda e, s=s, tsl=tsl: e.dma_start(out=xs[s][:], in_=src_d[tsl, :]), writes=[t_xs[s]], kind="d")
            for nh in range(2):
                bank, t_bank = banks.next()
                for kc in range(8):
                    P.add("pe", lambda e, bank=bank, kc=kc, nh=nh, tsl=tsl: e.matmul(
                        bank[:], lhsT=mT[:, kc, tsl], rhs=wout[:, kc, nh * 512:(nh + 1) * 512], start=(kc == 0), stop=(kc == 7)),
                        reads=[t_mT[kc][t // 4]] + t_wout[nh * 4:(nh + 1) * 4], writes=[t_bank])
                P.add("dve", lambda e, bank=bank, nh=nh, s=s: e.scalar_tensor_tensor(
                    out=rr[s][:, nh * 512:(nh + 1) * 512], in0=xs[s][:, nh * 512:(nh + 1) * 512], scalar=ALPHA, in1=bank[:],
                    op0=ALU.mult, op1=ALU.add), reads=[t_xs[s], t_bank], writes=[t_rr[s]])
            for k in range(2):
                P.add("dve", lambda e, s=s, k=k: e.bn_stats(out=stats[s][:, k, :], in_=rr[s][:, k * 512:(k + 1) * 512]),
                      reads=[t_rr[s]], writes=[t_stats[s]])
            P.add("dve", lambda e, s=s: e.bn_aggr(out=mv[s][:], in_=stats[s][:].rearrange("p a b -> p (a b)")),
                  reads=[t_stats[s]], writes=[t_mv[s]])
            P.add("act", lambda e, s=s: e.activation(out=sd[s][:], in_=mv[s][:, 1:2], func=AF.Sqrt, bias=epst[:, 0:1], scale=1.0),
                  reads=[t_mv[s], t_eps], writes=[t_sd[s]])
            P.add("dve", lambda e, s=s: e.reciprocal(out=rstd[s][:], in_=sd[s][:]), reads=[t_sd[s]], writes=[t_rstd[s]])
            P.add("dve", lambda e, s=s: e.tensor_scalar(
                out=rr[s][:], in0=rr[s][:], scalar1=mv[s][:, 0:1], scalar2=rstd[s][:, 0:1], op0=ALU.subtract, op1=ALU.mult),
                reads=[t_rr[s], t_mv[s], t_rstd[s]], writes=[t_rr[s]])
            P.add("pool", lambda e, s=s: e.tensor_tensor(out=oo[s][:], in0=rr[s][:], in1=gam[:], op=ALU.mult),
                  reads=[t_rr[s], t_gam], writes=[t_oo[s]])
            P.add("pool", lambda e, s=s: e.tensor_tensor(out=oo[s][:], in0=oo[s][:], in1=bet[:], op=ALU.add),
                  reads=[t_oo[s], t_bet], writes=[t_oo[s]])
            P.add("pool", lambda e, s=s, tsl=tsl: e.dma_start(out=dst_d[tsl, :], in_=oo[s][:]),
                  reads=[t_oo[s]], writes=[t_dst[t]], kind="d")
        P.wait_all("pool", t_dst)
        env.P.emit("m4")

def build_nc(stages=("ffn1", "mix", "ffn2"), dbg_mix=False, parts=("dsa", "ret", "mem", "merge")):
    nc = bass.Bass("TRN2", target_bir_lowering=False)
    din = lambda n, shape: nc.dram_tensor(n, shape, F32, kind="ExternalInput").ap()
    x_d = din("x", [L, D])
    mem_d = din("mem", [256, D])
    f1wi = din("ffn1_w_in", [D, 2 * DFF]); f1wo = din("ffn1_w_out", [DFF, D])
    ln1g = din("ln1_g", [1, D]); ln1b = din("ln1_b", [1, D])
    w_in = din("w_in", [D, W_IN_COLS]); t5 = din("t5_table", [32, 8])
    gng = din("ret_gn_g", [1, 512]); gnb = din("ret_gn_b", [1, 512])
    wmkv = din("w_mem_kv", [D, 1024])
    wbr = din("w_br_ret", [512, D]); wbd = din("w_br_dsa", [512, D]); wbm = din("w_br_mem", [512, D])
    wo = din("w_out", [D, D]); ln2g = din("ln2_g", [1, D]); ln2b = din("ln2_b", [1, D])
    f2wi = din("ffn2_w_in", [D, 2 * DFF]); f2wo = din("ffn2_w_out", [DFF, D])
    ln3g = din("ln3_g", [1, D]); ln3b = din("ln3_b", [1, D])
    ident = din("c_ident", [128, 128])
    cJ = din("c_J", [128, 128]); coh = din("c_oh", [32, 384]); ccausal = din("c_causal", [128, 128])
    cpow2 = din("c_pow2", [128, NIT])
    ccos = din("c_cos", [128, L]); csin = din("c_sin", [128, L])
    cdecay = din("c_decay", [128, 512]); ckdec = din("c_kdec", [128, 256]); cqdec = din("c_qdec", [128, 256])
    gscr = nc.dram_tensor("g_scr", [8, 384], F32).ap()
    out_d = nc.dram_tensor("out", [L, D], F32, kind="ExternalOutput").ap()
    x1_d = nc.dram_tensor("x1_scr", [L, D], F32).ap()
    x2_d = nc.dram_tensor("x2_scr", [L, D], F32).ap()
    with contextlib.ExitStack() as semstack:
        dmaq = {}
        cur = x_d
        for i, s in enumerate(stages):
            last = i == len(stages) - 1
            if s == "ffn1":
                dst = out_d if last else x1_d
                ffn_stage(nc, semstack, dmaq, "f1", cur, f1wi, f1wo, ln1g, ln1b, dst, ident)
                cur = dst
            elif s == "mix":
                dst = out_d if last else x2_d
                with contextlib.ExitStack() as outer:
                    x1T = outer.enter_context(nc.sbuf_tensor("x1T", [128, 8, L], BF16))
                    mix_load_stage(nc, semstack, dmaq, cur, ident, x1T)
                    o_dsaT = outer.enter_context(nc.sbuf_tensor("o_dsaT", [128, 4, L], BF16))
                    if "dsa" in parts:
                        dsa_stage(nc, semstack, dmaq, x1T, o_dsaT, w_in, t5, ident, cJ, coh, ccausal, cpow2, gscr)
                    o_retT = outer.enter_context(nc.sbuf_tensor("o_retT", [128, 4, L], BF16))
                    if "ret" in parts:
                        ret_stage(nc, semstack, dmaq, x1T, o_retT, w_in, gng, gnb, ident, ccos, csin, cdecay, ckdec, cqdec)
                    o_memT = outer.enter_context(nc.sbuf_tensor("o_memT", [128, 4, L], BF16))
                    if "mem" in parts:
                        mem_stage(nc, semstack, dmaq, x1T, o_memT, mem_d, w_in, wmkv, ident)
                    if dbg_mix:
                        dbg = nc.dram_tensor("dbg", [128, 12, L], BF16, kind="ExternalOutput").ap()
                        with contextlib.ExitStack() as st:
                            env = Env(nc, semstack, dmaq, "dbg", st)
                            tt = toks(3)
                            for i, (o, pn) in enumerate(((o_retT, "ret"), (o_dsaT, "dsa"), (o_memT, "mem"))):
                                if pn not in parts:
                                    continue
                                env.P.add("sync", lambda e, i=i, o=o: e.dma_start(out=dbg[:, 4 * i:4 * i + 4, :], in_=o[:]), writes=[tt[i]], kind="d")
                            env.P.wait_all("sync", tt)
                            env.P.emit("dbg")
                    if "merge" in parts:
                        merge_stage(nc, semstack, dmaq, x1T, [o_retT, o_dsaT, o_memT], w_in, [wbr, wbd, wbm], wo, ln2g, ln2b, cur, dst)
                cur = dst
            elif s == "mixdbg":
                with contextlib.ExitStack() as outer:
                    x1T = outer.enter_context(nc.sbuf_tensor("x1T", [128, 8, L], BF16))
                    mix_load_stage(nc, semstack, dmaq, cur, ident, x1T)
                    o_dsaT = outer.enter_context(nc.sbuf_tensor("o_dsaT", [128, 4, L], BF16))
                    dsa_stage(nc, semstack, dmaq, x1T, o_dsaT, w_in, t5, ident, cJ, coh, ccausal, cpow2, gscr)
                    o_memT = outer.enter_context(nc.sbuf_tensor("o_memT", [128, 4, L], BF16))
                    mem_stage(nc, semstack, dmaq, x1T, o_memT, mem_d, w_in, wmkv, ident)
                    dbg = nc.dram_tensor("dbg", [128, 8, L], BF16, kind="ExternalOutput").ap()
                    with contextlib.ExitStack() as st:
                        env = Env(nc, semstack, dmaq, "dbg", st)
                        t1 = Tok(); t2 = Tok()
                        env.P.add("sync", lambda e: e.dma_start(out=dbg[:, 0:4, :], in_=o_dsaT[:]), writes=[t1], kind="d")
                        env.P.add("sync", lambda e: e.dma_start(out=dbg[:, 4:8, :], in_=o_memT[:]), writes=[t2], kind="d")
                        env.P.wait_all("sync", [t1, t2])
                        env.P.emit("dbg")
            elif s == "ffn2":
                dst = out_d if last else x2_d
                ffn_stage(nc, semstack, dmaq, "f2", cur, f2wi, f2wo, ln3g, ln3b, dst, ident)
                cur = dst
    return nc


_CACHE = {}


def _t5_bucket(n):
    n = np.maximum(n, 0)
    nf = np.maximum(n, 1).astype(np.float32)
    large = 16 + (np.log(nf / np.float32(16)) / np.float32(np.log(128 / 16)) * np.float32(16)).astype(np.int32)
    large = np.minimum(large, 31)
    return np.where(n < 16, n, large)


def make_consts():
    c = {}
    c["c_J"] = np.ascontiguousarray(np.eye(128, dtype=np.float32)[::-1])
    oh = np.zeros((32, 384), np.float32)
    for u in range(383):
        d = u - 127
        if d >= 0:
            oh[_t5_bucket(np.array(d)), u] += 1.0
            oh[31, u] -= 1.0
    c["c_oh"] = oh
    q = np.arange(128)[:, None]; sk = np.arange(128)[None, :]
    c["c_causal"] = np.where(sk <= q, 0.0, -1.0e30).astype(np.float32)
    half = 32
    freqs = (np.float32(10000.0) ** (-np.arange(half, dtype=np.float32) / np.float32(half))).astype(np.float32)
    ang = np.arange(L, dtype=np.float32)[None, :] * freqs[np.arange(128) % 32][:, None]
    c["c_cos"] = np.cos(ang).astype(np.float32)
    c["c_sin"] = np.sin(ang).astype(np.float32)
    lg = np.log(np.array(GAMMAS, dtype=np.float32))
    i = np.arange(128)
    dec = np.zeros((128, 4, 128), np.float32)
    for h in range(4):
        diff = i[None, :] - i[:, None]
        dec[:, h, :] = np.where(diff >= 0, np.exp(np.maximum(diff, 0).astype(np.float32) * lg[h]), 0.0)
    c["c_decay"] = np.ascontiguousarray(dec[:, [0, 2, 1, 3], :]).reshape(128, 512)
    kdec = np.zeros((128, 4, 64), np.float32)
    for h in range(4):
        kdec[:, h, :] = np.exp((127 - i).astype(np.float32) * lg[h])[:, None]
    c["c_kdec"] = kdec.reshape(128, 256)
    qdec = np.zeros((128, 2, 128), np.float32)
    for p in range(128):
        for cc in range(2):
            qdec[p, cc, :] = np.exp((i + 1).astype(np.float32) * lg[2 * cc + p // 64])
    c["c_qdec"] = qdec.reshape(128, 256)
    c["c_pow2"] = np.tile((2.0 ** -(np.arange(NIT) + 1.0)).astype(np.float32)[None, :], (128, 1))
    return c


def make_in_maps(inputs):
    f = lambda a: np.ascontiguousarray(np.asarray(a, dtype=np.float32))
    shared = {
        "ffn1_w_in": f(inputs["ffn1_w_in"][0]), "ffn1_w_out": f(inputs["ffn1_w_out"][0]),
        "ln1_g": f(inputs["ln1_g"]), "ln1_b": f(inputs["ln1_b"]),
        "w_in": f(inputs["w_in"][0]), "t5_table": f(inputs["t5_table"]),
        "ret_gn_g": f(inputs["ret_gn_g"]), "ret_gn_b": f(inputs["ret_gn_b"]),
        "w_mem_kv": f(inputs["w_mem_kv"][0]),
        "w_br_ret": f(inputs["w_br_ret"][0]), "w_br_dsa": f(inputs["w_br_dsa"][0]), "w_br_mem": f(inputs["w_br_mem"][0]),
        "w_out": f(inputs["w_out"][0]), "ln2_g": f(inputs["ln2_g"]), "ln2_b": f(inputs["ln2_b"]),
        "ffn2_w_in": f(inputs["ffn2_w_in"][0]), "ffn2_w_out": f(inputs["ffn2_w_out"][0]),
        "ln3_g": f(inputs["ln3_g"]), "ln3_b": f(inputs["ln3_b"]),
        "c_ident": np.eye(128, dtype=np.float32),
    }
    shared.update(make_consts())
    x = f(inputs["x"])
    mem = f(inputs["mem"])
    return [dict(shared, x=x[b], mem=mem[b]) for b in range(x.shape[0])]


def kernel(**inputs):
    if "nc" not in _CACHE:
        _CACHE["nc"] = build_nc()
    nc = _CACHE["nc"]
    in_maps = make_in_maps(inputs)
    res = run_bass_kernel_spmd(nc, in_maps, core_ids=list(range(len(in_maps))))
    return np.stack([np.asarray(r["out"], dtype=np.float32) for r in res.results], axis=0)
```

```python
import contextlib
import numpy as np
import concourse.bass as bass
import concourse.mybir as mybir
from concourse.bass_utils import run_bass_kernel_spmd

F32 = mybir.dt.float32
BF16 = mybir.dt.bfloat16
AF = mybir.ActivationFunctionType
ALU = mybir.AluOpType

L = 2048
D = 1024
DFF = 2816
NCH = DFF // 128
ALPHA = 2.0 ** 0.25
LN_EPS = 1e-5
W_IN_COLS = 7240

ENGS = ("pe", "dve", "act", "pool", "sync")
SEM_ROT = {"c": 8000}
DMA_RING = 8
RELAX_SAME_ENGINE = True


class Tok:
    __slots__ = ("w", "r", "name")

    def __init__(self, name=""):
        self.w = None
        self.r = {}
        self.name = name


class Ins:
    __slots__ = ("eng", "kind", "fn", "deps", "idx", "inc", "sem", "val")

    def __init__(self, eng, kind, fn):
        self.eng = eng
        self.kind = kind
        self.fn = fn
        self.deps = []
        self.inc = False
        self.sem = None
        self.val = 0


class Prog:
    def __init__(self, nc, semstack, dmaq):
        self.nc = nc
        self.semstack = semstack
        self.dmaq = dmaq
        self.streams = {e: [] for e in ENGS}
        self.n = 0

    def add(self, eng, fn, reads=(), writes=(), kind="c"):
        ins = Ins(eng, kind, fn)
        st = self.streams[eng]
        ins.idx = len(st)
        deps = {}

        def dep(d, typ):
            if d is None or d is ins:
                return
            if d.eng == eng and d.kind == "c" and kind == "c":
                if eng == "pe":
                    return
                if typ != "RAW":
                    return
                if RELAX_SAME_ENGINE and eng in ("dve", "act") and ins.idx - d.idx >= 2:
                    return
            deps[id(d)] = d

        for t in reads:
            dep(t.w, "RAW")
        for t in writes:
            dep(t.w, "WAW")
            for d in t.r.values():
                dep(d, "WAR")
        if kind == "d":
            q = self.dmaq.setdefault(eng, {"n": 0, "last": {}, "sems": []})
            n = q["n"]
            q["n"] += 1
            r = n % DMA_RING
            if len(q["sems"]) <= r:
                q["sems"].append(self.semstack.enter_context(self.nc.semaphore(f"s_dma_{eng}_{r}")))
            ins.sem = q["sems"][r]
            ins.val = 16 * (n // DMA_RING + 1)
            ins.inc = True
            prev = q["last"].get(r)
            if prev is not None:
                deps[id(prev)] = prev
            q["last"][r] = ins
        ins.deps = list(deps.values())
        for d in ins.deps:
            d.inc = True
        for t in reads:
            t.r[eng + kind] = ins
        for t in writes:
            t.w = ins
            t.r = {}
        st.append(ins)
        self.n += 1
        return ins

    def wait_all(self, eng, toks):
        return self.add(eng, None, reads=toks, kind="w")

    def emit(self, name):
        nc = self.nc
        nsem = 0
        for e in ENGS:
            for kind in ("c",):
                cnt = 0
                cur = None
                for ins in self.streams[e]:
                    if ins.kind != kind or not ins.inc:
                        continue
                    if cur is None or cnt >= SEM_ROT[kind]:
                        cur = self.semstack.enter_context(nc.semaphore(f"s_{name}_{e}_{kind}_{nsem}"))
                        nsem += 1
                        cnt = 0
                    cnt += 1
                    ins.sem = cur
                    ins.val = cnt * (16 if kind == "d" else 1)
        with nc.Block() as block:
            engmap = {"pe": block.tensor, "dve": block.vector, "act": block.scalar,
                      "pool": block.gpsimd, "sync": block.sync}
            for e in ENGS:
                stream = self.streams[e]
                if not stream:
                    continue

                def body(eobj, stream=stream):
                    waited = {}
                    for ins in stream:
                        for d in ins.deps:
                            k = id(d.sem)
                            if waited.get(k, 0) >= d.val:
                                continue
                            eobj.wait_ge(d.sem, d.val)
                            waited[k] = d.val
                        if ins.fn is None:
                            continue
                        r = ins.fn(eobj)
                        if ins.inc:
                            r.then_inc(ins.sem, 16 if ins.kind == "d" else 1)

                engmap[e](body)


def bcast_rows(ap2d, nrows):
    return bass.AP(ap2d.tensor, ap2d.offset, [[0, nrows], [1, ap2d.shape[-1]]])


def ffn_stage(nc, semstack, dmaq, name, src_d, w_in_d, w_out_d, g_d, b_d, dst_d, ident_d):
    with contextlib.ExitStack() as st:
        P = Prog(nc, semstack, dmaq)
        sb = lambda n, shape, dt: st.enter_context(nc.sbuf_tensor(f"{name}_{n}", shape, dt))
        ps = lambda n, shape, dt: st.enter_context(nc.psum_tensor(f"{name}_{n}", shape, dt))
        HT = 1024
        xT = [sb(f"xT{i}", [128, 8, HT], BF16) for i in range(2)]
        gT = sb("gT", [128, NCH, HT], BF16)
        wout = sb("wout", [128, NCH, D], BF16)
        wst = [sb(f"wst{i}", [128, 2, 8, 128], F32) for i in range(2)]
        wbf = [sb(f"wbf{i}", [128, 2, 8, 128], BF16) for i in range(3)]
        wost = [sb(f"wost{i}", [128, D], F32) for i in range(2)]
        xs = [sb(f"xs{i}", [128, D], F32) for i in range(2)]
        xb = [sb(f"xb{i}", [128, D], BF16) for i in range(2)]
        sA = [sb(f"sA{i}", [128, 512], F32) for i in range(2)]
        rr = [sb(f"rr{i}", [128, D], F32) for i in range(2)]
        oo = [sb(f"oo{i}", [128, D], F32) for i in range(2)]
        gam = sb("gam", [128, D], F32)
        bet = sb("bet", [128, D], F32)
        idf = sb("idf", [128, 128], F32)
        idb = sb("idb", [128, 128], BF16)
        epst = sb("epst", [128, 1], F32)
        stats = [sb(f"stats{i}", [128, 2, 6], F32) for i in range(2)]
        mv = [sb(f"mv{i}", [128, 2], F32) for i in range(2)]
        sd = [sb(f"sd{i}", [128, 1], F32) for i in range(2)]
        rstd = [sb(f"rstd{i}", [128, 1], F32) for i in range(2)]
        pT = [ps(f"pT{i}", [128, 1024], BF16) for i in range(2)]
        pB = [ps(f"pB{i}", [128, 512], F32) for i in range(6)]

        def toks(n, k):
            return [Tok(f"{n}{i}") for i in range(k)]

        t_xT = [toks("xT", 8) for _ in range(2)]
        t_gT = [[Tok() for _ in range(2)] for _ in range(NCH)]
        t_wout = toks("wout", NCH)
        t_wst = toks("wst", 2); t_wbf = toks("wbf", 3); t_wost = toks("wost", 2)
        t_xs = toks("xs", 2); t_xb = toks("xb", 2); t_sA = toks("sA", 2)
        t_ys = toks("ys", 2); t_rr = toks("rr", 2); t_oo = toks("oo", 2)
        t_gam = Tok(); t_bet = Tok(); t_idf = Tok(); t_idb = Tok(); t_eps = Tok()
        t_stats = toks("st", 2); t_mv = toks("mv", 2); t_sd = toks("sd", 2); t_rstd = toks("rs", 2)
        t_pT = toks("pT", 2); t_pB = toks("pB", 6)
        t_dst = toks("dst", 16)

        P.add("sync", lambda e: e.dma_start(out=idf[:], in_=ident_d), writes=[t_idf], kind="d")
        P.add("sync", lambda e: e.dma_start(out=gam[:], in_=bcast_rows(g_d, 128)), writes=[t_gam], kind="d")
        P.add("sync", lambda e: e.dma_start(out=bet[:], in_=bcast_rows(b_d, 128)), writes=[t_bet], kind="d")
        P.add("dve", lambda e: e.tensor_copy(out=idb[:], in_=idf[:]), reads=[t_idf], writes=[t_idb])
        P.add("dve", lambda e: e.memset(epst[:], LN_EPS), writes=[t_eps])

        cnt = {"x": 0, "pb": 0, "sa": 0, "c": 0}

        def stage_a(h):
            for tl in range(8):
                t = h * 8 + tl
                s = cnt["x"] % 2
                cnt["x"] += 1
                P.add("sync", lambda e, s=s, t=t: e.dma_start(out=xs[s][:], in_=src_d[t * 128:(t + 1) * 128, :]),
                      writes=[t_xs[s]], kind="d")
                P.add("dve", lambda e, s=s: e.tensor_copy(out=xb[s][:], in_=xs[s][:]), reads=[t_xs[s]], writes=[t_xb[s]])
                for kc in range(8):
                    P.add("pe", lambda e, s=s, kc=kc: e.transpose(out=pT[s][:, kc * 128:(kc + 1) * 128],
                                                                  in_=xb[s][:, kc * 128:(kc + 1) * 128], identity=idb[:]),
                          reads=[t_xb[s], t_idb], writes=[t_pT[s]])
                P.add("act", lambda e, s=s, h=h, tl=tl: e.activation(
                    out=xT[h][:, :, tl * 128:(tl + 1) * 128],
                    in_=pT[s][:].rearrange("p (a b) -> p a b", a=8), func=AF.Copy),
                    reads=[t_pT[s]], writes=[t_xT[h][tl]])

        wcount = {"c": 0}

        def load_w(c, with_out):
            s = wcount["c"] % 2
            bs = wcount["c"] % 3
            wcount["c"] += 1
            for j in range(2):
                col0 = j * DFF + c * 128
                P.add("sync", lambda e, s=s, j=j, col0=col0: e.dma_start(
                    out=wst[s][:, j], in_=w_in_d[:, col0:col0 + 128].rearrange("(kc p) n -> p kc n", p=128)),
                    writes=[t_wst[s]], kind="d")
            P.add("act", lambda e, s=s, bs=bs: e.activation(out=wbf[bs][:].rearrange("p a b c -> p (a b c)"),
                                                            in_=wst[s][:].rearrange("p a b c -> p (a b c)"), func=AF.Copy),
                  reads=[t_wst[s]], writes=[t_wbf[bs]])
            if with_out:
                P.add("sync", lambda e, s=s, c=c: e.dma_start(out=wost[s][:], in_=w_out_d[c * 128:(c + 1) * 128, :]),
                      writes=[t_wost[s]], kind="d")
                P.add("pool", lambda e, s=s, c=c: e.tensor_copy(out=wout[:, c, :], in_=wost[s][:]),
                      reads=[t_wost[s]], writes=[t_wout[c]])
            return bs

        wq = {"issued": 0, "bs": {}}

        def ensure_w(k):
            while wq["issued"] <= min(k, 2 * NCH - 1):
                i = wq["issued"]
                wq["bs"][i] = load_w(i % NCH, i < NCH)
                wq["issued"] += 1

        def stage_b(h):
            for c in range(NCH):
                ensure_w(h * NCH + c + 2)
                bs = wq["bs"][h * NCH + c]
                for tb in range(2):
                    pa = (cnt["pb"] % 2) * 2
                    cnt["pb"] += 1
                    for j in range(2):
                        for kc in range(8):
                            P.add("pe", lambda e, pa=pa, j=j, kc=kc, bs=bs, tb=tb, h=h: e.matmul(
                                pB[pa + j][:], lhsT=wbf[bs][:, j, kc, :], rhs=xT[h][:, kc, tb * 512:(tb + 1) * 512],
                                start=(kc == 0), stop=(kc == 7)),
                                reads=[t_wbf[bs]] + t_xT[h][tb * 4:(tb + 1) * 4], writes=[t_pB[pa + j]])
                    s = cnt["sa"] % 2
                    cnt["sa"] += 1
                    P.add("act", lambda e, s=s, pa=pa: e.activation(out=sA[s][:], in_=pB[pa][:], func=AF.Silu),
                          reads=[t_pB[pa]], writes=[t_sA[s]])
                    P.add("dve", lambda e, s=s, pa=pa, c=c, tb=tb: e.scalar_tensor_tensor(
                        out=gT[:, c, tb * 512:(tb + 1) * 512], in0=sA[s][:], scalar=0.5, in1=pB[pa + 1][:],
                        op0=ALU.mult, op1=ALU.mult),
                        reads=[t_sA[s], t_pB[pa + 1]], writes=[t_gT[c][tb]])

        def stage_c(h):
            for tl in range(8):
                t = h * 8 + tl
                s = cnt["c"] % 2
                cnt["c"] += 1
                pa = (cnt["pb"] % 2) * 2
                cnt["pb"] += 1
                sx = cnt["x"] % 2
                cnt["x"] += 1
                P.add("sync", lambda e, sx=sx, t=t: e.dma_start(out=xs[sx][:], in_=src_d[t * 128:(t + 1) * 128, :]),
                      writes=[t_xs[sx]], kind="d")
                for nh in range(2):
                    for kc in range(NCH):
                        P.add("pe", lambda e, pa=pa, nh=nh, kc=kc, tl=tl: e.matmul(
                            pB[pa + nh][:], lhsT=gT[:, kc, tl * 128:(tl + 1) * 128], rhs=wout[:, kc, nh * 512:(nh + 1) * 512],
                            start=(kc == 0), stop=(kc == NCH - 1)),
                            reads=[t_gT[kc][tl // 4], t_wout[kc]], writes=[t_pB[pa + nh]])
                    P.add("dve", lambda e, pa=pa, nh=nh, s=s, sx=sx: e.scalar_tensor_tensor(
                        out=rr[s][:, nh * 512:(nh + 1) * 512], in0=xs[sx][:, nh * 512:(nh + 1) * 512], scalar=ALPHA,
                        in1=pB[pa + nh][:], op0=ALU.mult, op1=ALU.add),
                        reads=[t_xs[sx], t_pB[pa + nh]], writes=[t_rr[s]])
                for k in range(2):
                    P.add("dve", lambda e, s=s, k=k: e.bn_stats(out=stats[s][:, k, :], in_=rr[s][:, k * 512:(k + 1) * 512]),
                          reads=[t_rr[s]], writes=[t_stats[s]])
                P.add("dve", lambda e, s=s: e.bn_aggr(out=mv[s][:], in_=stats[s][:].rearrange("p a b -> p (a b)")),
                      reads=[t_stats[s]], writes=[t_mv[s]])
                P.add("act", lambda e, s=s: e.activation(out=sd[s][:], in_=mv[s][:, 1:2], func=AF.Sqrt, bias=epst[:, 0:1], scale=1.0),
                      reads=[t_mv[s], t_eps], writes=[t_sd[s]])
                P.add("dve", lambda e, s=s: e.reciprocal(out=rstd[s][:], in_=sd[s][:]), reads=[t_sd[s]], writes=[t_rstd[s]])
                P.add("dve", lambda e, s=s: e.tensor_scalar(
                    out=rr[s][:], in0=rr[s][:], scalar1=mv[s][:, 0:1], scalar2=rstd[s][:, 0:1], op0=ALU.subtract, op1=ALU.mult),
                    reads=[t_rr[s], t_mv[s], t_rstd[s]], writes=[t_rr[s]])
                P.add("pool", lambda e, s=s: e.tensor_tensor(out=oo[s][:], in0=rr[s][:], in1=gam[:], op=ALU.mult),
                      reads=[t_rr[s], t_gam], writes=[t_oo[s]])
                P.add("pool", lambda e, s=s: e.tensor_tensor(out=oo[s][:], in0=oo[s][:], in1=bet[:], op=ALU.add),
                      reads=[t_oo[s], t_bet], writes=[t_oo[s]])
                P.add("pool", lambda e, s=s, t=t: e.dma_start(out=dst_d[t * 128:(t + 1) * 128, :], in_=oo[s][:]),
                      reads=[t_oo[s]], writes=[t_dst[t]], kind="d")

        ensure_w(1)
        stage_a(0)
        stage_b(0)
        stage_a(1)
        stage_c(0)
        stage_b(1)
        stage_c(1)
        P.wait_all("pool", t_dst)
        P.emit(name)


C_RQ, C_RK, C_RV, C_RG = 0, 256, 512, 1024
C_DQ, C_DK, C_DV, C_IQ, C_IK, C_IW, C_MQ, C_G = 1536, 2048, 2560, 3072, 3584, 3648, 3656, 4168
IW_SCALE = float(8 ** -0.5 * 64 ** -0.5)
NIT = 16
GAMMAS = [1.0 - 2.0 ** (-5.0 - h) for h in range(4)]


def toks(k):
    return [Tok() for _ in range(k)]


class Env:
    def __init__(self, nc, semstack, dmaq, name, st):
        self.nc = nc
        self.P = Prog(nc, semstack, dmaq)
        self.name = name
        self.st = st
        self.k = 0

    def sb(self, n, shape, dt):
        return self.st.enter_context(self.nc.sbuf_tensor(f"{self.name}_{n}", shape, dt))

    def ps(self, n, shape, dt):
        return self.st.enter_context(self.nc.psum_tensor(f"{self.name}_{n}", shape, dt))


class WStream:
    def __init__(self, env, nst=2, nbf=3):
        self.env = env
        self.st = [env.sb(f"wsst{i}", [128, 8, 128], F32) for i in range(nst)]
        self.bf = [env.sb(f"wsbf{i}", [128, 8, 128], BF16) for i in range(nbf)]
        self.t_st = toks(nst)
        self.t_bf = toks(nbf)
        self.n = 0

    def load(self, pieces, KC=8, dst=None, t_dst=None):
        P = self.env.P
        s = self.n % len(self.st)
        b = self.n % len(self.bf)
        self.n += 1
        stt = self.st[s]
        for (c0, src, sc) in pieces:
            w = src.shape[-1]
            P.add("sync", lambda e, stt=stt, c0=c0, src=src, w=w, KC=KC: e.dma_start(
                out=stt[:, 0:KC, c0:c0 + w], in_=src.rearrange("(kc p) n -> p kc n", p=128)),
                writes=[self.t_st[s]], kind="d")
        if dst is None:
            tot = max(c0 + src.shape[-1] for (c0, src, sc) in pieces)
            out_t = self.bf[b]
            out_fn = lambda c0, w: out_t[:, 0:KC, c0:c0 + w]
            t_out = self.t_bf[b]
        else:
            out_fn = dst
            t_out = t_dst
        if all(sc == 1.0 for (_, _, sc) in pieces):
            lo = min(c0 for (c0, _, _) in pieces)
            hi = max(c0 + src.shape[-1] for (c0, src, _) in pieces)
            P.add("pool", lambda e, lo=lo, hi=hi, stt=stt, KC=KC: e.tensor_copy(out=out_fn(lo, hi - lo), in_=stt[:, 0:KC, lo:hi]),
                  reads=[self.t_st[s]], writes=[t_out])
        else:
            for (c0, src, sc) in pieces:
                w = src.shape[-1]
                P.add("dve", lambda e, c0=c0, w=w, sc=sc, stt=stt, KC=KC: e.tensor_scalar(
                    out=out_fn(c0, w), in0=stt[:, 0:KC, c0:c0 + w], scalar1=float(sc), scalar2=None, op0=ALU.mult),
                    reads=[self.t_st[s]], writes=[t_out])
        return (self.bf[b] if dst is None else None), t_out


def load_transposed(env, src_d, ntiles, dstT, t_dstT, idb, t_idb, pT, t_pT):
    P = env.P
    xs = [env.sb(f"ltxs{i}", [128, D], F32) for i in range(2)]
    xb = [env.sb(f"ltxb{i}", [128, D], BF16) for i in range(2)]
    t_xs = toks(2); t_xb = toks(2)
    for t in range(ntiles):
        s = t % 2
        P.add("sync", lambda e, s=s, t=t: e.dma_start(out=xs[s][:], in_=src_d[t * 128:(t + 1) * 128, :]),
              writes=[t_xs[s]], kind="d")
        P.add("dve", lambda e, s=s: e.tensor_copy(out=xb[s][:], in_=xs[s][:]), reads=[t_xs[s]], writes=[t_xb[s]])
        for kc in range(8):
            P.add("pe", lambda e, s=s, kc=kc: e.transpose(out=pT[s][:, kc * 128:(kc + 1) * 128],
                                                          in_=xb[s][:, kc * 128:(kc + 1) * 128], identity=idb[:]),
                  reads=[t_xb[s], t_idb], writes=[t_pT[s]])
        P.add("act", lambda e, s=s, t=t: e.activation(
            out=dstT[:, :, t * 128:(t + 1) * 128], in_=pT[s][:].rearrange("p (a b) -> p a b", a=8), func=AF.Copy),
            reads=[t_pT[s]], writes=[t_dstT[t]])


def load_ident(env, ident_d):
    P = env.P
    idf = env.sb("idf", [128, 128], F32)
    idb = env.sb("idb", [128, 128], BF16)
    t_idf = Tok(); t_idb = Tok()
    P.add("sync", lambda e: e.dma_start(out=idf[:], in_=ident_d), writes=[t_idf], kind="d")
    P.add("dve", lambda e: e.tensor_copy(out=idb[:], in_=idf[:]), reads=[t_idf], writes=[t_idb])
    return idf, t_idf, idb, t_idb


class Banks:
    def __init__(self, env, n):
        self.b = [env.ps(f"bk{i}", [128, 512], F32) for i in range(n)]
        self.t = toks(n)
        self.i = 0
        self.n = n

    def next(self):
        i = self.i % self.n
        self.i += 1
        return self.b[i], self.t[i]


def proj_fm(env, ws, banks, pieces, ncols, rhs_fn, rhs_toks_fn, nblk, blkw, evac, KC=8):
    P = env.P
    wt, t_w = ws.load(pieces, KC=KC)
    for tb in range(nblk):
        bank, t_bank = banks.next()
        for kc in range(KC):
            P.add("pe", lambda e, bank=bank, kc=kc, tb=tb, wt=wt: e.matmul(
                bank[0:ncols, 0:blkw], lhsT=wt[:, kc, 0:ncols], rhs=rhs_fn(kc, tb), start=(kc == 0), stop=(kc == KC - 1)),
                reads=[t_w] + rhs_toks_fn(tb), writes=[t_bank])
        evac(tb, bank, t_bank)


def mix_load_stage(nc, semstack, dmaq, x1_d, ident_d, x1T):
    with contextlib.ExitStack() as st:
        env = Env(nc, semstack, dmaq, "m0", st)
        idf, t_idf, idb, t_idb = load_ident(env, ident_d)
        pT = [env.ps(f"pT{i}", [128, 1024], BF16) for i in range(2)]
        t_pT = toks(2)
        t_x1T = toks(16)
        load_transposed(env, x1_d, 16, x1T, t_x1T, idb, t_idb, pT, t_pT)
        env.P.emit("m0")


def mem_stage(nc, semstack, dmaq, x1T, oT, mem_d, w_in, wmkv, ident_d):
    with contextlib.ExitStack() as st:
        env = Env(nc, semstack, dmaq, "m3", st)
        P = env.P
        idf, t_idf, idb, t_idb = load_ident(env, ident_d)
        pT = [env.ps(f"pT{i}", [128, 1024], BF16) for i in range(2)]
        t_pT = toks(2)
        banks = Banks(env, 2)
        pN2 = [env.ps(f"pN{i}", [128, 512], F32) for i in range(2)]; t_pN2 = toks(2)
        pD2 = [env.ps(f"pD{i}", [128, 512], F32) for i in range(2)]; t_pD2 = toks(2)
        memT = env.sb("memT", [128, 8, 256], BF16); t_memT = toks(2)
        mkT = env.sb("mkT", [128, 4, 256], BF16); t_mkT = toks(4)
        mvv = env.sb("mv", [128, 2, 512], BF16); t_mv = toks(2)
        wmv = env.sb("wmv", [128, 8, 512], BF16); t_wmv = toks(4)
        mqT = env.sb("mqT", [128, 4, L], BF16); t_mqT = [toks(4) for _ in range(4)]
        ones = env.sb("ones", [128, 128], BF16); t_ones = Tok()
        E = [env.sb(f"E{i}", [128, 512], BF16) for i in range(4)]; t_E = toks(4)
        rden = [env.sb(f"rden{i}", [128, 512], F32) for i in range(2)]; t_rden = toks(2)
        ws = WStream(env)
        P.add("pool", lambda e: e.memset(ones[:], 1.0), writes=[t_ones])
        load_transposed(env, mem_d, 2, memT, t_memT, idb, t_idb, pT, t_pT)
        t_x1T = []
        for h in range(4):
            def evac(tb, bank, t_bank, h=h):
                P.add("act", lambda e: e.activation(out=mkT[:, h, :], in_=bank[:, 0:256], func=AF.Copy),
                      reads=[t_bank], writes=[t_mkT[h]])
            proj_fm(env, ws, banks, [(0, wmkv[:, h * 128:(h + 1) * 128], 1.0)], 128,
                    lambda kc, tb: memT[:, kc, :], lambda tb: t_memT, 1, 256, evac)
        for c in range(4):
            ws.load([(0, wmkv[:, 512 + c * 128:512 + (c + 1) * 128], 1.0)],
                    dst=lambda c0, w, c=c: wmv[:, :, c * 128 + c0:c * 128 + c0 + w], t_dst=t_wmv[c])
        for mt in range(2):
            bank, t_bank = banks.next()
            for kc in range(8):
                P.add("pe", lambda e, bank=bank, kc=kc, mt=mt: e.matmul(
                    bank[:], lhsT=memT[:, kc, mt * 128:(mt + 1) * 128], rhs=wmv[:, kc, :], start=(kc == 0), stop=(kc == 7)),
                    reads=[t_memT[mt]] + t_wmv, writes=[t_bank])
            P.add("act", lambda e, bank=bank, mt=mt: e.activation(out=mvv[:, mt, :], in_=bank[:], func=AF.Copy),
                  reads=[t_bank], writes=[t_mv[mt]])
        for h in range(4):
            def evac(tb, bank, t_bank, h=h):
                P.add("act", lambda e: e.activation(out=mqT[:, h, tb * 512:(tb + 1) * 512], in_=bank[:], func=AF.Copy,
                                                    scale=float(128 ** -0.5)),
                      reads=[t_bank], writes=[t_mqT[h][tb]])
            proj_fm(env, ws, banks, [(0, w_in[:, C_MQ + h * 128:C_MQ + (h + 1) * 128], 1.0)], 128,
                    lambda kc, tb: x1T[:, kc, tb * 512:(tb + 1) * 512], lambda tb: [], 4, 512, evac)
        it = 0
        for h in range(4):
            for qb in range(4):
                for mt in range(2):
                    bank, t_bank = banks.next()
                    ei = (it * 2 + mt) % 4
                    P.add("pe", lambda e, bank=bank, h=h, qb=qb, mt=mt: e.matmul(
                        bank[:], lhsT=mkT[:, h, mt * 128:(mt + 1) * 128], rhs=mqT[:, h, qb * 512:(qb + 1) * 512],
                        start=True, stop=True), reads=[t_mkT[h], t_mqT[h][qb]], writes=[t_bank])
                    P.add("act", lambda e, bank=bank, ei=ei: e.activation(out=E[ei][:], in_=bank[:], func=AF.Exp),
                          reads=[t_bank], writes=[t_E[ei]])
                pN = pN2[it % 2]; t_pN = t_pN2[it % 2]; pD = pD2[it % 2]; t_pD = t_pD2[it % 2]
                for mt in range(2):
                    ei = (it * 2 + mt) % 4
                    P.add("pe", lambda e, ei=ei, h=h, mt=mt, pN=pN: e.matmul(
                        pN[:], lhsT=mvv[:, mt, h * 128:(h + 1) * 128], rhs=E[ei][:], start=(mt == 0), stop=(mt == 1)),
                        reads=[t_mv[mt], t_E[ei]], writes=[t_pN])
                    P.add("pe", lambda e, ei=ei, mt=mt, pD=pD: e.matmul(
                        pD[:], lhsT=ones[:], rhs=E[ei][:], start=(mt == 0), stop=(mt == 1)),
                        reads=[t_ones, t_E[ei]], writes=[t_pD])
                r = it % 2
                P.add("dve", lambda e, r=r, pD=pD: e.reciprocal(out=rden[r][:], in_=pD[:]), reads=[t_pD], writes=[t_rden[r]])
                P.add("dve", lambda e, r=r, h=h, qb=qb, pN=pN: e.tensor_tensor(
                    out=oT[:, h, qb * 512:(qb + 1) * 512], in0=pN[:], in1=rden[r][:], op=ALU.mult),
                    reads=[t_pN, t_rden[r]], writes=[Tok()])
                it += 1
        env.P.emit("m3")


def dsa_stage(nc, semstack, dmaq, x1T, oT, w_in, t5_d, ident_d, J_d, oh_d, causal_d, pow2_d, gscr_d):
    with contextlib.ExitStack() as st:
        env = Env(nc, semstack, dmaq, "m1", st)
        P = env.P
        idf, t_idf, idb, t_idb = load_ident(env, ident_d)
        banks = Banks(env, 3)
        pO = [[env.ps(f"pO{i}{j}", [128, 512], F32) for j in range(2)] for i in range(2)]
        t_pO = [toks(2) for _ in range(2)]
        pM = env.ps("pM", [128, 1024], BF16); t_pM = Tok()
        qT = env.sb("qT", [128, 4, L], BF16); t_qT = [toks(4) for _ in range(4)]
        kT = env.sb("kT", [128, 4, L], BF16); t_kT = [toks(4) for _ in range(4)]
        qiT = env.sb("qiT", [128, 4, L], BF16); t_qiT = [toks(4) for _ in range(4)]
        kiT = env.sb("kiT", [128, L], BF16); t_kiT = toks(4)
        vaug = env.sb("vaug", [128, 16, 8, 65], BF16); t_v = toks(16); t_vones = Tok()
        iw = env.sb("iw", [128, 16, 8], F32); t_iw = toks(16)
        wv = env.sb("wv", [128, 8, 512], BF16); t_wv = toks(4)
        wiw = env.sb("wiw", [128, 8, 8], BF16); t_wiw = Tok()
        Sc2 = [env.sb(f"Sc{i}", [128, L], F32) for i in range(2)]; t_Sc2 = toks(2)
        junk2 = [env.sb(f"junk{i}", [128, L], BF16) for i in range(2)]; t_junk2 = toks(2)
        mask = [env.sb(f"mask{i}", [128, L], BF16) for i in range(2)]; t_mask = toks(2)
        maskTp = [env.sb(f"maskTp{i}", [128, 16, 2, 128], BF16) for i in range(2)]
        t_maskTp = [toks(2) for _ in range(2)]
        tI = [env.sb(f"tI{i}", [128, 512], F32) for i in range(2)]; t_tI = toks(2)
        E = [env.sb(f"E{i}", [128, 512], BF16) for i in range(2)]; t_E = toks(2)
        PT = [env.sb(f"PT{i}", [128, 512], BF16) for i in range(3)]; t_PT = toks(3)
        BT = env.sb("BT", [128, 8, 256], BF16); t_BT = toks(8)
        H = Sc2[1][:].rearrange("p (h n) -> p h n", h=8); t_H = t_Sc2[1]
        rden8 = [env.sb(f"rden{i}", [128, 2, 4], F32) for i in range(2)]; t_rden8 = toks(2)
        o_tm = [env.sb(f"otm{i}", [128, 512], BF16) for i in range(2)]; t_otm = toks(2)
        Jf = env.sb("Jf", [128, 128], F32); t_J = Tok()
        caus = env.sb("caus", [128, 128], F32); t_caus = Tok()
        pow2 = env.sb("pow2", [128, NIT], F32); t_pow2 = Tok()
        tabS = env.sb("tabS", [32, 8], F32); t_tab = Tok()
        ohS = env.sb("ohS", [32, 384], F32); t_oh = Tok()
        Gs = env.sb("Gs", [8, 384], F32); t_Gs = Tok()
        thrneg = env.sb("thrneg", [128, 1], F32); t_thrneg = Tok()
        mx8_2 = [env.sb(f"mx8{i}", [128, 8], F32) for i in range(2)]; t_mx8_2 = toks(2)
        mn_2 = [env.sb(f"mn{i}", [128, 1], F32) for i in range(2)]; t_mn_2 = toks(2)
        rng_2 = [env.sb(f"rng{i}", [128, 1], F32) for i in range(2)]; t_rng_2 = toks(2)
        thr_2 = [env.sb(f"thr{i}", [128, 1], F32) for i in range(2)]; t_thr_2 = toks(2)
        S2_2 = [env.sb(f"S2{i}", [128, NIT], F32) for i in range(2)]; t_S2_2 = toks(2)
        cnt_2 = [env.sb(f"cnt{i}", [128, 1], F32) for i in range(2)]; t_cnt_2 = toks(2)
        ee_2 = [env.sb(f"ee{i}", [128, 1], F32) for i in range(2)]; t_ee_2 = toks(2)
        ws = WStream(env)
        t_gscr = Tok()
        print("[dsa] sbuf bytes remaining/partition:", nc.sbuf_bytes_remaining() if callable(nc.sbuf_bytes_remaining) else nc.sbuf_bytes_remaining)

        P.add("sync", lambda e: e.dma_start(out=Jf[:], in_=J_d), writes=[t_J], kind="d")
        P.add("sync", lambda e: e.dma_start(out=caus[:], in_=causal_d), writes=[t_caus], kind="d")
        P.add("sync", lambda e: e.dma_start(out=pow2[:], in_=pow2_d), writes=[t_pow2], kind="d")
        P.add("sync", lambda e: e.dma_start(out=tabS[:], in_=t5_d), writes=[t_tab], kind="d")
        P.add("sync", lambda e: e.dma_start(out=ohS[:], in_=oh_d), writes=[t_oh], kind="d")
        P.add("pool", lambda e: e.memset(vaug[:, :, :, 64:65], 1.0), writes=[t_vones])
        for i in range(2):
            P.add("pool", lambda e, i=i: e.memset(maskTp[i][:], 0.0), writes=t_maskTp[i])
        P.add("pool", lambda e: e.memset(thrneg[:], -1.0e29), writes=[t_thrneg])
        bank, t_bank = banks.next()
        P.add("pe", lambda e, bank=bank: e.matmul(bank[0:8, 0:384], lhsT=tabS[:], rhs=ohS[:], start=True, stop=True),
              reads=[t_tab, t_oh], writes=[t_bank])
        P.add("act", lambda e, bank=bank: e.activation(out=Gs[:], in_=bank[0:8, 0:384], func=AF.Copy), reads=[t_bank], writes=[t_Gs])
        P.add("sync", lambda e: e.dma_start(out=gscr_d, in_=Gs[:]), reads=[t_Gs], writes=[t_gscr], kind="d")
        hank = bass.AP(gscr_d.tensor, gscr_d.offset, [[1, 128], [384, 8], [1, 256]])
        P.add("sync", lambda e: e.dma_start(out=H, in_=hank), reads=[t_gscr], writes=[t_H], kind="d")
        for h in range(8):
            bank, t_bank = banks.next()
            P.add("pe", lambda e, bank=bank, h=h: e.matmul(bank[:, 0:256], lhsT=Jf[:], rhs=H[:, h, :], start=True, stop=True),
                  reads=[t_J, t_H], writes=[t_bank])
            P.add("act", lambda e, bank=bank, h=h: e.activation(out=BT[:, h, :], in_=bank[:, 0:256], func=AF.Copy),
                  reads=[t_bank], writes=[t_BT[h]])

        ev = {"i": 0}

        def evac_to(dstfn, t_dstfn, scale, only_act=False):
            def evac(tb, bank, t_bank):
                eng = "act" if (only_act or ev["i"] % 2 == 0) else "dve"
                ev["i"] += 1
                if eng == "act":
                    P.add("act", lambda e: e.activation(out=dstfn(tb), in_=bank[:], func=AF.Copy, scale=float(scale)),
                          reads=[t_bank], writes=[t_dstfn(tb)])
                else:
                    P.add("dve", lambda e: e.tensor_scalar(out=dstfn(tb), in0=bank[:], scalar1=float(scale), scalar2=None, op0=ALU.mult),
                          reads=[t_bank], writes=[t_dstfn(tb)])
            return evac

        xrhs = lambda kc, tb: x1T[:, kc, tb * 512:(tb + 1) * 512]

        def proj_indexer():
            for c in range(4):
                proj_fm(env, ws, banks, [(0, w_in[:, C_IQ + c * 128:C_IQ + (c + 1) * 128], 1.0)], 128, xrhs, lambda tb: [], 4, 512,
                        evac_to(lambda tb, c=c: qiT[:, c, tb * 512:(tb + 1) * 512], lambda tb, c=c: t_qiT[c][tb], 1.0))
            proj_fm(env, ws, banks, [(0, w_in[:, C_IK:C_IK + 64], 1.0), (64, w_in[:, C_IK:C_IK + 64], 1.0)], 128, xrhs, lambda tb: [], 4, 512,
                    evac_to(lambda tb: kiT[:, tb * 512:(tb + 1) * 512], lambda tb: t_kiT[tb], 1.0))
            ws.load([(0, w_in[:, C_IW:C_IW + 8], 1.0)], dst=lambda c0, w: wiw[:, :, c0:c0 + w], t_dst=t_wiw)
            for t in range(16):
                bank, t_bank = banks.next()
                for kc in range(8):
                    P.add("pe", lambda e, bank=bank, kc=kc, t=t: e.matmul(
                        bank[:, 0:8], lhsT=x1T[:, kc, t * 128:(t + 1) * 128], rhs=wiw[:, kc, :], start=(kc == 0), stop=(kc == 7)),
                        reads=[t_wiw], writes=[t_bank])
                P.add("dve", lambda e, bank=bank, t=t: e.tensor_scalar(out=iw[:, t, :], in0=bank[:, 0:8], scalar1=IW_SCALE, scalar2=None, op0=ALU.mult),
                      reads=[t_bank], writes=[t_iw[t]])

        def proj_qkv():
            for c in range(4):
                proj_fm(env, ws, banks, [(0, w_in[:, C_DQ + c * 128:C_DQ + (c + 1) * 128], 1.0)], 128, xrhs, lambda tb: [], 4, 512,
                        evac_to(lambda tb, c=c: qT[:, c, tb * 512:(tb + 1) * 512], lambda tb, c=c: t_qT[c][tb], 0.125, only_act=True))
                proj_fm(env, ws, banks, [(0, w_in[:, C_DK + c * 128:C_DK + (c + 1) * 128], 1.0)], 128, xrhs, lambda tb: [], 4, 512,
                        evac_to(lambda tb, c=c: kT[:, c, tb * 512:(tb + 1) * 512], lambda tb, c=c: t_kT[c][tb], 1.0, only_act=True))
            for c in range(4):
                ws.load([(0, w_in[:, C_DV + c * 128:C_DV + (c + 1) * 128], 1.0)],
                        dst=lambda c0, w, c=c: wv[:, :, c * 128 + c0:c * 128 + c0 + w], t_dst=t_wv[c])
            for t in range(16):
                bank, t_bank = banks.next()
                for kc in range(8):
                    P.add("pe", lambda e, bank=bank, kc=kc, t=t: e.matmul(
                        bank[:], lhsT=x1T[:, kc, t * 128:(t + 1) * 128], rhs=wv[:, kc, :], start=(kc == 0), stop=(kc == 7)),
                        reads=t_wv, writes=[t_bank])
                P.add("act", lambda e, bank=bank, t=t: e.activation(out=vaug[:, t, :, 0:64], in_=bank[:].rearrange("p (h d) -> p h d", h=8),
                                                                    func=AF.Copy), reads=[t_bank], writes=[t_v[t]])

        cI = {"i": 0}

        def idx_phase(n):
            Sc = Sc2[n % 2]; t_Sc = t_Sc2[n % 2]
            W = 128 * (n + 1)
            nb = (W + 511) // 512
            for h in range(8):
                hp = (h % 2) * 64
                for kb in range(nb):
                    w = min(512, W - kb * 512)
                    bank, t_bank = banks.next()
                    P.add("pe", lambda e, bank=bank, h=h, hp=hp, kb=kb, w=w, n=n: e.matmul(
                        bank[:, 0:w], lhsT=qiT[hp:hp + 64, h // 2, n * 128:(n + 1) * 128], rhs=kiT[hp:hp + 64, kb * 512:kb * 512 + w],
                        start=True, stop=True),
                        reads=[t_qiT[h // 2][n // 4]] + t_kiT[0:nb], writes=[t_bank])
                    s = cI["i"] % 2
                    cI["i"] += 1
                    P.add("act", lambda e, bank=bank, s=s, w=w: e.activation(out=tI[s][:, 0:w], in_=bank[:, 0:w], func=AF.Relu),
                          reads=[t_bank], writes=[t_tI[s]])
                    if h == 0:
                        P.add("dve", lambda e, s=s, kb=kb, w=w, n=n: e.tensor_scalar(
                            out=Sc[:, kb * 512:kb * 512 + w], in0=tI[s][:, 0:w], scalar1=iw[:, n, 0:1], scalar2=None, op0=ALU.mult),
                            reads=[t_tI[s], t_iw[n]], writes=[t_Sc])
                    else:
                        P.add("dve", lambda e, s=s, kb=kb, w=w, n=n, h=h: e.scalar_tensor_tensor(
                            out=Sc[:, kb * 512:kb * 512 + w], in0=tI[s][:, 0:w], scalar=iw[:, n, h:h + 1],
                            in1=Sc[:, kb * 512:kb * 512 + w], op0=ALU.mult, op1=ALU.add),
                            reads=[t_tI[s], t_iw[n], t_Sc], writes=[t_Sc])
            P.add("dve", lambda e, n=n: e.tensor_tensor(out=Sc[:, n * 128:(n + 1) * 128], in0=Sc[:, n * 128:(n + 1) * 128],
                                                        in1=caus[:], op=ALU.add),
                  reads=[t_Sc, t_caus], writes=[t_Sc])

        def bis_ops(n):
            ch = n % 2
            Sc = Sc2[ch]; t_Sc = t_Sc2[ch]; junk = junk2[ch]; t_junk = t_junk2[ch]
            mx8 = mx8_2[ch]; t_mx8 = t_mx8_2[ch]; mn = mn_2[ch]; t_mn = t_mn_2[ch]; rng = rng_2[ch]; t_rng = t_rng_2[ch]
            thr = thr_2[ch]; t_thr = t_thr_2[ch]; S2 = S2_2[ch]; t_S2 = t_S2_2[ch]; cnt = cnt_2[ch]; t_cnt = t_cnt_2[ch]
            ee = ee_2[ch]; t_ee = t_ee_2[ch]
            W = 128 * (n + 1)
            mk = mask[ch]
            t_mk = t_mask[ch]
            if n < 2:
                yield lambda: P.add("dve", lambda e: e.tensor_scalar(out=mk[:, 0:W], in0=Sc[:, 0:W], scalar1=thrneg[:, 0:1], scalar2=None, op0=ALU.is_ge),
                                    reads=[t_Sc, t_thrneg], writes=[t_mk])
                return
            yield lambda: P.add("dve", lambda e: e.max(out=mx8[:], in_=Sc[:, 0:W]), reads=[t_Sc], writes=[t_mx8])
            yield lambda: P.add("dve", lambda e: e.tensor_reduce(out=mn[:], in_=Sc[:, 0:n * 128], axis=mybir.AxisListType.X, op=ALU.min),
                                reads=[t_Sc], writes=[t_mn])
            yield lambda: P.add("dve", lambda e: e.tensor_tensor(out=rng[:], in0=mx8[:, 0:1], in1=mn[:], op=ALU.subtract),
                                reads=[t_mx8, t_mn], writes=[t_rng])
            yield lambda: P.add("dve", lambda e: e.scalar_tensor_tensor(out=thr[:], in0=rng[:], scalar=0.5, in1=mn[:], op0=ALU.mult, op1=ALU.add),
                                reads=[t_rng, t_mn], writes=[t_thr])
            yield lambda: P.add("dve", lambda e: e.tensor_scalar(out=S2[:], in0=pow2[:], scalar1=rng[:, 0:1], scalar2=None, op0=ALU.mult),
                                reads=[t_pow2, t_rng], writes=[t_S2])
            for k in range(NIT):
                yield lambda: P.add("dve", lambda e: e.tensor_scalar(out=junk[:, 0:W], in0=Sc[:, 0:W], scalar1=thr[:, 0:1], scalar2=None,
                                                                     op0=ALU.is_ge, op1=ALU.add, accum_out=cnt[:, 0:1]),
                                    reads=[t_Sc, t_thr], writes=[t_junk, t_cnt])
                yield lambda: P.add("dve", lambda e: e.tensor_scalar(out=ee[:], in0=cnt[:], scalar1=255.5, scalar2=0.5, op0=ALU.is_ge, op1=ALU.subtract),
                                    reads=[t_cnt], writes=[t_ee])
                yield lambda k=k: P.add("dve", lambda e: e.scalar_tensor_tensor(out=thr[:], in0=ee[:], scalar=S2[:, k:k + 1], in1=thr[:],
                                                                                op0=ALU.mult, op1=ALU.add),
                                        reads=[t_ee, t_S2, t_thr], writes=[t_thr])
            yield lambda: P.add("dve", lambda e: e.tensor_scalar(out=mk[:, 0:W], in0=Sc[:, 0:W], scalar1=thr[:, 0:1], scalar2=None, op0=ALU.is_ge),
                                reads=[t_Sc, t_thr], writes=[t_mk])

        def bis_pair(a, b):
            ga, gb = bis_ops(a), bis_ops(b)
            done_a = done_b = False
            while not (done_a and done_b):
                if not done_a:
                    f = next(ga, None)
                    if f is None:
                        done_a = True
                    else:
                        f()
                if not done_b:
                    f = next(gb, None)
                    if f is None:
                        done_b = True
                    else:
                        f()

        def maskT_phase(n):
            mk = mask[n % 2]; t_mk = t_mask[n % 2]
            mp = maskTp[(n // 2) % 2]; t_mp = t_maskTp[(n // 2) % 2][n % 2]
            for m0 in range(0, n + 1, 8):
                k = min(8, n + 1 - m0)
                for i in range(k):
                    m = m0 + i
                    P.add("pe", lambda e, i=i, m=m: e.transpose(out=pM[:, i * 128:(i + 1) * 128], in_=mk[:, m * 128:(m + 1) * 128], identity=idb[:]),
                          reads=[t_mk, t_idb], writes=[t_pM])
                P.add("act", lambda e, m0=m0, k=k, n=n: e.activation(
                    out=mp[:, m0:m0 + k, n % 2, :], in_=pM[:, 0:k * 128].rearrange("p (a b) -> p a b", a=k), func=AF.Copy),
                    reads=[t_pM], writes=[t_mp])

        cA = {"e": 0, "p": 0}

        def att_pair(kp):
            a, b = 2 * kp, 2 * kp + 1
            mp = maskTp[kp % 2]; t_mp = t_maskTp[kp % 2]
            steps = [(c, m0, hh) for c in range(4) for m0 in range(0, b + 1, 2) for hh in range(2)]

            def emit_scores(c, m0, hh):
                h = 2 * c + hh; hp = hh * 64
                bank, t_bank = banks.next()
                for i in range(2):
                    m = m0 + i
                    biases = []
                    if m == a - 1:
                        biases.append((0, 128))
                    if m == a:
                        biases.append((0, 0)); biases.append((128, 128))
                    if m == b:
                        biases.append((128, 0))
                    P.add("pe", lambda e, bank=bank, i=i, m=m, hp=hp, c=c, nb=len(biases): e.matmul(
                        bank[:, i * 256:(i + 1) * 256], lhsT=kT[hp:hp + 64, c, m * 128:(m + 1) * 128],
                        rhs=qT[hp:hp + 64, c, a * 128:(a + 2) * 128], start=True, stop=(nb == 0)),
                        reads=[t_kT[c][m // 4], t_qT[c][a // 4]], writes=[t_bank])
                    for bi, (qo, off) in enumerate(biases):
                        P.add("pe", lambda e, bank=bank, i=i, h=h, qo=qo, off=off, last=(bi == len(biases) - 1): e.matmul(
                            bank[:, i * 256 + qo:i * 256 + qo + 128], lhsT=idb[:], rhs=BT[:, h, off:off + 128], start=False, stop=last),
                            reads=[t_idb, t_BT[h]], writes=[t_bank])
                se = cA["e"] % 2
                cA["e"] += 1
                P.add("act", lambda e, bank=bank, se=se: e.activation(out=E[se][:], in_=bank[:], func=AF.Exp),
                      reads=[t_bank], writes=[t_E[se]])
                sp = cA["p"] % 3
                cA["p"] += 1
                P.add("pool", lambda e, se=se, sp=sp, m0=m0: e.tensor_tensor(
                    out=PT[sp][:], in0=E[se][:], in1=mp[:, m0:m0 + 2, :, :].rearrange("p a b q -> p (a b q)"), op=ALU.mult),
                    reads=[t_E[se]] + t_mp, writes=[t_PT[sp]])
                return sp

            def emit_pv(c, m0, hh, sp):
                h = 2 * c + hh
                for i in range(2):
                    m = m0 + i
                    for t in (a, b):
                        if m > t:
                            continue
                        P.add("pe", lambda e, sp=sp, i=i, m=m, h=h, hh=hh, c=c, t=t, a=a: e.matmul(
                            pO[t % 2][hh][:, c * 65:(c + 1) * 65], lhsT=PT[sp][:, i * 256 + (t - a) * 128:i * 256 + (t - a) * 128 + 128],
                            rhs=vaug[:, m, h, :], start=(m == 0), stop=(m == t)),
                            reads=[t_v[m], t_vones, t_PT[sp]], writes=[t_pO[t % 2][hh]])

            prev = None
            for st_ in steps:
                sp = emit_scores(*st_)
                if prev is not None:
                    emit_pv(*prev)
                prev = st_ + (sp,)
            emit_pv(*prev)
            for t in (a, b):
                r = t % 2
                for hh in range(2):
                    P.add("dve", lambda e, r=r, hh=hh: e.reciprocal(
                        out=rden8[r][:, hh, :], in_=pO[r][hh][:, 0:260].rearrange("p (c d) -> p c d", c=4)[:, :, 64]),
                        reads=[t_pO[r][hh]], writes=[t_rden8[r]])
                for h in range(8):
                    P.add("act", lambda e, r=r, h=h: e.activation(
                        out=o_tm[r][:, h * 64:(h + 1) * 64], in_=pO[r][h % 2][:, (h // 2) * 65:(h // 2) * 65 + 64], func=AF.Copy,
                        scale=rden8[r][:, h % 2, h // 2:h // 2 + 1]),
                        reads=[t_pO[r][h % 2], t_rden8[r]], writes=[t_otm[r]])
                for c4 in range(4):
                    P.add("pe", lambda e, r=r, c4=c4: e.transpose(out=pM[:, c4 * 128:(c4 + 1) * 128], in_=o_tm[r][:, c4 * 128:(c4 + 1) * 128], identity=idb[:]),
                          reads=[t_otm[r], t_idb], writes=[t_pM])
                P.add("dve", lambda e, t=t: e.tensor_copy(out=oT[:, :, t * 128:(t + 1) * 128], in_=pM[:, 0:512].rearrange("p (a b) -> p a b", a=4)),
                      reads=[t_pM], writes=[Tok()])

        proj_indexer()
        idx_phase(14); idx_phase(15); bis_pair(14, 15)
        proj_qkv()
        maskT_phase(14); maskT_phase(15)
        for k in range(7, -1, -1):
            a, b = 2 * k, 2 * k + 1
            if k > 0:
                idx_phase(a - 2); idx_phase(b - 2)
                bis_pair(a - 2, b - 2)
            att_pair(k)
            if k > 0:
                maskT_phase(a - 2); maskT_phase(b - 2)
        env.P.emit("m1")


def ret_stage(nc, semstack, dmaq, x1T, oT, w_in, gng_d, gnb_d, ident_d, cos_d, sin_d, decay_d, kdec_d, qdec_d):
    with contextlib.ExitStack() as st:
        env = Env(nc, semstack, dmaq, "m2", st)
        P = env.P
        idf, t_idf, idb, t_idb = load_ident(env, ident_d)
        banks = Banks(env, 7)
        pM = env.ps("pM", [128, 1024], BF16); t_pM = Tok()
        rqT = env.sb("rqT", [128, 2, L], BF16); t_rqT = [toks(4) for _ in range(2)]
        rkT = env.sb("rkT", [128, 2, L], BF16); t_rkT = [toks(4) for _ in range(2)]
        qdT = env.sb("qdT", [128, 2, L], BF16); t_qdT = toks(16)
        rv = env.sb("rv", [128, 16, 512], BF16); t_rv = toks(16)
        kd = env.sb("kd", [128, 16, 256], BF16); t_kd = toks(16)
        wrv = env.sb("wrv", [128, 8, 512], BF16); t_wrv = toks(4)
        wrg = env.sb("wrg", [128, 8, 512], BF16); t_wrg = toks(4)
        cosT = env.sb("cosT", [128, L], F32); t_cos = Tok()
        sinT = env.sb("sinT", [128, L], F32); t_sin = Tok()
        decT = env.sb("decT", [128, 512], F32); t_dec = Tok()
        kdec = env.sb("kdec", [128, 256], F32); t_kdec = Tok()
        qdec = env.sb("qdec", [128, 2, 128], F32); t_qdec = Tok()
        gng = env.sb("gng", [128, 512], F32); t_gng = Tok()
        gnb = env.sb("gnb", [128, 512], F32); t_gnb = Tok()
        tmp = [env.sb(f"tmp{i}", [128, 512], F32) for i in range(2)]; t_tmp = toks(2)
        tmp2 = [env.sb(f"tmpb{i}", [128, 512], F32) for i in range(2)]; t_tmp2 = toks(2)
        PdT = [env.sb(f"PdT{i}", [128, 512], BF16) for i in range(2)]; t_PdT = toks(2)
        state = env.sb("state", [128, 2, 128], F32); t_state = Tok()
        state_bf = env.sb("state_bf", [128, 2, 128], BF16); t_state_bf = Tok()
        on = [env.sb(f"on{i}", [128, 512], F32) for i in range(2)]; t_on = toks(2)
        sg = [env.sb(f"sg{i}", [128, 512], F32) for i in range(2)]; t_sg = toks(2)
        orb = [env.sb(f"orb{i}", [128, 512], BF16) for i in range(2)]; t_orb = toks(2)
        stats = [env.sb(f"stats{i}", [128, 4, 6], F32) for i in range(2)]; t_stats = toks(2)
        mvv = [env.sb(f"mvv{i}", [128, 4, 2], F32) for i in range(2)]; t_mvv = toks(2)
        sd = [env.sb(f"sd{i}", [128, 4], F32) for i in range(2)]; t_sd = toks(2)
        rstd = [env.sb(f"rstd{i}", [128, 4], F32) for i in range(2)]; t_rstd = toks(2)
        epst = env.sb("epst", [128, 1], F32); t_eps = Tok()
        ws = WStream(env)
        for (dst, src, tk) in ((cosT, cos_d, t_cos), (sinT, sin_d, t_sin), (decT, decay_d, t_dec), (kdec, kdec_d, t_kdec)):
            P.add("sync", lambda e, dst=dst, src=src: e.dma_start(out=dst[:], in_=src), writes=[tk], kind="d")
        P.add("sync", lambda e: e.dma_start(out=qdec[:].rearrange("p a b -> p (a b)"), in_=qdec_d), writes=[t_qdec], kind="d")
        P.add("sync", lambda e: e.dma_start(out=gng[:], in_=bcast_rows(gng_d, 128)), writes=[t_gng], kind="d")
        P.add("sync", lambda e: e.dma_start(out=gnb[:], in_=bcast_rows(gnb_d, 128)), writes=[t_gnb], kind="d")
        P.add("dve", lambda e: e.memset(epst[:], LN_EPS), writes=[t_eps])
        neghalf = env.sb("neghalf", [128, 4], F32); t_neghalf = Tok()
        P.add("pool", lambda e: e.memset(neghalf[:], -0.5), writes=[t_neghalf])

        cR = {"i": 0}
        for (c0, dstT, t_dstT, sc) in ((C_RQ, rqT, t_rqT, 1.0), (C_RK, rkT, t_rkT, 0.125)):
            for c in range(2):
                base = c0 + c * 128
                wn, t_wn = ws.load([(0, w_in[:, base:base + 128], 1.0)])
                pieces = []
                for hh in range(2):
                    pieces.append((hh * 64, w_in[:, base + hh * 64 + 32:base + hh * 64 + 64], -1.0))
                    pieces.append((hh * 64 + 32, w_in[:, base + hh * 64:base + hh * 64 + 32], 1.0))
                wr, t_wr = ws.load(pieces)
                for tb in range(4):
                    bq, t_bq = banks.next()
                    br, t_br = banks.next()
                    for (bank, t_bank, wt, t_w) in ((bq, t_bq, wn, t_wn), (br, t_br, wr, t_wr)):
                        for kc in range(8):
                            P.add("pe", lambda e, bank=bank, wt=wt, kc=kc, tb=tb: e.matmul(
                                bank[:], lhsT=wt[:, kc, :], rhs=x1T[:, kc, tb * 512:(tb + 1) * 512], start=(kc == 0), stop=(kc == 7)),
                                reads=[t_w], writes=[t_bank])
                    s = cR["i"] % 2
                    cR["i"] += 1
                    tsl = slice(tb * 512, (tb + 1) * 512)
                    P.add("dve", lambda e, s=s, bq=bq, tsl=tsl: e.tensor_tensor(out=tmp[s][:], in0=bq[:], in1=cosT[:, tsl], op=ALU.mult),
                          reads=[t_bq, t_cos], writes=[t_tmp[s]])
                    P.add("dve", lambda e, s=s, br=br, tsl=tsl, sc=sc: e.scalar_tensor_tensor(
                        out=tmp2[s][:], in0=br[:], scalar=float(sc), in1=sinT[:, tsl], op0=ALU.mult, op1=ALU.mult),
                        reads=[t_br, t_sin], writes=[t_tmp2[s]])
                    P.add("dve", lambda e, s=s, tsl=tsl, sc=sc, dstT=dstT, c=c: e.scalar_tensor_tensor(
                        out=dstT[:, c, tsl], in0=tmp[s][:], scalar=float(sc), in1=tmp2[s][:], op0=ALU.mult, op1=ALU.add),
                        reads=[t_tmp[s], t_tmp2[s]], writes=[t_dstT[c][tb]])
        for c in range(4):
            ws.load([(0, w_in[:, C_RV + c * 128:C_RV + (c + 1) * 128], 1.0)],
                    dst=lambda c0, w, c=c: wrv[:, :, c * 128 + c0:c * 128 + c0 + w], t_dst=t_wrv[c])
        for c in range(4):
            ws.load([(0, w_in[:, C_RG + c * 128:C_RG + (c + 1) * 128], 1.0)],
                    dst=lambda c0, w, c=c: wrg[:, :, c * 128 + c0:c * 128 + c0 + w], t_dst=t_wrg[c])
        for t in range(16):
            tsl = slice(t * 128, (t + 1) * 128)
            bank, t_bank = banks.next()
            for kc in range(8):
                P.add("pe", lambda e, bank=bank, kc=kc, tsl=tsl: e.matmul(
                    bank[:], lhsT=x1T[:, kc, tsl], rhs=wrv[:, kc, :], start=(kc == 0), stop=(kc == 7)),
                    reads=t_wrv, writes=[t_bank])
            P.add("act", lambda e, bank=bank, t=t: e.activation(out=rv[:, t, :], in_=bank[:], func=AF.Copy),
                  reads=[t_bank], writes=[t_rv[t]])
            for c in range(2):
                P.add("pe", lambda e, c=c, tsl=tsl: e.transpose(out=pM[:, c * 128:(c + 1) * 128], in_=rkT[:, c, tsl], identity=idb[:]),
                      reads=[t_rkT[c][t // 4], t_idb], writes=[t_pM])
            P.add("dve", lambda e, t=t: e.tensor_tensor(out=kd[:, t, :], in0=pM[:, 0:256], in1=kdec[:], op=ALU.mult),
                  reads=[t_pM, t_kdec], writes=[t_kd[t]])
            for c in range(2):
                P.add("dve", lambda e, c=c, tsl=tsl: e.tensor_tensor(out=qdT[:, c, tsl], in0=rqT[:, c, tsl], in1=qdec[:, c, :], op=ALU.mult),
                      reads=[t_rqT[c][t // 4], t_qdec], writes=[t_qdT[t]])
        eo_banks = [[(banks.b[2 * i + j], banks.t[2 * i + j]) for j in range(2)] for i in range(2)]
        kg = Banks.__new__(Banks)
        kg.b = [banks.b[i] for i in (4, 5, 6)]; kg.t = [banks.t[i] for i in (4, 5, 6)]; kg.i = 0; kg.n = 3

        def phase_a(n):
            tsl = slice(n * 128, (n + 1) * 128)
            s = n % 2
            eo = eo_banks[n % 2]
            for h in range(4):
                hp = (h % 2) * 64; c = h // 2
                bk, t_bk = eo[h % 2]
                P.add("pe", lambda e, bk=bk, hp=hp, c=c, tsl=tsl: e.matmul(
                    bk[:, c * 128:(c + 1) * 128], lhsT=rkT[hp:hp + 64, c, tsl], rhs=rqT[hp:hp + 64, c, tsl], start=True, stop=True),
                    reads=[t_rkT[c][n // 4], t_rqT[c][n // 4]], writes=[t_bk])
            for par in range(2):
                bk, t_bk = eo[par]
                P.add("dve", lambda e, bk=bk, s=s, par=par: e.tensor_tensor(
                    out=PdT[s][:, par * 256:(par + 1) * 256], in0=bk[:, 0:256], in1=decT[:, par * 256:(par + 1) * 256], op=ALU.mult),
                    reads=[t_bk, t_dec], writes=[t_PdT[s]])
            for h in range(4):
                hp = (h % 2) * 64; c = h // 2
                pos = (h % 2) * 2 + c
                bk, t_bk = eo[h % 2]
                P.add("pe", lambda e, bk=bk, h=h, c=c, pos=pos, s=s, n=n: e.matmul(
                    bk[:, 256 + c * 128:256 + (c + 1) * 128], lhsT=PdT[s][:, pos * 128:(pos + 1) * 128], rhs=rv[:, n, h * 128:(h + 1) * 128],
                    start=True, stop=(n == 0)), reads=[t_PdT[s], t_rv[n]], writes=[t_bk])
                if n > 0:
                    P.add("pe", lambda e, bk=bk, hp=hp, c=c, tsl=tsl: e.matmul(
                        bk[:, 256 + c * 128:256 + (c + 1) * 128], lhsT=qdT[hp:hp + 64, c, tsl], rhs=state_bf[hp:hp + 64, c, :],
                        start=False, stop=True), reads=[t_qdT[n], t_state_bf], writes=[t_bk])
            if n < 15:
                bK, t_bK = kg.next()
                for h in range(4):
                    hp = (h % 2) * 64; c = h // 2
                    P.add("pe", lambda e, bK=bK, h=h, hp=hp, c=c, n=n: e.matmul(
                        bK[hp:hp + 64, c * 128:(c + 1) * 128], lhsT=kd[:, n, h * 64:(h + 1) * 64], rhs=rv[:, n, h * 128:(h + 1) * 128],
                        start=True, stop=True), reads=[t_kd[n], t_rv[n]], writes=[t_bK])
                for h in range(4):
                    hp = (h % 2) * 64; c = h // 2
                    if n == 0:
                        P.add("dve", lambda e, bK=bK, hp=hp, c=c: e.tensor_copy(out=state[hp:hp + 64, c, :], in_=bK[hp:hp + 64, c * 128:(c + 1) * 128]),
                              reads=[t_bK], writes=[t_state])
                    else:
                        cd = float(np.float32(np.exp(np.float32(128.0) * np.log(np.float32(GAMMAS[h])))))
                        P.add("dve", lambda e, bK=bK, hp=hp, c=c, cd=cd: e.scalar_tensor_tensor(
                            out=state[hp:hp + 64, c, :], in0=state[hp:hp + 64, c, :], scalar=cd, in1=bK[hp:hp + 64, c * 128:(c + 1) * 128],
                            op0=ALU.mult, op1=ALU.add), reads=[t_bK, t_state], writes=[t_state])
                P.add("act", lambda e: e.activation(out=state_bf[:], in_=state[:], func=AF.Copy), reads=[t_state], writes=[t_state_bf])
            bG, t_bG = kg.next()
            for kc in range(8):
                P.add("pe", lambda e, bG=bG, kc=kc, tsl=tsl: e.matmul(
                    bG[:], lhsT=x1T[:, kc, tsl], rhs=wrg[:, kc, :], start=(kc == 0), stop=(kc == 7)), reads=t_wrg, writes=[t_bG])
            P.add("act", lambda e, bG=bG, s=s: e.activation(out=sg[s][:], in_=bG[:], func=AF.Silu), reads=[t_bG], writes=[t_sg[s]])

        def phase_b(n):
            tsl = slice(n * 128, (n + 1) * 128)
            s = n % 2
            eo = eo_banks[n % 2]
            osl = lambda h: slice(256 + (h // 2) * 128, 256 + (h // 2 + 1) * 128)
            for h in range(4):
                bk, t_bk = eo[h % 2]
                P.add("dve", lambda e, bk=bk, h=h, s=s: e.bn_stats(out=stats[s][:, h, :], in_=bk[:, osl(h)]),
                      reads=[t_bk], writes=[t_stats[s]])
            for h in range(4):
                P.add("dve", lambda e, h=h, s=s: e.bn_aggr(out=mvv[s][:, h, :], in_=stats[s][:, h, :]), reads=[t_stats[s]], writes=[t_mvv[s]])
            P.add("pool", lambda e, s=s: e.tensor_scalar(out=sd[s][:], in0=mvv[s][:, :, 1], scalar1=LN_EPS, scalar2=None, op0=ALU.add),
                  reads=[t_mvv[s]], writes=[t_sd[s]])
            P.add("pool", lambda e, s=s: e.tensor_tensor(out=rstd[s][:], in0=sd[s][:], in1=neghalf[:], op=ALU.pow),
                  reads=[t_sd[s], t_neghalf], writes=[t_rstd[s]])
            for h in range(4):
                bk, t_bk = eo[h % 2]
                P.add("dve", lambda e, bk=bk, h=h, s=s: e.tensor_scalar(
                    out=on[s][:, h * 128:(h + 1) * 128], in0=bk[:, osl(h)], scalar1=mvv[s][:, h, 0:1],
                    scalar2=rstd[s][:, h:h + 1], op0=ALU.subtract, op1=ALU.mult),
                    reads=[t_bk, t_mvv[s], t_rstd[s]], writes=[t_on[s]])
            P.add("dve", lambda e, s=s: e.tensor_tensor(out=on[s][:], in0=on[s][:], in1=gng[:], op=ALU.mult),
                  reads=[t_on[s], t_gng], writes=[t_on[s]])
            P.add("dve", lambda e, s=s: e.tensor_tensor(out=on[s][:], in0=on[s][:], in1=gnb[:], op=ALU.add),
                  reads=[t_on[s], t_gnb], writes=[t_on[s]])
            P.add("dve", lambda e, s=s: e.tensor_tensor(out=orb[s][:], in0=on[s][:], in1=sg[s][:], op=ALU.mult),
                  reads=[t_on[s], t_sg[s]], writes=[t_orb[s]])
            for c4 in range(4):
                P.add("pe", lambda e, c4=c4, s=s: e.transpose(out=pM[:, c4 * 128:(c4 + 1) * 128], in_=orb[s][:, c4 * 128:(c4 + 1) * 128], identity=idb[:]),
                      reads=[t_orb[s], t_idb], writes=[t_pM])
            P.add("act", lambda e, tsl=tsl: e.activation(out=oT[:, :, tsl], in_=pM[:, 0:512].rearrange("p (a b) -> p a b", a=4), func=AF.Copy),
                  reads=[t_pM], writes=[Tok()])

        phase_a(0)
        for n in range(16):
            if n + 1 < 16:
                phase_a(n + 1)
            phase_b(n)
        env.P.emit("m2")


def merge_stage(nc, semstack, dmaq, x1T, oTs, w_in, wbrs, wo_d, g_d, b_d, src_d, dst_d):
    with contextlib.ExitStack() as st:
        env = Env(nc, semstack, dmaq, "m4", st)
        P = env.P
        banks = Banks(env, 8)
        mT = env.sb("mT", [128, 8, L], BF16); t_mT = [toks(4) for _ in range(8)]
        wout = env.sb("wout", [128, 8, D], BF16); t_wout = toks(8)
        sgm = [env.sb(f"sgm{i}", [128, 512], F32) for i in range(2)]; t_sgm = toks(2)
        tmp = [env.sb(f"tmp{i}", [128, 512], F32) for i in range(2)]; t_tmp = toks(2)
        acc = env.sb("acc", [128, L], F32); t_acc = toks(4)
        xs = [env.sb(f"xs{i}", [128, D], F32) for i in range(2)]; t_xs = toks(2)
        rr = [env.sb(f"rr{i}", [128, D], F32) for i in range(2)]; t_rr = toks(2)
        oo = [env.sb(f"oo{i}", [128, D], F32) for i in range(2)]; t_oo = toks(2)
        gam = env.sb("gam", [128, D], F32); t_gam = Tok()
        bet = env.sb("bet", [128, D], F32); t_bet = Tok()
        epst = env.sb("epst", [128, 1], F32); t_eps = Tok()
        stats = [env.sb(f"stats{i}", [128, 2, 6], F32) for i in range(2)]; t_stats = toks(2)
        mv = [env.sb(f"mv{i}", [128, 2], F32) for i in range(2)]; t_mv = toks(2)
        sd = [env.sb(f"sd{i}", [128, 1], F32) for i in range(2)]; t_sd = toks(2)
        rstd = [env.sb(f"rstd{i}", [128, 1], F32) for i in range(2)]; t_rstd = toks(2)
        t_dst = toks(16)
        ws = WStream(env, nst=2, nbf=6)
        P.add("sync", lambda e: e.dma_start(out=gam[:], in_=bcast_rows(g_d, 128)), writes=[t_gam], kind="d")
        P.add("sync", lambda e: e.dma_start(out=bet[:], in_=bcast_rows(b_d, 128)), writes=[t_bet], kind="d")
        P.add("dve", lambda e: e.memset(epst[:], LN_EPS), writes=[t_eps])
        cM = {"s": 0}
        for j in range(8):
            ws.load([(0, wo_d[:, j * 128:(j + 1) * 128], 1.0)], dst=lambda c0, w, j=j: wout[:, :, j * 128 + c0:j * 128 + c0 + w], t_dst=t_wout[j])
            for b in range(3):
                wgb, t_wgb = ws.load([(0, w_in[:, C_G + b * D + j * 128:C_G + b * D + (j + 1) * 128], 1.0)])
                wbb, t_wbb = ws.load([(0, wbrs[b][:, j * 128:(j + 1) * 128], 1.0)], KC=4)
                for tb in range(4):
                    tsl = slice(tb * 512, (tb + 1) * 512)
                    bG, t_bG = banks.next()
                    for kc in range(8):
                        P.add("pe", lambda e, bG=bG, wgb=wgb, kc=kc, tsl=tsl: e.matmul(
                            bG[:], lhsT=wgb[:, kc, :], rhs=x1T[:, kc, tsl], start=(kc == 0), stop=(kc == 7)),
                            reads=[t_wgb], writes=[t_bG])
                    bB, t_bB = banks.next()
                    for kc in range(4):
                        P.add("pe", lambda e, bB=bB, wbb=wbb, kc=kc, tsl=tsl, b=b: e.matmul(
                            bB[:], lhsT=wbb[:, kc, :], rhs=oTs[b][:, kc, tsl], start=(kc == 0), stop=(kc == 3)),
                            reads=[t_wbb], writes=[t_bB])
                    s = cM["s"] % 2
                    cM["s"] += 1
                    P.add("act", lambda e, bG=bG, s=s: e.activation(out=sgm[s][:], in_=bG[:], func=AF.Sigmoid), reads=[t_bG], writes=[t_sgm[s]])
                    if b == 0:
                        P.add("dve", lambda e, bB=bB, s=s, tsl=tsl: e.tensor_tensor(out=acc[:, tsl], in0=sgm[s][:], in1=bB[:], op=ALU.mult),
                              reads=[t_sgm[s], t_bB], writes=[t_acc[tb]])
                    else:
                        P.add("dve", lambda e, bB=bB, s=s: e.tensor_tensor(out=tmp[s][:], in0=sgm[s][:], in1=bB[:], op=ALU.mult),
                              reads=[t_sgm[s], t_bB], writes=[t_tmp[s]])
                        if b == 1:
                            P.add("dve", lambda e, s=s, tsl=tsl: e.tensor_tensor(out=acc[:, tsl], in0=acc[:, tsl], in1=tmp[s][:], op=ALU.add),
                                  reads=[t_acc[tb], t_tmp[s]], writes=[t_acc[tb]])
                        else:
                            P.add("dve", lambda e, s=s, j=j, tsl=tsl: e.tensor_tensor(out=mT[:, j, tsl], in0=acc[:, tsl], in1=tmp[s][:], op=ALU.add),
                                  reads=[t_acc[tb], t_tmp[s]], writes=[t_mT[j][tb]])
        for t in range(16):
            s = t % 2
            tsl = slice(t * 128, (t + 1) * 128)
            P.add("sync", lambda e, s=s, tsl=tsl: e.dma_start(out=xs[s][:], in_=src_d[tsl, :]), writes=[t_xs[s]], kind="d")
            for nh in range(2):
                bank, t_bank = banks.next()
                for kc in range(8):
                    P.add("pe", lambda e, bank=bank, kc=kc, nh=nh, tsl=tsl: e.matmul(
                        bank[:], lhsT=mT[:, kc, tsl], rhs=wout[:, kc, nh * 512:(nh + 1) * 512], start=(kc == 0), stop=(kc == 7)),
                        reads=[t_mT[kc][t // 4]] + t_wout[nh * 4:(nh + 1) * 4], writes=[t_bank])
                P.add("dve", lambda e, bank=bank, nh=nh, s=s: e.scalar_tensor_tensor(
                    out=rr[s][:, nh * 512:(nh + 1) * 512], in0=xs[s][:, nh * 512:(nh + 1) * 512], scalar=ALPHA, in1=bank[:],
                    op0=ALU.mult, op1=ALU.add), reads=[t_xs[s], t_bank], writes=[t_rr[s]])
            for k in range(2):
                P.add("dve", lambda e, s=s, k=k: e.bn_stats(out=stats[s][:, k, :], in_=rr[s][:, k * 512:(k + 1) * 512]),
                      reads=[t_rr[s]], writes=[t_stats[s]])
            P.add("dve", lambda e, s=s: e.bn_aggr(out=mv[s][:], in_=stats[s][:].rearrange("p a b -> p (a b)")),
                  reads=[t_stats[s]], writes=[t_mv[s]])
            P.add("act", lambda e, s=s: e.activation(out=sd[s][:], in_=mv[s][:, 1:2], func=AF.Sqrt, bias=epst[:, 0:1], scale=1.0),
                  reads=[t_mv[s], t_eps], writes=[t_sd[s]])
            P.add("dve", lambda e, s=s: e.reciprocal(out=rstd[s][:], in_=sd[s][:]), reads=[t_sd[s]], writes=[t_rstd[s]])
            P.add("dve", lambda e, s=s: e.tensor_scalar(
                out=rr[s][:], in0=rr[s][:], scalar1=mv[s][:, 0:1], scalar2=rstd[s][:, 0:1], op0=ALU.subtract, op1=ALU.mult),
                reads=[t_rr[s], t_mv[s], t_rstd[s]], writes=[t_rr[s]])
            P.add("pool", lambda e, s=s: e.tensor_tensor(out=oo[s][:], in0=rr[s][:], in1=gam[:], op=ALU.mult),
                  reads=[t_rr[s], t_gam], writes=[t_oo[s]])
            P.add("pool", lambda e, s=s: e.tensor_tensor(out=oo[s][:], in0=oo[s][:], in1=bet[:], op=ALU.add),
                  reads=[t_oo[s], t_bet], writes=[t_oo[s]])
            P.add("pool", lambda e, s=s, tsl=tsl: e.dma_start(out=dst_d[tsl, :], in_=oo[s][:]),
                  reads=[t_oo[s]], writes=[t_dst[t]], kind="d")
        P.wait_all("pool", t_dst)
        env.P.emit("m4")

def build_nc(stages=("ffn1", "mix", "ffn2"), dbg_mix=False, parts=("dsa", "ret", "mem", "merge")):
    nc = bass.Bass("TRN2", target_bir_lowering=False)
    din = lambda n, shape: nc.dram_tensor(n, shape, F32, kind="ExternalInput").ap()
    x_d = din("x", [L, D])
    mem_d = din("mem", [256, D])
    f1wi = din("ffn1_w_in", [D, 2 * DFF]); f1wo = din("ffn1_w_out", [DFF, D])
    ln1g = din("ln1_g", [1, D]); ln1b = din("ln1_b", [1, D])
    w_in = din("w_in", [D, W_IN_COLS]); t5 = din("t5_table", [32, 8])
    gng = din("ret_gn_g", [1, 512]); gnb = din("ret_gn_b", [1, 512])
    wmkv = din("w_mem_kv", [D, 1024])
    wbr = din("w_br_ret", [512, D]); wbd = din("w_br_dsa", [512, D]); wbm = din("w_br_mem", [512, D])
    wo = din("w_out", [D, D]); ln2g = din("ln2_g", [1, D]); ln2b = din("ln2_b", [1, D])
    f2wi = din("ffn2_w_in", [D, 2 * DFF]); f2wo = din("ffn2_w_out", [DFF, D])
    ln3g = din("ln3_g", [1, D]); ln3b = din("ln3_b", [1, D])
    ident = din("c_ident", [128, 128])
    cJ = din("c_J", [128, 128]); coh = din("c_oh", [32, 384]); ccausal = din("c_causal", [128, 128])
    cpow2 = din("c_pow2", [128, NIT])
    ccos = din("c_cos", [128, L]); csin = din("c_sin", [128, L])
    cdecay = din("c_decay", [128, 512]); ckdec = din("c_kdec", [128, 256]); cqdec = din("c_qdec", [128, 256])
    gscr = nc.dram_tensor("g_scr", [8, 384], F32).ap()
    out_d = nc.dram_tensor("out", [L, D], F32, kind="ExternalOutput").ap()
    x1_d = nc.dram_tensor("x1_scr", [L, D], F32).ap()
    x2_d = nc.dram_tensor("x2_scr", [L, D], F32).ap()
    with contextlib.ExitStack() as semstack:
        dmaq = {}
        cur = x_d
        for i, s in enumerate(stages):
            last = i == len(stages) - 1
            if s == "ffn1":
                dst = out_d if last else x1_d
                ffn_stage(nc, semstack, dmaq, "f1", cur, f1wi, f1wo, ln1g, ln1b, dst, ident)
                cur = dst
            elif s == "mix":
                dst = out_d if last else x2_d
                with contextlib.ExitStack() as outer:
                    x1T = outer.enter_context(nc.sbuf_tensor("x1T", [128, 8, L], BF16))
                    mix_load_stage(nc, semstack, dmaq, cur, ident, x1T)
                    o_dsaT = outer.enter_context(nc.sbuf_tensor("o_dsaT", [128, 4, L], BF16))
                    if "dsa" in parts:
                        dsa_stage(nc, semstack, dmaq, x1T, o_dsaT, w_in, t5, ident, cJ, coh, ccausal, cpow2, gscr)
                    o_retT = outer.enter_context(nc.sbuf_tensor("o_retT", [128, 4, L], BF16))
                    if "ret" in parts:
                        ret_stage(nc, semstack, dmaq, x1T, o_retT, w_in, gng, gnb, ident, ccos, csin, cdecay, ckdec, cqdec)
                    o_memT = outer.enter_context(nc.sbuf_tensor("o_memT", [128, 4, L], BF16))
                    if "mem" in parts:
                        mem_stage(nc, semstack, dmaq, x1T, o_memT, mem_d, w_in, wmkv, ident)
                    if dbg_mix:
                        dbg = nc.dram_tensor("dbg", [128, 12, L], BF16, kind="ExternalOutput").ap()
                        with contextlib.ExitStack() as st:
                            env = Env(nc, semstack, dmaq, "dbg", st)
                            tt = toks(3)
                            for i, (o, pn) in enumerate(((o_retT, "ret"), (o_dsaT, "dsa"), (o_memT, "mem"))):
                                if pn not in parts:
                                    continue
                                env.P.add("sync", lambda e, i=i, o=o: e.dma_start(out=dbg[:, 4 * i:4 * i + 4, :], in_=o[:]), writes=[tt[i]], kind="d")
                            env.P.wait_all("sync", tt)
                            env.P.emit("dbg")
                    if "merge" in parts:
                        merge_stage(nc, semstack, dmaq, x1T, [o_retT, o_dsaT, o_memT], w_in, [wbr, wbd, wbm], wo, ln2g, ln2b, cur, dst)
                cur = dst
            elif s == "mixdbg":
                with contextlib.ExitStack() as outer:
                    x1T = outer.enter_context(nc.sbuf_tensor("x1T", [128, 8, L], BF16))
                    mix_load_stage(nc, semstack, dmaq, cur, ident, x1T)
                    o_dsaT = outer.enter_context(nc.sbuf_tensor("o_dsaT", [128, 4, L], BF16))
                    dsa_stage(nc, semstack, dmaq, x1T, o_dsaT, w_in, t5, ident, cJ, coh, ccausal, cpow2, gscr)
                    o_memT = outer.enter_context(nc.sbuf_tensor("o_memT", [128, 4, L], BF16))
                    mem_stage(nc, semstack, dmaq, x1T, o_memT, mem_d, w_in, wmkv, ident)
                    dbg = nc.dram_tensor("dbg", [128, 8, L], BF16, kind="ExternalOutput").ap()
                    with contextlib.ExitStack() as st:
                        env = Env(nc, semstack, dmaq, "dbg", st)
                        t1 = Tok(); t2 = Tok()
                        env.P.add("sync", lambda e: e.dma_start(out=dbg[:, 0:4, :], in_=o_dsaT[:]), writes=[t1], kind="d")
                        env.P.add("sync", lambda e: e.dma_start(out=dbg[:, 4:8, :], in_=o_memT[:]), writes=[t2], kind="d")
                        env.P.wait_all("sync", [t1, t2])
                        env.P.emit("dbg")
            elif s == "ffn2":
                dst = out_d if last else x2_d
                ffn_stage(nc, semstack, dmaq, "f2", cur, f2wi, f2wo, ln3g, ln3b, dst, ident)
                cur = dst
    return nc


_CACHE = {}


def _t5_bucket(n):
    n = np.maximum(n, 0)
    nf = np.maximum(n, 1).astype(np.float32)
    large = 16 + (np.log(nf / np.float32(16)) / np.float32(np.log(128 / 16)) * np.float32(16)).astype(np.int32)
    large = np.minimum(large, 31)
    return np.where(n < 16, n, large)


def make_consts():
    c = {}
    c["c_J"] = np.ascontiguousarray(np.eye(128, dtype=np.float32)[::-1])
    oh = np.zeros((32, 384), np.float32)
    for u in range(383):
        d = u - 127
        if d >= 0:
            oh[_t5_bucket(np.array(d)), u] += 1.0
            oh[31, u] -= 1.0
    c["c_oh"] = oh
    q = np.arange(128)[:, None]; sk = np.arange(128)[None, :]
    c["c_causal"] = np.where(sk <= q, 0.0, -1.0e30).astype(np.float32)
    half = 32
    freqs = (np.float32(10000.0) ** (-np.arange(half, dtype=np.float32) / np.float32(half))).astype(np.float32)
    ang = np.arange(L, dtype=np.float32)[None, :] * freqs[np.arange(128) % 32][:, None]
    c["c_cos"] = np.cos(ang).astype(np.float32)
    c["c_sin"] = np.sin(ang).astype(np.float32)
    lg = np.log(np.array(GAMMAS, dtype=np.float32))
    i = np.arange(128)
    dec = np.zeros((128, 4, 128), np.float32)
    for h in range(4):
        diff = i[None, :] - i[:, None]
        dec[:, h, :] = np.where(diff >= 0, np.exp(np.maximum(diff, 0).astype(np.float32) * lg[h]), 0.0)
    c["c_decay"] = np.ascontiguousarray(dec[:, [0, 2, 1, 3], :]).reshape(128, 512)
    kdec = np.zeros((128, 4, 64), np.float32)
    for h in range(4):
        kdec[:, h, :] = np.exp((127 - i).astype(np.float32) * lg[h])[:, None]
    c["c_kdec"] = kdec.reshape(128, 256)
    qdec = np.zeros((128, 2, 128), np.float32)
    for p in range(128):
        for cc in range(2):
            qdec[p, cc, :] = np.exp((i + 1).astype(np.float32) * lg[2 * cc + p // 64])
    c["c_qdec"] = qdec.reshape(128, 256)
    c["c_pow2"] = np.tile((2.0 ** -(np.arange(NIT) + 1.0)).astype(np.float32)[None, :], (128, 1))
    return c


def make_in_maps(inputs):
    f = lambda a: np.ascontiguousarray(np.asarray(a, dtype=np.float32))
    shared = {
        "ffn1_w_in": f(inputs["ffn1_w_in"][0]), "ffn1_w_out": f(inputs["ffn1_w_out"][0]),
        "ln1_g": f(inputs["ln1_g"]), "ln1_b": f(inputs["ln1_b"]),
        "w_in": f(inputs["w_in"][0]), "t5_table": f(inputs["t5_table"]),
        "ret_gn_g": f(inputs["ret_gn_g"]), "ret_gn_b": f(inputs["ret_gn_b"]),
        "w_mem_kv": f(inputs["w_mem_kv"][0]),
        "w_br_ret": f(inputs["w_br_ret"][0]), "w_br_dsa": f(inputs["w_br_dsa"][0]), "w_br_mem": f(inputs["w_br_mem"][0]),
        "w_out": f(inputs["w_out"][0]), "ln2_g": f(inputs["ln2_g"]), "ln2_b": f(inputs["ln2_b"]),
        "ffn2_w_in": f(inputs["ffn2_w_in"][0]), "ffn2_w_out": f(inputs["ffn2_w_out"][0]),
        "ln3_g": f(inputs["ln3_g"]), "ln3_b": f(inputs["ln3_b"]),
        "c_ident": np.eye(128, dtype=np.float32),
    }
    shared.update(make_consts())
    x = f(inputs["x"])
    mem = f(inputs["mem"])
    return [dict(shared, x=x[b], mem=mem[b]) for b in range(x.shape[0])]


def kernel(**inputs):
    if "nc" not in _CACHE:
        _CACHE["nc"] = build_nc()
    nc = _CACHE["nc"]
    in_maps = make_in_maps(inputs)
    res = run_bass_kernel_spmd(nc, in_maps, core_ids=list(range(len(in_maps))))
    return np.stack([np.asarray(r["out"], dtype=np.float32) for r in res.results], axis=0)
```

```python
import contextlib
import numpy as np
import concourse.bass as bass
import concourse.mybir as mybir
from concourse.bass_utils import run_bass_kernel_spmd

F32 = mybir.dt.float32
BF16 = mybir.dt.bfloat16
AF = mybir.ActivationFunctionType
ALU = mybir.AluOpType

L = 2048
D = 1024
DFF = 2816
NCH = DFF // 128
ALPHA = 2.0 ** 0.25
LN_EPS = 1e-5
W_IN_COLS = 7240

ENGS = ("pe", "dve", "act", "pool", "sync")
SEM_ROT = {"c": 8000}
DMA_RING = 8
RELAX_SAME_ENGINE = False


class Tok:
    __slots__ = ("w", "r", "name")

    def __init__(self, name=""):
        self.w = None
        self.r = {}
        self.name = name


class Ins:
    __slots__ = ("eng", "kind", "fn", "deps", "idx", "inc", "sem", "val")

    def __init__(self, eng, kind, fn):
        self.eng = eng
        self.kind = kind
        self.fn = fn
        self.deps = []
        self.inc = False
        self.sem = None
        self.val = 0


class Prog:
    def __init__(self, nc, semstack, dmaq):
        self.nc = nc
        self.semstack = semstack
        self.dmaq = dmaq
        self.streams = {e: [] for e in ENGS}
        self.n = 0

    def add(self, eng, fn, reads=(), writes=(), kind="c"):
        ins = Ins(eng, kind, fn)
        st = self.streams[eng]
        ins.idx = len(st)
        deps = {}

        def dep(d, typ):
            if d is None or d is ins:
                return
            if d.eng == eng and d.kind == "c" and kind == "c":
                if eng == "pe":
                    return
                if typ != "RAW":
                    return
                if RELAX_SAME_ENGINE and eng in ("dve", "act") and ins.idx - d.idx >= 2:
                    return
            deps[id(d)] = d

        for t in reads:
            dep(t.w, "RAW")
        for t in writes:
            dep(t.w, "WAW")
            for d in t.r.values():
                dep(d, "WAR")
        if kind == "d":
            q = self.dmaq.setdefault(eng, {"n": 0, "last": {}, "sems": []})
            n = q["n"]
            q["n"] += 1
            r = n % DMA_RING
            if len(q["sems"]) <= r:
                q["sems"].append(self.semstack.enter_context(self.nc.semaphore(f"s_dma_{eng}_{r}")))
            ins.sem = q["sems"][r]
            ins.val = 16 * (n // DMA_RING + 1)
            ins.inc = True
            prev = q["last"].get(r)
            if prev is not None:
                deps[id(prev)] = prev
            q["last"][r] = ins
        ins.deps = list(deps.values())
        for d in ins.deps:
            d.inc = True
        for t in reads:
            t.r[eng + kind] = ins
        for t in writes:
            t.w = ins
            t.r = {}
        st.append(ins)
        self.n += 1
        return ins

    def wait_all(self, eng, toks):
        return self.add(eng, None, reads=toks, kind="w")

    def emit(self, name):
        nc = self.nc
        nsem = 0
        for e in ENGS:
            for kind in ("c",):
                cnt = 0
                cur = None
                for ins in self.streams[e]:
                    if ins.kind != kind or not ins.inc:
                        continue
                    if cur is None or cnt >= SEM_ROT[kind]:
                        cur = self.semstack.enter_context(nc.semaphore(f"s_{name}_{e}_{kind}_{nsem}"))
                        nsem += 1
                        cnt = 0
                    cnt += 1
                    ins.sem = cur
                    ins.val = cnt * (16 if kind == "d" else 1)
        with nc.Block() as block:
            engmap = {"pe": block.tensor, "dve": block.vector, "act": block.scalar,
                      "pool": block.gpsimd, "sync": block.sync}
            for e in ENGS:
                stream = self.streams[e]
                if not stream:
                    continue

                def body(eobj, stream=stream):
                    waited = {}
                    for ins in stream:
                        for d in ins.deps:
                            k = id(d.sem)
                            if waited.get(k, 0) >= d.val:
                                continue
                            eobj.wait_ge(d.sem, d.val)
                            waited[k] = d.val
                        if ins.fn is None:
                            continue
                        r = ins.fn(eobj)
                        if ins.inc:
                            r.then_inc(ins.sem, 16 if ins.kind == "d" else 1)

                engmap[e](body)


def bcast_rows(ap2d, nrows):
    return bass.AP(ap2d.tensor, ap2d.offset, [[0, nrows], [1, ap2d.shape[-1]]])


def ffn_stage(nc, semstack, dmaq, name, src_d, w_in_d, w_out_d, g_d, b_d, dst_d, ident_d):
    with contextlib.ExitStack() as st:
        P = Prog(nc, semstack, dmaq)
        sb = lambda n, shape, dt: st.enter_context(nc.sbuf_tensor(f"{name}_{n}", shape, dt))
        ps = lambda n, shape, dt: st.enter_context(nc.psum_tensor(f"{name}_{n}", shape, dt))
        HT = 1024
        xT = [sb(f"xT{i}", [128, 8, HT], BF16) for i in range(2)]
        gT = sb("gT", [128, NCH, HT], BF16)
        wout = sb("wout", [128, NCH, D], BF16)
        wst = [sb(f"wst{i}", [128, 2, 8, 128], F32) for i in range(2)]
        wbf = [sb(f"wbf{i}", [128, 2, 8, 128], BF16) for i in range(3)]
        wost = [sb(f"wost{i}", [128, D], F32) for i in range(2)]
        xs = [sb(f"xs{i}", [128, D], F32) for i in range(2)]
        xb = [sb(f"xb{i}", [128, D], BF16) for i in range(2)]
        sA = [sb(f"sA{i}", [128, 512], F32) for i in range(2)]
        rr = [sb(f"rr{i}", [128, D], F32) for i in range(2)]
        oo = [sb(f"oo{i}", [128, D], F32) for i in range(2)]
        gam = sb("gam", [128, D], F32)
        bet = sb("bet", [128, D], F32)
        idf = sb("idf", [128, 128], F32)
        idb = sb("idb", [128, 128], BF16)
        epst = sb("epst", [128, 1], F32)
        stats = [sb(f"stats{i}", [128, 2, 6], F32) for i in range(2)]
        mv = [sb(f"mv{i}", [128, 2], F32) for i in range(2)]
        sd = [sb(f"sd{i}", [128, 1], F32) for i in range(2)]
        rstd = [sb(f"rstd{i}", [128, 1], F32) for i in range(2)]
        pT = [ps(f"pT{i}", [128, 1024], BF16) for i in range(2)]
        pB = [ps(f"pB{i}", [128, 512], F32) for i in range(6)]

        def toks(n, k):
            return [Tok(f"{n}{i}") for i in range(k)]

        t_xT = [toks("xT", 8) for _ in range(2)]
        t_gT = [[Tok() for _ in range(2)] for _ in range(NCH)]
        t_wout = toks("wout", NCH)
        t_wst = toks("wst", 2); t_wbf = toks("wbf", 3); t_wost = toks("wost", 2)
        t_xs = toks("xs", 2); t_xb = toks("xb", 2); t_sA = toks("sA", 2)
        t_ys = toks("ys", 2); t_rr = toks("rr", 2); t_oo = toks("oo", 2)
        t_gam = Tok(); t_bet = Tok(); t_idf = Tok(); t_idb = Tok(); t_eps = Tok()
        t_stats = toks("st", 2); t_mv = toks("mv", 2); t_sd = toks("sd", 2); t_rstd = toks("rs", 2)
        t_pT = toks("pT", 2); t_pB = toks("pB", 6)
        t_dst = toks("dst", 16)

        P.add("sync", lambda e: e.dma_start(out=idf[:], in_=ident_d), writes=[t_idf], kind="d")
        P.add("sync", lambda e: e.dma_start(out=gam[:], in_=bcast_rows(g_d, 128)), writes=[t_gam], kind="d")
        P.add("sync", lambda e: e.dma_start(out=bet[:], in_=bcast_rows(b_d, 128)), writes=[t_bet], kind="d")
        P.add("dve", lambda e: e.tensor_copy(out=idb[:], in_=idf[:]), reads=[t_idf], writes=[t_idb])
        P.add("dve", lambda e: e.memset(epst[:], LN_EPS), writes=[t_eps])

        cnt = {"x": 0, "pb": 0, "sa": 0, "c": 0}

        def stage_a(h):
            for tl in range(8):
                t = h * 8 + tl
                s = cnt["x"] % 2
                cnt["x"] += 1
                P.add("sync", lambda e, s=s, t=t: e.dma_start(out=xs[s][:], in_=src_d[t * 128:(t + 1) * 128, :]),
                      writes=[t_xs[s]], kind="d")
                P.add("dve", lambda e, s=s: e.tensor_copy(out=xb[s][:], in_=xs[s][:]), reads=[t_xs[s]], writes=[t_xb[s]])
                for kc in range(8):
                    P.add("pe", lambda e, s=s, kc=kc: e.transpose(out=pT[s][:, kc * 128:(kc + 1) * 128],
                                                                  in_=xb[s][:, kc * 128:(kc + 1) * 128], identity=idb[:]),
                          reads=[t_xb[s], t_idb], writes=[t_pT[s]])
                P.add("act", lambda e, s=s, h=h, tl=tl: e.activation(
                    out=xT[h][:, :, tl * 128:(tl + 1) * 128],
                    in_=pT[s][:].rearrange("p (a b) -> p a b", a=8), func=AF.Copy),
                    reads=[t_pT[s]], writes=[t_xT[h][tl]])

        wcount = {"c": 0}

        def load_w(c, with_out):
            s = wcount["c"] % 2
            bs = wcount["c"] % 3
            wcount["c"] += 1
            for j in range(2):
                col0 = j * DFF + c * 128
                P.add("sync", lambda e, s=s, j=j, col0=col0: e.dma_start(
                    out=wst[s][:, j], in_=w_in_d[:, col0:col0 + 128].rearrange("(kc p) n -> p kc n", p=128)),
                    writes=[t_wst[s]], kind="d")
            P.add("act", lambda e, s=s, bs=bs: e.activation(out=wbf[bs][:].rearrange("p a b c -> p (a b c)"),
                                                            in_=wst[s][:].rearrange("p a b c -> p (a b c)"), func=AF.Copy),
                  reads=[t_wst[s]], writes=[t_wbf[bs]])
            if with_out:
                P.add("sync", lambda e, s=s, c=c: e.dma_start(out=wost[s][:], in_=w_out_d[c * 128:(c + 1) * 128, :]),
                      writes=[t_wost[s]], kind="d")
                P.add("pool", lambda e, s=s, c=c: e.tensor_copy(out=wout[:, c, :], in_=wost[s][:]),
                      reads=[t_wost[s]], writes=[t_wout[c]])
            return bs

        wq = {"issued": 0, "bs": {}}

        def ensure_w(k):
            while wq["issued"] <= min(k, 2 * NCH - 1):
                i = wq["issued"]
                wq["bs"][i] = load_w(i % NCH, i < NCH)
                wq["issued"] += 1

        def stage_b(h):
            for c in range(NCH):
                ensure_w(h * NCH + c + 2)
                bs = wq["bs"][h * NCH + c]
                for tb in range(2):
                    pa = (cnt["pb"] % 2) * 2
                    cnt["pb"] += 1
                    for j in range(2):
                        for kc in range(8):
                            P.add("pe", lambda e, pa=pa, j=j, kc=kc, bs=bs, tb=tb, h=h: e.matmul(
                                pB[pa + j][:], lhsT=wbf[bs][:, j, kc, :], rhs=xT[h][:, kc, tb * 512:(tb + 1) * 512],
                                start=(kc == 0), stop=(kc == 7)),
                                reads=[t_wbf[bs]] + t_xT[h][tb * 4:(tb + 1) * 4], writes=[t_pB[pa + j]])
                    s = cnt["sa"] % 2
                    cnt["sa"] += 1
                    P.add("act", lambda e, s=s, pa=pa: e.activation(out=sA[s][:], in_=pB[pa][:], func=AF.Silu),
                          reads=[t_pB[pa]], writes=[t_sA[s]])
                    P.add("dve", lambda e, s=s, pa=pa, c=c, tb=tb: e.scalar_tensor_tensor(
                        out=gT[:, c, tb * 512:(tb + 1) * 512], in0=sA[s][:], scalar=0.5, in1=pB[pa + 1][:],
                        op0=ALU.mult, op1=ALU.mult),
                        reads=[t_sA[s], t_pB[pa + 1]], writes=[t_gT[c][tb]])

        def stage_c(h):
            for tl in range(8):
                t = h * 8 + tl
                s = cnt["c"] % 2
                cnt["c"] += 1
                pa = (cnt["pb"] % 2) * 2
                cnt["pb"] += 1
                sx = cnt["x"] % 2
                cnt["x"] += 1
                P.add("sync", lambda e, sx=sx, t=t: e.dma_start(out=xs[sx][:], in_=src_d[t * 128:(t + 1) * 128, :]),
                      writes=[t_xs[sx]], kind="d")
                for nh in range(2):
                    for kc in range(NCH):
                        P.add("pe", lambda e, pa=pa, nh=nh, kc=kc, tl=tl: e.matmul(
                            pB[pa + nh][:], lhsT=gT[:, kc, tl * 128:(tl + 1) * 128], rhs=wout[:, kc, nh * 512:(nh + 1) * 512],
                            start=(kc == 0), stop=(kc == NCH - 1)),
                            reads=[t_gT[kc][tl // 4], t_wout[kc]], writes=[t_pB[pa + nh]])
                    P.add("dve", lambda e, pa=pa, nh=nh, s=s, sx=sx: e.scalar_tensor_tensor(
                        out=rr[s][:, nh * 512:(nh + 1) * 512], in0=xs[sx][:, nh * 512:(nh + 1) * 512], scalar=ALPHA,
                        in1=pB[pa + nh][:], op0=ALU.mult, op1=ALU.add),
                        reads=[t_xs[sx], t_pB[pa + nh]], writes=[t_rr[s]])
                for k in range(2):
                    P.add("dve", lambda e, s=s, k=k: e.bn_stats(out=stats[s][:, k, :], in_=rr[s][:, k * 512:(k + 1) * 512]),
                          reads=[t_rr[s]], writes=[t_stats[s]])
                P.add("dve", lambda e, s=s: e.bn_aggr(out=mv[s][:], in_=stats[s][:].rearrange("p a b -> p (a b)")),
                      reads=[t_stats[s]], writes=[t_mv[s]])
                P.add("act", lambda e, s=s: e.activation(out=sd[s][:], in_=mv[s][:, 1:2], func=AF.Sqrt, bias=epst[:, 0:1], scale=1.0),
                      reads=[t_mv[s], t_eps], writes=[t_sd[s]])
                P.add("dve", lambda e, s=s: e.reciprocal(out=rstd[s][:], in_=sd[s][:]), reads=[t_sd[s]], writes=[t_rstd[s]])
                P.add("dve", lambda e, s=s: e.tensor_scalar(
                    out=rr[s][:], in0=rr[s][:], scalar1=mv[s][:, 0:1], scalar2=rstd[s][:, 0:1], op0=ALU.subtract, op1=ALU.mult),
                    reads=[t_rr[s], t_mv[s], t_rstd[s]], writes=[t_rr[s]])
                P.add("pool", lambda e, s=s: e.tensor_tensor(out=oo[s][:], in0=rr[s][:], in1=gam[:], op=ALU.mult),
                      reads=[t_rr[s], t_gam], writes=[t_oo[s]])
                P.add("pool", lambda e, s=s: e.tensor_tensor(out=oo[s][:], in0=oo[s][:], in1=bet[:], op=ALU.add),
                      reads=[t_oo[s], t_bet], writes=[t_oo[s]])
                P.add("pool", lambda e, s=s, t=t: e.dma_start(out=dst_d[t * 128:(t + 1) * 128, :], in_=oo[s][:]),
                      reads=[t_oo[s]], writes=[t_dst[t]], kind="d")

        ensure_w(1)
        stage_a(0)
        stage_b(0)
        stage_a(1)
        stage_c(0)
        stage_b(1)
        stage_c(1)
        P.wait_all("pool", t_dst)
        P.emit(name)


C_RQ, C_RK, C_RV, C_RG = 0, 256, 512, 1024
C_DQ, C_DK, C_DV, C_IQ, C_IK, C_IW, C_MQ, C_G = 1536, 2048, 2560, 3072, 3584, 3648, 3656, 4168
IW_SCALE = float(8 ** -0.5 * 64 ** -0.5)
NIT = 16
GAMMAS = [1.0 - 2.0 ** (-5.0 - h) for h in range(4)]


def toks(k):
    return [Tok() for _ in range(k)]


class Env:
    def __init__(self, nc, semstack, dmaq, name, st):
        self.nc = nc
        self.P = Prog(nc, semstack, dmaq)
        self.name = name
        self.st = st
        self.k = 0

    def sb(self, n, shape, dt):
        return self.st.enter_context(self.nc.sbuf_tensor(f"{self.name}_{n}", shape, dt))

    def ps(self, n, shape, dt):
        return self.st.enter_context(self.nc.psum_tensor(f"{self.name}_{n}", shape, dt))


class WStream:
    def __init__(self, env, nst=2, nbf=3):
        self.env = env
        self.st = [env.sb(f"wsst{i}", [128, 8, 128], F32) for i in range(nst)]
        self.bf = [env.sb(f"wsbf{i}", [128, 8, 128], BF16) for i in range(nbf)]
        self.t_st = toks(nst)
        self.t_bf = toks(nbf)
        self.n = 0

    def load(self, pieces, KC=8, dst=None, t_dst=None):
        P = self.env.P
        s = self.n % len(self.st)
        b = self.n % len(self.bf)
        self.n += 1
        stt = self.st[s]
        for (c0, src, sc) in pieces:
            w = src.shape[-1]
            P.add("sync", lambda e, stt=stt, c0=c0, src=src, w=w, KC=KC: e.dma_start(
                out=stt[:, 0:KC, c0:c0 + w], in_=src.rearrange("(kc p) n -> p kc n", p=128)),
                writes=[self.t_st[s]], kind="d")
        if dst is None:
            tot = max(c0 + src.shape[-1] for (c0, src, sc) in pieces)
            out_t = self.bf[b]
            out_fn = lambda c0, w: out_t[:, 0:KC, c0:c0 + w]
            t_out = self.t_bf[b]
        else:
            out_fn = dst
            t_out = t_dst
        if all(sc == 1.0 for (_, _, sc) in pieces):
            lo = min(c0 for (c0, _, _) in pieces)
            hi = max(c0 + src.shape[-1] for (c0, src, _) in pieces)
            P.add("pool", lambda e, lo=lo, hi=hi, stt=stt, KC=KC: e.tensor_copy(out=out_fn(lo, hi - lo), in_=stt[:, 0:KC, lo:hi]),
                  reads=[self.t_st[s]], writes=[t_out])
        else:
            for (c0, src, sc) in pieces:
                w = src.shape[-1]
                P.add("dve", lambda e, c0=c0, w=w, sc=sc, stt=stt, KC=KC: e.tensor_scalar(
                    out=out_fn(c0, w), in0=stt[:, 0:KC, c0:c0 + w], scalar1=float(sc), scalar2=None, op0=ALU.mult),
                    reads=[self.t_st[s]], writes=[t_out])
        return (self.bf[b] if dst is None else None), t_out


def load_transposed(env, src_d, ntiles, dstT, t_dstT, idb, t_idb, pT, t_pT):
    P = env.P
    xs = [env.sb(f"ltxs{i}", [128, D], F32) for i in range(2)]
    xb = [env.sb(f"ltxb{i}", [128, D], BF16) for i in range(2)]
    t_xs = toks(2); t_xb = toks(2)
    for t in range(ntiles):
        s = t % 2
        P.add("sync", lambda e, s=s, t=t: e.dma_start(out=xs[s][:], in_=src_d[t * 128:(t + 1) * 128, :]),
              writes=[t_xs[s]], kind="d")
        P.add("dve", lambda e, s=s: e.tensor_copy(out=xb[s][:], in_=xs[s][:]), reads=[t_xs[s]], writes=[t_xb[s]])
        for kc in range(8):
            P.add("pe", lambda e, s=s, kc=kc: e.transpose(out=pT[s][:, kc * 128:(kc + 1) * 128],
                                                          in_=xb[s][:, kc * 128:(kc + 1) * 128], identity=idb[:]),
                  reads=[t_xb[s], t_idb], writes=[t_pT[s]])
        P.add("act", lambda e, s=s, t=t: e.activation(
            out=dstT[:, :, t * 128:(t + 1) * 128], in_=pT[s][:].rearrange("p (a b) -> p a b", a=8), func=AF.Copy),
            reads=[t_pT[s]], writes=[t_dstT[t]])


def load_ident(env, ident_d):
    P = env.P
    idf = env.sb("idf", [128, 128], F32)
    idb = env.sb("idb", [128, 128], BF16)
    t_idf = Tok(); t_idb = Tok()
    P.add("sync", lambda e: e.dma_start(out=idf[:], in_=ident_d), writes=[t_idf], kind="d")
    P.add("dve", lambda e: e.tensor_copy(out=idb[:], in_=idf[:]), reads=[t_idf], writes=[t_idb])
    return idf, t_idf, idb, t_idb


class Banks:
    def __init__(self, env, n):
        self.b = [env.ps(f"bk{i}", [128, 512], F32) for i in range(n)]
        self.t = toks(n)
        self.i = 0
        self.n = n

    def next(self):
        i = self.i % self.n
        self.i += 1
        return self.b[i], self.t[i]


def proj_fm(env, ws, banks, pieces, ncols, rhs_fn, rhs_toks_fn, nblk, blkw, evac, KC=8):
    P = env.P
    wt, t_w = ws.load(pieces, KC=KC)
    for tb in range(nblk):
        bank, t_bank = banks.next()
        for kc in range(KC):
            P.add("pe", lambda e, bank=bank, kc=kc, tb=tb, wt=wt: e.matmul(
                bank[0:ncols, 0:blkw], lhsT=wt[:, kc, 0:ncols], rhs=rhs_fn(kc, tb), start=(kc == 0), stop=(kc == KC - 1)),
                reads=[t_w] + rhs_toks_fn(tb), writes=[t_bank])
        evac(tb, bank, t_bank)


def mix_load_stage(nc, semstack, dmaq, x1_d, ident_d, x1T):
    with contextlib.ExitStack() as st:
        env = Env(nc, semstack, dmaq, "m0", st)
        idf, t_idf, idb, t_idb = load_ident(env, ident_d)
        pT = [env.ps(f"pT{i}", [128, 1024], BF16) for i in range(2)]
        t_pT = toks(2)
        t_x1T = toks(16)
        load_transposed(env, x1_d, 16, x1T, t_x1T, idb, t_idb, pT, t_pT)
        env.P.emit("m0")


def mem_stage(nc, semstack, dmaq, x1T, oT, mem_d, w_in, wmkv, ident_d):
    with contextlib.ExitStack() as st:
        env = Env(nc, semstack, dmaq, "m3", st)
        P = env.P
        idf, t_idf, idb, t_idb = load_ident(env, ident_d)
        pT = [env.ps(f"pT{i}", [128, 1024], BF16) for i in range(2)]
        t_pT = toks(2)
        banks = Banks(env, 2)
        pN2 = [env.ps(f"pN{i}", [128, 512], F32) for i in range(2)]; t_pN2 = toks(2)
        pD2 = [env.ps(f"pD{i}", [128, 512], F32) for i in range(2)]; t_pD2 = toks(2)
        memT = env.sb("memT", [128, 8, 256], BF16); t_memT = toks(2)
        mkT = env.sb("mkT", [128, 4, 256], BF16); t_mkT = toks(4)
        mvv = env.sb("mv", [128, 2, 512], BF16); t_mv = toks(2)
        wmv = env.sb("wmv", [128, 8, 512], BF16); t_wmv = toks(4)
        mqT = env.sb("mqT", [128, 4, L], BF16); t_mqT = [toks(4) for _ in range(4)]
        ones = env.sb("ones", [128, 128], BF16); t_ones = Tok()
        E = [env.sb(f"E{i}", [128, 512], BF16) for i in range(4)]; t_E = toks(4)
        rden = [env.sb(f"rden{i}", [128, 512], F32) for i in range(2)]; t_rden = toks(2)
        ws = WStream(env)
        P.add("pool", lambda e: e.memset(ones[:], 1.0), writes=[t_ones])
        load_transposed(env, mem_d, 2, memT, t_memT, idb, t_idb, pT, t_pT)
        t_x1T = []
        for h in range(4):
            def evac(tb, bank, t_bank, h=h):
                P.add("act", lambda e: e.activation(out=mkT[:, h, :], in_=bank[:, 0:256], func=AF.Copy),
                      reads=[t_bank], writes=[t_mkT[h]])
            proj_fm(env, ws, banks, [(0, wmkv[:, h * 128:(h + 1) * 128], 1.0)], 128,
                    lambda kc, tb: memT[:, kc, :], lambda tb: t_memT, 1, 256, evac)
        for c in range(4):
            ws.load([(0, wmkv[:, 512 + c * 128:512 + (c + 1) * 128], 1.0)],
                    dst=lambda c0, w, c=c: wmv[:, :, c * 128 + c0:c * 128 + c0 + w], t_dst=t_wmv[c])
        for mt in range(2):
            bank, t_bank = banks.next()
            for kc in range(8):
                P.add("pe", lambda e, bank=bank, kc=kc, mt=mt: e.matmul(
                    bank[:], lhsT=memT[:, kc, mt * 128:(mt + 1) * 128], rhs=wmv[:, kc, :], start=(kc == 0), stop=(kc == 7)),
                    reads=[t_memT[mt]] + t_wmv, writes=[t_bank])
            P.add("act", lambda e, bank=bank, mt=mt: e.activation(out=mvv[:, mt, :], in_=bank[:], func=AF.Copy),
                  reads=[t_bank], writes=[t_mv[mt]])
        for h in range(4):
            def evac(tb, bank, t_bank, h=h):
                P.add("act", lambda e: e.activation(out=mqT[:, h, tb * 512:(tb + 1) * 512], in_=bank[:], func=AF.Copy,
                                                    scale=float(128 ** -0.5)),
                      reads=[t_bank], writes=[t_mqT[h][tb]])
            proj_fm(env, ws, banks, [(0, w_in[:, C_MQ + h * 128:C_MQ + (h + 1) * 128], 1.0)], 128,
                    lambda kc, tb: x1T[:, kc, tb * 512:(tb + 1) * 512], lambda tb: [], 4, 512, evac)
        it = 0
        for h in range(4):
            for qb in range(4):
                for mt in range(2):
                    bank, t_bank = banks.next()
                    ei = (it * 2 + mt) % 4
                    P.add("pe", lambda e, bank=bank, h=h, qb=qb, mt=mt: e.matmul(
                        bank[:], lhsT=mkT[:, h, mt * 128:(mt + 1) * 128], rhs=mqT[:, h, qb * 512:(qb + 1) * 512],
                        start=True, stop=True), reads=[t_mkT[h], t_mqT[h][qb]], writes=[t_bank])
                    P.add("act", lambda e, bank=bank, ei=ei: e.activation(out=E[ei][:], in_=bank[:], func=AF.Exp),
                          reads=[t_bank], writes=[t_E[ei]])
                pN = pN2[it % 2]; t_pN = t_pN2[it % 2]; pD = pD2[it % 2]; t_pD = t_pD2[it % 2]
                for mt in range(2):
                    ei = (it * 2 + mt) % 4
                    P.add("pe", lambda e, ei=ei, h=h, mt=mt, pN=pN: e.matmul(
                        pN[:], lhsT=mvv[:, mt, h * 128:(h + 1) * 128], rhs=E[ei][:], start=(mt == 0), stop=(mt == 1)),
                        reads=[t_mv[mt], t_E[ei]], writes=[t_pN])
                    P.add("pe", lambda e, ei=ei, mt=mt, pD=pD: e.matmul(
                        pD[:], lhsT=ones[:], rhs=E[ei][:], start=(mt == 0), stop=(mt == 1)),
                        reads=[t_ones, t_E[ei]], writes=[t_pD])
                r = it % 2
                P.add("dve", lambda e, r=r, pD=pD: e.reciprocal(out=rden[r][:], in_=pD[:]), reads=[t_pD], writes=[t_rden[r]])
                P.add("dve", lambda e, r=r, h=h, qb=qb, pN=pN: e.tensor_tensor(
                    out=oT[:, h, qb * 512:(qb + 1) * 512], in0=pN[:], in1=rden[r][:], op=ALU.mult),
                    reads=[t_pN, t_rden[r]], writes=[Tok()])
                it += 1
        env.P.emit("m3")


def dsa_stage(nc, semstack, dmaq, x1T, oT, w_in, t5_d, ident_d, J_d, oh_d, causal_d, pow2_d, gscr_d):
    with contextlib.ExitStack() as st:
        env = Env(nc, semstack, dmaq, "m1", st)
        P = env.P
        idf, t_idf, idb, t_idb = load_ident(env, ident_d)
        banks = Banks(env, 3)
        pO = [[env.ps(f"pO{i}{j}", [128, 512], F32) for j in range(2)] for i in range(2)]
        t_pO = [toks(2) for _ in range(2)]
        pM = env.ps("pM", [128, 1024], BF16); t_pM = Tok()
        qT = env.sb("qT", [128, 4, L], BF16); t_qT = [toks(4) for _ in range(4)]
        kT = env.sb("kT", [128, 4, L], BF16); t_kT = [toks(4) for _ in range(4)]
        qiT = env.sb("qiT", [128, 4, L], BF16); t_qiT = [toks(4) for _ in range(4)]
        kiT = env.sb("kiT", [128, L], BF16); t_kiT = toks(4)
        vaug = env.sb("vaug", [128, 16, 8, 65], BF16); t_v = toks(16); t_vones = Tok()
        iw = env.sb("iw", [128, 16, 8], F32); t_iw = toks(16)
        wv = env.sb("wv", [128, 8, 512], BF16); t_wv = toks(4)
        wiw = env.sb("wiw", [128, 8, 8], BF16); t_wiw = Tok()
        Sc2 = [env.sb(f"Sc{i}", [128, L], F32) for i in range(2)]; t_Sc2 = toks(2)
        junk2 = [env.sb(f"junk{i}", [128, L], BF16) for i in range(2)]; t_junk2 = toks(2)
        mask = [env.sb(f"mask{i}", [128, L], BF16) for i in range(2)]; t_mask = toks(2)
        maskTp = [env.sb(f"maskTp{i}", [128, 16, 2, 128], BF16) for i in range(2)]
        t_maskTp = [toks(2) for _ in range(2)]
        tI = [env.sb(f"tI{i}", [128, 512], F32) for i in range(2)]; t_tI = toks(2)
        E = [env.sb(f"E{i}", [128, 512], BF16) for i in range(2)]; t_E = toks(2)
        PT = [env.sb(f"PT{i}", [128, 512], BF16) for i in range(3)]; t_PT = toks(3)
        BT = env.sb("BT", [128, 8, 256], BF16); t_BT = toks(8)
        H = Sc2[1][:].rearrange("p (h n) -> p h n", h=8); t_H = t_Sc2[1]
        rden8 = [env.sb(f"rden{i}", [128, 2, 4], F32) for i in range(2)]; t_rden8 = toks(2)
        o_tm = [env.sb(f"otm{i}", [128, 512], BF16) for i in range(2)]; t_otm = toks(2)
        Jf = env.sb("Jf", [128, 128], F32); t_J = Tok()
        caus = env.sb("caus", [128, 128], F32); t_caus = Tok()
        pow2 = env.sb("pow2", [128, NIT], F32); t_pow2 = Tok()
        tabS = env.sb("tabS", [32, 8], F32); t_tab = Tok()
        ohS = env.sb("ohS", [32, 384], F32); t_oh = Tok()
        Gs = env.sb("Gs", [8, 384], F32); t_Gs = Tok()
        thrneg = env.sb("thrneg", [128, 1], F32); t_thrneg = Tok()
        mx8_2 = [env.sb(f"mx8{i}", [128, 8], F32) for i in range(2)]; t_mx8_2 = toks(2)
        mn_2 = [env.sb(f"mn{i}", [128, 1], F32) for i in range(2)]; t_mn_2 = toks(2)
        rng_2 = [env.sb(f"rng{i}", [128, 1], F32) for i in range(2)]; t_rng_2 = toks(2)
        thr_2 = [env.sb(f"thr{i}", [128, 1], F32) for i in range(2)]; t_thr_2 = toks(2)
        S2_2 = [env.sb(f"S2{i}", [128, NIT], F32) for i in range(2)]; t_S2_2 = toks(2)
        cnt_2 = [env.sb(f"cnt{i}", [128, 1], F32) for i in range(2)]; t_cnt_2 = toks(2)
        ee_2 = [env.sb(f"ee{i}", [128, 1], F32) for i in range(2)]; t_ee_2 = toks(2)
        ws = WStream(env)
        t_gscr = Tok()
        print("[dsa] sbuf bytes remaining/partition:", nc.sbuf_bytes_remaining() if callable(nc.sbuf_bytes_remaining) else nc.sbuf_bytes_remaining)

        P.add("sync", lambda e: e.dma_start(out=Jf[:], in_=J_d), writes=[t_J], kind="d")
        P.add("sync", lambda e: e.dma_start(out=caus[:], in_=causal_d), writes=[t_caus], kind="d")
        P.add("sync", lambda e: e.dma_start(out=pow2[:], in_=pow2_d), writes=[t_pow2], kind="d")
        P.add("sync", lambda e: e.dma_start(out=tabS[:], in_=t5_d), writes=[t_tab], kind="d")
        P.add("sync", lambda e: e.dma_start(out=ohS[:], in_=oh_d), writes=[t_oh], kind="d")
        P.add("pool", lambda e: e.memset(vaug[:, :, :, 64:65], 1.0), writes=[t_vones])
        for i in range(2):
            P.add("pool", lambda e, i=i: e.memset(maskTp[i][:], 0.0), writes=t_maskTp[i])
        P.add("pool", lambda e: e.memset(thrneg[:], -1.0e29), writes=[t_thrneg])
        bank, t_bank = banks.next()
        P.add("pe", lambda e, bank=bank: e.matmul(bank[0:8, 0:384], lhsT=tabS[:], rhs=ohS[:], start=True, stop=True),
              reads=[t_tab, t_oh], writes=[t_bank])
        P.add("act", lambda e, bank=bank: e.activation(out=Gs[:], in_=bank[0:8, 0:384], func=AF.Copy), reads=[t_bank], writes=[t_Gs])
        P.add("sync", lambda e: e.dma_start(out=gscr_d, in_=Gs[:]), reads=[t_Gs], writes=[t_gscr], kind="d")
        hank = bass.AP(gscr_d.tensor, gscr_d.offset, [[1, 128], [384, 8], [1, 256]])
        P.add("sync", lambda e: e.dma_start(out=H, in_=hank), reads=[t_gscr], writes=[t_H], kind="d")
        for h in range(8):
            bank, t_bank = banks.next()
            P.add("pe", lambda e, bank=bank, h=h: e.matmul(bank[:, 0:256], lhsT=Jf[:], rhs=H[:, h, :], start=True, stop=True),
                  reads=[t_J, t_H], writes=[t_bank])
            P.add("act", lambda e, bank=bank, h=h: e.activation(out=BT[:, h, :], in_=bank[:, 0:256], func=AF.Copy),
                  reads=[t_bank], writes=[t_BT[h]])

        ev = {"i": 0}

        def evac_to(dstfn, t_dstfn, scale, only_act=False):
            def evac(tb, bank, t_bank):
                eng = "act" if (only_act or ev["i"] % 2 == 0) else "dve"
                ev["i"] += 1
                if eng == "act":
                    P.add("act", lambda e: e.activation(out=dstfn(tb), in_=bank[:], func=AF.Copy, scale=float(scale)),
                          reads=[t_bank], writes=[t_dstfn(tb)])
                else:
                    P.add("dve", lambda e: e.tensor_scalar(out=dstfn(tb), in0=bank[:], scalar1=float(scale), scalar2=None, op0=ALU.mult),
                          reads=[t_bank], writes=[t_dstfn(tb)])
            return evac

        xrhs = lambda kc, tb: x1T[:, kc, tb * 512:(tb + 1) * 512]

        def proj_indexer():
            for c in range(4):
                proj_fm(env, ws, banks, [(0, w_in[:, C_IQ + c * 128:C_IQ + (c + 1) * 128], 1.0)], 128, xrhs, lambda tb: [], 4, 512,
                        evac_to(lambda tb, c=c: qiT[:, c, tb * 512:(tb + 1) * 512], lambda tb, c=c: t_qiT[c][tb], 1.0))
            proj_fm(env, ws, banks, [(0, w_in[:, C_IK:C_IK + 64], 1.0), (64, w_in[:, C_IK:C_IK + 64], 1.0)], 128, xrhs, lambda tb: [], 4, 512,
                    evac_to(lambda tb: kiT[:, tb * 512:(tb + 1) * 512], lambda tb: t_kiT[tb], 1.0))
            ws.load([(0, w_in[:, C_IW:C_IW + 8], 1.0)], dst=lambda c0, w: wiw[:, :, c0:c0 + w], t_dst=t_wiw)
            for t in range(16):
                bank, t_bank = banks.next()
                for kc in range(8):
                    P.add("pe", lambda e, bank=bank, kc=kc, t=t: e.matmul(
                        bank[:, 0:8], lhsT=x1T[:, kc, t * 128:(t + 1) * 128], rhs=wiw[:, kc, :], start=(kc == 0), stop=(kc == 7)),
                        reads=[t_wiw], writes=[t_bank])
                P.add("dve", lambda e, bank=bank, t=t: e.tensor_scalar(out=iw[:, t, :], in0=bank[:, 0:8], scalar1=IW_SCALE, scalar2=None, op0=ALU.mult),
                      reads=[t_bank], writes=[t_iw[t]])

        def proj_qkv():
            for c in range(4):
                proj_fm(env, ws, banks, [(0, w_in[:, C_DQ + c * 128:C_DQ + (c + 1) * 128], 1.0)], 128, xrhs, lambda tb: [], 4, 512,
                        evac_to(lambda tb, c=c: qT[:, c, tb * 512:(tb + 1) * 512], lambda tb, c=c: t_qT[c][tb], 0.125, only_act=True))
                proj_fm(env, ws, banks, [(0, w_in[:, C_DK + c * 128:C_DK + (c + 1) * 128], 1.0)], 128, xrhs, lambda tb: [], 4, 512,
                        evac_to(lambda tb, c=c: kT[:, c, tb * 512:(tb + 1) * 512], lambda tb, c=c: t_kT[c][tb], 1.0, only_act=True))
            for c in range(4):
                ws.load([(0, w_in[:, C_DV + c * 128:C_DV + (c + 1) * 128], 1.0)],
                        dst=lambda c0, w, c=c: wv[:, :, c * 128 + c0:c * 128 + c0 + w], t_dst=t_wv[c])
            for t in range(16):
                bank, t_bank = banks.next()
                for kc in range(8):
                    P.add("pe", lambda e, bank=bank, kc=kc, t=t: e.matmul(
                        bank[:], lhsT=x1T[:, kc, t * 128:(t + 1) * 128], rhs=wv[:, kc, :], start=(kc == 0), stop=(kc == 7)),
                        reads=t_wv, writes=[t_bank])
                P.add("act", lambda e, bank=bank, t=t: e.activation(out=vaug[:, t, :, 0:64], in_=bank[:].rearrange("p (h d) -> p h d", h=8),
                                                                    func=AF.Copy), reads=[t_bank], writes=[t_v[t]])

        cI = {"i": 0}

        def idx_phase(n):
            Sc = Sc2[n % 2]; t_Sc = t_Sc2[n % 2]
            W = 128 * (n + 1)
            nb = (W + 511) // 512
            for h in range(8):
                hp = (h % 2) * 64
                for kb in range(nb):
                    w = min(512, W - kb * 512)
                    bank, t_bank = banks.next()
                    P.add("pe", lambda e, bank=bank, h=h, hp=hp, kb=kb, w=w, n=n: e.matmul(
                        bank[:, 0:w], lhsT=qiT[hp:hp + 64, h // 2, n * 128:(n + 1) * 128], rhs=kiT[hp:hp + 64, kb * 512:kb * 512 + w],
                        start=True, stop=True),
                        reads=[t_qiT[h // 2][n // 4]] + t_kiT[0:nb], writes=[t_bank])
                    s = cI["i"] % 2
                    cI["i"] += 1
                    P.add("act", lambda e, bank=bank, s=s, w=w: e.activation(out=tI[s][:, 0:w], in_=bank[:, 0:w], func=AF.Relu),
                          reads=[t_bank], writes=[t_tI[s]])
                    if h == 0:
                        P.add("dve", lambda e, s=s, kb=kb, w=w, n=n: e.tensor_scalar(
                            out=Sc[:, kb * 512:kb * 512 + w], in0=tI[s][:, 0:w], scalar1=iw[:, n, 0:1], scalar2=None, op0=ALU.mult),
                            reads=[t_tI[s], t_iw[n]], writes=[t_Sc])
                    else:
                        P.add("dve", lambda e, s=s, kb=kb, w=w, n=n, h=h: e.scalar_tensor_tensor(
                            out=Sc[:, kb * 512:kb * 512 + w], in0=tI[s][:, 0:w], scalar=iw[:, n, h:h + 1],
                            in1=Sc[:, kb * 512:kb * 512 + w], op0=ALU.mult, op1=ALU.add),
                            reads=[t_tI[s], t_iw[n], t_Sc], writes=[t_Sc])
            P.add("dve", lambda e, n=n: e.tensor_tensor(out=Sc[:, n * 128:(n + 1) * 128], in0=Sc[:, n * 128:(n + 1) * 128],
                                                        in1=caus[:], op=ALU.add),
                  reads=[t_Sc, t_caus], writes=[t_Sc])

        def bis_ops(n):
            ch = n % 2
            Sc = Sc2[ch]; t_Sc = t_Sc2[ch]; junk = junk2[ch]; t_junk = t_junk2[ch]
            mx8 = mx8_2[ch]; t_mx8 = t_mx8_2[ch]; mn = mn_2[ch]; t_mn = t_mn_2[ch]; rng = rng_2[ch]; t_rng = t_rng_2[ch]
            thr = thr_2[ch]; t_thr = t_thr_2[ch]; S2 = S2_2[ch]; t_S2 = t_S2_2[ch]; cnt = cnt_2[ch]; t_cnt = t_cnt_2[ch]
            ee = ee_2[ch]; t_ee = t_ee_2[ch]
            W = 128 * (n + 1)
            mk = mask[ch]
            t_mk = t_mask[ch]
            if n < 2:
                yield lambda: P.add("dve", lambda e: e.tensor_scalar(out=mk[:, 0:W], in0=Sc[:, 0:W], scalar1=thrneg[:, 0:1], scalar2=None, op0=ALU.is_ge),
                                    reads=[t_Sc, t_thrneg], writes=[t_mk])
                return
            yield lambda: P.add("dve", lambda e: e.max(out=mx8[:], in_=Sc[:, 0:W]), reads=[t_Sc], writes=[t_mx8])
            yield lambda: P.add("dve", lambda e: e.tensor_reduce(out=mn[:], in_=Sc[:, 0:n * 128], axis=mybir.AxisListType.X, op=ALU.min),
                                reads=[t_Sc], writes=[t_mn])
            yield lambda: P.add("dve", lambda e: e.tensor_tensor(out=rng[:], in0=mx8[:, 0:1], in1=mn[:], op=ALU.subtract),
                                reads=[t_mx8, t_mn], writes=[t_rng])
            yield lambda: P.add("dve", lambda e: e.scalar_tensor_tensor(out=thr[:], in0=rng[:], scalar=0.5, in1=mn[:], op0=ALU.mult, op1=ALU.add),
                                reads=[t_rng, t_mn], writes=[t_thr])
            yield lambda: P.add("dve", lambda e: e.tensor_scalar(out=S2[:], in0=pow2[:], scalar1=rng[:, 0:1], scalar2=None, op0=ALU.mult),
                                reads=[t_pow2, t_rng], writes=[t_S2])
            for k in range(NIT):
                yield lambda: P.add("dve", lambda e: e.tensor_scalar(out=junk[:, 0:W], in0=Sc[:, 0:W], scalar1=thr[:, 0:1], scalar2=None,
                                                                     op0=ALU.is_ge, op1=ALU.add, accum_out=cnt[:, 0:1]),
                                    reads=[t_Sc, t_thr], writes=[t_junk, t_cnt])
                yield lambda: P.add("dve", lambda e: e.tensor_scalar(out=ee[:], in0=cnt[:], scalar1=255.5, scalar2=0.5, op0=ALU.is_ge, op1=ALU.subtract),
                                    reads=[t_cnt], writes=[t_ee])
                yield lambda k=k: P.add("dve", lambda e: e.scalar_tensor_tensor(out=thr[:], in0=ee[:], scalar=S2[:, k:k + 1], in1=thr[:],
                                                                                op0=ALU.mult, op1=ALU.add),
                                        reads=[t_ee, t_S2, t_thr], writes=[t_thr])
            yield lambda: P.add("dve", lambda e: e.tensor_scalar(out=mk[:, 0:W], in0=Sc[:, 0:W], scalar1=thr[:, 0:1], scalar2=None, op0=ALU.is_ge),
                                reads=[t_Sc, t_thr], writes=[t_mk])

        def bis_pair(a, b):
            ga, gb = bis_ops(a), bis_ops(b)
            done_a = done_b = False
            while not (done_a and done_b):
                if not done_a:
                    f = next(ga, None)
                    if f is None:
                        done_a = True
                    else:
                        f()
                if not done_b:
                    f = next(gb, None)
                    if f is None:
                        done_b = True
                    else:
                        f()

        def maskT_phase(n):
            mk = mask[n % 2]; t_mk = t_mask[n % 2]
            mp = maskTp[(n // 2) % 2]; t_mp = t_maskTp[(n // 2) % 2][n % 2]
            for m0 in range(0, n + 1, 8):
                k = min(8, n + 1 - m0)
                for i in range(k):
                    m = m0 + i
                    P.add("pe", lambda e, i=i, m=m: e.transpose(out=pM[:, i * 128:(i + 1) * 128], in_=mk[:, m * 128:(m + 1) * 128], identity=idb[:]),
                          reads=[t_mk, t_idb], writes=[t_pM])
                P.add("act", lambda e, m0=m0, k=k, n=n: e.activation(
                    out=mp[:, m0:m0 + k, n % 2, :], in_=pM[:, 0:k * 128].rearrange("p (a b) -> p a b", a=k), func=AF.Copy),
                    reads=[t_pM], writes=[t_mp])

        cA = {"e": 0, "p": 0}

        def att_pair(kp):
            a, b = 2 * kp, 2 * kp + 1
            mp = maskTp[kp % 2]; t_mp = t_maskTp[kp % 2]
            steps = [(c, m0, hh) for c in range(4) for m0 in range(0, b + 1, 2) for hh in range(2)]

            def emit_scores(c, m0, hh):
                h = 2 * c + hh; hp = hh * 64
                bank, t_bank = banks.next()
                for i in range(2):
                    m = m0 + i
                    biases = []
                    if m == a - 1:
                        biases.append((0, 128))
                    if m == a:
                        biases.append((0, 0)); biases.append((128, 128))
                    if m == b:
                        biases.append((128, 0))
                    P.add("pe", lambda e, bank=bank, i=i, m=m, hp=hp, c=c, nb=len(biases): e.matmul(
                        bank[:, i * 256:(i + 1) * 256], lhsT=kT[hp:hp + 64, c, m * 128:(m + 1) * 128],
                        rhs=qT[hp:hp + 64, c, a * 128:(a + 2) * 128], start=True, stop=(nb == 0)),
                        reads=[t_kT[c][m // 4], t_qT[c][a // 4]], writes=[t_bank])
                    for bi, (qo, off) in enumerate(biases):
                        P.add("pe", lambda e, bank=bank, i=i, h=h, qo=qo, off=off, last=(bi == len(biases) - 1): e.matmul(
                            bank[:, i * 256 + qo:i * 256 + qo + 128], lhsT=idb[:], rhs=BT[:, h, off:off + 128], start=False, stop=last),
                            reads=[t_idb, t_BT[h]], writes=[t_bank])
                se = cA["e"] % 2
                cA["e"] += 1
                P.add("act", lambda e, bank=bank, se=se: e.activation(out=E[se][:], in_=bank[:], func=AF.Exp),
                      reads=[t_bank], writes=[t_E[se]])
                sp = cA["p"] % 3
                cA["p"] += 1
                P.add("pool", lambda e, se=se, sp=sp, m0=m0: e.tensor_tensor(
                    out=PT[sp][:], in0=E[se][:], in1=mp[:, m0:m0 + 2, :, :].rearrange("p a b q -> p (a b q)"), op=ALU.mult),
                    reads=[t_E[se]] + t_mp, writes=[t_PT[sp]])
                return sp

            def emit_pv(c, m0, hh, sp):
                h = 2 * c + hh
                for i in range(2):
                    m = m0 + i
                    for t in (a, b):
                        if m > t:
                            continue
                        P.add("pe", lambda e, sp=sp, i=i, m=m, h=h, hh=hh, c=c, t=t, a=a: e.matmul(
                            pO[t % 2][hh][:, c * 65:(c + 1) * 65], lhsT=PT[sp][:, i * 256 + (t - a) * 128:i * 256 + (t - a) * 128 + 128],
                            rhs=vaug[:, m, h, :], start=(m == 0), stop=(m == t)),
                            reads=[t_v[m], t_vones, t_PT[sp]], writes=[t_pO[t % 2][hh]])

            prev = None
            for st_ in steps:
                sp = emit_scores(*st_)
                if prev is not None:
                    emit_pv(*prev)
                prev = st_ + (sp,)
            emit_pv(*prev)
            for t in (a, b):
                r = t % 2
                for hh in range(2):
                    P.add("dve", lambda e, r=r, hh=hh: e.reciprocal(
                        out=rden8[r][:, hh, :], in_=pO[r][hh][:, 0:260].rearrange("p (c d) -> p c d", c=4)[:, :, 64]),
                        reads=[t_pO[r][hh]], writes=[t_rden8[r]])
                for h in range(8):
                    P.add("act", lambda e, r=r, h=h: e.activation(
                        out=o_tm[r][:, h * 64:(h + 1) * 64], in_=pO[r][h % 2][:, (h // 2) * 65:(h // 2) * 65 + 64], func=AF.Copy,
                        scale=rden8[r][:, h % 2, h // 2:h // 2 + 1]),
                        reads=[t_pO[r][h % 2], t_rden8[r]], writes=[t_otm[r]])
                for c4 in range(4):
                    P.add("pe", lambda e, r=r, c4=c4: e.transpose(out=pM[:, c4 * 128:(c4 + 1) * 128], in_=o_tm[r][:, c4 * 128:(c4 + 1) * 128], identity=idb[:]),
                          reads=[t_otm[r], t_idb], writes=[t_pM])
                P.add("dve", lambda e, t=t: e.tensor_copy(out=oT[:, :, t * 128:(t + 1) * 128], in_=pM[:, 0:512].rearrange("p (a b) -> p a b", a=4)),
                      reads=[t_pM], writes=[Tok()])

        proj_indexer()
        idx_phase(14); idx_phase(15); bis_pair(14, 15)
        proj_qkv()
        maskT_phase(14); maskT_phase(15)
        for k in range(7, -1, -1):
            a, b = 2 * k, 2 * k + 1
            if k > 0:
                idx_phase(a - 2); idx_phase(b - 2)
                bis_pair(a - 2, b - 2)
            att_pair(k)
            if k > 0:
                maskT_phase(a - 2); maskT_phase(b - 2)
        env.P.emit("m1")


def ret_stage(nc, semstack, dmaq, x1T, oT, w_in, gng_d, gnb_d, ident_d, cos_d, sin_d, decay_d, kdec_d, qdec_d):
    with contextlib.ExitStack() as st:
        env = Env(nc, semstack, dmaq, "m2", st)
        P = env.P
        idf, t_idf, idb, t_idb = load_ident(env, ident_d)
        banks = Banks(env, 7)
        pM = env.ps("pM", [128, 1024], BF16); t_pM = Tok()
        rqT = env.sb("rqT", [128, 2, L], BF16); t_rqT = [toks(4) for _ in range(2)]
        rkT = env.sb("rkT", [128, 2, L], BF16); t_rkT = [toks(4) for _ in range(2)]
        qdT = env.sb("qdT", [128, 2, L], BF16); t_qdT = toks(16)
        rv = env.sb("rv", [128, 16, 512], BF16); t_rv = toks(16)
        kd = env.sb("kd", [128, 16, 256], BF16); t_kd = toks(16)
        wrv = env.sb("wrv", [128, 8, 512], BF16); t_wrv = toks(4)
        wrg = env.sb("wrg", [128, 8, 512], BF16); t_wrg = toks(4)
        cosT = env.sb("cosT", [128, L], F32); t_cos = Tok()
        sinT = env.sb("sinT", [128, L], F32); t_sin = Tok()
        decT = env.sb("decT", [128, 512], F32); t_dec = Tok()
        kdec = env.sb("kdec", [128, 256], F32); t_kdec = Tok()
        qdec = env.sb("qdec", [128, 2, 128], F32); t_qdec = Tok()
        gng = env.sb("gng", [128, 512], F32); t_gng = Tok()
        gnb = env.sb("gnb", [128, 512], F32); t_gnb = Tok()
        tmp = [env.sb(f"tmp{i}", [128, 512], F32) for i in range(2)]; t_tmp = toks(2)
        tmp2 = [env.sb(f"tmpb{i}", [128, 512], F32) for i in range(2)]; t_tmp2 = toks(2)
        PdT = [env.sb(f"PdT{i}", [128, 512], BF16) for i in range(2)]; t_PdT = toks(2)
        state = env.sb("state", [128, 2, 128], F32); t_state = Tok()
        state_bf = env.sb("state_bf", [128, 2, 128], BF16); t_state_bf = Tok()
        on = [env.sb(f"on{i}", [128, 512], F32) for i in range(2)]; t_on = toks(2)
        sg = [env.sb(f"sg{i}", [128, 512], F32) for i in range(2)]; t_sg = toks(2)
        orb = [env.sb(f"orb{i}", [128, 512], BF16) for i in range(2)]; t_orb = toks(2)
        stats = [env.sb(f"stats{i}", [128, 4, 6], F32) for i in range(2)]; t_stats = toks(2)
        mvv = [env.sb(f"mvv{i}", [128, 4, 2], F32) for i in range(2)]; t_mvv = toks(2)
        sd = [env.sb(f"sd{i}", [128, 4], F32) for i in range(2)]; t_sd = toks(2)
        rstd = [env.sb(f"rstd{i}", [128, 4], F32) for i in range(2)]; t_rstd = toks(2)
        epst = env.sb("epst", [128, 1], F32); t_eps = Tok()
        ws = WStream(env)
        for (dst, src, tk) in ((cosT, cos_d, t_cos), (sinT, sin_d, t_sin), (decT, decay_d, t_dec), (kdec, kdec_d, t_kdec)):
            P.add("sync", lambda e, dst=dst, src=src: e.dma_start(out=dst[:], in_=src), writes=[tk], kind="d")
        P.add("sync", lambda e: e.dma_start(out=qdec[:].rearrange("p a b -> p (a b)"), in_=qdec_d), writes=[t_qdec], kind="d")
        P.add("sync", lambda e: e.dma_start(out=gng[:], in_=bcast_rows(gng_d, 128)), writes=[t_gng], kind="d")
        P.add("sync", lambda e: e.dma_start(out=gnb[:], in_=bcast_rows(gnb_d, 128)), writes=[t_gnb], kind="d")
        P.add("dve", lambda e: e.memset(epst[:], LN_EPS), writes=[t_eps])
        neghalf = env.sb("neghalf", [128, 4], F32); t_neghalf = Tok()
        P.add("pool", lambda e: e.memset(neghalf[:], -0.5), writes=[t_neghalf])

        cR = {"i": 0}
        for (c0, dstT, t_dstT, sc) in ((C_RQ, rqT, t_rqT, 1.0), (C_RK, rkT, t_rkT, 0.125)):
            for c in range(2):
                base = c0 + c * 128
                wn, t_wn = ws.load([(0, w_in[:, base:base + 128], 1.0)])
                pieces = []
                for hh in range(2):
                    pieces.append((hh * 64, w_in[:, base + hh * 64 + 32:base + hh * 64 + 64], -1.0))
                    pieces.append((hh * 64 + 32, w_in[:, base + hh * 64:base + hh * 64 + 32], 1.0))
                wr, t_wr = ws.load(pieces)
                for tb in range(4):
                    bq, t_bq = banks.next()
                    br, t_br = banks.next()
                    for (bank, t_bank, wt, t_w) in ((bq, t_bq, wn, t_wn), (br, t_br, wr, t_wr)):
                        for kc in range(8):
                            P.add("pe", lambda e, bank=bank, wt=wt, kc=kc, tb=tb: e.matmul(
                                bank[:], lhsT=wt[:, kc, :], rhs=x1T[:, kc, tb * 512:(tb + 1) * 512], start=(kc == 0), stop=(kc == 7)),
                                reads=[t_w], writes=[t_bank])
                    s = cR["i"] % 2
                    cR["i"] += 1
                    tsl = slice(tb * 512, (tb + 1) * 512)
                    P.add("dve", lambda e, s=s, bq=bq, tsl=tsl: e.tensor_tensor(out=tmp[s][:], in0=bq[:], in1=cosT[:, tsl], op=ALU.mult),
                          reads=[t_bq, t_cos], writes=[t_tmp[s]])
                    P.add("dve", lambda e, s=s, br=br, tsl=tsl, sc=sc: e.scalar_tensor_tensor(
                        out=tmp2[s][:], in0=br[:], scalar=float(sc), in1=sinT[:, tsl], op0=ALU.mult, op1=ALU.mult),
                        reads=[t_br, t_sin], writes=[t_tmp2[s]])
                    P.add("dve", lambda e, s=s, tsl=tsl, sc=sc, dstT=dstT, c=c: e.scalar_tensor_tensor(
                        out=dstT[:, c, tsl], in0=tmp[s][:], scalar=float(sc), in1=tmp2[s][:], op0=ALU.mult, op1=ALU.add),
                        reads=[t_tmp[s], t_tmp2[s]], writes=[t_dstT[c][tb]])
        for c in range(4):
            ws.load([(0, w_in[:, C_RV + c * 128:C_RV + (c + 1) * 128], 1.0)],
                    dst=lambda c0, w, c=c: wrv[:, :, c * 128 + c0:c * 128 + c0 + w], t_dst=t_wrv[c])
        for c in range(4):
            ws.load([(0, w_in[:, C_RG + c * 128:C_RG + (c + 1) * 128], 1.0)],
                    dst=lambda c0, w, c=c: wrg[:, :, c * 128 + c0:c * 128 + c0 + w], t_dst=t_wrg[c])
        for t in range(16):
            tsl = slice(t * 128, (t + 1) * 128)
            bank, t_bank = banks.next()
            for kc in range(8):
                P.add("pe", lambda e, bank=bank, kc=kc, tsl=tsl: e.matmul(
                    bank[:], lhsT=x1T[:, kc, tsl], rhs=wrv[:, kc, :], start=(kc == 0), stop=(kc == 7)),
                    reads=t_wrv, writes=[t_bank])
            P.add("act", lambda e, bank=bank, t=t: e.activation(out=rv[:, t, :], in_=bank[:], func=AF.Copy),
                  reads=[t_bank], writes=[t_rv[t]])
            for c in range(2):
                P.add("pe", lambda e, c=c, tsl=tsl: e.transpose(out=pM[:, c * 128:(c + 1) * 128], in_=rkT[:, c, tsl], identity=idb[:]),
                      reads=[t_rkT[c][t // 4], t_idb], writes=[t_pM])
            P.add("dve", lambda e, t=t: e.tensor_tensor(out=kd[:, t, :], in0=pM[:, 0:256], in1=kdec[:], op=ALU.mult),
                  reads=[t_pM, t_kdec], writes=[t_kd[t]])
            for c in range(2):
                P.add("dve", lambda e, c=c, tsl=tsl: e.tensor_tensor(out=qdT[:, c, tsl], in0=rqT[:, c, tsl], in1=qdec[:, c, :], op=ALU.mult),
                      reads=[t_rqT[c][t // 4], t_qdec], writes=[t_qdT[t]])
        eo_banks = [[(banks.b[2 * i + j], banks.t[2 * i + j]) for j in range(2)] for i in range(2)]
        kg = Banks.__new__(Banks)
        kg.b = [banks.b[i] for i in (4, 5, 6)]; kg.t = [banks.t[i] for i in (4, 5, 6)]; kg.i = 0; kg.n = 3

        def phase_a(n):
            tsl = slice(n * 128, (n + 1) * 128)
            s = n % 2
            eo = eo_banks[n % 2]
            for h in range(4):
                hp = (h % 2) * 64; c = h // 2
                bk, t_bk = eo[h % 2]
                P.add("pe", lambda e, bk=bk, hp=hp, c=c, tsl=tsl: e.matmul(
                    bk[:, c * 128:(c + 1) * 128], lhsT=rkT[hp:hp + 64, c, tsl], rhs=rqT[hp:hp + 64, c, tsl], start=True, stop=True),
                    reads=[t_rkT[c][n // 4], t_rqT[c][n // 4]], writes=[t_bk])
            for par in range(2):
                bk, t_bk = eo[par]
                P.add("dve", lambda e, bk=bk, s=s, par=par: e.tensor_tensor(
                    out=PdT[s][:, par * 256:(par + 1) * 256], in0=bk[:, 0:256], in1=decT[:, par * 256:(par + 1) * 256], op=ALU.mult),
                    reads=[t_bk, t_dec], writes=[t_PdT[s]])
            for h in range(4):
                hp = (h % 2) * 64; c = h // 2
                pos = (h % 2) * 2 + c
                bk, t_bk = eo[h % 2]
                P.add("pe", lambda e, bk=bk, h=h, c=c, pos=pos, s=s, n=n: e.matmul(
                    bk[:, 256 + c * 128:256 + (c + 1) * 128], lhsT=PdT[s][:, pos * 128:(pos + 1) * 128], rhs=rv[:, n, h * 128:(h + 1) * 128],
                    start=True, stop=(n == 0)), reads=[t_PdT[s], t_rv[n]], writes=[t_bk])
                if n > 0:
                    P.add("pe", lambda e, bk=bk, hp=hp, c=c, tsl=tsl: e.matmul(
                        bk[:, 256 + c * 128:256 + (c + 1) * 128], lhsT=qdT[hp:hp + 64, c, tsl], rhs=state_bf[hp:hp + 64, c, :],
                        start=False, stop=True), reads=[t_qdT[n], t_state_bf], writes=[t_bk])
            if n < 15:
                bK, t_bK = kg.next()
                for h in range(4):
                    hp = (h % 2) * 64; c = h // 2
                    P.add("pe", lambda e, bK=bK, h=h, hp=hp, c=c, n=n: e.matmul(
                        bK[hp:hp + 64, c * 128:(c + 1) * 128], lhsT=kd[:, n, h * 64:(h + 1) * 64], rhs=rv[:, n, h * 128:(h + 1) * 128],
                        start=True, stop=True), reads=[t_kd[n], t_rv[n]], writes=[t_bK])
                for h in range(4):
                    hp = (h % 2) * 64; c = h // 2
                    if n == 0:
                        P.add("dve", lambda e, bK=bK, hp=hp, c=c: e.tensor_copy(out=state[hp:hp + 64, c, :], in_=bK[hp:hp + 64, c * 128:(c + 1) * 128]),
                              reads=[t_bK], writes=[t_state])
                    else:
                        cd = float(np.float32(np.exp(np.float32(128.0) * np.log(np.float32(GAMMAS[h])))))
                        P.add("dve", lambda e, bK=bK, hp=hp, c=c, cd=cd: e.scalar_tensor_tensor(
                            out=state[hp:hp + 64, c, :], in0=state[hp:hp + 64, c, :], scalar=cd, in1=bK[hp:hp + 64, c * 128:(c + 1) * 128],
                            op0=ALU.mult, op1=ALU.add), reads=[t_bK, t_state], writes=[t_state])
                P.add("act", lambda e: e.activation(out=state_bf[:], in_=state[:], func=AF.Copy), reads=[t_state], writes=[t_state_bf])
            bG, t_bG = kg.next()
            for kc in range(8):
                P.add("pe", lambda e, bG=bG, kc=kc, tsl=tsl: e.matmul(
                    bG[:], lhsT=x1T[:, kc, tsl], rhs=wrg[:, kc, :], start=(kc == 0), stop=(kc == 7)), reads=t_wrg, writes=[t_bG])
            P.add("act", lambda e, bG=bG, s=s: e.activation(out=sg[s][:], in_=bG[:], func=AF.Silu), reads=[t_bG], writes=[t_sg[s]])

        def phase_b(n):
            tsl = slice(n * 128, (n + 1) * 128)
            s = n % 2
            eo = eo_banks[n % 2]
            osl = lambda h: slice(256 + (h // 2) * 128, 256 + (h // 2 + 1) * 128)
            for h in range(4):
                bk, t_bk = eo[h % 2]
                P.add("dve", lambda e, bk=bk, h=h, s=s: e.bn_stats(out=stats[s][:, h, :], in_=bk[:, osl(h)]),
                      reads=[t_bk], writes=[t_stats[s]])
            for h in range(4):
                P.add("dve", lambda e, h=h, s=s: e.bn_aggr(out=mvv[s][:, h, :], in_=stats[s][:, h, :]), reads=[t_stats[s]], writes=[t_mvv[s]])
            P.add("pool", lambda e, s=s: e.tensor_scalar(out=sd[s][:], in0=mvv[s][:, :, 1], scalar1=LN_EPS, scalar2=None, op0=ALU.add),
                  reads=[t_mvv[s]], writes=[t_sd[s]])
            P.add("pool", lambda e, s=s: e.tensor_tensor(out=rstd[s][:], in0=sd[s][:], in1=neghalf[:], op=ALU.pow),
                  reads=[t_sd[s], t_neghalf], writes=[t_rstd[s]])
            for h in range(4):
                bk, t_bk = eo[h % 2]
                P.add("dve", lambda e, bk=bk, h=h, s=s: e.tensor_scalar(
                    out=on[s][:, h * 128:(h + 1) * 128], in0=bk[:, osl(h)], scalar1=mvv[s][:, h, 0:1],
                    scalar2=rstd[s][:, h:h + 1], op0=ALU.subtract, op1=ALU.mult),
                    reads=[t_bk, t_mvv[s], t_rstd[s]], writes=[t_on[s]])
            P.add("dve", lambda e, s=s: e.tensor_tensor(out=on[s][:], in0=on[s][:], in1=gng[:], op=ALU.mult),
                  reads=[t_on[s], t_gng], writes=[t_on[s]])
            P.add("dve", lambda e, s=s: e.tensor_tensor(out=on[s][:], in0=on[s][:], in1=gnb[:], op=ALU.add),
                  reads=[t_on[s], t_gnb], writes=[t_on[s]])
            P.add("dve", lambda e, s=s: e.tensor_tensor(out=orb[s][:], in0=on[s][:], in1=sg[s][:], op=ALU.mult),
                  reads=[t_on[s], t_sg[s]], writes=[t_orb[s]])
            for c4 in range(4):
                P.add("pe", lambda e, c4=c4, s=s: e.transpose(out=pM[:, c4 * 128:(c4 + 1) * 128], in_=orb[s][:, c4 * 128:(c4 + 1) * 128], identity=idb[:]),
                      reads=[t_orb[s], t_idb], writes=[t_pM])
            P.add("act", lambda e, tsl=tsl: e.activation(out=oT[:, :, tsl], in_=pM[:, 0:512].rearrange("p (a b) -> p a b", a=4), func=AF.Copy),
                  reads=[t_pM], writes=[Tok()])

        phase_a(0)
        for n in range(16):
            if n + 1 < 16:
                phase_a(n + 1)
            phase_b(n)
        env.P.emit("m2")


def merge_stage(nc, semstack, dmaq, x1T, oTs, w_in, wbrs, wo_d, g_d, b_d, src_d, dst_d):
    with contextlib.ExitStack() as st:
        env = Env(nc, semstack, dmaq, "m4", st)
        P = env.P
        banks = Banks(env, 8)
        mT = env.sb("mT", [128, 8, L], BF16); t_mT = [toks(4) for _ in range(8)]
        wout = env.sb("wout", [128, 8, D], BF16); t_wout = toks(8)
        sgm = [env.sb(f"sgm{i}", [128, 512], F32) for i in range(2)]; t_sgm = toks(2)
        tmp = [env.sb(f"tmp{i}", [128, 512], F32) for i in range(2)]; t_tmp = toks(2)
        acc = env.sb("acc", [128, L], F32); t_acc = toks(4)
        xs = [env.sb(f"xs{i}", [128, D], F32) for i in range(2)]; t_xs = toks(2)
        rr = [env.sb(f"rr{i}", [128, D], F32) for i in range(2)]; t_rr = toks(2)
        oo = [env.sb(f"oo{i}", [128, D], F32) for i in range(2)]; t_oo = toks(2)
        gam = env.sb("gam", [128, D], F32); t_gam = Tok()
        bet = env.sb("bet", [128, D], F32); t_bet = Tok()
        epst = env.sb("epst", [128, 1], F32); t_eps = Tok()
        stats = [env.sb(f"stats{i}", [128, 2, 6], F32) for i in range(2)]; t_stats = toks(2)
        mv = [env.sb(f"mv{i}", [128, 2], F32) for i in range(2)]; t_mv = toks(2)
        sd = [env.sb(f"sd{i}", [128, 1], F32) for i in range(2)]; t_sd = toks(2)
        rstd = [env.sb(f"rstd{i}", [128, 1], F32) for i in range(2)]; t_rstd = toks(2)
        t_dst = toks(16)
        ws = WStream(env, nst=2, nbf=6)
        P.add("sync", lambda e: e.dma_start(out=gam[:], in_=bcast_rows(g_d, 128)), writes=[t_gam], kind="d")
        P.add("sync", lambda e: e.dma_start(out=bet[:], in_=bcast_rows(b_d, 128)), writes=[t_bet], kind="d")
        P.add("dve", lambda e: e.memset(epst[:], LN_EPS), writes=[t_eps])
        cM = {"s": 0}
        for j in range(8):
            ws.load([(0, wo_d[:, j * 128:(j + 1) * 128], 1.0)], dst=lambda c0, w, j=j: wout[:, :, j * 128 + c0:j * 128 + c0 + w], t_dst=t_wout[j])
            for b in range(3):
                wgb, t_wgb = ws.load([(0, w_in[:, C_G + b * D + j * 128:C_G + b * D + (j + 1) * 128], 1.0)])
                wbb, t_wbb = ws.load([(0, wbrs[b][:, j * 128:(j + 1) * 128], 1.0)], KC=4)
                for tb in range(4):
                    tsl = slice(tb * 512, (tb + 1) * 512)
                    bG, t_bG = banks.next()
                    for kc in range(8):
                        P.add("pe", lambda e, bG=bG, wgb=wgb, kc=kc, tsl=tsl: e.matmul(
                            bG[:], lhsT=wgb[:, kc, :], rhs=x1T[:, kc, tsl], start=(kc == 0), stop=(kc == 7)),
                            reads=[t_wgb], writes=[t_bG])
                    bB, t_bB = banks.next()
                    for kc in range(4):
                        P.add("pe", lambda e, bB=bB, wbb=wbb, kc=kc, tsl=tsl, b=b: e.matmul(
                            bB[:], lhsT=wbb[:, kc, :], rhs=oTs[b][:, kc, tsl], start=(kc == 0), stop=(kc == 3)),
                            reads=[t_wbb], writes=[t_bB])
                    s = cM["s"] % 2
                    cM["s"] += 1
                    P.add("act", lambda e, bG=bG, s=s: e.activation(out=sgm[s][:], in_=bG[:], func=AF.Sigmoid), reads=[t_bG], writes=[t_sgm[s]])
                    if b == 0:
                        P.add("dve", lambda e, bB=bB, s=s, tsl=tsl: e.tensor_tensor(out=acc[:, tsl], in0=sgm[s][:], in1=bB[:], op=ALU.mult),
                              reads=[t_sgm[s], t_bB], writes=[t_acc[tb]])
                    else:
                        P.add("dve", lambda e, bB=bB, s=s: e.tensor_tensor(out=tmp[s][:], in0=sgm[s][:], in1=bB[:], op=ALU.mult),
                              reads=[t_sgm[s], t_bB], writes=[t_tmp[s]])
                        if b == 1:
                            P.add("dve", lambda e, s=s, tsl=tsl: e.tensor_tensor(out=acc[:, tsl], in0=acc[:, tsl], in1=tmp[s][:], op=ALU.add),
                                  reads=[t_acc[tb], t_tmp[s]], writes=[t_acc[tb]])
                        else:
                            P.add("dve", lambda e, s=s, j=j, tsl=tsl: e.tensor_tensor(out=mT[:, j, tsl], in0=acc[:, tsl], in1=tmp[s][:], op=ALU.add),
                                  reads=[t_acc[tb], t_tmp[s]], writes=[t_mT[j][tb]])
        for t in range(16):
            s = t % 2
            tsl = slice(t * 128, (t + 1) * 128)
            P.add("sync", lambda e, s=s, tsl=tsl: e.dma_start(out=xs[s][:], in_=src_d[tsl, :]), writes=[t_xs[s]], kind="d")
            for nh in range(2):
                bank, t_bank = banks.next()
                for kc in range(8):
                    P.add("pe", lambda e, bank=bank, kc=kc, nh=nh, tsl=tsl: e.matmul(
                        bank[:], lhsT=mT[:, kc, tsl], rhs=wout[:, kc, nh * 512:(nh + 1) * 512], start=(kc == 0), stop=(kc == 7)),
                        reads=[t_mT[kc][t // 4]] + t_wout[nh * 4:(nh + 1) * 4], writes=[t_bank])
                P.add("dve", lambda e, bank=bank, nh=nh, s=s: e.scalar_tensor_tensor(
                    out=rr[s][:, nh * 512:(nh + 1) * 512], in0=xs[s][:, nh * 512:(nh + 1) * 512], scalar=ALPHA, in1=bank[:],
                    op0=ALU.mult, op1=ALU.add), reads=[t_xs[s], t_bank], writes=[t_rr[s]])
            for k in range(2):
                P.add("dve", lambda e, s=s, k=k: e.bn_stats(out=stats[s][:, k, :], in_=rr[s][:, k * 512:(k + 1) * 512]),
                      reads=[t_rr[s]], writes=[t_stats[s]])
            P.add("dve", lambda e, s=s: e.bn_aggr(out=mv[s][:], in_=stats[s][:].rearrange("p a b -> p (a b)")),
                  reads=[t_stats[s]], writes=[t_mv[s]])
            P.add("act", lambda e, s=s: e.activation(out=sd[s][:], in_=mv[s][:, 1:2], func=AF.Sqrt, bias=epst[:, 0:1], scale=1.0),
                  reads=[t_mv[s], t_eps], writes=[t_sd[s]])
            P.add("dve", lambda e, s=s: e.reciprocal(out=rstd[s][:], in_=sd[s][:]), reads=[t_sd[s]], writes=[t_rstd[s]])
            P.add("dve", lambda e, s=s: e.tensor_scalar(
                out=rr[s][:], in0=rr[s][:], scalar1=mv[s][:, 0:1], scalar2=rstd[s][:, 0:1], op0=ALU.subtract, op1=ALU.mult),
                reads=[t_rr[s], t_mv[s], t_rstd[s]], writes=[t_rr[s]])
            P.add("pool", lambda e, s=s: e.tensor_tensor(out=oo[s][:], in0=rr[s][:], in1=gam[:], op=ALU.mult),
                  reads=[t_rr[s], t_gam], writes=[t_oo[s]])
            P.add("pool", lambda e, s=s: e.tensor_tensor(out=oo[s][:], in0=oo[s][:], in1=bet[:], op=ALU.add),
                  reads=[t_oo[s], t_bet], writes=[t_oo[s]])
            P.add("pool", lambda e, s=s, tsl=tsl: e.dma_start(out=dst_d[tsl, :], in_=oo[s][:]),
                  reads=[t_oo[s]], writes=[t_dst[t]], kind="d")
        P.wait_all("pool", t_dst)
        env.P.emit("m4")

def build_nc(stages=("ffn1", "mix", "ffn2"), dbg_mix=False, parts=("dsa", "ret", "mem", "merge")):
    nc = bass.Bass("TRN2", target_bir_lowering=False)
    din = lambda n, shape: nc.dram_tensor(n, shape, F32, kind="ExternalInput").ap()
    x_d = din("x", [L, D])
    mem_d = din("mem", [256, D])
    f1wi = din("ffn1_w_in", [D, 2 * DFF]); f1wo = din("ffn1_w_out", [DFF, D])
    ln1g = din("ln1_g", [1, D]); ln1b = din("ln1_b", [1, D])
    w_in = din("w_in", [D, W_IN_COLS]); t5 = din("t5_table", [32, 8])
    gng = din("ret_gn_g", [1, 512]); gnb = din("ret_gn_b", [1, 512])
    wmkv = din("w_mem_kv", [D, 1024])
    wbr = din("w_br_ret", [512, D]); wbd = din("w_br_dsa", [512, D]); wbm = din("w_br_mem", [512, D])
    wo = din("w_out", [D, D]); ln2g = din("ln2_g", [1, D]); ln2b = din("ln2_b", [1, D])
    f2wi = din("ffn2_w_in", [D, 2 * DFF]); f2wo = din("ffn2_w_out", [DFF, D])
    ln3g = din("ln3_g", [1, D]); ln3b = din("ln3_b", [1, D])
    ident = din("c_ident", [128, 128])
    cJ = din("c_J", [128, 128]); coh = din("c_oh", [32, 384]); ccausal = din("c_causal", [128, 128])
    cpow2 = din("c_pow2", [128, NIT])
    ccos = din("c_cos", [128, L]); csin = din("c_sin", [128, L])
    cdecay = din("c_decay", [128, 512]); ckdec = din("c_kdec", [128, 256]); cqdec = din("c_qdec", [128, 256])
    gscr = nc.dram_tensor("g_scr", [8, 384], F32).ap()
    out_d = nc.dram_tensor("out", [L, D], F32, kind="ExternalOutput").ap()
    x1_d = nc.dram_tensor("x1_scr", [L, D], F32).ap()
    x2_d = nc.dram_tensor("x2_scr", [L, D], F32).ap()
    with contextlib.ExitStack() as semstack:
        dmaq = {}
        cur = x_d
        for i, s in enumerate(stages):
            last = i == len(stages) - 1
            if s == "ffn1":
                dst = out_d if last else x1_d
                ffn_stage(nc, semstack, dmaq, "f1", cur, f1wi, f1wo, ln1g, ln1b, dst, ident)
                cur = dst
            elif s == "mix":
                dst = out_d if last else x2_d
                with contextlib.ExitStack() as outer:
                    x1T = outer.enter_context(nc.sbuf_tensor("x1T", [128, 8, L], BF16))
                    mix_load_stage(nc, semstack, dmaq, cur, ident, x1T)
                    o_dsaT = outer.enter_context(nc.sbuf_tensor("o_dsaT", [128, 4, L], BF16))
                    if "dsa" in parts:
                        dsa_stage(nc, semstack, dmaq, x1T, o_dsaT, w_in, t5, ident, cJ, coh, ccausal, cpow2, gscr)
                    o_retT = outer.enter_context(nc.sbuf_tensor("o_retT", [128, 4, L], BF16))
                    if "ret" in parts:
                        ret_stage(nc, semstack, dmaq, x1T, o_retT, w_in, gng, gnb, ident, ccos, csin, cdecay, ckdec, cqdec)
                    o_memT = outer.enter_context(nc.sbuf_tensor("o_memT", [128, 4, L], BF16))
                    if "mem" in parts:
                        mem_stage(nc, semstack, dmaq, x1T, o_memT, mem_d, w_in, wmkv, ident)
                    if dbg_mix:
                        dbg = nc.dram_tensor("dbg", [128, 12, L], BF16, kind="ExternalOutput").ap()
                        with contextlib.ExitStack() as st:
                            env = Env(nc, semstack, dmaq, "dbg", st)
                            tt = toks(3)
                            for i, (o, pn) in enumerate(((o_retT, "ret"), (o_dsaT, "dsa"), (o_memT, "mem"))):
                                if pn not in parts:
                                    continue
                                env.P.add("sync", lambda e, i=i, o=o: e.dma_start(out=dbg[:, 4 * i:4 * i + 4, :], in_=o[:]), writes=[tt[i]], kind="d")
                            env.P.wait_all("sync", tt)
                            env.P.emit("dbg")
                    if "merge" in parts:
                        merge_stage(nc, semstack, dmaq, x1T, [o_retT, o_dsaT, o_memT], w_in, [wbr, wbd, wbm], wo, ln2g, ln2b, cur, dst)
                cur = dst
            elif s == "mixdbg":
                with contextlib.ExitStack() as outer:
                    x1T = outer.enter_context(nc.sbuf_tensor("x1T", [128, 8, L], BF16))
                    mix_load_stage(nc, semstack, dmaq, cur, ident, x1T)
                    o_dsaT = outer.enter_context(nc.sbuf_tensor("o_dsaT", [128, 4, L], BF16))
                    dsa_stage(nc, semstack, dmaq, x1T, o_dsaT, w_in, t5, ident, cJ, coh, ccausal, cpow2, gscr)
                    o_memT = outer.enter_context(nc.sbuf_tensor("o_memT", [128, 4, L], BF16))
                    mem_stage(nc, semstack, dmaq, x1T, o_memT, mem_d, w_in, wmkv, ident)
                    dbg = nc.dram_tensor("dbg", [128, 8, L], BF16, kind="ExternalOutput").ap()
                    with contextlib.ExitStack() as st:
                        env = Env(nc, semstack, dmaq, "dbg", st)
                        t1 = Tok(); t2 = Tok()
                        env.P.add("sync", lambda e: e.dma_start(out=dbg[:, 0:4, :], in_=o_dsaT[:]), writes=[t1], kind="d")
                        env.P.add("sync", lambda e: e.dma_start(out=dbg[:, 4:8, :], in_=o_memT[:]), writes=[t2], kind="d")
                        env.P.wait_all("sync", [t1, t2])
                        env.P.emit("dbg")
            elif s == "ffn2":
                dst = out_d if last else x2_d
                ffn_stage(nc, semstack, dmaq, "f2", cur, f2wi, f2wo, ln3g, ln3b, dst, ident)
                cur = dst
    return nc


_CACHE = {}


def _t5_bucket(n):
    n = np.maximum(n, 0)
    nf = np.maximum(n, 1).astype(np.float32)
    large = 16 + (np.log(nf / np.float32(16)) / np.float32(np.log(128 / 16)) * np.float32(16)).astype(np.int32)
    large = np.minimum(large, 31)
    return np.where(n < 16, n, large)


def make_consts():
    c = {}
    c["c_J"] = np.ascontiguousarray(np.eye(128, dtype=np.float32)[::-1])
    oh = np.zeros((32, 384), np.float32)
    for u in range(383):
        d = u - 127
        if d >= 0:
            oh[_t5_bucket(np.array(d)), u] += 1.0
            oh[31, u] -= 1.0
    c["c_oh"] = oh
    q = np.arange(128)[:, None]; sk = np.arange(128)[None, :]
    c["c_causal"] = np.where(sk <= q, 0.0, -1.0e30).astype(np.float32)
    half = 32
    freqs = (np.float32(10000.0) ** (-np.arange(half, dtype=np.float32) / np.float32(half))).astype(np.float32)
    ang = np.arange(L, dtype=np.float32)[None, :] * freqs[np.arange(128) % 32][:, None]
    c["c_cos"] = np.cos(ang).astype(np.float32)
    c["c_sin"] = np.sin(ang).astype(np.float32)
    lg = np.log(np.array(GAMMAS, dtype=np.float32))
    i = np.arange(128)
    dec = np.zeros((128, 4, 128), np.float32)
    for h in range(4):
        diff = i[None, :] - i[:, None]
        dec[:, h, :] = np.where(diff >= 0, np.exp(np.maximum(diff, 0).astype(np.float32) * lg[h]), 0.0)
    c["c_decay"] = np.ascontiguousarray(dec[:, [0, 2, 1, 3], :]).reshape(128, 512)
    kdec = np.zeros((128, 4, 64), np.float32)
    for h in range(4):
        kdec[:, h, :] = np.exp((127 - i).astype(np.float32) * lg[h])[:, None]
    c["c_kdec"] = kdec.reshape(128, 256)
    qdec = np.zeros((128, 2, 128), np.float32)
    for p in range(128):
        for cc in range(2):
            qdec[p, cc, :] = np.exp((i + 1).astype(np.float32) * lg[2 * cc + p // 64])
    c["c_qdec"] = qdec.reshape(128, 256)
    c["c_pow2"] = np.tile((2.0 ** -(np.arange(NIT) + 1.0)).astype(np.float32)[None, :], (128, 1))
    return c


def make_in_maps(inputs):
    f = lambda a: np.ascontiguousarray(np.asarray(a, dtype=np.float32))
    shared = {
        "ffn1_w_in": f(inputs["ffn1_w_in"][0]), "ffn1_w_out": f(inputs["ffn1_w_out"][0]),
        "ln1_g": f(inputs["ln1_g"]), "ln1_b": f(inputs["ln1_b"]),
        "w_in": f(inputs["w_in"][0]), "t5_table": f(inputs["t5_table"]),
        "ret_gn_g": f(inputs["ret_gn_g"]), "ret_gn_b": f(inputs["ret_gn_b"]),
        "w_mem_kv": f(inputs["w_mem_kv"][0]),
        "w_br_ret": f(inputs["w_br_ret"][0]), "w_br_dsa": f(inputs["w_br_dsa"][0]), "w_br_mem": f(inputs["w_br_mem"][0]),
        "w_out": f(inputs["w_out"][0]), "ln2_g": f(inputs["ln2_g"]), "ln2_b": f(inputs["ln2_b"]),
        "ffn2_w_in": f(inputs["ffn2_w_in"][0]), "ffn2_w_out": f(inputs["ffn2_w_out"][0]),
        "ln3_g": f(inputs["ln3_g"]), "ln3_b": f(inputs["ln3_b"]),
        "c_ident": np.eye(128, dtype=np.float32),
    }
    shared.update(make_consts())
    x = f(inputs["x"])
    mem = f(inputs["mem"])
    return [dict(shared, x=x[b], mem=mem[b]) for b in range(x.shape[0])]


def kernel(**inputs):
    if "nc" not in _CACHE:
        _CACHE["nc"] = build_nc()
    nc = _CACHE["nc"]
    in_maps = make_in_maps(inputs)
    res = run_bass_kernel_spmd(nc, in_maps, core_ids=list(range(len(in_maps))))
    return np.stack([np.asarray(r["out"], dtype=np.float32) for r in res.results], axis=0)
```
